# Optimizing a Trainium2 kernel written in Bass

```python
import math
import jax
import jax.numpy as jnp
from jax import lax
import numpy as np

D_MODEL = 2048
BATCH = 4
SEQ = 2048
DEPTH = 2

PLE_DIM = 256
D_FF = 5632
RMS_EPS = 1e-6
D_SSM = D_MODEL // 2
SSM_GROUP = 16
SSM_GROUPS = D_SSM // SSM_GROUP
SSM_STATE = 64
D_ATTN = D_MODEL // 2
HEAD_DIM = 128
N_HEADS = D_ATTN // HEAD_DIM
ROT_DIM = HEAD_DIM // 4
ROPE_THETA = 500000.0
MOBA_BLOCK = 256
MOBA_TOP_K = 3
Q_CHUNK = 64
D_IN = D_SSM + 3 * D_ATTN + 2 * D_MODEL

kernel_name = "hybrid_s5_moba_macaron_gated"


def rms_norm(x, g):
    x32 = x.astype(jnp.float32)
    y = x32 * lax.rsqrt(jnp.mean(x32 * x32, axis=-1, keepdims=True) + RMS_EPS)
    return (y * g.astype(jnp.float32)).astype(x.dtype)


def swiglu(x, w_gate, w_up, w_down):
    return (jax.nn.silu(x @ w_gate) * (x @ w_up)) @ w_down


def rotary_tables(seq_len):
    pos = jnp.arange(seq_len, dtype=jnp.float32)
    inv_freq = 1.0 / (ROPE_THETA ** (jnp.arange(0, ROT_DIM, 2, dtype=jnp.float32) / ROT_DIM))
    ang = pos[:, None] * inv_freq[None, :]
    return jnp.cos(ang), jnp.sin(ang)


def apply_partial_rotary(x, cos, sin):
    half = ROT_DIM // 2
    cos = cos.astype(x.dtype)
    sin = sin.astype(x.dtype)
    x1 = x[..., :half]
    x2 = x[..., half:ROT_DIM]
    return jnp.concatenate([x1 * cos - x2 * sin, x2 * cos + x1 * sin, x[..., ROT_DIM:]], axis=-1)


def split_heads(t):
    bsz, seq_len, _ = t.shape
    return t.reshape(bsz, seq_len, N_HEADS, HEAD_DIM).transpose(0, 2, 1, 3)


def _complex_scan_combine(left, right):
    a1r, a1i, b1r, b1i = left
    a2r, a2i, b2r, b2i = right
    return (a1r * a2r - a1i * a2i,
            a1r * a2i + a1i * a2r,
            a2r * b1r - a2i * b1i + b2r,
            a2r * b1i + a2i * b1r + b2i)


def s5_ssm(u, a_re, a_im, log_dt, b_re, b_im, c_re, c_im, d_skip):
    f32 = jnp.float32
    bsz, seq_len, _ = u.shape
    u32 = u.astype(f32).reshape(bsz, seq_len, SSM_GROUPS, SSM_GROUP)
    a_re = a_re.astype(f32)
    a_im = a_im.astype(f32)
    dt = jnp.exp(log_dt.astype(f32))[:, None]
    mag = jnp.exp(a_re * dt)
    lam_re = mag * jnp.cos(a_im * dt)
    lam_im = mag * jnp.sin(a_im * dt)
    den = a_re * a_re + a_im * a_im
    num_re = lam_re - 1.0
    coef_re = (num_re * a_re + lam_im * a_im) / den
    coef_im = (lam_im * a_re - num_re * a_im) / den
    bu_re = jnp.einsum('blgp,gnp->blgn', u32, b_re.astype(f32))
    bu_im = jnp.einsum('blgp,gnp->blgn', u32, b_im.astype(f32))
    in_re = coef_re * bu_re - coef_im * bu_im
    in_im = coef_re * bu_im + coef_im * bu_re
    lam_re_t = jnp.broadcast_to(lam_re, in_re.shape)
    lam_im_t = jnp.broadcast_to(lam_im, in_im.shape)
    _, _, s_re, s_im = lax.associative_scan(
        _complex_scan_combine, (lam_re_t, lam_im_t, in_re, in_im), axis=1)
    y = (jnp.einsum('blgn,gpn->blgp', s_re, c_re.astype(f32))
         - jnp.einsum('blgn,gpn->blgp', s_im, c_im.astype(f32)))
    y = y.reshape(bsz, seq_len, D_SSM) + d_skip.astype(f32) * u32.reshape(bsz, seq_len, D_SSM)
    return y.astype(u.dtype)


def _gather_blocks(xb, ids):
    return jax.vmap(jax.vmap(lambda t, i: t[i]))(xb, ids)


def moba_attention(q, k, v):
    f32 = jnp.float32
    bsz, n_heads, seq_len, hd = q.shape
    nb = -(-seq_len // MOBA_BLOCK)
    pad = nb * MOBA_BLOCK - seq_len
    kb = jnp.pad(k, ((0, 0), (0, 0), (0, pad), (0, 0))).reshape(bsz, n_heads, nb, MOBA_BLOCK, hd)
    vb = jnp.pad(v, ((0, 0), (0, 0), (0, pad), (0, 0))).reshape(bsz, n_heads, nb, MOBA_BLOCK, hd)
    scale = 1.0 / math.sqrt(hd)
    k_mean = jnp.mean(kb.astype(f32), axis=3)
    gate = jnp.einsum('bhtd,bhnd->bhtn', q.astype(f32), k_mean)
    q_blk = jnp.arange(seq_len) // MOBA_BLOCK
    past = jnp.arange(nb)[None, :] < q_blk[:, None]
    gate = jnp.where(past, gate, -jnp.inf)
    n_sel = min(MOBA_TOP_K, nb)
    _, sel = lax.top_k(gate, n_sel)
    sel_valid = jnp.arange(n_sel)[None, :] < q_blk[:, None]

    def one_chunk(c):
        start = c * Q_CHUNK
        qc = lax.dynamic_slice_in_dim(q, start, Q_CHUNK, axis=2)
        ids = lax.dynamic_slice_in_dim(sel, start, Q_CHUNK, axis=2)
        ok = lax.dynamic_slice_in_dim(sel_valid, start, Q_CHUNK, axis=0)
        own = start // MOBA_BLOCK
        k_own = lax.dynamic_index_in_dim(kb, own, axis=2, keepdims=False)
        v_own = lax.dynamic_index_in_dim(vb, own, axis=2, keepdims=False)
        q_pos = start + jnp.arange(Q_CHUNK)
        k_pos = own * MOBA_BLOCK + jnp.arange(MOBA_BLOCK)
        s_own = jnp.einsum('bhqd,bhkd->bhqk', qc, k_own).astype(f32) * scale
        scores = [jnp.where(k_pos[None, :] <= q_pos[:, None], s_own, -jnp.inf)]
        for slot in range(n_sel):
            kg = _gather_blocks(kb, ids[..., slot])
            s = jnp.einsum('bhqd,bhqkd->bhqk', qc, kg).astype(f32) * scale
            scores.append(jnp.where(ok[:, slot, None], s, -jnp.inf))
        probs = jax.nn.softmax(jnp.concatenate(scores, axis=-1), axis=-1).astype(v.dtype)
        out = jnp.einsum('bhqk,bhkd->bhqd', probs[..., :MOBA_BLOCK], v_own)
        for slot in range(n_sel):
            vg = _gather_blocks(vb, ids[..., slot])
            lo = (slot + 1) * MOBA_BLOCK
            out = out + jnp.einsum('bhqk,bhqkd->bhqd', probs[..., lo:lo + MOBA_BLOCK], vg)
        return out

    outs = lax.map(one_chunk, jnp.arange(seq_len // Q_CHUNK))
    return outs.transpose(1, 2, 0, 3, 4).reshape(bsz, n_heads, seq_len, hd)


def setup_inputs(seed: int = 0) -> dict:
    key = jax.random.key(seed)
    ks = jax.random.split(key, 32)
    f32 = jnp.float32

    def lin(k, shape):
        return jax.random.normal(k, shape, f32) * (shape[-2] ** -0.5)

    def gain(k, shape):
        return 1.0 + 0.02 * jax.random.normal(k, shape, f32)

    n_idx = jnp.arange(SSM_STATE, dtype=f32)
    return {
        "x": jax.random.normal(ks[0], (BATCH, SEQ, D_MODEL), f32),
        "p": jax.random.normal(ks[1], (DEPTH, BATCH, SEQ, PLE_DIM), f32),
        "ffn1_norm": gain(ks[2], (DEPTH, D_MODEL)),
        "ffn1_w_gate": lin(ks[3], (DEPTH, D_MODEL, D_FF)),
        "ffn1_w_up": lin(ks[4], (DEPTH, D_MODEL, D_FF)),
        "ffn1_w_down": lin(ks[5], (DEPTH, D_FF, D_MODEL)),
        "mix_norm": gain(ks[6], (DEPTH, D_MODEL)),
        "w_in": lin(ks[7], (DEPTH, D_MODEL, D_IN)),
        "ssm_a_re": -0.5 + 0.01 * jax.random.normal(ks[8], (DEPTH, SSM_GROUPS, SSM_STATE), f32),
        "ssm_a_im": math.pi * n_idx + 0.01 * jax.random.normal(ks[9], (DEPTH, SSM_GROUPS, SSM_STATE), f32),
        "ssm_log_dt": jax.random.uniform(ks[10], (DEPTH, SSM_GROUPS), f32,
                                         minval=math.log(1e-3), maxval=math.log(1e-1)),
        "ssm_b_re": jax.random.normal(ks[11], (DEPTH, SSM_GROUPS, SSM_STATE, SSM_GROUP), f32) * (2 * SSM_GROUP) ** -0.5,
        "ssm_b_im": jax.random.normal(ks[12], (DEPTH, SSM_GROUPS, SSM_STATE, SSM_GROUP), f32) * (2 * SSM_GROUP) ** -0.5,
        "ssm_c_re": jax.random.normal(ks[13], (DEPTH, SSM_GROUPS, SSM_GROUP, SSM_STATE), f32) * (2 * SSM_STATE) ** -0.5,
        "ssm_c_im": jax.random.normal(ks[14], (DEPTH, SSM_GROUPS, SSM_GROUP, SSM_STATE), f32) * (2 * SSM_STATE) ** -0.5,
        "ssm_d": jax.random.normal(ks[15], (DEPTH, D_SSM), f32),
        "ssm_w_glu": lin(ks[16], (DEPTH, D_SSM, 2 * D_SSM)),
        "w_branch_ssm": lin(ks[17], (DEPTH, D_SSM, D_MODEL)),
        "w_branch_attn": lin(ks[18], (DEPTH, D_ATTN, D_MODEL)),
        "w_out": lin(ks[19], (DEPTH, D_MODEL, D_MODEL)),
        "ffn2_norm": gain(ks[20], (DEPTH, D_MODEL)),
        "ffn2_w_gate": lin(ks[21], (DEPTH, D_MODEL, D_FF)),
        "ffn2_w_up": lin(ks[22], (DEPTH, D_MODEL, D_FF)),
        "ffn2_w_down": lin(ks[23], (DEPTH, D_FF, D_MODEL)),
        "ple_norm": gain(ks[24], (DEPTH, D_MODEL)),
        "ple_w_up": lin(ks[25], (DEPTH, PLE_DIM, D_MODEL)),
        "ple_w_gate": lin(ks[26], (DEPTH, D_MODEL, D_MODEL)),
        "final_norm": gain(ks[27], (D_MODEL,)),
    }


def reference(x, p, ffn1_norm, ffn1_w_gate, ffn1_w_up, ffn1_w_down, mix_norm, w_in,
              ssm_a_re, ssm_a_im, ssm_log_dt, ssm_b_re, ssm_b_im, ssm_c_re, ssm_c_im, ssm_d,
              ssm_w_glu, w_branch_ssm, w_branch_attn, w_out, ffn2_norm, ffn2_w_gate, ffn2_w_up,
              ffn2_w_down, ple_norm, ple_w_up, ple_w_gate, final_norm):
    bsz, seq_len, _ = x.shape
    cos, sin = rotary_tables(seq_len)
    cuts = [D_SSM, D_SSM + D_ATTN, D_SSM + 2 * D_ATTN, D_SSM + 3 * D_ATTN,
            D_SSM + 3 * D_ATTN + D_MODEL]
    h = x
    for i in range(DEPTH):
        h = h + 0.5 * swiglu(rms_norm(h, ffn1_norm[i]), ffn1_w_gate[i], ffn1_w_up[i], ffn1_w_down[i])
        u = rms_norm(h, mix_norm[i])
        z = u @ w_in[i]
        z_ssm, q, k, v, g_ssm, g_attn = jnp.split(z, cuts, axis=-1)
        y_a = jax.nn.gelu(s5_ssm(z_ssm, ssm_a_re[i], ssm_a_im[i], ssm_log_dt[i], ssm_b_re[i],
                                 ssm_b_im[i], ssm_c_re[i], ssm_c_im[i], ssm_d[i]))
        glu = y_a @ ssm_w_glu[i]
        y_a = glu[..., :D_SSM] * jax.nn.sigmoid(glu[..., D_SSM:])
        qh = apply_partial_rotary(split_heads(q), cos, sin)
        kh = apply_partial_rotary(split_heads(k), cos, sin)
        vh = split_heads(v)
        y_b = moba_attention(qh, kh, vh).transpose(0, 2, 1, 3).reshape(bsz, seq_len, D_ATTN)
        merged = (jax.nn.sigmoid(g_ssm) * (y_a @ w_branch_ssm[i])
                  + jax.nn.sigmoid(g_attn) * (y_b @ w_branch_attn[i]))
        h = h + merged @ w_out[i]
        h = h + 0.5 * swiglu(rms_norm(h, ffn2_norm[i]), ffn2_w_gate[i], ffn2_w_up[i], ffn2_w_down[i])
        h = h + (p[i] @ ple_w_up[i]) * jax.nn.sigmoid(rms_norm(h, ple_norm[i]) @ ple_w_gate[i])
    return rms_norm(h, final_norm)
```

```python
import numpy as np
from contextlib import ExitStack
import ml_dtypes
import concourse.bass as bass
import concourse.mybir as mybir
from concourse.bass_utils import run_bass_kernel_spmd

F32 = mybir.dt.float32
BF16 = mybir.dt.bfloat16
I32 = mybir.dt.int32
AF = mybir.ActivationFunctionType
ALU = mybir.AluOpType
AX = mybir.AxisListType

D = 2048
DFF = 5632
NT = 1024
L = 2048
KC = D // 128
EPS = 1e-6
DEPTH = 2
PLE = 256
BIG = 1.0e4
SCALE = 1.0 / float(np.sqrt(128.0))
ENGS = ["pe", "act", "dve", "pool", "sp"]
TWO_PI = float(2.0 * np.pi)

WSPEC = [
    ("ffn1_w_gate", D, DFF), ("ffn1_w_up", D, DFF), ("ffn1_w_down", DFF, D),
    ("w_in", D, 8192), ("ssm_w_glu", 1024, 2048), ("w_branch_ssm", 1024, 2048),
    ("w_branch_attn", 1024, 2048), ("w_out", D, D),
    ("ffn2_w_gate", D, DFF), ("ffn2_w_up", D, DFF), ("ffn2_w_down", DFF, D),
    ("ple_w_up", PLE, D), ("ple_w_gate", D, D),
]
NORMS = ["ffn1_norm", "mix_norm", "ffn2_norm", "ple_norm"]


class Buf:
    __slots__ = ("name", "w", "r", "pw")

    def __init__(self, name):
        self.name = name
        self.w = None
        self.r = []
        self.pw = []


class TB:
    def __init__(self, t, b):
        self.t = t
        self.b = b

    def __getitem__(self, idx):
        return self.t[idx]


def _b(x):
    return x.b if isinstance(x, TB) else x


class Sched:
    def __init__(self, nc, es, ndma=16):
        self.nc = nc
        self.es = es
        self.prog = {e: [] for e in ENGS}
        self.cnt = {e: 0 for e in ENGS}
        self.known = {e: {} for e in ENGS}
        self.sems = {}
        self.ndma = ndma
        self.dma_val = {}
        self.dma_rr = {e: 0 for e in ENGS}
        self.dry = False
        self.nbuf = 0

    def buf(self, name=None):
        self.nbuf += 1
        return Buf(name or f"b{self.nbuf}")

    def sem(self, key):
        if key not in self.sems:
            self.sems[key] = self.es.enter_context(self.nc.semaphore(key))
        return self.sems[key]

    def _deps(self, eng, reads, writes, pwrites=()):
        toks = []
        for b in reads:
            b = _b(b)
            if b.w is not None:
                toks.append(b.w)
            toks.extend(b.pw)
        for b in writes:
            b = _b(b)
            if b.w is not None:
                toks.append(b.w)
            toks.extend(b.r)
            toks.extend(b.pw)
        for b in pwrites:
            b = _b(b)
            if b.w is not None:
                toks.append(b.w)
            toks.extend(b.r)
        best = {}
        for k, v in toks:
            if v > best.get(k, 0):
                best[k] = v
        waits = []
        kn = self.known[eng]
        for k, v in best.items():
            if eng == "pe" and k == "c_pe":
                continue
            if kn.get(k, 0) >= v:
                continue
            kn[k] = v
            waits.append((k, v))
        return waits

    @staticmethod
    def _compact(lst):
        best = {}
        for k, v in lst:
            if v > best.get(k, 0):
                best[k] = v
        return list(best.items())

    def _commit(self, tok, reads, writes, pwrites=()):
        for b in reads:
            b = _b(b)
            b.r.append(tok)
            if len(b.r) > 48:
                b.r = self._compact(b.r)
        for b in writes:
            b = _b(b)
            b.w = tok
            b.r = []
            b.pw = []
        for b in pwrites:
            b = _b(b)
            b.pw.append(tok)
            if len(b.pw) > 48:
                b.pw = self._compact(b.pw)

    def op(self, eng, fns, reads=(), writes=(), pwrites=()):
        if self.dry:
            return None
        if not isinstance(fns, (list, tuple)):
            fns = [fns]
        waits = self._deps(eng, reads, writes, pwrites)
        self.cnt[eng] += 1
        tok = ("c_" + eng, self.cnt[eng])
        self.prog[eng].append((waits, list(fns), (tok[0], 1)))
        self._commit(tok, reads, writes, pwrites)
        return tok

    def dma(self, eng, fn, reads=(), writes=(), pwrites=(), inc=16):
        if self.dry:
            return None
        i = self.dma_rr[eng]
        self.dma_rr[eng] = (i + 1) % self.ndma
        key = f"d_{eng}_{i}" if inc == 16 else f"x_{eng}_{i}"
        prev = self.dma_val.get(key, 0)
        waits = self._deps(eng, reads, writes, pwrites)
        if prev > 0 and self.known[eng].get(key, 0) < prev:
            self.known[eng][key] = prev
            waits.append((key, prev))
        val = prev + inc
        self.dma_val[key] = val
        tok = (key, val)
        self.prog[eng].append((waits, [fn], (key, inc)))
        self._commit(tok, reads, writes, pwrites)
        return tok

    def barrier(self):
        if self.dry:
            return
        toks = [("c_" + e, self.cnt[e]) for e in ENGS if self.cnt[e] > 0]
        toks += [(k, v) for k, v in self.dma_val.items()]
        for e in ENGS:
            self.wait_all(e, toks)

    def wait_all(self, eng, toks):
        if self.dry:
            return
        waits = []
        for k, v in toks:
            if self.known[eng].get(k, 0) < v:
                self.known[eng][k] = v
                waits.append((k, v))
        if waits:
            self.prog[eng].append((waits, [], None))

    def emit(self):
        nc = self.nc
        for e in ENGS:
            for waits, fns, inc in self.prog[e]:
                for k, v in waits:
                    self.sem(k)
                if inc is not None:
                    self.sem(inc[0])
        with nc.Block() as block:
            def runner(ename):
                def run(eng):
                    for waits, fns, inc in self.prog[ename]:
                        for k, v in waits:
                            eng.wait_ge(self.sems[k], v)
                        ins = None
                        for f in fns:
                            ins = f(eng)
                        if inc is not None:
                            ins.then_inc(self.sems[inc[0]], inc[1])
                return run
            block.tensor(runner("pe"))
            block.scalar(runner("act"))
            block.vector(runner("dve"))
            block.gpsimd(runner("pool"))
            block.sync(runner("sp"))


class Ctx:
    def __init__(self, nc, es):
        self.nc = nc
        self.es = es
        self.S = Sched(nc, es)
        self.nname = 0
        self.pbank = []
        for i in range(8):
            t = es.enter_context(nc.psum_tensor(f"ps{i}", [128, 512], F32))
            self.pbank.append(TB(t, self.S.buf(f"ps{i}")))
        self.prr = 0
        self.NSLOT = 4
        self.wslots = None
        self.wplan = []
        self.wissued = 0
        self.wnext = 0
        self.wrel_count = 0
        self.wphase = []
        self.wlimit = 0

    def jval(self, e, mult):
        if not hasattr(self, "_jv"):
            self._jv = {}
        key = (id(e), mult)
        if key not in self._jv:
            self._jv[key] = e.snap((e.partition_id() % 2) * mult)
        return self._jv[key]

    ARENA_ELEMS = 106240

    def sb(self, shape, dtype, name=None, es=None):
        if not hasattr(self, "arena"):
            self.arena = self.es.enter_context(self.nc.sbuf_tensor("arena", [128, self.ARENA_ELEMS], BF16))
            self.aoff = 0
            self.scopes = set()
        if es is not None and id(es) not in self.scopes:
            self.scopes.add(id(es))
            mark = self.aoff

            def rel(mark=mark, key=id(es)):
                self.aoff = mark
                self.scopes.discard(key)
            es.callback(rel)
        self.nname += 1
        nm = f"{name or 't'}_{self.nname}"
        esz = 2 if dtype == BF16 else 4
        n = 1
        for d in shape[1:]:
            n *= d
        nbf = (n * esz + 1) // 2
        nbf = (nbf + 15) // 16 * 16
        assert self.aoff + nbf <= self.ARENA_ELEMS, (nm, self.aoff, nbf)
        ap = self.arena[0:shape[0], self.aoff:self.aoff + (n * esz) // 2]
        self.aoff += nbf
        self.apeak = max(getattr(self, "apeak", 0), self.aoff)
        if dtype != BF16:
            ap = ap.bitcast(dtype)
        if len(shape) > 2:
            names = " ".join(f"d{i}" for i in range(1, len(shape)))
            ap = ap.rearrange(f"p ({names}) -> p {names}", **{f"d{i}": shape[i] for i in range(1, len(shape) - 1)})
        return TB(ap, self.S.buf(nm))

    def dram(self, name, shape, dtype):
        return TB(self.nc.dram_tensor(name, shape, dtype).ap(), self.S.buf(name))

    def psum(self):
        b = self.pbank[self.prr]
        self.prr = (self.prr + 1) % 6
        return b

    def wphase_begin(self):
        starts = self.wphase
        cur = self.wnext
        nxt = [s for s in starts if s > cur]
        self.wlimit = nxt[0] if nxt else len(self.wplan)
        assert self.wissued == self.wnext, (self.wissued, self.wnext)
        self.wbase = self.wnext
        for _ in range(self.NSLOT):
            self._wissue()

    def _wissue(self):
        S = self.S
        j = self.wissued
        if j >= self.wlimit:
            return
        w2, k2, kc2, c2, n2 = self.wplan[j]
        ci, lr, lc = w2.find(k2, kc2, c2, n2)
        ch = w2.full[ci]
        sl = self.wslots[(j - self.wbase) % self.NSLOT]
        dst = sl.t[:, 0:kc2 * n2].rearrange("p (k c) -> p k c", k=kc2)
        src = ch.t[lr:lr + kc2 * 128, lc:lc + n2].rearrange("(k p) c -> p k c", p=128)
        S.dma("sp", (lambda d, s: (lambda e: e.dma_start(out=d, in_=s)))(dst, src),
              reads=(ch,), writes=(sl,))
        self.wissued += 1

    def wget(self, w, k0, kc, c0, ncols):
        S = self.S
        idx = self.wnext
        self.wnext += 1
        assert self.wplan[idx][1:] == (k0, kc, c0, ncols) and self.wplan[idx][0] is w, (idx, self.wplan[idx][1:], (k0, kc, c0, ncols))
        assert idx < self.wissued, (idx, self.wissued)
        sl = self.wslots[(idx - self.wbase) % self.NSLOT]
        return TB(sl.t[:, 0:kc * ncols].rearrange("p (k c) -> p k c", k=kc), sl.b)

    def wrel(self, n=1):
        if self.S.dry:
            return
        for _ in range(n):
            self._wissue()


def wchunks(K, M):
    if K > 2048:
        return [(r0, min(1024, K - r0), 0, M) for r0 in range(0, K, 1024)]
    if K == 2048:
        return [(0, K, c0, min(1024, M - c0)) for c0 in range(0, M, 1024)]
    return [(0, K, 0, M)]


class WMat:
    def __init__(self, C, tag, K, M, nw):
        self.K, self.M, self.nw = K, M, nw
        self.chunks = wchunks(K, M)
        self.bounce, self.half, self.full, self.off = [], [], [], []
        o = 0
        for i, (r0, nr, c0, ncw) in enumerate(self.chunks):
            self.off.append(o)
            o += (nr // nw) * ncw
            self.bounce.append(C.dram(f"wb_{tag}_{i}", [nr // nw, ncw], BF16))
            self.half.append(C.dram(f"wh_{tag}_{i}", [nr // 2, ncw], BF16) if nw == 8 else None)
            self.full.append(C.dram(f"wf_{tag}_{i}", [nr, ncw], BF16))
        self.per_rank = o

    def find(self, k0, kc, c0, ncols):
        for i, (r0, nr, cc0, ncw) in enumerate(self.chunks):
            if r0 <= k0 and k0 + kc * 128 <= r0 + nr and cc0 <= c0 and c0 + ncols <= cc0 + ncw:
                return i, k0 - r0, c0 - cc0
        raise AssertionError((k0, kc, c0, ncols))


def mm(out, lhsT, rhs, start, stop):
    return lambda e: e.matmul(out, lhsT, rhs, start=start, stop=stop)


def tsl(th):
    return slice(th * 512, th * 512 + 512)


def emit_rmsnorm(C, T, h, gain_col, out):
    S = C.S
    for th in range(NT // 512):
        ts = tsl(th)
        pt = C.psum()
        for kc in range(KC):
            sq = T.sq[kc % 2]
            S.op("act", (lambda o, i: (lambda e: e.activation(o, i, AF.Square)))(sq[:, :], h[:, kc, ts]),
                 reads=(h,), writes=(sq,))
            S.op("pe", mm(pt[:, :], C.ones[:, :], sq[:, :], kc == 0, kc == KC - 1),
                 reads=(sq, C.ones), writes=(pt,))
        rstd = T.rstd
        S.op("act", (lambda p_: (lambda e: e.activation(rstd[:, :], p_, AF.Sqrt, bias=C.eps_ap[:, 0:1], scale=1.0 / D)))(pt[:, :]),
             reads=(pt, C.eps_ap), writes=(rstd,))
        S.op("dve", lambda e: e.reciprocal(rstd[:, :], rstd[:, :]), reads=(rstd,), writes=(rstd,))
        for kc in range(KC):
            S.op("dve", (lambda o, i, g: (lambda e: e.scalar_tensor_tensor(o, i, g, rstd[:, :], ALU.mult, ALU.mult)))(
                out[:, kc, ts], h[:, kc, ts], C.gains[:, gain_col + kc:gain_col + kc + 1]),
                reads=(h, rstd, C.gains), pwrites=(out,))


def emit_ffn(C, T, h, xn, wg, wu, wd):
    S = C.S
    nchunk = DFF // 512
    for ch in range(nchunk):
        gw = C.wget(wg, 0, KC, ch * 512, 512)
        uw = C.wget(wu, 0, KC, ch * 512, 512)
        dw = C.wget(wd, ch * 512, 4, 0, D)
        if S.dry:
            continue
        hid = T.hid[ch % 2]
        for mi in range(4):
            for th in range(NT // 512):
                ts = tsl(th)
                pg = C.psum()
                S.op("pe", [mm(pg[:, :], gw[:, kc, mi * 128:mi * 128 + 128], xn[:, kc, ts], kc == 0, kc == KC - 1)
                            for kc in range(KC)], reads=(gw, xn), writes=(pg,))
                pu = C.psum()
                S.op("pe", [mm(pu[:, :], uw[:, kc, mi * 128:mi * 128 + 128], xn[:, kc, ts], kc == 0, kc == KC - 1)
                            for kc in range(KC)], reads=(uw, xn), writes=(pu,))
                sg = T.sg[(mi * 2 + th) % 2]
                S.op("act", (lambda o, i: (lambda e: e.activation(o, i, AF.Silu)))(sg[:, :], pg[:, :]),
                     reads=(pg,), writes=(sg,))
                S.op("dve", (lambda o, a, b: (lambda e: e.tensor_tensor(o, a, b, ALU.mult)))(
                    hid[:, mi, ts], pu[:, :], sg[:, :]), reads=(pu, sg), pwrites=(hid,))
        C.wrel(2)
        for mo in range(KC):
            for th in range(NT // 512):
                ts = tsl(th)
                pd = C.psum()
                S.op("pe", [mm(pd[:, :], dw[:, mi, mo * 128:mo * 128 + 128], hid[:, mi, ts], mi == 0, mi == 3)
                            for mi in range(4)], reads=(dw, hid), writes=(pd,))
                S.op("dve", (lambda o, a: (lambda e: e.scalar_tensor_tensor(o, a, 0.5, o, ALU.mult, ALU.add)))(
                    h[:, mo, ts], pd[:, :]), reads=(pd,), pwrites=(h,))
        C.wrel(1)


def emit_linear(C, w, K, c0, ncols, xs, epi, toks):
    S = C.S
    kc_n = K // 128
    bc = min(8192 // kc_n, ncols)
    for cb in range(ncols // bc):
        wb = C.wget(w, 0, kc_n, c0 + cb * bc, bc)
        if not S.dry:
            for mi in range(bc // 128):
                mo = (cb * bc) // 128 + mi
                for ti, ts in enumerate(toks):
                    ps = C.psum()
                    ops = []
                    rd = [wb]
                    for kc in range(kc_n):
                        ap, tb = xs(kc, ts)
                        ops.append(mm(ps[:, :], wb[:, kc, mi * 128:mi * 128 + 128], ap, kc == 0, kc == kc_n - 1))
                        if tb not in rd:
                            rd.append(tb)
                    S.op("pe", ops, reads=tuple(rd), writes=(ps,))
                    epi(mo, ti, ps)
        C.wrel(1)


class TokTiles:
    def __init__(self, C, es):
        S = C.S
        self.h = C.sb([128, KC, NT], F32, "h", es)
        self.xn = C.sb([128, KC, NT], BF16, "xn", es)
        C.wslots = [C.sb([128, 8192], BF16, f"wslot{i}", es) for i in range(C.NSLOT)]
        self.tarena = C.sb([128, 12288], BF16, "tarena", es)
        a = self.tarena.t
        def view(off, n, pat=None, **kw):
            ap = a[:, off:off + n]
            if pat:
                ap = ap.rearrange(pat, **kw)
            return TB(ap, S.buf())
        self.hid = [view(0, 4096, "p (m t) -> p m t", m=4), view(4096, 4096, "p (m t) -> p m t", m=4)]
        self.zst = [view(8192 + i * 512, 512) for i in range(4)]
        self.ys = view(0, 4096, "p (k t) -> p k t", k=8)
        self.y_a = view(4096, 4096, "p (k t) -> p k t", k=8)
        self.ya = view(8192, 4096, "p (k t) -> p k t", k=8)
        self.pb = view(0, 2048, "p (k t) -> p k t", k=2)
        self.sq = [C.sb([128, 512], BF16, f"sq{i}", es) for i in range(2)]
        self.rstd = C.sb([128, 512], F32, "rstd", es)
        self.sg = [C.sb([128, 512], BF16, f"sg{i}", es) for i in range(2)]
        self.sgs = [C.sb([128, 512], BF16, f"sgs{i}", es) for i in range(2)]
        self.sga = [C.sb([128, 512], BF16, f"sga{i}", es) for i in range(2)]
        self.t1 = [C.sb([128, 512], BF16, f"t1{i}", es) for i in range(2)]
        self.f32tmp = [C.sb([128, 512], F32, f"f32tmp{i}", es) for i in range(1)]
        self.pstage = C.sb([128, 2, 256], F32, "pstage", es)


def cp(eng_kind, o, i):
    if eng_kind == "act":
        return lambda e: e.activation(o, i, AF.Copy)
    return lambda e: e.tensor_copy(o, i)


def seg_a(C, T, l):
    S = C.S
    wf = C.wf[l]
    emit_rmsnorm(C, T, T.h, (l * 4 + 0) * KC, T.xn)
    emit_ffn(C, T, T.h, T.xn, wf["ffn1_w_gate"], wf["ffn1_w_up"], wf["ffn1_w_down"])
    emit_rmsnorm(C, T, T.h, (l * 4 + 1) * KC, T.xn)
    zx_in = C.zx_in[l]
    sg_d = C.sg_d[l]
    cnt = [0]

    def xs(kc, ts):
        return T.xn[:, kc, ts], T.xn

    def epi(mo, ti, ps):
        st = T.zst[cnt[0] % 4]
        eng = "act" if cnt[0] % 2 == 0 else "dve"
        cnt[0] += 1
        if mo < 32:
            S.op(eng, cp(eng, st[:, :], ps[:, :]), reads=(ps,), writes=(st,))
            zp = zx_in[mo // 8]
            S.dma("sp", (lambda o, i: (lambda e: e.dma_start(out=o, in_=i)))(
                zp.t[(mo % 8) * 128:(mo % 8 + 1) * 128, ti * 512:(ti + 1) * 512], st[:, :]),
                reads=(st,), pwrites=(zp,))
        else:
            S.op("act", (lambda o, i: (lambda e: e.activation(o, i, AF.Sigmoid)))(st[:, :], ps[:, :]),
                 reads=(ps,), writes=(st,))
            S.dma("sp", (lambda o, i: (lambda e: e.dma_start(out=o, in_=i)))(
                sg_d.t[(mo - 32) * 128:(mo - 31) * 128, ti * 512:(ti + 1) * 512], st[:, :]),
                reads=(st,), pwrites=(sg_d,))

    emit_linear(C, wf["w_in"], D, 0, 8192, xs, epi, [tsl(0), tsl(1)])
    for part in range(4):
        C.coll_pair(zx_in[part], C.zx_out[l][part], pop=(part == 3))
    S.dma("sp", lambda e: e.dma_start(out=C.hsp.t.rearrange("(k p) t -> p k t", p=128), in_=T.h[:, :, :]),
          reads=(T.h,), writes=(C.hsp,))


def seg_c(C, T, l, last):
    S = C.S
    wf = C.wf[l]
    yxs, yxa = C.yx_out[l]
    sg_d = C.sg_d[l]
    S.dma("sp", lambda e: e.dma_start(out=T.h[:, :, :], in_=C.hsp.t.rearrange("(k p) t -> p k t", p=128)),
          reads=(C.hsp,), writes=(T.h,))
    for hf in range(2):
        tcol = hf * 512
        for jj in range(2):
            for (dst, yx) in ((T.ys, yxs), (T.ya, yxa)):
                r0 = jj * 512

                def ld(e, dst=dst, r0=r0, jj=jj, tcol=tcol, yx=yx):
                    j1024 = C.jval(e, 1024)
                    return e.dma_start(out=dst[:, jj * 4:(jj + 1) * 4, :],
                                       in_=yx.t[r0:r0 + 512, tcol:tcol + 1536][:, bass.ds(j1024, 512)].rearrange("(c p) t -> p c t", p=128))
                S.dma("act", ld, reads=(yx,), pwrites=(dst,))
        for kc in range(8):
            x = T.ys[:, kc, :]
            tf = T.f32tmp[0]
            S.op("act", (lambda x: (lambda e: e.activation(tf[:, :], x, AF.Square)))(x), reads=(T.ys,), writes=(tf,))
            S.op("dve", lambda e: e.tensor_scalar(tf[:, :], tf[:, :], 0.044715, 1.0, ALU.mult, ALU.add),
                 reads=(tf,), writes=(tf,))
            S.op("dve", (lambda x: (lambda e: e.tensor_tensor(tf[:, :], tf[:, :], x, ALU.mult)))(x),
                 reads=(tf, T.ys), writes=(tf,))
            S.op("act", lambda e: e.activation(tf[:, :], tf[:, :], AF.Sigmoid, scale=1.5957691216057308),
                 reads=(tf,), writes=(tf,))
            S.op("dve", (lambda x: (lambda e: e.tensor_tensor(x, x, tf[:, :], ALU.mult)))(x),
                 reads=(tf,), pwrites=(T.ys,))
        wa = C.wget(wf["ssm_w_glu"], 0, 8, 0, 1024)
        wb = C.wget(wf["ssm_w_glu"], 0, 8, 1024, 1024)
        if not S.dry:
            for i in range(8):
                pa = C.psum()
                S.op("pe", [mm(pa[:, :], wa[:, kc, i * 128:i * 128 + 128], T.ys[:, kc, :], kc == 0, kc == 7)
                            for kc in range(8)], reads=(wa, T.ys), writes=(pa,))
                pb_ = C.psum()
                S.op("pe", [mm(pb_[:, :], wb[:, kc, i * 128:i * 128 + 128], T.ys[:, kc, :], kc == 0, kc == 7)
                            for kc in range(8)], reads=(wb, T.ys), writes=(pb_,))
                sg = T.sg[i % 2]
                S.op("act", (lambda o, i_: (lambda e: e.activation(o, i_, AF.Sigmoid)))(sg[:, :], pb_[:, :]),
                     reads=(pb_,), writes=(sg,))
                S.op("dve", (lambda o, a, b: (lambda e: e.tensor_tensor(o, a, b, ALU.mult)))(
                    T.y_a[:, i, :], pa[:, :], sg[:, :]), reads=(pa, sg), pwrites=(T.y_a,))
        C.wrel(2)
        for half in range(2):
            wA = C.wget(wf["w_branch_ssm"], 0, 8, half * 1024, 1024)
            wB = C.wget(wf["w_branch_attn"], 0, 8, half * 1024, 1024)
            if not S.dry:
                for mi in range(8):
                    mo = half * 8 + mi
                    sgs, sga = T.sgs[mo % 2], T.sga[mo % 2]
                    S.dma("sp", (lambda o, i: (lambda e: e.dma_start(out=o, in_=i)))(
                        sgs[:, :], sg_d.t[mo * 128:(mo + 1) * 128, tcol:tcol + 512]), reads=(sg_d,), writes=(sgs,))
                    S.dma("sp", (lambda o, i: (lambda e: e.dma_start(out=o, in_=i)))(
                        sga[:, :], sg_d.t[(16 + mo) * 128:(17 + mo) * 128, tcol:tcol + 512]), reads=(sg_d,), writes=(sga,))
                    p1 = C.psum()
                    S.op("pe", [mm(p1[:, :], wA[:, kc, mi * 128:mi * 128 + 128], T.y_a[:, kc, :], kc == 0, kc == 7)
                                for kc in range(8)], reads=(wA, T.y_a), writes=(p1,))
                    p2 = C.psum()
                    S.op("pe", [mm(p2[:, :], wB[:, kc, mi * 128:mi * 128 + 128], T.ya[:, kc, :], kc == 0, kc == 7)
                                for kc in range(8)], reads=(wB, T.ya), writes=(p2,))
                    t1, t2 = T.t1[0], T.t1[1]
                    S.op("dve", (lambda a, b: (lambda e: e.tensor_tensor(t1[:, :], a, b, ALU.mult)))(p1[:, :], sgs[:, :]),
                         reads=(p1, sgs), writes=(t1,))
                    S.op("dve", (lambda a, b: (lambda e: e.tensor_tensor(t2[:, :], a, b, ALU.mult)))(p2[:, :], sga[:, :]),
                         reads=(p2, sga), writes=(t2,))
                    S.op("dve", (lambda o: (lambda e: e.tensor_tensor(o, t1[:, :], t2[:, :], ALU.add)))(T.xn[:, mo, 0:512]),
                         reads=(t1, t2), pwrites=(T.xn,))
            C.wrel(2)

        def xs(kc, ts):
            return T.xn[:, kc, 0:512], T.xn

        def epi(mo, ti, ps, tcol=tcol):
            S.op("dve", (lambda o, a: (lambda e: e.tensor_tensor(o, a, o, ALU.add)))(
                T.h[:, mo, tcol:tcol + 512], ps[:, :]), reads=(ps,), pwrites=(T.h,))
        emit_linear(C, wf["w_out"], D, 0, D, xs, epi, [slice(0, 512)])
    S.barrier()
    emit_rmsnorm(C, T, T.h, (l * 4 + 2) * KC, T.xn)
    emit_ffn(C, T, T.h, T.xn, wf["ffn2_w_gate"], wf["ffn2_w_up"], wf["ffn2_w_down"])
    S.barrier()
    emit_rmsnorm(C, T, T.h, (l * 4 + 3) * KC, T.xn)
    for qd in range(4):
        S.dma("sp", (lambda qd: (lambda e: e.dma_start(
            out=T.pstage[:, :, :], in_=C.pT.t[l, :, qd * 256:(qd + 1) * 256].rearrange("(k p) t -> p k t", p=128))))(qd),
            reads=(C.pT,), writes=(T.pstage,))
        S.op("dve", (lambda qd: (lambda e: e.tensor_copy(T.pb[:, :, qd * 256:(qd + 1) * 256], T.pstage[:, :, :])))(qd),
             reads=(T.pstage,), pwrites=(T.pb,))
    for cb in range(4):
        wpg = C.wget(wf["ple_w_gate"], 0, KC, cb * 512, 512)
        wpu = C.wget(wf["ple_w_up"], 0, 2, cb * 512, 512)
        if not S.dry:
            for mi in range(4):
                mo = cb * 4 + mi
                for th in range(2):
                    ts = tsl(th)
                    pg = C.psum()
                    S.op("pe", [mm(pg[:, :], wpg[:, kc, mi * 128:mi * 128 + 128], T.xn[:, kc, ts], kc == 0, kc == KC - 1)
                                for kc in range(KC)], reads=(wpg, T.xn), writes=(pg,))
                    pu = C.psum()
                    S.op("pe", [mm(pu[:, :], wpu[:, kc, mi * 128:mi * 128 + 128], T.pb[:, kc, ts], kc == 0, kc == 1)
                                for kc in range(2)], reads=(wpu, T.pb), writes=(pu,))
                    sg = T.f32tmp[0]
                    S.op("act", (lambda i_: (lambda e: e.activation(sg[:, :], i_, AF.Sigmoid)))(pg[:, :]),
                         reads=(pg,), writes=(sg,))
                    S.op("dve", (lambda a: (lambda e: e.tensor_tensor(sg[:, :], a, sg[:, :], ALU.mult)))(pu[:, :]),
                         reads=(pu, sg), writes=(sg,))
                    S.op("dve", (lambda o: (lambda e: e.tensor_tensor(o, o, sg[:, :], ALU.add)))(T.h[:, mo, ts]),
                         reads=(sg,), pwrites=(T.h,))
        C.wrel(2)
    S.barrier()
    if last:
        emit_rmsnorm(C, T, T.h, DEPTH * 4 * KC, T.h)
        tok = S.dma("sp", lambda e: e.dma_start(out=C.oT.t.rearrange("(k p) t -> p k t", p=128), in_=T.h[:, :, :]),
                    reads=(T.h,), writes=(C.oT,))
        if not S.dry:
            S.wait_all("sp", [tok])


def tt(o, a, b, op):
    return lambda e: e.tensor_tensor(o, a, b, op)


def ssm_prep(C, l):
    S = C.S
    P = C.ssm_in
    with ExitStack() as es:
        def sm(name, shape=(128, 16), dt=F32):
            return C.sb(list(shape), dt, name, es)

        def ew(eng, fn, reads, writes):
            S.op(eng, fn, reads=reads, writes=writes)

        def load(name, shape):
            t = sm(name, shape)
            src = P[name].t[l]
            S.dma("sp", (lambda o, i: (lambda e: e.dma_start(out=o, in_=i)))(t.t[tuple(slice(None) for _ in shape)], src),
                  reads=(P[name],), writes=(t,))
            return t
        ar_in = load("ssm_ar", (128, 16))
        ai_in = load("ssm_ai", (128, 16))
        ldt = load("ssm_ldt", (128, 16))
        ctre = load("ssm_ct_re", (128, 16, 16))
        ctim = load("ssm_ct_im", (128, 16, 16))
        btre = load("ssm_bt_re", (128, 16, 16))
        btim = load("ssm_bt_im", (128, 16, 16))
        A = lambda t: t[:, :]
        dt_ = sm("dt")
        ew("act", lambda e: e.activation(A(dt_), A(ldt), AF.Exp), (ldt,), (dt_,))
        ar, ai, mag = sm("ar"), sm("ai"), sm("mag")
        ew("dve", tt(A(ar), A(ar_in), A(dt_), ALU.mult), (ar_in, dt_), (ar,))
        ew("dve", tt(A(ai), A(ai_in), A(dt_), ALU.mult), (ai_in, dt_), (ai,))
        ew("act", lambda e: e.activation(A(mag), A(ar), AF.Exp), (ar,), (mag,))

        def sin_of(x, offset, name):
            xo, y, kf, r, m = sm(name + "xo"), sm(name + "y"), sm(name + "kf"), sm(name + "r"), sm(name + "m")
            ki = sm(name + "ki", (128, 16), I32)
            ew("dve", lambda e: e.tensor_scalar(A(xo), A(x), float(offset), None, ALU.add), (x,), (xo,))
            ew("dve", lambda e: e.tensor_scalar(A(y), A(xo), 1.0 / TWO_PI, 0.5, ALU.mult, ALU.add), (xo,), (y,))
            ew("dve", lambda e: e.tensor_copy(A(ki), A(y)), (y,), (ki,))
            ew("dve", lambda e: e.tensor_copy(A(kf), A(ki)), (ki,), (kf,))
            ew("dve", lambda e: e.scalar_tensor_tensor(A(r), A(kf), -TWO_PI, A(xo), ALU.mult, ALU.add), (kf, xo), (r,))
            ew("dve", lambda e: e.tensor_scalar(A(m), A(r), float(np.pi), None, ALU.is_gt), (r,), (m,))
            ew("dve", lambda e: e.scalar_tensor_tensor(A(r), A(m), -TWO_PI, A(r), ALU.mult, ALU.add), (m, r), (r,))
            ew("dve", lambda e: e.tensor_scalar(A(m), A(r), float(-np.pi), None, ALU.is_lt), (r,), (m,))
            ew("dve", lambda e: e.scalar_tensor_tensor(A(r), A(m), TWO_PI, A(r), ALU.mult, ALU.add), (m, r), (r,))
            s = sm(name + "s")
            ew("act", lambda e: e.activation(A(s), A(r), AF.Sin), (r,), (s,))
            return s
        sn = sin_of(ai, 0.0, "sn")
        cs = sin_of(ai, np.pi / 2.0, "cs")
        lr, li = sm("lr"), sm("li")
        ew("dve", tt(A(lr), A(mag), A(cs), ALU.mult), (mag, cs), (lr,))
        ew("dve", tt(A(li), A(mag), A(sn), ALU.mult), (mag, sn), (li,))
        den, t0, t1_, nr, cr, ci = sm("den"), sm("t0"), sm("t1"), sm("nr"), sm("cr"), sm("ci")
        ew("dve", tt(A(den), A(ar_in), A(ar_in), ALU.mult), (ar_in,), (den,))
        ew("dve", tt(A(t0), A(ai_in), A(ai_in), ALU.mult), (ai_in,), (t0,))
        ew("dve", tt(A(den), A(den), A(t0), ALU.add), (den, t0), (den,))
        ew("dve", lambda e: e.reciprocal(A(den), A(den)), (den,), (den,))
        ew("dve", lambda e: e.tensor_scalar(A(nr), A(lr), -1.0, None, ALU.add), (lr,), (nr,))
        ew("dve", tt(A(t0), A(nr), A(ar_in), ALU.mult), (nr, ar_in), (t0,))
        ew("dve", tt(A(t1_), A(li), A(ai_in), ALU.mult), (li, ai_in), (t1_,))
        ew("dve", tt(A(t0), A(t0), A(t1_), ALU.add), (t0, t1_), (t0,))
        ew("dve", tt(A(cr), A(t0), A(den), ALU.mult), (t0, den), (cr,))
        ew("dve", tt(A(t0), A(li), A(ar_in), ALU.mult), (li, ar_in), (t0,))
        ew("dve", tt(A(t1_), A(nr), A(ai_in), ALU.mult), (nr, ai_in), (t1_,))
        ew("dve", tt(A(t0), A(t0), A(t1_), ALU.subtract), (t0, t1_), (t0,))
        ew("dve", tt(A(ci), A(t0), A(den), ALU.mult), (t0, den), (ci,))
        B3 = [128, 16, 16]
        bbre, bbim, u0, u1 = sm("bbre", B3), sm("bbim", B3), sm("u0", B3), sm("u1", B3)
        crb = cr.t[:, :, None].to_broadcast(B3)
        cib = ci.t[:, :, None].to_broadcast(B3)
        F3 = lambda t: t[:, :, :]
        ew("dve", tt(F3(u0), crb, F3(btre), ALU.mult), (cr, btre), (u0,))
        ew("dve", tt(F3(u1), cib, F3(btim), ALU.mult), (ci, btim), (u1,))
        ew("dve", tt(F3(bbre), F3(u0), F3(u1), ALU.subtract), (u0, u1), (bbre,))
        ew("dve", tt(F3(u0), crb, F3(btim), ALU.mult), (cr, btim), (u0,))
        ew("dve", tt(F3(u1), cib, F3(btre), ALU.mult), (ci, btre), (u1,))
        ew("dve", tt(F3(bbim), F3(u0), F3(u1), ALU.add), (u0, u1), (bbim,))
        powre, powim = sm("powre", (128, 16, 9)), sm("powim", (128, 16, 9))
        ew("dve", lambda e: e.memset(powre[:, :, 0:1], 1.0), (), (powre,))
        ew("dve", lambda e: e.memset(powim[:, :, 0:1], 0.0), (), (powim,))
        ew("dve", lambda e: e.tensor_copy(powre[:, :, 1], A(lr)), (lr, powre), (powre,))
        ew("dve", lambda e: e.tensor_copy(powim[:, :, 1], A(li)), (li, powim), (powim,))
        for j in range(2, 9):
            ew("dve", tt(A(t0), powre[:, :, j - 1], A(lr), ALU.mult), (powre, lr), (t0,))
            ew("dve", tt(A(t1_), powim[:, :, j - 1], A(li), ALU.mult), (powim, li), (t1_,))
            ew("dve", tt(powre[:, :, j], A(t0), A(t1_), ALU.subtract), (t0, t1_, powre), (powre,))
            ew("dve", tt(A(t0), powre[:, :, j - 1], A(li), ALU.mult), (powre, li), (t0,))
            ew("dve", tt(A(t1_), powim[:, :, j - 1], A(lr), ALU.mult), (powim, lr), (t1_,))
            ew("dve", tt(powim[:, :, j], A(t0), A(t1_), ALU.add), (t0, t1_, powim), (powim,))
        lam8 = sm("lam8", (128, 16, 2))
        ew("dve", lambda e: e.tensor_copy(lam8[:, :, 0], powre[:, :, 8]), (powre,), (lam8,))
        ew("dve", lambda e: e.tensor_copy(lam8[:, :, 1], powim[:, :, 8]), (powim, lam8), (lam8,))
        S.dma("sp", lambda e: e.dma_start(out=C.lam8_d[l].t, in_=lam8[:, :, :]), reads=(lam8,), writes=(C.lam8_d[l],))
        prre, prim = sm("prre", (128, 16, 8)), sm("prim", (128, 16, 8))
        for tau in range(8):
            ew("dve", (lambda tau: (lambda e: e.tensor_copy(prre[:, :, tau], powre[:, :, 7 - tau])))(tau), (powre, prre), (prre,))
            ew("dve", (lambda tau: (lambda e: e.tensor_copy(prim[:, :, tau], powim[:, :, 7 - tau])))(tau), (powim, prim), (prim,))
        C4 = [128, 16, 9, 16]
        clre, clim, v0, v1 = sm("clre", C4), sm("clim", C4), sm("v0", C4), sm("v1", C4)
        F4 = lambda t: t[:, :, :, :]
        ctre_b = ctre.t[:, :, None, :].to_broadcast(C4)
        ctim_b = ctim.t[:, :, None, :].to_broadcast(C4)
        pre_b = powre.t[:, :, :, None].to_broadcast(C4)
        pim_b = powim.t[:, :, :, None].to_broadcast(C4)
        ew("dve", tt(F4(v0), ctre_b, pre_b, ALU.mult), (ctre, powre), (v0,))
        ew("dve", tt(F4(v1), ctim_b, pim_b, ALU.mult), (ctim, powim), (v1,))
        ew("dve", tt(F4(clre), F4(v0), F4(v1), ALU.subtract), (v0, v1), (clre,))
        ew("dve", tt(F4(v0), ctre_b, pim_b, ALU.mult), (ctre, powim), (v0,))
        ew("dve", tt(F4(v1), ctim_b, pre_b, ALU.mult), (ctim, powre), (v1,))
        ew("dve", tt(F4(v0), F4(v0), F4(v1), ALU.add), (v0, v1), (v0,))
        ew("dve", lambda e: e.tensor_scalar(F4(clim), F4(v0), -1.0, None, ALU.mult), (v0,), (clim,))
        w4re, w4im = sm("w4re", (128, 16, 128), BF16), sm("w4im", (128, 16, 128), BF16)
        ew("act", lambda e: e.activation(w4re.t[:, :, :].rearrange("p g (t q) -> p g t q", t=8), clre[:, :, 1:9, :], AF.Copy), (clre,), (w4re,))
        ew("act", lambda e: e.activation(w4im.t[:, :, :].rearrange("p g (t q) -> p g t q", t=8), clim[:, :, 1:9, :], AF.Copy), (clim,), (w4im,))
        S.dma("sp", lambda e: e.dma_start(out=C.w4re_d[l].t, in_=w4re[:, :, :]), reads=(w4re,), writes=(C.w4re_d[l],))
        S.dma("sp", lambda e: e.dma_start(out=C.w4im_d[l].t, in_=w4im[:, :, :]), reads=(w4im,), writes=(C.w4im_d[l],))
        T4 = [128, 16, 8, 16]
        t2re, t2im = sm("t2re", T4), sm("t2im", T4)
        w0, w1 = v0.t[:, :, 0:8, :], v1.t[:, :, 0:8, :]
        prre_b = prre.t[:, :, :, None].to_broadcast(T4)
        prim_b = prim.t[:, :, :, None].to_broadcast(T4)
        bbre_b = bbre.t[:, :, None, :].to_broadcast(T4)
        bbim_b = bbim.t[:, :, None, :].to_broadcast(T4)
        ew("dve", tt(w0, prre_b, bbre_b, ALU.mult), (prre, bbre), (v0,))
        ew("dve", tt(w1, prim_b, bbim_b, ALU.mult), (prim, bbim), (v1,))
        ew("dve", tt(F4(t2re), w0, w1, ALU.subtract), (v0, v1), (t2re,))
        ew("dve", tt(w0, prre_b, bbim_b, ALU.mult), (prre, bbim), (v0,))
        ew("dve", tt(w1, prim_b, bbre_b, ALU.mult), (prim, bbre), (v1,))
        ew("dve", tt(F4(t2im), w0, w1, ALU.add), (v0, v1), (t2im,))
        w2re, w2im = sm("w2re", (128, 32, 64), BF16), sm("w2im", (128, 32, 64), BF16)
        for (src, dst) in ((t2re, w2re), (t2im, w2im)):
            for gh in range(2):
                for gb in range(2):
                    ps = C.psum()
                    ops = []
                    for gi in range(8):
                        gp = gb * 8 + gi
                        ops.append((lambda o, i, idn: (lambda e: e.transpose(o, i, idn)))(
                            ps[:, gi * 64:(gi + 1) * 64],
                            src.t[64 * gh:64 * gh + 64, gp, :, :].rearrange("p a b -> p (a b)"),
                            C.ident[64 * gh:64 * gh + 64, 64 * gh:64 * gh + 64]))
                    S.op("pe", ops, reads=(src, C.ident), writes=(ps,))
                    g0 = gh * 16 + gb * 8
                    ew("act", (lambda dst, g0, ps: (lambda e: e.activation(
                        dst.t[:, g0:g0 + 8, :], ps[:, :].rearrange("p (g n) -> p g n", g=8), AF.Copy)))(dst, g0, ps),
                        (ps,), (dst,))
        S.dma("sp", lambda e: e.dma_start(out=C.w2re_d[l].t, in_=w2re[:, :, :]), reads=(w2re,), writes=(C.w2re_d[l],))
        S.dma("sp", lambda e: e.dma_start(out=C.w2im_d[l].t, in_=w2im[:, :, :]), reads=(w2im,), writes=(C.w2im_d[l],))
        w1sb = sm("w1sb", (128, 32, 128), BF16)
        sets = []
        for i in range(2):
            st = dict(bre=sm(f"bwre{i}", (128, 240)), bim=sm(f"bwim{i}", (128, 240)),
                      cre=sm(f"cwre{i}", (128, 240)), cim=sm(f"cwim{i}", (128, 240)))
            for k in st:
                ew("dve", (lambda t: (lambda e: e.memset(t[:, :], 0.0)))(st[k]), (), (st[k],))
            sets.append(st)
        for gp in range(16):
            st = sets[gp % 2]
            ew("dve", (lambda st, gp: (lambda e: e.tensor_copy(st["bre"][:, 112:128], bbre[:, gp, :])))(st, gp), (bbre, st["bre"]), (st["bre"],))
            ew("dve", (lambda st, gp: (lambda e: e.tensor_copy(st["bim"][:, 112:128], bbim[:, gp, :])))(st, gp), (bbim, st["bim"]), (st["bim"],))
            ew("act", (lambda st, gp: (lambda e: e.activation(st["cre"].t[:, 112:240].rearrange("p (j q) -> p j q", j=8), clre[:, gp, 0:8, :], AF.Copy)))(st, gp), (clre, st["cre"]), (st["cre"],))
            ew("act", (lambda st, gp: (lambda e: e.activation(st["cim"].t[:, 112:240].rearrange("p (j q) -> p j q", j=8), clim[:, gp, 0:8, :], AF.Copy)))(st, gp), (clim, st["cim"]), (st["cim"],))
            for gh in range(2):
                ps = C.psum()
                ops = []
                rows = slice(64 * gh, 64 * gh + 64)
                for tau in range(8):
                    for ri, (bk, ck) in enumerate((("bre", "cre"), ("bim", "cim"))):
                        ops.append(mm(ps[:, 0:128], st[bk][rows, 112 - 16 * tau:240 - 16 * tau],
                                      st[ck][rows, (7 - tau) * 16:(7 - tau) * 16 + 128],
                                      tau == 0 and ri == 0, tau == 7 and ri == 1))
                S.op("pe", ops, reads=(st["bre"], st["bim"], st["cre"], st["cim"]), writes=(ps,))
                g = gh * 16 + gp
                S.op("dve", (lambda g, ps: (lambda e: e.tensor_copy(w1sb[:, g, :], ps[:, 0:128])))(g, ps),
                     reads=(ps,), pwrites=(w1sb,))
        S.dma("sp", lambda e: e.dma_start(out=C.w1_d[l].t, in_=w1sb[:, :, :]), reads=(w1sb,), writes=(C.w1_d[l],))
        S.barrier()


def seg_b(C, l):
    S = C.S
    zxp = C.zx_out[l]
    yxs_in, yxa_in = C.yx_in[l]
    with ExitStack() as es:
        with ExitStack() as es1:
            zs = C.sb([128, 4, L], BF16, "zs", es1)
            XY = C.sb([128, 32 * 256], BF16, "XY", es1)
            Sst = C.sb([128, 16, 256, 2], F32, "Sst", es1)
            spre = [C.sb([128, 16, 256], BF16, f"spre{i}", es1) for i in range(2)]
            YO = C.sb([128, 32 * 256], BF16, "YO", es1)
            w1 = C.sb([128, 32, 128], BF16, "w1", es1)
            w2 = [C.sb([128, 32, 64], BF16, f"w2{i}", es1) for i in range(2)]
            w4 = [C.sb([128, 16, 128], BF16, f"w4{i}", es1) for i in range(2)]
            lam8 = C.sb([128, 16, 2], F32, "lam8", es1)
            lrr = C.sb([128, 16, 2], F32, "lrr", es1)
            lii = C.sb([128, 16, 2], F32, "lii", es1)
            tA = C.sb([128, 16, 2], F32, "tA", es1)
            tB = C.sb([128, 16, 2], F32, "tB", es1)
            dd = C.sb([128, 4], F32, "dd", es1)

            def ld(dst_ap, src_ap, rd, wr, pw=False):
                S.dma("sp", (lambda o, i: (lambda e: e.dma_start(out=o, in_=i)))(dst_ap, src_ap), reads=rd,
                      writes=() if pw else wr, pwrites=wr if pw else ())
            ld(w1[:, :, :], C.w1_d[l].t, (C.w1_d[l],), (w1,))
            ld(w2[0][:, :, :], C.w2re_d[l].t, (C.w2re_d[l],), (w2[0],))
            ld(w2[1][:, :, :], C.w2im_d[l].t, (C.w2im_d[l],), (w2[1],))
            ld(w4[0][:, :, :], C.w4re_d[l].t, (C.w4re_d[l],), (w4[0],))
            ld(w4[1][:, :, :], C.w4im_d[l].t, (C.w4im_d[l],), (w4[1],))
            ld(lam8[:, :, :], C.lam8_d[l].t, (C.lam8_d[l],), (lam8,))
            ld(dd[:, :], C.ssm_in["ssm_dd"].t[l], (C.ssm_in["ssm_dd"],), (dd,))
            for jj in range(2):
                def ldz(e, jj=jj):
                    j512 = C.jval(e, 512)
                    b0 = jj * 1024
                    return e.dma_start(out=zs[:, :, jj * NT:(jj + 1) * NT],
                                       in_=zxp[0].t[b0:b0 + 1024, :][bass.ds(j512, 512), :].rearrange("(c p) t -> p c t", p=128))
                S.dma("sp", ldz, reads=(zxp[0],), pwrites=(zs,))
            S.op("dve", lambda e: e.tensor_copy(lrr[:, :, 0], lam8[:, :, 0]), reads=(lam8,), writes=(lrr,))
            S.op("dve", lambda e: e.tensor_copy(lrr[:, :, 1], lam8[:, :, 0]), reads=(lam8, lrr), writes=(lrr,))
            S.op("dve", lambda e: e.tensor_scalar(lii[:, :, 0], lam8[:, :, 1], -1.0, None, ALU.mult), reads=(lam8,), writes=(lii,))
            S.op("dve", lambda e: e.tensor_copy(lii[:, :, 1], lam8[:, :, 1]), reads=(lam8, lii), writes=(lii,))
            zperm = TB(YO.t[:, :].rearrange("p (k t c) -> p k t c", k=4, t=8), YO.b)
            for cc in range(4):
                eng = ("act", "dve")[cc % 2]
                S.op(eng, cp(eng, zperm[:, cc, :, :], zs[:, cc, :].rearrange("p (c t) -> p t c", t=8)),
                     reads=(zs,), pwrites=(YO,))
            for cc in range(4):
                ld(C.zd.t[:, cc * 128:(cc + 1) * 128, :].rearrange("t p c -> p t c"), zperm[:, cc, :, :], (YO,), (C.zd,), pw=True)
            X = TB(XY.t[:, :].rearrange("p (g c) -> p g c", g=32), XY.b)
            for tau in range(8):
                ld(X[16 * tau:16 * tau + 16, :, :], C.zd.t[tau].rearrange("(g q) c -> q g c", q=16), (C.zd,), (XY,), pw=True)
            for gp in range(16):
                ps = C.psum()
                ops = []
                for gh in range(2):
                    g = gh * 16 + gp
                    for ri in range(2):
                        ops.append(mm(ps[64 * gh:64 * gh + 64, ri * 256:(ri + 1) * 256], w2[ri][:, g, :], X[:, g, :], True, True))
                S.op("pe", ops, reads=(w2[0], w2[1], XY), writes=(ps,))
                eng = ("act", "dve")[gp % 2]
                S.op(eng, cp(eng, Sst[:, gp, :, :], ps[:, :].rearrange("p (r c) -> p c r", r=2)), reads=(ps,), pwrites=(Sst,))
            for c in range(1, 256):
                S.op("dve", (lambda c: (lambda e: e.tensor_tensor(tA[:, :, :], Sst[:, :, c - 1, :], lrr[:, :, :], ALU.mult)))(c),
                     reads=(Sst, lrr), writes=(tA,))
                S.op("dve", (lambda c: (lambda e: e.tensor_tensor(tB[:, :, :], Sst[:, :, c - 1, ::-1], lii[:, :, :], ALU.mult)))(c),
                     reads=(Sst, lii), writes=(tB,))
                S.op("dve", (lambda c: (lambda e: e.tensor_tensor(tA[:, :, :], tA[:, :, :], tB[:, :, :], ALU.add)))(c),
                     reads=(tA, tB), writes=(tA,))
                S.op("dve", (lambda c: (lambda e: e.tensor_tensor(Sst[:, :, c, :], Sst[:, :, c, :], tA[:, :, :], ALU.add)))(c),
                     reads=(tA,), pwrites=(Sst,))
            for ri in range(2):
                eng = ("act", "dve")[ri]
                S.op("dve", (lambda ri: (lambda e: e.memset(spre[ri][:, :, 0:1], 0.0)))(ri), writes=(spre[ri],))
                S.op(eng, cp(eng, spre[ri][:, :, 1:256], Sst[:, :, 0:255, ri]), reads=(Sst, spre[ri]), writes=(spre[ri],))
            Yb = TB(YO.t[:, :].rearrange("p (g c) -> p g c", g=32), YO.b)
            for g2 in range(16):
                ps = C.psum()
                ops = []
                for k in range(2):
                    g = g2 * 2 + k
                    gh, gp = g // 16, g % 16
                    rows = slice(64 * gh, 64 * gh + 64)
                    o = ps[:, k * 256:(k + 1) * 256]
                    ops.append(mm(o, w1[:, g, :], X[:, g, :], True, False))
                    ops.append(mm(o, w4[0][rows, gp, :], spre[0][rows, gp, :], False, False))
                    ops.append(mm(o, w4[1][rows, gp, :], spre[1][rows, gp, :], False, True))
                S.op("pe", ops, reads=(w1, XY, w4[0], w4[1], spre[0], spre[1]), writes=(ps,))
                eng = ("act", "dve")[g2 % 2]
                S.op(eng, cp(eng, Yb[:, 2 * g2:2 * g2 + 2, :], ps[:, :].rearrange("p (g c) -> p g c", g=2)),
                     reads=(ps,), pwrites=(YO,))
            for t in range(8):
                ld(C.yd.t[t].rearrange("g p c -> p g c"), Yb[16 * t:16 * t + 16, :, :], (YO,), (C.yd,), pw=True)
            Ysel = TB(XY.t[:, :].rearrange("p (k t c) -> p k t c", k=4, t=8), XY.b)
            for cc in range(4):
                ld(Ysel[:, cc, :, :], C.yd.t[:, cc * 8:(cc + 1) * 8, :, :].rearrange("t g p c -> (g p) t c"), (C.yd,), (XY,), pw=True)
            yout = TB(YO.t[:, :].rearrange("p (k n) -> p k n", k=4), YO.b)
            for cc in range(4):
                S.op("dve", (lambda cc: (lambda e: e.scalar_tensor_tensor(
                    yout[:, cc, :].rearrange("p (c t) -> p c t", t=8),
                    zs[:, cc, :].rearrange("p (c t) -> p c t", t=8), dd[:, cc:cc + 1],
                    Ysel[:, cc, :, :].rearrange("p t c -> p c t"), ALU.mult, ALU.add)))(cc),
                    reads=(zs, dd, XY), pwrites=(YO,))
            for cc in range(4):
                ld(yxs_in.t[cc * 128:(cc + 1) * 128, :], yout[:, cc, :], (YO,), (yxs_in,), pw=True)
            C.coll_pair(yxs_in, C.yx_out[l][0], pop=False)
            S.barrier()
        with ExitStack() as es2:
            qT = C.sb([128, 4, L], BF16, "qT", es2)
            kT = C.sb([128, 4, L], BF16, "kT", es2)
            vst = C.sb([128, 4, L], BF16, "vst", es2)
            Vt = C.sb([128, 16, 4, 128], BF16, "Vt", es2)
            ao = C.sb([128, 4, L], BF16, "ao", es2)
            cosT = C.sb([32, L], F32, "cosT", es2)
            sinT = C.sb([32, L], F32, "sinT", es2)
            rt = [C.sb([32, 512], F32, f"rt{i}", es2) for i in range(2)]
            ksum = C.sb([128, 4, 8], F32, "ksum", es2)
            khi = C.sb([128, 4, 8], BF16, "khi", es2)
            klo = C.sb([128, 4, 8], BF16, "klo", es2)
            kd = C.sb([128, 4, 8], F32, "kd", es2)
            Gm = C.sb([128, 16, 8], F32, "Gm", es2)
            m8 = C.sb([128, 8], F32, "m8", es2)
            sel = C.sb([128, 16, 8], F32, "sel", es2)
            biasT = C.sb([8, 4, L], BF16, "biasT", es2)
            PT = [C.sb([128, 256], BF16, f"PT{i}", es2) for i in range(3)]
            rec = C.sb([128, 256], F32, "rec", es2)
            ld = lambda d, s_, rd, wr, pw=False: S.dma(
                "sp", (lambda o, i: (lambda e: e.dma_start(out=o, in_=i)))(d, s_), reads=rd,
                writes=() if pw else wr, pwrites=wr if pw else ())
            ld(cosT[:, :], C.consts["rot_cos"].t, (C.consts["rot_cos"],), (cosT,))
            ld(sinT[:, :], C.consts["rot_sin"].t, (C.consts["rot_sin"],), (sinT,))
            for (dst, part) in ((qT, 1), (kT, 2), (vst, 3)):
                for jj in range(2):
                    def ldq(e, dst=dst, part=part, jj=jj):
                        j512 = C.jval(e, 512)
                        b0 = jj * 1024
                        return e.dma_start(out=dst[:, :, jj * NT:(jj + 1) * NT],
                                           in_=zxp[part].t[b0:b0 + 1024, :][bass.ds(j512, 512), :].rearrange("(c p) t -> p c t", p=128))
                    S.dma("sp", ldq, reads=(zxp[part],), pwrites=(dst,))
            for hh in range(4):
                for k4 in range(4):
                    ps = C.psum()
                    psb = ps.t[:, 0:256].bitcast(BF16)
                    S.op("pe", [(lambda o, i_: (lambda e: e.transpose(o, i_, C.identb[:, :])))(
                        psb[:, i * 128:(i + 1) * 128], vst[:, hh, (k4 * 4 + i) * 128:(k4 * 4 + i + 1) * 128])
                                for i in range(4)], reads=(vst, C.identb), writes=(ps,))
                    eng = ("act", "dve")[k4 % 2]
                    S.op(eng, cp(eng, Vt[:, k4 * 4:k4 * 4 + 4, hh, :], psb.rearrange("p (k d) -> p k d", k=4)),
                         reads=(ps,), pwrites=(Vt,))
            it = 0
            for x in (qT, kT):
                for hh in range(4):
                    for tt_ in range(4):
                        ts = slice(tt_ * 512, tt_ * 512 + 512)
                        ps = C.psum()
                        S.op("pe", mm(ps[0:32, :], C.permT[0:32, 0:32], x[0:32, hh, ts], True, True),
                             reads=(x, C.permT), writes=(ps,))
                        r0, r1 = rt[0], rt[1]
                        S.op("dve", (lambda ps, ts: (lambda e: e.tensor_tensor(r0[:, :], ps[0:32, :], sinT[:, ts], ALU.mult)))(ps, ts),
                             reads=(ps, sinT), writes=(r0,))
                        S.op("pool", (lambda x, hh, ts: (lambda e: e.tensor_tensor(r1[:, :], x[0:32, hh, ts], cosT[:, ts], ALU.mult)))(x, hh, ts),
                             reads=(x, cosT), writes=(r1,))
                        S.op("dve", (lambda x, hh, ts: (lambda e: e.tensor_tensor(x[0:32, hh, ts], r0[:, :], r1[:, :], ALU.add)))(x, hh, ts),
                             reads=(r0, r1), writes=(x,))
                        it += 1
            for hh in range(4):
                S.op("dve", (lambda hh: (lambda e: e.tensor_reduce(ksum[:, hh, :], kT[:, hh, :].rearrange("p (n t) -> p n t", t=256), AX.X, ALU.add)))(hh),
                     reads=(kT,), pwrites=(ksum,))
            S.op("dve", lambda e: e.tensor_copy(khi[:, :, :], ksum[:, :, :]), reads=(ksum,), writes=(khi,))
            S.op("dve", lambda e: e.tensor_tensor(kd[:, :, :], ksum[:, :, :], khi[:, :, :], ALU.subtract), reads=(ksum, khi), writes=(kd,))
            S.op("dve", lambda e: e.tensor_copy(klo[:, :, :], kd[:, :, :]), reads=(kd,), writes=(klo,))
            for hh in range(4):
                ps = C.psum()
                ops = []
                for qt in range(16):
                    o = ps[:, qt * 8:(qt + 1) * 8]
                    ops.append(mm(o, qT[:, hh, qt * 128:(qt + 1) * 128], khi[:, hh, :], True, False))
                    ops.append(mm(o, qT[:, hh, qt * 128:(qt + 1) * 128], klo[:, hh, :], False, True))
                S.op("pe", ops, reads=(qT, khi, klo), writes=(ps,))
                S.op("dve", (lambda ps: (lambda e: e.tensor_tensor(Gm[:, :, :], ps[:, 0:128].rearrange("p (q n) -> p q n", n=8), C.negm[:, :, :], ALU.add)))(ps),
                     reads=(ps, C.negm), writes=(Gm,))
                for qt in range(16):
                    S.op("dve", (lambda qt: (lambda e: e.max(m8[:, :], Gm[:, qt, :])))(qt), reads=(Gm,), writes=(m8,))
                    S.op("dve", (lambda qt: (lambda e: e.tensor_scalar(sel[:, qt, :], Gm[:, qt, :], m8[:, 2:3], 1.0, ALU.is_ge, ALU.subtract)))(qt),
                         reads=(Gm, m8), pwrites=(sel,))
                for q4 in range(4):
                    ps2 = C.psum()
                    S.op("pe", [(lambda o, i_: (lambda e: e.transpose(o, i_, C.ident[:, :])))(
                        ps2[0:8, i * 128:(i + 1) * 128], sel[:, q4 * 4 + i, :])
                                for i in range(4)], reads=(sel, C.ident), writes=(ps2,))
                    S.op("act", cp("act", biasT[0:8, hh, q4 * 512:(q4 + 1) * 512], ps2[0:8, :]), reads=(ps2,), pwrites=(biasT,))
            O_acc, S_acc = C.pbank[6], C.pbank[7]
            for hh in range(4):
                for QB in range(8):
                    qs = QB * 256
                    nkt = 2 * QB + 2
                    pend = []

                    def score(kt, hh=hh, QB=QB, qs=qs):
                        sc = C.psum()
                        ops = [mm(sc[:, 0:256], kT[:, hh, kt * 128:(kt + 1) * 128], qT[:, hh, qs:qs + 256], True, False)]
                        rd = [kT, qT]
                        if kt < 2 * QB:
                            ops.append(mm(sc[:, 0:256], C.oh[0:8, kt // 2, :], biasT[0:8, hh, qs:qs + 256], False, True))
                            rd += [C.oh, biasT]
                        else:
                            ops.append(mm(sc[:, 0:256], C.identb[:, :], C.cm[:, kt - 2 * QB, :], False, True))
                            rd += [C.identb, C.cm]
                        S.op("pe", ops, reads=tuple(rd), writes=(sc,))
                        pt = PT[kt % 3]
                        S.op("act", (lambda sc, pt: (lambda e: e.activation(pt[:, :], sc[:, 0:256], AF.Exp, scale=SCALE)))(sc, pt),
                             reads=(sc,), writes=(pt,))
                        return pt

                    def accum(kt, pt, hh=hh, nkt=nkt):
                        S.op("pe", [mm(O_acc[:, 0:256], Vt[:, kt, hh, :], pt[:, :], kt == 0, kt == nkt - 1),
                                    mm(S_acc[:, 0:256], C.ones[:, :], pt[:, :], kt == 0, kt == nkt - 1)],
                             reads=(Vt, pt, C.ones), writes=(), pwrites=(O_acc, S_acc))
                    for kt in range(nkt):
                        pend.append((kt, score(kt)))
                        if len(pend) > 1:
                            accum(*pend.pop(0))
                    while pend:
                        accum(*pend.pop(0))
                    S.op("dve", lambda e: e.reciprocal(rec[:, :], S_acc[:, 0:256]), reads=(S_acc,), writes=(rec,))
                    S.op("dve", (lambda hh, qs: (lambda e: e.tensor_tensor(ao[:, hh, qs:qs + 256], O_acc[:, 0:256], rec[:, :], ALU.mult)))(hh, qs),
                         reads=(O_acc, rec), pwrites=(ao,))
            for hh in range(4):
                ld(yxa_in.t[hh * 128:(hh + 1) * 128, :], ao[:, hh, :], (ao,), (yxa_in,), pw=True)
            S.barrier()
    C.coll_pair(yxa_in, C.yx_out[l][1], pop=True)


CONST_SPECS = {
    "ident": ([128, 128], F32), "identb": ([128, 128], BF16), "permT": ([32, 32], BF16),
    "oh": ([8, 8, 128], BF16), "cm": ([128, 2, 256], BF16), "negm": ([128, 16, 8], F32),
    "rot_cos": ([32, L], F32), "rot_sin": ([32, L], F32),
}
SSM_SPECS = {
    "ssm_ar": [DEPTH, 128, 16], "ssm_ai": [DEPTH, 128, 16], "ssm_ldt": [DEPTH, 128, 16],
    "ssm_ct_re": [DEPTH, 128, 16, 16], "ssm_ct_im": [DEPTH, 128, 16, 16],
    "ssm_bt_re": [DEPTH, 128, 16, 16], "ssm_bt_im": [DEPTH, 128, 16, 16], "ssm_dd": [DEPTH, 128, 4],
}


def build_program(ncores=8, mode="full"):
    nc = bass.Bass("TRN2", target_bir_lowering=False)
    NW = ncores
    pairs = [[2 * i, 2 * i + 1] for i in range(ncores // 2)]
    with ExitStack() as es:
        C = Ctx(nc, es)
        S = C.S
        ext = lambda name, shape, dt: TB(nc.dram_tensor(name, shape, dt, kind="ExternalInput").ap(), S.buf(name))
        C.xT = ext("xT", [D, NT], F32)
        C.pT = ext("pT", [DEPTH, PLE, NT], F32)
        C.gains_d = ext("gains", [128, (DEPTH * 4 + 1) * KC], F32)
        C.ssm_in = {k: ext(k, shp, F32) for k, shp in SSM_SPECS.items()}
        C.consts = {k: ext(k, shp, dt) for k, (shp, dt) in CONST_SPECS.items()}
        C.wf = [{n: WMat(C, f"{l}_{n}", K, M, NW) for n, K, M in WSPEC} for l in range(DEPTH)]
        SEGA_W = ["ffn1_w_gate", "ffn1_w_up", "ffn1_w_down", "w_in"]
        wsh = {n: ext("w_" + n, [DEPTH if mode == "full" else 1, C.wf[0][n].per_rank], F32) for n, K, M in WSPEC
               if mode == "full" or (mode == "sega" and n in SEGA_W)}
        if mode == "segb":
            C.zin = ext("zin", [8192, NT], BF16)
        C.oT = TB(nc.dram_tensor("oT", [D, NT], F32, kind="ExternalOutput").ap(), S.buf("oT"))
        C.zx_in = [[C.dram(f"zx_in{l}_{i}", [1024, NT], BF16) for i in range(4)] for l in range(DEPTH)]
        C.zx_out = [[C.dram(f"zx_out{l}_{i}", [2048, NT], BF16) for i in range(4)] for l in range(DEPTH)]
        C.yx_in = [[C.dram(f"yx_in{l}_{i}", [512, L], BF16) for i in range(2)] for l in range(DEPTH)]
        C.yx_out = [[C.dram(f"yx_out{l}_{i}", [1024, L], BF16) for i in range(2)] for l in range(DEPTH)]
        C.sg_d = [C.dram(f"sg_d{l}", [4096, NT], BF16) for l in range(DEPTH)]
        C.hsp = C.dram("hsp", [D, NT], F32)
        C.zd = C.dram("zd", [8, 512, 256], BF16)
        C.yd = C.dram("yd", [8, 32, 16, 256], BF16)
        C.w1_d = [C.dram(f"w1_d{l}", [128, 32, 128], BF16) for l in range(DEPTH)]
        C.w2re_d = [C.dram(f"w2re_d{l}", [128, 32, 64], BF16) for l in range(DEPTH)]
        C.w2im_d = [C.dram(f"w2im_d{l}", [128, 32, 64], BF16) for l in range(DEPTH)]
        C.w4re_d = [C.dram(f"w4re_d{l}", [128, 16, 128], BF16) for l in range(DEPTH)]
        C.w4im_d = [C.dram(f"w4im_d{l}", [128, 16, 128], BF16) for l in range(DEPTH)]
        C.lam8_d = [C.dram(f"lam8_d{l}", [128, 16, 2], F32) for l in range(DEPTH)]
        C.ones = C.sb([128, 128], BF16, "ones")
        C.eps_ap = C.sb([128, 1], F32, "eps")
        C.gains = C.sb([128, (DEPTH * 4 + 1) * KC], F32, "gains")

        mid = ["ssm_w_glu", "w_branch_ssm", "w_branch_attn", "w_out"]
        tail = ["ple_w_gate", "ple_w_up"]

        def ffn_chunks(l, pre):
            out = []
            g, u, d = C.wf[l][pre + "_w_gate"], C.wf[l][pre + "_w_up"], C.wf[l][pre + "_w_down"]
            for i in range(len(g.chunks)):
                out += [(l, pre + "_w_gate", i), (l, pre + "_w_up", i), (l, pre + "_w_down", i)]
            return out

        quads = [[0, 1, 2, 3], [4, 5, 6, 7]]
        xpairs = [[0, 4], [1, 5], [2, 6], [3, 7]]

        def coll(groups, src, dst):
            S.dma("pool", lambda e: e.collective_compute("AllGather", ALU.bypass, replica_groups=groups,
                                                         ins=[src.t], outs=[dst.t]),
                  reads=(src,), writes=(dst,), inc=1)

        def chunk_list(l, names):
            return [(l, n, i) for n in names for i in range(len(C.wf[l][n].chunks))]

        def bounce(l, n, i):
            w = C.wf[l][n]
            r0, nr, c0, ncw = w.chunks[i]
            src = wsh[n].t[l, w.off[i]:w.off[i] + (nr // NW) * ncw].rearrange("(r c) -> r c", c=ncw)
            S.dma("pool", (lambda o, s_: (lambda e: e.dma_start(out=o, in_=s_)))(w.bounce[i].t, src),
                  reads=(wsh[n],), writes=(w.bounce[i],))

        def gather(chunks):
            LOOK = 12
            for k in range(min(LOOK, len(chunks))):
                bounce(*chunks[k])
            for k, (l, n, i) in enumerate(chunks):
                w = C.wf[l][n]
                if NW == 8:
                    coll(quads, w.bounce[i], w.half[i])
                    coll(xpairs, w.half[i], w.full[i])
                else:
                    coll([list(range(NW))], w.bounce[i], w.full[i])
                if k + LOOK < len(chunks):
                    bounce(*chunks[k + LOOK])

        def coll_pair(src, dst, pop=True):
            S.dma("pool", lambda e: e.collective_compute("AllGather", ALU.bypass, replica_groups=pairs,
                                                         ins=[src.t], outs=[dst.t]),
                  reads=(src,), writes=(dst,), inc=1)
            if pop and C.gbatches:
                gather(C.gbatches.pop(0))
        C.coll_pair = coll_pair

        def load_consts(names, scope):
            for k in names:
                shp, dt = CONST_SPECS[k]
                t = C.sb(shp, dt, k, scope)
                idx = tuple(slice(None) for _ in shp)
                S.dma("sp", (lambda o, i: (lambda e: e.dma_start(out=o, in_=i)))(t.t[idx], C.consts[k].t),
                      reads=(C.consts[k],), writes=(t,))
                setattr(C, k, t)

        def body():
            order0 = ["ffn1_w_gate", "ffn1_w_up", "ffn1_w_down"]
            C.gbatches = [ffn_chunks(1, "ffn1") + chunk_list(1, ["w_in"]), chunk_list(1, mid) + ffn_chunks(1, "ffn2") + chunk_list(1, tail)]
            S.op("dve", lambda e: e.memset(C.ones[:, :], 1.0), writes=(C.ones,))
            S.op("dve", lambda e: e.memset(C.eps_ap[:, :], EPS), writes=(C.eps_ap,))
            S.dma("sp", lambda e: e.dma_start(out=C.gains[:, :], in_=C.gains_d.t), reads=(C.gains_d,), writes=(C.gains,))
            gather(ffn_chunks(0, "ffn1") + chunk_list(0, ["w_in"]) + chunk_list(0, mid) + ffn_chunks(0, "ffn2") + chunk_list(0, tail))
            with ExitStack() as p0:
                load_consts(["ident"], p0)
                for l in range(DEPTH):
                    ssm_prep(C, l)
                S.barrier()
            for l in range(DEPTH):
                if l == 0:
                    ph = ExitStack()
                    T = TokTiles(C, ph)
                    C.wphase_begin()
                    S.dma("sp", (lambda h_: (lambda e: e.dma_start(out=h_[:, :, :], in_=C.xT.t.rearrange("(k p) t -> p k t", p=128))))(T.h),
                          reads=(C.xT,), writes=(T.h,))
                seg_a(C, T, l)
                S.barrier()
                ph.close()
                with ExitStack() as pb:
                    load_consts(["ident", "identb", "permT", "oh", "cm", "negm"], pb)
                    seg_b(C, l)
                ph = ExitStack()
                T = TokTiles(C, ph)
                C.wphase_begin()
                seg_c(C, T, l, last=(l == DEPTH - 1))
            S.barrier()
            ph.close()

        def plan_ffn(l, pre):
            out = []
            for ch in range(DFF // 512):
                out += [(pre + "_w_gate", 0, KC, ch * 512, 512), (pre + "_w_up", 0, KC, ch * 512, 512),
                        (pre + "_w_down", ch * 512, 4, 0, D)]
            return [(l,) + b for b in out]

        def plan_a(l):
            return plan_ffn(l, "ffn1") + [(l, "w_in", 0, KC, cb * 512, 512) for cb in range(16)]

        def plan_c(l):
            out = []
            for hf in range(2):
                out += [("ssm_w_glu", 0, 8, 0, 1024), ("ssm_w_glu", 0, 8, 1024, 1024)]
                for half in range(2):
                    out += [("w_branch_ssm", 0, 8, half * 1024, 1024), ("w_branch_attn", 0, 8, half * 1024, 1024)]
                out += [("w_out", 0, KC, cb * 512, 512) for cb in range(4)]
            out = [(l,) + b for b in out] + plan_ffn(l, "ffn2")
            for cb in range(4):
                out += [(l, "ple_w_gate", 0, KC, cb * 512, 512), (l, "ple_w_up", 0, 2, cb * 512, 512)]
            return out
        def body_prep():
            with ExitStack() as p0:
                load_consts(["ident"], p0)
                ssm_prep(C, 0)
                S.barrier()
            toks = []
            for nm, src in (("o_w1", C.w1_d[0]), ("o_w2re", C.w2re_d[0]), ("o_w2im", C.w2im_d[0]),
                            ("o_w4re", C.w4re_d[0]), ("o_w4im", C.w4im_d[0]), ("o_lam8", C.lam8_d[0])):
                shp = [int(x) for x in src.t.shape]
                dt = F32 if nm == "o_lam8" else BF16
                o = nc.dram_tensor(nm, shp, dt, kind="ExternalOutput").ap()
                toks.append(S.dma("sp", (lambda o, s_: (lambda e: e.dma_start(out=o, in_=s_)))(o, src.t), reads=(src,)))
            S.wait_all("sp", toks)

        def body_segb():
            S.op("dve", lambda e: e.memset(C.ones[:, :], 1.0), writes=(C.ones,))
            with ExitStack() as p0:
                load_consts(["ident"], p0)
                ssm_prep(C, 0)
                S.barrier()
            for jj in range(2):
                for part in range(4):
                    S.dma("sp", (lambda jj, part: (lambda e: e.dma_start(
                        out=C.zx_out[0][part].t[jj * 1024:(jj + 1) * 1024, :],
                        in_=C.zin.t[jj * 4096 + part * 1024:jj * 4096 + (part + 1) * 1024, :])))(jj, part),
                        reads=(C.zin,), pwrites=(C.zx_out[0][part],))
            C.coll_pair = lambda a, b, pop=True: None
            with ExitStack() as pb:
                load_consts(["ident", "identb", "permT", "oh", "cm", "negm"], pb)
                seg_b(C, 0)
            o = nc.dram_tensor("o_yx", [1024, L], BF16, kind="ExternalOutput").ap()
            tk = S.dma("sp", lambda e: e.dma_start(out=o[0:512, :], in_=C.yx_in[0][0].t), reads=(C.yx_in[0][0],))
            tk2 = S.dma("sp", lambda e: e.dma_start(out=o[512:1024, :], in_=C.yx_in[0][1].t), reads=(C.yx_in[0][1],))
            S.wait_all("sp", [tk, tk2])

        def body_sega():
            S.op("dve", lambda e: e.memset(C.ones[:, :], 1.0), writes=(C.ones,))
            S.op("dve", lambda e: e.memset(C.eps_ap[:, :], EPS), writes=(C.eps_ap,))
            S.dma("sp", lambda e: e.dma_start(out=C.gains[:, :], in_=C.gains_d.t), reads=(C.gains_d,), writes=(C.gains,))
            C.gbatches = []
            gather(ffn_chunks(0, "ffn1") + chunk_list(0, ["w_in"]))
            ph = ExitStack()
            T = TokTiles(C, ph)
            C.wphase_begin()
            S.dma("sp", (lambda h_: (lambda e: e.dma_start(out=h_[:, :, :], in_=C.xT.t.rearrange("(k p) t -> p k t", p=128))))(T.h),
                  reads=(C.xT,), writes=(T.h,))
            seg_a(C, T, 0)
            S.barrier()
            ph.close()
            o1 = nc.dram_tensor("o_zx", [8192, NT], BF16, kind="ExternalOutput").ap()
            o2 = nc.dram_tensor("o_h", [D, NT], F32, kind="ExternalOutput").ap()
            tks = [S.dma("sp", (lambda i: (lambda e: e.dma_start(out=o1[i * 2048:(i + 1) * 2048, :], in_=C.zx_out[0][i].t)))(i),
                         reads=(C.zx_out[0][i],)) for i in range(4)]
            tks.append(S.dma("sp", lambda e: e.dma_start(out=o2, in_=C.hsp.t), reads=(C.hsp,)))
            S.wait_all("sp", tks)

        if mode == "sega":
            for (l, n, k0, kc, c0, ncols) in plan_a(0):
                C.wplan.append((C.wf[l][n], k0, kc, c0, ncols))
            C.wphase.append(0)
            body_sega()
            S.emit()
            return nc
        if mode == "prep":
            body_prep()
            S.emit()
            return nc
        if mode == "segb":
            body_segb()
            S.emit()
            return nc
        phases = [plan_a(0), plan_c(0) + plan_a(1), plan_c(1)]
        for ph_ in phases:
            C.wphase.append(len(C.wplan))
            for (l, n, k0, kc, c0, ncols) in ph_:
                C.wplan.append((C.wf[l][n], k0, kc, c0, ncols))
        body()
        S.emit()
    return nc


def _consts():
    bf = ml_dtypes.bfloat16
    c = {}
    c["ident"] = np.eye(128, dtype=np.float32)
    c["identb"] = np.eye(128, dtype=np.float32).astype(bf)
    pm = np.zeros((32, 32), np.float32)
    for d in range(16):
        pm[d + 16, d] = -1.0
        pm[d, d + 16] = 1.0
    c["permT"] = pm.astype(bf)
    oh = np.zeros((8, 8, 128), np.float32)
    for n in range(8):
        oh[n, n, :] = BIG
    c["oh"] = oh.astype(bf)
    cm = np.zeros((128, 2, 256), np.float32)
    k = np.arange(128)[:, None]
    q = np.arange(256)[None, :]
    cm[:, 0, :] = np.where(k > q, -BIG, 0.0)
    cm[:, 1, :] = np.where(k + 128 > q, -BIG, 0.0)
    c["cm"] = cm.astype(bf)
    negm = np.zeros((128, 16, 8), np.float32)
    for qt in range(16):
        for n in range(8):
            if n >= qt // 2:
                negm[:, qt, n] = -1.0e30
    c["negm"] = negm
    pos = np.arange(L, dtype=np.float32)
    inv_freq = (1.0 / (np.float32(500000.0) ** (np.arange(0, 32, 2, dtype=np.float32) / np.float32(32)))).astype(np.float32)
    ang = pos[None, :] * inv_freq[:, None]
    c["rot_cos"] = np.concatenate([np.cos(ang), np.cos(ang)], 0).astype(np.float32)
    c["rot_sin"] = np.concatenate([np.sin(ang), np.sin(ang)], 0).astype(np.float32)
    return c


def make_in_maps(inputs, ncores=8, batch_of_core=None):
    NW = ncores
    consts = _consts()
    x = inputs["x"]
    p = inputs["p"]
    gl = [inputs[n][l] for l in range(DEPTH) for n in NORMS] + [inputs["final_norm"]]
    gains = np.ascontiguousarray(np.concatenate([g.reshape(KC, 128).T for g in gl], axis=1).astype(np.float32))
    in_maps = []
    for r in range(ncores):
        b = (r // 2) if batch_of_core is None else batch_of_core[r]
        j = r % 2
        m = {}
        m["xT"] = np.ascontiguousarray(x[b, j * NT:(j + 1) * NT, :].T)
        m["pT"] = np.ascontiguousarray(np.transpose(p[:, b, j * NT:(j + 1) * NT, :], (0, 2, 1)))
        m["gains"] = gains
        gs = slice(32 * j, 32 * j + 32)

        def nlay(a):
            a = a.reshape((DEPTH, 2, 16, 64) + a.shape[3:])
            a = np.moveaxis(a, 3, 2)
            return np.ascontiguousarray(a.reshape((DEPTH, 128, 16) + a.shape[4:]))
        m["ssm_ar"] = nlay(inputs["ssm_a_re"][:, gs])
        m["ssm_ai"] = nlay(inputs["ssm_a_im"][:, gs])
        m["ssm_ldt"] = nlay(np.broadcast_to(inputs["ssm_log_dt"][:, gs, None], (DEPTH, 32, 64)))
        m["ssm_bt_re"] = nlay(inputs["ssm_b_re"][:, gs])
        m["ssm_bt_im"] = nlay(inputs["ssm_b_im"][:, gs])
        m["ssm_ct_re"] = nlay(np.transpose(inputs["ssm_c_re"][:, gs], (0, 1, 3, 2)))
        m["ssm_ct_im"] = nlay(np.transpose(inputs["ssm_c_im"][:, gs], (0, 1, 3, 2)))
        m["ssm_dd"] = np.ascontiguousarray(np.transpose(
            inputs["ssm_d"][:, 512 * j:512 * j + 512].reshape(DEPTH, 4, 128), (0, 2, 1)))
        for name, K, M in WSPEC:
            w = inputs[name]
            parts = []
            for (r0, nr, c0, ncw) in wchunks(K, M):
                q = nr // NW
                parts.append(w[:, r0 + r * q:r0 + (r + 1) * q, c0:c0 + ncw].reshape(DEPTH, -1))
            m["w_" + name] = np.ascontiguousarray(np.concatenate(parts, axis=1))
        m.update(consts)
        in_maps.append(m)
    return in_maps


_NC_CACHE = {}


def kernel(**inputs):
    inputs = {k: np.asarray(v) for k, v in inputs.items()}
    if 8 not in _NC_CACHE:
        _NC_CACHE[8] = build_program(8)
    nc = _NC_CACHE[8]
    in_maps = make_in_maps(inputs, 8)
    res = run_bass_kernel_spmd(nc, in_maps, core_ids=list(range(8)))
    out = np.empty((4, L, D), np.float32)
    for r in range(8):
        out[r // 2, (r % 2) * NT:(r % 2 + 1) * NT, :] = res.results[r]["oT"].T
    return out
```

```python
import numpy as np
from contextlib import ExitStack
import ml_dtypes
import concourse.bass as bass
import concourse.mybir as mybir
from concourse.bass_utils import run_bass_kernel_spmd

F32 = mybir.dt.float32
BF16 = mybir.dt.bfloat16
I32 = mybir.dt.int32
AF = mybir.ActivationFunctionType
ALU = mybir.AluOpType
AX = mybir.AxisListType

D = 2048
DFF = 5632
NT = 1024
L = 2048
KC = D // 128
EPS = 1e-6
DEPTH = 2
PLE = 256
BIG = 1.0e4
SCALE = 1.0 / float(np.sqrt(128.0))
ENGS = ["pe", "act", "dve", "pool", "sp"]
TWO_PI = float(2.0 * np.pi)

WSPEC = [
    ("ffn1_w_gate", D, DFF), ("ffn1_w_up", D, DFF), ("ffn1_w_down", DFF, D),
    ("w_in", D, 8192), ("ssm_w_glu", 1024, 2048), ("w_branch_ssm", 1024, 2048),
    ("w_branch_attn", 1024, 2048), ("w_out", D, D),
    ("ffn2_w_gate", D, DFF), ("ffn2_w_up", D, DFF), ("ffn2_w_down", DFF, D),
    ("ple_w_up", PLE, D), ("ple_w_gate", D, D),
]
NORMS = ["ffn1_norm", "mix_norm", "ffn2_norm", "ple_norm"]


class Buf:
    __slots__ = ("name", "w", "r", "pw")

    def __init__(self, name):
        self.name = name
        self.w = None
        self.r = []
        self.pw = []


class TB:
    def __init__(self, t, b):
        self.t = t
        self.b = b

    def __getitem__(self, idx):
        return self.t[idx]


def _b(x):
    return x.b if isinstance(x, TB) else x


class Sched:
    def __init__(self, nc, es, ndma=16):
        self.nc = nc
        self.es = es
        self.prog = {e: [] for e in ENGS}
        self.cnt = {e: 0 for e in ENGS}
        self.known = {e: {} for e in ENGS}
        self.sems = {}
        self.ndma = ndma
        self.dma_val = {}
        self.dma_rr = {e: 0 for e in ENGS}
        self.dry = False
        self.nbuf = 0

    def buf(self, name=None):
        self.nbuf += 1
        return Buf(name or f"b{self.nbuf}")

    def sem(self, key):
        if key not in self.sems:
            self.sems[key] = self.es.enter_context(self.nc.semaphore(key))
        return self.sems[key]

    def _deps(self, eng, reads, writes, pwrites=()):
        toks = []
        for b in reads:
            b = _b(b)
            if b.w is not None:
                toks.append(b.w)
            toks.extend(b.pw)
        for b in writes:
            b = _b(b)
            if b.w is not None:
                toks.append(b.w)
            toks.extend(b.r)
            toks.extend(b.pw)
        for b in pwrites:
            b = _b(b)
            if b.w is not None:
                toks.append(b.w)
            toks.extend(b.r)
        best = {}
        for k, v in toks:
            if v > best.get(k, 0):
                best[k] = v
        waits = []
        kn = self.known[eng]
        for k, v in best.items():
            if eng == "pe" and k == "c_pe":
                continue
            if kn.get(k, 0) >= v:
                continue
            kn[k] = v
            waits.append((k, v))
        return waits

    @staticmethod
    def _compact(lst):
        best = {}
        for k, v in lst:
            if v > best.get(k, 0):
                best[k] = v
        return list(best.items())

    def _commit(self, tok, reads, writes, pwrites=()):
        for b in reads:
            b = _b(b)
            b.r.append(tok)
            if len(b.r) > 48:
                b.r = self._compact(b.r)
        for b in writes:
            b = _b(b)
            b.w = tok
            b.r = []
            b.pw = []
        for b in pwrites:
            b = _b(b)
            b.pw.append(tok)
            if len(b.pw) > 48:
                b.pw = self._compact(b.pw)

    def op(self, eng, fns, reads=(), writes=(), pwrites=()):
        if self.dry:
            return None
        if not isinstance(fns, (list, tuple)):
            fns = [fns]
        waits = self._deps(eng, reads, writes, pwrites)
        self.cnt[eng] += 1
        tok = ("c_" + eng, self.cnt[eng])
        self.prog[eng].append((waits, list(fns), (tok[0], 1)))
        self._commit(tok, reads, writes, pwrites)
        return tok

    def dma(self, eng, fn, reads=(), writes=(), pwrites=(), inc=16):
        if self.dry:
            return None
        i = self.dma_rr[eng]
        self.dma_rr[eng] = (i + 1) % self.ndma
        key = f"d_{eng}_{i}" if inc == 16 else f"x_{eng}_{i}"
        prev = self.dma_val.get(key, 0)
        waits = self._deps(eng, reads, writes, pwrites)
        if prev > 0 and self.known[eng].get(key, 0) < prev:
            self.known[eng][key] = prev
            waits.append((key, prev))
        val = prev + inc
        self.dma_val[key] = val
        tok = (key, val)
        self.prog[eng].append((waits, [fn], (key, inc)))
        self._commit(tok, reads, writes, pwrites)
        return tok

    def barrier(self):
        if self.dry:
            return
        toks = [("c_" + e, self.cnt[e]) for e in ENGS if self.cnt[e] > 0 and e != "pool"]
        toks += [(k, v) for k, v in self.dma_val.items() if "_pool_" not in k]
        for e in ENGS:
            if e != "pool":
                self.wait_all(e, toks)

    def wait_all(self, eng, toks):
        if self.dry:
            return
        waits = []
        for k, v in toks:
            if self.known[eng].get(k, 0) < v:
                self.known[eng][k] = v
                waits.append((k, v))
        if waits:
            self.prog[eng].append((waits, [], None))

    def emit(self):
        nc = self.nc
        for e in ENGS:
            for waits, fns, inc in self.prog[e]:
                for k, v in waits:
                    self.sem(k)
                if inc is not None:
                    self.sem(inc[0])
        with nc.Block() as block:
            def runner(ename):
                def run(eng):
                    for waits, fns, inc in self.prog[ename]:
                        for k, v in waits:
                            eng.wait_ge(self.sems[k], v)
                        ins = None
                        for f in fns:
                            ins = f(eng)
                        if inc is not None:
                            ins.then_inc(self.sems[inc[0]], inc[1])
                return run
            block.tensor(runner("pe"))
            block.scalar(runner("act"))
            block.vector(runner("dve"))
            block.gpsimd(runner("pool"))
            block.sync(runner("sp"))


class Ctx:
    def __init__(self, nc, es):
        self.nc = nc
        self.es = es
        self.S = Sched(nc, es)
        self.nname = 0
        self.pbank = []
        for i in range(8):
            t = es.enter_context(nc.psum_tensor(f"ps{i}", [128, 512], F32))
            self.pbank.append(TB(t, self.S.buf(f"ps{i}")))
        self.prr = 0
        self.NSLOT = 4
        self.wslots = None
        self.wplan = []
        self.wissued = 0
        self.wnext = 0
        self.wrel_count = 0
        self.wphase = []
        self.wlimit = 0

    def jval(self, e, mult):
        if not hasattr(self, "_jv"):
            self._jv = {}
        key = (id(e), mult)
        if key not in self._jv:
            self._jv[key] = e.snap((e.partition_id() % 2) * mult)
        return self._jv[key]

    ARENA_ELEMS = 106240

    def sb(self, shape, dtype, name=None, es=None):
        if not hasattr(self, "arena"):
            self.arena = self.es.enter_context(self.nc.sbuf_tensor("arena", [128, self.ARENA_ELEMS], BF16))
            self.aoff = 0
            self.scopes = set()
        if es is not None and id(es) not in self.scopes:
            self.scopes.add(id(es))
            mark = self.aoff

            def rel(mark=mark, key=id(es)):
                self.aoff = mark
                self.scopes.discard(key)
            es.callback(rel)
        self.nname += 1
        nm = f"{name or 't'}_{self.nname}"
        esz = 2 if dtype == BF16 else 4
        n = 1
        for d in shape[1:]:
            n *= d
        nbf = (n * esz + 1) // 2
        nbf = (nbf + 15) // 16 * 16
        assert self.aoff + nbf <= self.ARENA_ELEMS, (nm, self.aoff, nbf)
        ap = self.arena[0:shape[0], self.aoff:self.aoff + (n * esz) // 2]
        self.aoff += nbf
        self.apeak = max(getattr(self, "apeak", 0), self.aoff)
        if dtype != BF16:
            ap = ap.bitcast(dtype)
        if len(shape) > 2:
            names = " ".join(f"d{i}" for i in range(1, len(shape)))
            ap = ap.rearrange(f"p ({names}) -> p {names}", **{f"d{i}": shape[i] for i in range(1, len(shape) - 1)})
        return TB(ap, self.S.buf(nm))

    def dram(self, name, shape, dtype):
        return TB(self.nc.dram_tensor(name, shape, dtype).ap(), self.S.buf(name))

    def psum(self):
        b = self.pbank[self.prr]
        self.prr = (self.prr + 1) % 6
        return b

    def wphase_begin(self):
        starts = self.wphase
        cur = self.wnext
        nxt = [s for s in starts if s > cur]
        self.wlimit = nxt[0] if nxt else len(self.wplan)
        assert self.wissued == self.wnext, (self.wissued, self.wnext)
        self.wbase = self.wnext
        for _ in range(self.NSLOT):
            self._wissue()

    def _wissue(self):
        S = self.S
        j = self.wissued
        if j >= self.wlimit:
            return
        w2, k2, kc2, c2, n2 = self.wplan[j]
        ci, lr, lc = w2.find(k2, kc2, c2, n2)
        ch = w2.full[ci]
        sl = self.wslots[(j - self.wbase) % self.NSLOT]
        dst = sl.t[:, 0:kc2 * n2].rearrange("p (k c) -> p k c", k=kc2)
        src = ch.t[lr:lr + kc2 * 128, lc:lc + n2].rearrange("(k p) c -> p k c", p=128)
        S.dma("sp", (lambda d, s: (lambda e: e.dma_start(out=d, in_=s)))(dst, src),
              reads=(ch,), writes=(sl,))
        self.wissued += 1

    def wget(self, w, k0, kc, c0, ncols):
        S = self.S
        idx = self.wnext
        self.wnext += 1
        assert self.wplan[idx][1:] == (k0, kc, c0, ncols) and self.wplan[idx][0] is w, (idx, self.wplan[idx][1:], (k0, kc, c0, ncols))
        assert idx < self.wissued, (idx, self.wissued)
        sl = self.wslots[(idx - self.wbase) % self.NSLOT]
        return TB(sl.t[:, 0:kc * ncols].rearrange("p (k c) -> p k c", k=kc), sl.b)

    def wrel(self, n=1):
        if self.S.dry:
            return
        for _ in range(n):
            self._wissue()


def wchunks(K, M):
    if K > 2048:
        return [(r0, min(1024, K - r0), 0, M) for r0 in range(0, K, 1024)]
    if K == 2048:
        return [(0, K, c0, min(1024, M - c0)) for c0 in range(0, M, 1024)]
    return [(0, K, 0, M)]


class WMat:
    def __init__(self, C, tag, K, M, nw):
        self.K, self.M, self.nw = K, M, nw
        self.chunks = wchunks(K, M)
        self.bounce, self.half, self.full, self.off = [], [], [], []
        o = 0
        for i, (r0, nr, c0, ncw) in enumerate(self.chunks):
            self.off.append(o)
            o += (nr // nw) * ncw
            self.bounce.append(C.dram(f"wb_{tag}_{i}", [nr // nw, ncw], BF16))
            self.half.append(C.dram(f"wh_{tag}_{i}", [nr // 2, ncw], BF16) if nw == 8 else None)
            self.full.append(C.dram(f"wf_{tag}_{i}", [nr, ncw], BF16))
        self.per_rank = o

    def find(self, k0, kc, c0, ncols):
        for i, (r0, nr, cc0, ncw) in enumerate(self.chunks):
            if r0 <= k0 and k0 + kc * 128 <= r0 + nr and cc0 <= c0 and c0 + ncols <= cc0 + ncw:
                return i, k0 - r0, c0 - cc0
        raise AssertionError((k0, kc, c0, ncols))


def mm(out, lhsT, rhs, start, stop):
    return lambda e: e.matmul(out, lhsT, rhs, start=start, stop=stop)


def tsl(th):
    return slice(th * 512, th * 512 + 512)


def emit_rmsnorm(C, T, h, gain_col, out):
    S = C.S
    for th in range(NT // 512):
        ts = tsl(th)
        pt = C.psum()
        for kc in range(KC):
            sq = T.sq[kc % 2]
            S.op("act", (lambda o, i: (lambda e: e.activation(o, i, AF.Square)))(sq[:, :], h[:, kc, ts]),
                 reads=(h,), writes=(sq,))
            S.op("pe", mm(pt[:, :], C.ones[:, :], sq[:, :], kc == 0, kc == KC - 1),
                 reads=(sq, C.ones), writes=(pt,))
        rstd = T.rstd
        S.op("act", (lambda p_: (lambda e: e.activation(rstd[:, :], p_, AF.Sqrt, bias=C.eps_ap[:, 0:1], scale=1.0 / D)))(pt[:, :]),
             reads=(pt, C.eps_ap), writes=(rstd,))
        S.op("dve", lambda e: e.reciprocal(rstd[:, :], rstd[:, :]), reads=(rstd,), writes=(rstd,))
        for kc in range(KC):
            S.op("dve", (lambda o, i, g: (lambda e: e.scalar_tensor_tensor(o, i, g, rstd[:, :], ALU.mult, ALU.mult)))(
                out[:, kc, ts], h[:, kc, ts], C.gains[:, gain_col + kc:gain_col + kc + 1]),
                reads=(h, rstd, C.gains), pwrites=(out,))


def emit_ffn(C, T, h, xn, wg, wu, wd):
    S = C.S
    nchunk = DFF // 512
    for ch in range(nchunk):
        gw = C.wget(wg, 0, KC, ch * 512, 512)
        uw = C.wget(wu, 0, KC, ch * 512, 512)
        dw = C.wget(wd, ch * 512, 4, 0, D)
        if S.dry:
            continue
        hid = T.hid[ch % 2]
        for mi in range(4):
            for th in range(NT // 512):
                ts = tsl(th)
                pg = C.psum()
                S.op("pe", [mm(pg[:, :], gw[:, kc, mi * 128:mi * 128 + 128], xn[:, kc, ts], kc == 0, kc == KC - 1)
                            for kc in range(KC)], reads=(gw, xn), writes=(pg,))
                pu = C.psum()
                S.op("pe", [mm(pu[:, :], uw[:, kc, mi * 128:mi * 128 + 128], xn[:, kc, ts], kc == 0, kc == KC - 1)
                            for kc in range(KC)], reads=(uw, xn), writes=(pu,))
                sg = T.sg[(mi * 2 + th) % 2]
                S.op("act", (lambda o, i: (lambda e: e.activation(o, i, AF.Silu)))(sg[:, :], pg[:, :]),
                     reads=(pg,), writes=(sg,))
                S.op("dve", (lambda o, a, b: (lambda e: e.tensor_tensor(o, a, b, ALU.mult)))(
                    hid[:, mi, ts], pu[:, :], sg[:, :]), reads=(pu, sg), pwrites=(hid,))
        C.wrel(2)
        for mo in range(KC):
            for th in range(NT // 512):
                ts = tsl(th)
                pd = C.psum()
                S.op("pe", [mm(pd[:, :], dw[:, mi, mo * 128:mo * 128 + 128], hid[:, mi, ts], mi == 0, mi == 3)
                            for mi in range(4)], reads=(dw, hid), writes=(pd,))
                S.op("dve", (lambda o, a: (lambda e: e.scalar_tensor_tensor(o, a, 0.5, o, ALU.mult, ALU.add)))(
                    h[:, mo, ts], pd[:, :]), reads=(pd,), pwrites=(h,))
        C.wrel(1)


def emit_linear(C, w, K, c0, ncols, xs, epi, toks):
    S = C.S
    kc_n = K // 128
    bc = min(8192 // kc_n, ncols)
    for cb in range(ncols // bc):
        wb = C.wget(w, 0, kc_n, c0 + cb * bc, bc)
        if not S.dry:
            for mi in range(bc // 128):
                mo = (cb * bc) // 128 + mi
                for ti, ts in enumerate(toks):
                    ps = C.psum()
                    ops = []
                    rd = [wb]
                    for kc in range(kc_n):
                        ap, tb = xs(kc, ts)
                        ops.append(mm(ps[:, :], wb[:, kc, mi * 128:mi * 128 + 128], ap, kc == 0, kc == kc_n - 1))
                        if tb not in rd:
                            rd.append(tb)
                    S.op("pe", ops, reads=tuple(rd), writes=(ps,))
                    epi(mo, ti, ps)
        C.wrel(1)


class TokTiles:
    def __init__(self, C, es):
        S = C.S
        self.h = C.sb([128, KC, NT], F32, "h", es)
        self.xn = C.sb([128, KC, NT], BF16, "xn", es)
        C.wslots = [C.sb([128, 8192], BF16, f"wslot{i}", es) for i in range(C.NSLOT)]
        self.tarena = C.sb([128, 12288], BF16, "tarena", es)
        a = self.tarena.t
        def view(off, n, pat=None, **kw):
            ap = a[:, off:off + n]
            if pat:
                ap = ap.rearrange(pat, **kw)
            return TB(ap, S.buf())
        self.hid = [view(0, 4096, "p (m t) -> p m t", m=4), view(4096, 4096, "p (m t) -> p m t", m=4)]
        self.zst = [view(8192 + i * 512, 512) for i in range(4)]
        self.ys = view(0, 4096, "p (k t) -> p k t", k=8)
        self.y_a = view(4096, 4096, "p (k t) -> p k t", k=8)
        self.ya = view(8192, 4096, "p (k t) -> p k t", k=8)
        self.pb = view(0, 2048, "p (k t) -> p k t", k=2)
        self.sq = [C.sb([128, 512], BF16, f"sq{i}", es) for i in range(2)]
        self.rstd = C.sb([128, 512], F32, "rstd", es)
        self.sg = [C.sb([128, 512], BF16, f"sg{i}", es) for i in range(2)]
        self.sgs = [C.sb([128, 512], BF16, f"sgs{i}", es) for i in range(2)]
        self.sga = [C.sb([128, 512], BF16, f"sga{i}", es) for i in range(2)]
        self.t1 = [C.sb([128, 512], BF16, f"t1{i}", es) for i in range(2)]
        self.f32tmp = [C.sb([128, 512], F32, f"f32tmp{i}", es) for i in range(1)]
        self.pstage = C.sb([128, 2, 256], F32, "pstage", es)


def cp(eng_kind, o, i):
    if eng_kind == "act":
        return lambda e: e.activation(o, i, AF.Copy)
    return lambda e: e.tensor_copy(o, i)


def seg_a(C, T, l):
    S = C.S
    wf = C.wf[l]
    emit_rmsnorm(C, T, T.h, (l * 4 + 0) * KC, T.xn)
    emit_ffn(C, T, T.h, T.xn, wf["ffn1_w_gate"], wf["ffn1_w_up"], wf["ffn1_w_down"])
    emit_rmsnorm(C, T, T.h, (l * 4 + 1) * KC, T.xn)
    zx_in = C.zx_in[l]
    sg_d = C.sg_d[l]
    cnt = [0]

    def xs(kc, ts):
        return T.xn[:, kc, ts], T.xn

    def epi(mo, ti, ps):
        st = T.zst[cnt[0] % 4]
        eng = "act" if cnt[0] % 2 == 0 else "dve"
        cnt[0] += 1
        if mo < 32:
            S.op(eng, cp(eng, st[:, :], ps[:, :]), reads=(ps,), writes=(st,))
            zp = zx_in[mo // 8]
            S.dma("sp", (lambda o, i: (lambda e: e.dma_start(out=o, in_=i)))(
                zp.t[(mo % 8) * 128:(mo % 8 + 1) * 128, ti * 512:(ti + 1) * 512], st[:, :]),
                reads=(st,), pwrites=(zp,))
        else:
            S.op("act", (lambda o, i: (lambda e: e.activation(o, i, AF.Sigmoid)))(st[:, :], ps[:, :]),
                 reads=(ps,), writes=(st,))
            S.dma("sp", (lambda o, i: (lambda e: e.dma_start(out=o, in_=i)))(
                sg_d.t[(mo - 32) * 128:(mo - 31) * 128, ti * 512:(ti + 1) * 512], st[:, :]),
                reads=(st,), pwrites=(sg_d,))

    emit_linear(C, wf["w_in"], D, 0, 8192, xs, epi, [tsl(0), tsl(1)])
    for part in range(4):
        C.coll_pair(zx_in[part], C.zx_out[l][part], pop=(part == 3))
    S.dma("sp", lambda e: e.dma_start(out=C.hsp.t.rearrange("(k p) t -> p k t", p=128), in_=T.h[:, :, :]),
          reads=(T.h,), writes=(C.hsp,))


def seg_c(C, T, l, last):
    S = C.S
    wf = C.wf[l]
    yxs, yxa = C.yx_out[l]
    sg_d = C.sg_d[l]
    S.dma("sp", lambda e: e.dma_start(out=T.h[:, :, :], in_=C.hsp.t.rearrange("(k p) t -> p k t", p=128)),
          reads=(C.hsp,), writes=(T.h,))
    for hf in range(2):
        tcol = hf * 512
        for jj in range(2):
            for (dst, yx) in ((T.ys, yxs), (T.ya, yxa)):
                r0 = jj * 512

                def ld(e, dst=dst, r0=r0, jj=jj, tcol=tcol, yx=yx):
                    j1024 = C.jval(e, 1024)
                    return e.dma_start(out=dst[:, jj * 4:(jj + 1) * 4, :],
                                       in_=yx.t[r0:r0 + 512, tcol:tcol + 1536][:, bass.ds(j1024, 512)].rearrange("(c p) t -> p c t", p=128))
                S.dma("act", ld, reads=(yx,), pwrites=(dst,))
        for kc in range(8):
            x = T.ys[:, kc, :]
            tf = T.f32tmp[0]
            S.op("act", (lambda x: (lambda e: e.activation(tf[:, :], x, AF.Square)))(x), reads=(T.ys,), writes=(tf,))
            S.op("dve", lambda e: e.tensor_scalar(tf[:, :], tf[:, :], 0.044715, 1.0, ALU.mult, ALU.add),
                 reads=(tf,), writes=(tf,))
            S.op("dve", (lambda x: (lambda e: e.tensor_tensor(tf[:, :], tf[:, :], x, ALU.mult)))(x),
                 reads=(tf, T.ys), writes=(tf,))
            S.op("act", lambda e: e.activation(tf[:, :], tf[:, :], AF.Sigmoid, scale=1.5957691216057308),
                 reads=(tf,), writes=(tf,))
            S.op("dve", (lambda x: (lambda e: e.tensor_tensor(x, x, tf[:, :], ALU.mult)))(x),
                 reads=(tf,), pwrites=(T.ys,))
        wa = C.wget(wf["ssm_w_glu"], 0, 8, 0, 1024)
        wb = C.wget(wf["ssm_w_glu"], 0, 8, 1024, 1024)
        if not S.dry:
            for i in range(8):
                pa = C.psum()
                S.op("pe", [mm(pa[:, :], wa[:, kc, i * 128:i * 128 + 128], T.ys[:, kc, :], kc == 0, kc == 7)
                            for kc in range(8)], reads=(wa, T.ys), writes=(pa,))
                pb_ = C.psum()
                S.op("pe", [mm(pb_[:, :], wb[:, kc, i * 128:i * 128 + 128], T.ys[:, kc, :], kc == 0, kc == 7)
                            for kc in range(8)], reads=(wb, T.ys), writes=(pb_,))
                sg = T.sg[i % 2]
                S.op("act", (lambda o, i_: (lambda e: e.activation(o, i_, AF.Sigmoid)))(sg[:, :], pb_[:, :]),
                     reads=(pb_,), writes=(sg,))
                S.op("dve", (lambda o, a, b: (lambda e: e.tensor_tensor(o, a, b, ALU.mult)))(
                    T.y_a[:, i, :], pa[:, :], sg[:, :]), reads=(pa, sg), pwrites=(T.y_a,))
        C.wrel(2)
        for half in range(2):
            wA = C.wget(wf["w_branch_ssm"], 0, 8, half * 1024, 1024)
            wB = C.wget(wf["w_branch_attn"], 0, 8, half * 1024, 1024)
            if not S.dry:
                for mi in range(8):
                    mo = half * 8 + mi
                    sgs, sga = T.sgs[mo % 2], T.sga[mo % 2]
                    S.dma("sp", (lambda o, i: (lambda e: e.dma_start(out=o, in_=i)))(
                        sgs[:, :], sg_d.t[mo * 128:(mo + 1) * 128, tcol:tcol + 512]), reads=(sg_d,), writes=(sgs,))
                    S.dma("sp", (lambda o, i: (lambda e: e.dma_start(out=o, in_=i)))(
                        sga[:, :], sg_d.t[(16 + mo) * 128:(17 + mo) * 128, tcol:tcol + 512]), reads=(sg_d,), writes=(sga,))
                    p1 = C.psum()
                    S.op("pe", [mm(p1[:, :], wA[:, kc, mi * 128:mi * 128 + 128], T.y_a[:, kc, :], kc == 0, kc == 7)
                                for kc in range(8)], reads=(wA, T.y_a), writes=(p1,))
                    p2 = C.psum()
                    S.op("pe", [mm(p2[:, :], wB[:, kc, mi * 128:mi * 128 + 128], T.ya[:, kc, :], kc == 0, kc == 7)
                                for kc in range(8)], reads=(wB, T.ya), writes=(p2,))
                    t1, t2 = T.t1[0], T.t1[1]
                    S.op("dve", (lambda a, b: (lambda e: e.tensor_tensor(t1[:, :], a, b, ALU.mult)))(p1[:, :], sgs[:, :]),
                         reads=(p1, sgs), writes=(t1,))
                    S.op("dve", (lambda a, b: (lambda e: e.tensor_tensor(t2[:, :], a, b, ALU.mult)))(p2[:, :], sga[:, :]),
                         reads=(p2, sga), writes=(t2,))
                    S.op("dve", (lambda o: (lambda e: e.tensor_tensor(o, t1[:, :], t2[:, :], ALU.add)))(T.xn[:, mo, 0:512]),
                         reads=(t1, t2), pwrites=(T.xn,))
            C.wrel(2)

        def xs(kc, ts):
            return T.xn[:, kc, 0:512], T.xn

        def epi(mo, ti, ps, tcol=tcol):
            S.op("dve", (lambda o, a: (lambda e: e.tensor_tensor(o, a, o, ALU.add)))(
                T.h[:, mo, tcol:tcol + 512], ps[:, :]), reads=(ps,), pwrites=(T.h,))
        emit_linear(C, wf["w_out"], D, 0, D, xs, epi, [slice(0, 512)])
    S.barrier()
    emit_rmsnorm(C, T, T.h, (l * 4 + 2) * KC, T.xn)
    emit_ffn(C, T, T.h, T.xn, wf["ffn2_w_gate"], wf["ffn2_w_up"], wf["ffn2_w_down"])
    S.barrier()
    emit_rmsnorm(C, T, T.h, (l * 4 + 3) * KC, T.xn)
    for qd in range(4):
        S.dma("sp", (lambda qd: (lambda e: e.dma_start(
            out=T.pstage[:, :, :], in_=C.pT.t[l, :, qd * 256:(qd + 1) * 256].rearrange("(k p) t -> p k t", p=128))))(qd),
            reads=(C.pT,), writes=(T.pstage,))
        S.op("dve", (lambda qd: (lambda e: e.tensor_copy(T.pb[:, :, qd * 256:(qd + 1) * 256], T.pstage[:, :, :])))(qd),
             reads=(T.pstage,), pwrites=(T.pb,))
    for cb in range(4):
        wpg = C.wget(wf["ple_w_gate"], 0, KC, cb * 512, 512)
        wpu = C.wget(wf["ple_w_up"], 0, 2, cb * 512, 512)
        if not S.dry:
            for mi in range(4):
                mo = cb * 4 + mi
                for th in range(2):
                    ts = tsl(th)
                    pg = C.psum()
                    S.op("pe", [mm(pg[:, :], wpg[:, kc, mi * 128:mi * 128 + 128], T.xn[:, kc, ts], kc == 0, kc == KC - 1)
                                for kc in range(KC)], reads=(wpg, T.xn), writes=(pg,))
                    pu = C.psum()
                    S.op("pe", [mm(pu[:, :], wpu[:, kc, mi * 128:mi * 128 + 128], T.pb[:, kc, ts], kc == 0, kc == 1)
                                for kc in range(2)], reads=(wpu, T.pb), writes=(pu,))
                    sg = T.f32tmp[0]
                    S.op("act", (lambda i_: (lambda e: e.activation(sg[:, :], i_, AF.Sigmoid)))(pg[:, :]),
                         reads=(pg,), writes=(sg,))
                    S.op("dve", (lambda a: (lambda e: e.tensor_tensor(sg[:, :], a, sg[:, :], ALU.mult)))(pu[:, :]),
                         reads=(pu, sg), writes=(sg,))
                    S.op("dve", (lambda o: (lambda e: e.tensor_tensor(o, o, sg[:, :], ALU.add)))(T.h[:, mo, ts]),
                         reads=(sg,), pwrites=(T.h,))
        C.wrel(2)
    S.barrier()
    if last:
        emit_rmsnorm(C, T, T.h, DEPTH * 4 * KC, T.h)
        tok = S.dma("sp", lambda e: e.dma_start(out=C.oT.t.rearrange("(k p) t -> p k t", p=128), in_=T.h[:, :, :]),
                    reads=(T.h,), writes=(C.oT,))
        if not S.dry:
            S.wait_all("sp", [tok])


def tt(o, a, b, op):
    return lambda e: e.tensor_tensor(o, a, b, op)


def ssm_prep(C, l):
    S = C.S
    P = C.ssm_in
    with ExitStack() as es:
        def sm(name, shape=(128, 16), dt=F32):
            return C.sb(list(shape), dt, name, es)

        def ew(eng, fn, reads, writes):
            S.op(eng, fn, reads=reads, writes=writes)

        def load(name, shape):
            t = sm(name, shape)
            src = P[name].t[l]
            S.dma("sp", (lambda o, i: (lambda e: e.dma_start(out=o, in_=i)))(t.t[tuple(slice(None) for _ in shape)], src),
                  reads=(P[name],), writes=(t,))
            return t
        ar_in = load("ssm_ar", (128, 16))
        ai_in = load("ssm_ai", (128, 16))
        ldt = load("ssm_ldt", (128, 16))
        ctre = load("ssm_ct_re", (128, 16, 16))
        ctim = load("ssm_ct_im", (128, 16, 16))
        btre = load("ssm_bt_re", (128, 16, 16))
        btim = load("ssm_bt_im", (128, 16, 16))
        A = lambda t: t[:, :]
        dt_ = sm("dt")
        ew("act", lambda e: e.activation(A(dt_), A(ldt), AF.Exp), (ldt,), (dt_,))
        ar, ai, mag = sm("ar"), sm("ai"), sm("mag")
        ew("dve", tt(A(ar), A(ar_in), A(dt_), ALU.mult), (ar_in, dt_), (ar,))
        ew("dve", tt(A(ai), A(ai_in), A(dt_), ALU.mult), (ai_in, dt_), (ai,))
        ew("act", lambda e: e.activation(A(mag), A(ar), AF.Exp), (ar,), (mag,))

        def sin_of(x, offset, name):
            xo, y, kf, r, m = sm(name + "xo"), sm(name + "y"), sm(name + "kf"), sm(name + "r"), sm(name + "m")
            ki = sm(name + "ki", (128, 16), I32)
            ew("dve", lambda e: e.tensor_scalar(A(xo), A(x), float(offset), None, ALU.add), (x,), (xo,))
            ew("dve", lambda e: e.tensor_scalar(A(y), A(xo), 1.0 / TWO_PI, 0.5, ALU.mult, ALU.add), (xo,), (y,))
            ew("dve", lambda e: e.tensor_copy(A(ki), A(y)), (y,), (ki,))
            ew("dve", lambda e: e.tensor_copy(A(kf), A(ki)), (ki,), (kf,))
            ew("dve", lambda e: e.scalar_tensor_tensor(A(r), A(kf), -TWO_PI, A(xo), ALU.mult, ALU.add), (kf, xo), (r,))
            ew("dve", lambda e: e.tensor_scalar(A(m), A(r), float(np.pi), None, ALU.is_gt), (r,), (m,))
            ew("dve", lambda e: e.scalar_tensor_tensor(A(r), A(m), -TWO_PI, A(r), ALU.mult, ALU.add), (m, r), (r,))
            ew("dve", lambda e: e.tensor_scalar(A(m), A(r), float(-np.pi), None, ALU.is_lt), (r,), (m,))
            ew("dve", lambda e: e.scalar_tensor_tensor(A(r), A(m), TWO_PI, A(r), ALU.mult, ALU.add), (m, r), (r,))
            s = sm(name + "s")
            ew("act", lambda e: e.activation(A(s), A(r), AF.Sin), (r,), (s,))
            return s
        sn = sin_of(ai, 0.0, "sn")
        cs = sin_of(ai, np.pi / 2.0, "cs")
        lr, li = sm("lr"), sm("li")
        ew("dve", tt(A(lr), A(mag), A(cs), ALU.mult), (mag, cs), (lr,))
        ew("dve", tt(A(li), A(mag), A(sn), ALU.mult), (mag, sn), (li,))
        den, t0, t1_, nr, cr, ci = sm("den"), sm("t0"), sm("t1"), sm("nr"), sm("cr"), sm("ci")
        ew("dve", tt(A(den), A(ar_in), A(ar_in), ALU.mult), (ar_in,), (den,))
        ew("dve", tt(A(t0), A(ai_in), A(ai_in), ALU.mult), (ai_in,), (t0,))
        ew("dve", tt(A(den), A(den), A(t0), ALU.add), (den, t0), (den,))
        ew("dve", lambda e: e.reciprocal(A(den), A(den)), (den,), (den,))
        ew("dve", lambda e: e.tensor_scalar(A(nr), A(lr), -1.0, None, ALU.add), (lr,), (nr,))
        ew("dve", tt(A(t0), A(nr), A(ar_in), ALU.mult), (nr, ar_in), (t0,))
        ew("dve", tt(A(t1_), A(li), A(ai_in), ALU.mult), (li, ai_in), (t1_,))
        ew("dve", tt(A(t0), A(t0), A(t1_), ALU.add), (t0, t1_), (t0,))
        ew("dve", tt(A(cr), A(t0), A(den), ALU.mult), (t0, den), (cr,))
        ew("dve", tt(A(t0), A(li), A(ar_in), ALU.mult), (li, ar_in), (t0,))
        ew("dve", tt(A(t1_), A(nr), A(ai_in), ALU.mult), (nr, ai_in), (t1_,))
        ew("dve", tt(A(t0), A(t0), A(t1_), ALU.subtract), (t0, t1_), (t0,))
        ew("dve", tt(A(ci), A(t0), A(den), ALU.mult), (t0, den), (ci,))
        B3 = [128, 16, 16]
        bbre, bbim, u0, u1 = sm("bbre", B3), sm("bbim", B3), sm("u0", B3), sm("u1", B3)
        crb = cr.t[:, :, None].to_broadcast(B3)
        cib = ci.t[:, :, None].to_broadcast(B3)
        F3 = lambda t: t[:, :, :]
        ew("dve", tt(F3(u0), crb, F3(btre), ALU.mult), (cr, btre), (u0,))
        ew("dve", tt(F3(u1), cib, F3(btim), ALU.mult), (ci, btim), (u1,))
        ew("dve", tt(F3(bbre), F3(u0), F3(u1), ALU.subtract), (u0, u1), (bbre,))
        ew("dve", tt(F3(u0), crb, F3(btim), ALU.mult), (cr, btim), (u0,))
        ew("dve", tt(F3(u1), cib, F3(btre), ALU.mult), (ci, btre), (u1,))
        ew("dve", tt(F3(bbim), F3(u0), F3(u1), ALU.add), (u0, u1), (bbim,))
        powre, powim = sm("powre", (128, 16, 9)), sm("powim", (128, 16, 9))
        ew("dve", lambda e: e.memset(powre[:, :, 0:1], 1.0), (), (powre,))
        ew("dve", lambda e: e.memset(powim[:, :, 0:1], 0.0), (), (powim,))
        ew("dve", lambda e: e.tensor_copy(powre[:, :, 1], A(lr)), (lr, powre), (powre,))
        ew("dve", lambda e: e.tensor_copy(powim[:, :, 1], A(li)), (li, powim), (powim,))
        for j in range(2, 9):
            ew("dve", tt(A(t0), powre[:, :, j - 1], A(lr), ALU.mult), (powre, lr), (t0,))
            ew("dve", tt(A(t1_), powim[:, :, j - 1], A(li), ALU.mult), (powim, li), (t1_,))
            ew("dve", tt(powre[:, :, j], A(t0), A(t1_), ALU.subtract), (t0, t1_, powre), (powre,))
            ew("dve", tt(A(t0), powre[:, :, j - 1], A(li), ALU.mult), (powre, li), (t0,))
            ew("dve", tt(A(t1_), powim[:, :, j - 1], A(lr), ALU.mult), (powim, lr), (t1_,))
            ew("dve", tt(powim[:, :, j], A(t0), A(t1_), ALU.add), (t0, t1_, powim), (powim,))
        lam8 = sm("lam8", (128, 16, 2))
        ew("dve", lambda e: e.tensor_copy(lam8[:, :, 0], powre[:, :, 8]), (powre,), (lam8,))
        ew("dve", lambda e: e.tensor_copy(lam8[:, :, 1], powim[:, :, 8]), (powim, lam8), (lam8,))
        S.dma("sp", lambda e: e.dma_start(out=C.lam8_d[l].t, in_=lam8[:, :, :]), reads=(lam8,), writes=(C.lam8_d[l],))
        prre, prim = sm("prre", (128, 16, 8)), sm("prim", (128, 16, 8))
        for tau in range(8):
            ew("dve", (lambda tau: (lambda e: e.tensor_copy(prre[:, :, tau], powre[:, :, 7 - tau])))(tau), (powre, prre), (prre,))
            ew("dve", (lambda tau: (lambda e: e.tensor_copy(prim[:, :, tau], powim[:, :, 7 - tau])))(tau), (powim, prim), (prim,))
        C4 = [128, 16, 9, 16]
        clre, clim, v0, v1 = sm("clre", C4), sm("clim", C4), sm("v0", C4), sm("v1", C4)
        F4 = lambda t: t[:, :, :, :]
        ctre_b = ctre.t[:, :, None, :].to_broadcast(C4)
        ctim_b = ctim.t[:, :, None, :].to_broadcast(C4)
        pre_b = powre.t[:, :, :, None].to_broadcast(C4)
        pim_b = powim.t[:, :, :, None].to_broadcast(C4)
        ew("dve", tt(F4(v0), ctre_b, pre_b, ALU.mult), (ctre, powre), (v0,))
        ew("dve", tt(F4(v1), ctim_b, pim_b, ALU.mult), (ctim, powim), (v1,))
        ew("dve", tt(F4(clre), F4(v0), F4(v1), ALU.subtract), (v0, v1), (clre,))
        ew("dve", tt(F4(v0), ctre_b, pim_b, ALU.mult), (ctre, powim), (v0,))
        ew("dve", tt(F4(v1), ctim_b, pre_b, ALU.mult), (ctim, powre), (v1,))
        ew("dve", tt(F4(v0), F4(v0), F4(v1), ALU.add), (v0, v1), (v0,))
        ew("dve", lambda e: e.tensor_scalar(F4(clim), F4(v0), -1.0, None, ALU.mult), (v0,), (clim,))
        w4re, w4im = sm("w4re", (128, 16, 128), BF16), sm("w4im", (128, 16, 128), BF16)
        ew("act", lambda e: e.activation(w4re.t[:, :, :].rearrange("p g (t q) -> p g t q", t=8), clre[:, :, 1:9, :], AF.Copy), (clre,), (w4re,))
        ew("act", lambda e: e.activation(w4im.t[:, :, :].rearrange("p g (t q) -> p g t q", t=8), clim[:, :, 1:9, :], AF.Copy), (clim,), (w4im,))
        S.dma("sp", lambda e: e.dma_start(out=C.w4re_d[l].t, in_=w4re[:, :, :]), reads=(w4re,), writes=(C.w4re_d[l],))
        S.dma("sp", lambda e: e.dma_start(out=C.w4im_d[l].t, in_=w4im[:, :, :]), reads=(w4im,), writes=(C.w4im_d[l],))
        T4 = [128, 16, 8, 16]
        t2re, t2im = sm("t2re", T4), sm("t2im", T4)
        w0, w1 = v0.t[:, :, 0:8, :], v1.t[:, :, 0:8, :]
        prre_b = prre.t[:, :, :, None].to_broadcast(T4)
        prim_b = prim.t[:, :, :, None].to_broadcast(T4)
        bbre_b = bbre.t[:, :, None, :].to_broadcast(T4)
        bbim_b = bbim.t[:, :, None, :].to_broadcast(T4)
        ew("dve", tt(w0, prre_b, bbre_b, ALU.mult), (prre, bbre), (v0,))
        ew("dve", tt(w1, prim_b, bbim_b, ALU.mult), (prim, bbim), (v1,))
        ew("dve", tt(F4(t2re), w0, w1, ALU.subtract), (v0, v1), (t2re,))
        ew("dve", tt(w0, prre_b, bbim_b, ALU.mult), (prre, bbim), (v0,))
        ew("dve", tt(w1, prim_b, bbre_b, ALU.mult), (prim, bbre), (v1,))
        ew("dve", tt(F4(t2im), w0, w1, ALU.add), (v0, v1), (t2im,))
        w2re, w2im = sm("w2re", (128, 32, 64), BF16), sm("w2im", (128, 32, 64), BF16)
        for (src, dst) in ((t2re, w2re), (t2im, w2im)):
            for gh in range(2):
                for gb in range(2):
                    ps = C.psum()
                    ops = []
                    for gi in range(8):
                        gp = gb * 8 + gi
                        ops.append((lambda o, i, idn: (lambda e: e.transpose(o, i, idn)))(
                            ps[:, gi * 64:(gi + 1) * 64],
                            src.t[64 * gh:64 * gh + 64, gp, :, :].rearrange("p a b -> p (a b)"),
                            C.ident[64 * gh:64 * gh + 64, 64 * gh:64 * gh + 64]))
                    S.op("pe", ops, reads=(src, C.ident), writes=(ps,))
                    g0 = gh * 16 + gb * 8
                    ew("act", (lambda dst, g0, ps: (lambda e: e.activation(
                        dst.t[:, g0:g0 + 8, :], ps[:, :].rearrange("p (g n) -> p g n", g=8), AF.Copy)))(dst, g0, ps),
                        (ps,), (dst,))
        S.dma("sp", lambda e: e.dma_start(out=C.w2re_d[l].t, in_=w2re[:, :, :]), reads=(w2re,), writes=(C.w2re_d[l],))
        S.dma("sp", lambda e: e.dma_start(out=C.w2im_d[l].t, in_=w2im[:, :, :]), reads=(w2im,), writes=(C.w2im_d[l],))
        w1sb = sm("w1sb", (128, 32, 128), BF16)
        sets = []
        for i in range(2):
            st = dict(bre=sm(f"bwre{i}", (128, 240)), bim=sm(f"bwim{i}", (128, 240)),
                      cre=sm(f"cwre{i}", (128, 240)), cim=sm(f"cwim{i}", (128, 240)))
            for k in st:
                ew("dve", (lambda t: (lambda e: e.memset(t[:, :], 0.0)))(st[k]), (), (st[k],))
            sets.append(st)
        for gp in range(16):
            st = sets[gp % 2]
            ew("dve", (lambda st, gp: (lambda e: e.tensor_copy(st["bre"][:, 112:128], bbre[:, gp, :])))(st, gp), (bbre, st["bre"]), (st["bre"],))
            ew("dve", (lambda st, gp: (lambda e: e.tensor_copy(st["bim"][:, 112:128], bbim[:, gp, :])))(st, gp), (bbim, st["bim"]), (st["bim"],))
            ew("act", (lambda st, gp: (lambda e: e.activation(st["cre"].t[:, 112:240].rearrange("p (j q) -> p j q", j=8), clre[:, gp, 0:8, :], AF.Copy)))(st, gp), (clre, st["cre"]), (st["cre"],))
            ew("act", (lambda st, gp: (lambda e: e.activation(st["cim"].t[:, 112:240].rearrange("p (j q) -> p j q", j=8), clim[:, gp, 0:8, :], AF.Copy)))(st, gp), (clim, st["cim"]), (st["cim"],))
            for gh in range(2):
                ps = C.psum()
                ops = []
                rows = slice(64 * gh, 64 * gh + 64)
                for tau in range(8):
                    for ri, (bk, ck) in enumerate((("bre", "cre"), ("bim", "cim"))):
                        ops.append(mm(ps[:, 0:128], st[bk][rows, 112 - 16 * tau:240 - 16 * tau],
                                      st[ck][rows, (7 - tau) * 16:(7 - tau) * 16 + 128],
                                      tau == 0 and ri == 0, tau == 7 and ri == 1))
                S.op("pe", ops, reads=(st["bre"], st["bim"], st["cre"], st["cim"]), writes=(ps,))
                g = gh * 16 + gp
                S.op("dve", (lambda g, ps: (lambda e: e.tensor_copy(w1sb[:, g, :], ps[:, 0:128])))(g, ps),
                     reads=(ps,), pwrites=(w1sb,))
        S.dma("sp", lambda e: e.dma_start(out=C.w1_d[l].t, in_=w1sb[:, :, :]), reads=(w1sb,), writes=(C.w1_d[l],))
        S.barrier()


def seg_b(C, l):
    S = C.S
    zxp = C.zx_out[l]
    yxs_in, yxa_in = C.yx_in[l]
    with ExitStack() as es:
        with ExitStack() as es1:
            zs = C.sb([128, 4, L], BF16, "zs", es1)
            XY = C.sb([128, 32 * 256], BF16, "XY", es1)
            Sst = C.sb([128, 16, 256, 2], F32, "Sst", es1)
            spre = [C.sb([128, 16, 256], BF16, f"spre{i}", es1) for i in range(2)]
            YO = C.sb([128, 32 * 256], BF16, "YO", es1)
            w1 = C.sb([128, 32, 128], BF16, "w1", es1)
            w2 = [C.sb([128, 32, 64], BF16, f"w2{i}", es1) for i in range(2)]
            w4 = [C.sb([128, 16, 128], BF16, f"w4{i}", es1) for i in range(2)]
            lam8 = C.sb([128, 16, 2], F32, "lam8", es1)
            lrr = C.sb([128, 16, 2], F32, "lrr", es1)
            lii = C.sb([128, 16, 2], F32, "lii", es1)
            tA = C.sb([128, 16, 2], F32, "tA", es1)
            tB = C.sb([128, 16, 2], F32, "tB", es1)
            dd = C.sb([128, 4], F32, "dd", es1)

            def ld(dst_ap, src_ap, rd, wr, pw=False):
                S.dma("sp", (lambda o, i: (lambda e: e.dma_start(out=o, in_=i)))(dst_ap, src_ap), reads=rd,
                      writes=() if pw else wr, pwrites=wr if pw else ())
            ld(w1[:, :, :], C.w1_d[l].t, (C.w1_d[l],), (w1,))
            ld(w2[0][:, :, :], C.w2re_d[l].t, (C.w2re_d[l],), (w2[0],))
            ld(w2[1][:, :, :], C.w2im_d[l].t, (C.w2im_d[l],), (w2[1],))
            ld(w4[0][:, :, :], C.w4re_d[l].t, (C.w4re_d[l],), (w4[0],))
            ld(w4[1][:, :, :], C.w4im_d[l].t, (C.w4im_d[l],), (w4[1],))
            ld(lam8[:, :, :], C.lam8_d[l].t, (C.lam8_d[l],), (lam8,))
            ld(dd[:, :], C.ssm_in["ssm_dd"].t[l], (C.ssm_in["ssm_dd"],), (dd,))
            for jj in range(2):
                def ldz(e, jj=jj):
                    j512 = C.jval(e, 512)
                    b0 = jj * 1024
                    return e.dma_start(out=zs[:, :, jj * NT:(jj + 1) * NT],
                                       in_=zxp[0].t[b0:b0 + 1024, :][bass.ds(j512, 512), :].rearrange("(c p) t -> p c t", p=128))
                S.dma("sp", ldz, reads=(zxp[0],), pwrites=(zs,))
            S.op("dve", lambda e: e.tensor_copy(lrr[:, :, 0], lam8[:, :, 0]), reads=(lam8,), writes=(lrr,))
            S.op("dve", lambda e: e.tensor_copy(lrr[:, :, 1], lam8[:, :, 0]), reads=(lam8, lrr), writes=(lrr,))
            S.op("dve", lambda e: e.tensor_scalar(lii[:, :, 0], lam8[:, :, 1], -1.0, None, ALU.mult), reads=(lam8,), writes=(lii,))
            S.op("dve", lambda e: e.tensor_copy(lii[:, :, 1], lam8[:, :, 1]), reads=(lam8, lii), writes=(lii,))
            zperm = TB(YO.t[:, :].rearrange("p (k t c) -> p k t c", k=4, t=8), YO.b)
            for cc in range(4):
                eng = ("act", "dve")[cc % 2]
                S.op(eng, cp(eng, zperm[:, cc, :, :], zs[:, cc, :].rearrange("p (c t) -> p t c", t=8)),
                     reads=(zs,), pwrites=(YO,))
            for cc in range(4):
                ld(C.zd.t[:, cc * 128:(cc + 1) * 128, :].rearrange("t p c -> p t c"), zperm[:, cc, :, :], (YO,), (C.zd,), pw=True)
            X = TB(XY.t[:, :].rearrange("p (g c) -> p g c", g=32), XY.b)
            for tau in range(8):
                ld(X[16 * tau:16 * tau + 16, :, :], C.zd.t[tau].rearrange("(g q) c -> q g c", q=16), (C.zd,), (XY,), pw=True)
            for gp in range(16):
                ps = C.psum()
                ops = []
                for gh in range(2):
                    g = gh * 16 + gp
                    for ri in range(2):
                        ops.append(mm(ps[64 * gh:64 * gh + 64, ri * 256:(ri + 1) * 256], w2[ri][:, g, :], X[:, g, :], True, True))
                S.op("pe", ops, reads=(w2[0], w2[1], XY), writes=(ps,))
                eng = ("act", "dve")[gp % 2]
                S.op(eng, cp(eng, Sst[:, gp, :, :], ps[:, :].rearrange("p (r c) -> p c r", r=2)), reads=(ps,), pwrites=(Sst,))
            for c in range(1, 256):
                S.op("dve", (lambda c: (lambda e: e.tensor_tensor(tA[:, :, :], Sst[:, :, c - 1, :], lrr[:, :, :], ALU.mult)))(c),
                     reads=(Sst, lrr), writes=(tA,))
                S.op("dve", (lambda c: (lambda e: e.tensor_tensor(tB[:, :, :], Sst[:, :, c - 1, ::-1], lii[:, :, :], ALU.mult)))(c),
                     reads=(Sst, lii), writes=(tB,))
                S.op("dve", (lambda c: (lambda e: e.tensor_tensor(tA[:, :, :], tA[:, :, :], tB[:, :, :], ALU.add)))(c),
                     reads=(tA, tB), writes=(tA,))
                S.op("dve", (lambda c: (lambda e: e.tensor_tensor(Sst[:, :, c, :], Sst[:, :, c, :], tA[:, :, :], ALU.add)))(c),
                     reads=(tA,), pwrites=(Sst,))
            for ri in range(2):
                eng = ("act", "dve")[ri]
                S.op("dve", (lambda ri: (lambda e: e.memset(spre[ri][:, :, 0:1], 0.0)))(ri), writes=(spre[ri],))
                S.op(eng, cp(eng, spre[ri][:, :, 1:256], Sst[:, :, 0:255, ri]), reads=(Sst, spre[ri]), writes=(spre[ri],))
            Yb = TB(YO.t[:, :].rearrange("p (g c) -> p g c", g=32), YO.b)
            for g2 in range(16):
                ps = C.psum()
                ops = []
                for k in range(2):
                    g = g2 * 2 + k
                    gh, gp = g // 16, g % 16
                    rows = slice(64 * gh, 64 * gh + 64)
                    o = ps[:, k * 256:(k + 1) * 256]
                    ops.append(mm(o, w1[:, g, :], X[:, g, :], True, False))
                    ops.append(mm(o, w4[0][rows, gp, :], spre[0][rows, gp, :], False, False))
                    ops.append(mm(o, w4[1][rows, gp, :], spre[1][rows, gp, :], False, True))
                S.op("pe", ops, reads=(w1, XY, w4[0], w4[1], spre[0], spre[1]), writes=(ps,))
                eng = ("act", "dve")[g2 % 2]
                S.op(eng, cp(eng, Yb[:, 2 * g2:2 * g2 + 2, :], ps[:, :].rearrange("p (g c) -> p g c", g=2)),
                     reads=(ps,), pwrites=(YO,))
            for t in range(8):
                ld(C.yd.t[t].rearrange("g p c -> p g c"), Yb[16 * t:16 * t + 16, :, :], (YO,), (C.yd,), pw=True)
            Ysel = TB(XY.t[:, :].rearrange("p (k t c) -> p k t c", k=4, t=8), XY.b)
            for cc in range(4):
                ld(Ysel[:, cc, :, :], C.yd.t[:, cc * 8:(cc + 1) * 8, :, :].rearrange("t g p c -> (g p) t c"), (C.yd,), (XY,), pw=True)
            yout = TB(YO.t[:, :].rearrange("p (k n) -> p k n", k=4), YO.b)
            for cc in range(4):
                S.op("dve", (lambda cc: (lambda e: e.scalar_tensor_tensor(
                    yout[:, cc, :].rearrange("p (c t) -> p c t", t=8),
                    zs[:, cc, :].rearrange("p (c t) -> p c t", t=8), dd[:, cc:cc + 1],
                    Ysel[:, cc, :, :].rearrange("p t c -> p c t"), ALU.mult, ALU.add)))(cc),
                    reads=(zs, dd, XY), pwrites=(YO,))
            for cc in range(4):
                ld(yxs_in.t[cc * 128:(cc + 1) * 128, :], yout[:, cc, :], (YO,), (yxs_in,), pw=True)
            C.coll_pair(yxs_in, C.yx_out[l][0], pop=False)
            S.barrier()
        with ExitStack() as es2:
            qT = C.sb([128, 4, L], BF16, "qT", es2)
            kT = C.sb([128, 4, L], BF16, "kT", es2)
            vst = C.sb([128, 4, L], BF16, "vst", es2)
            Vt = C.sb([128, 16, 4, 128], BF16, "Vt", es2)
            ao = C.sb([128, 4, L], BF16, "ao", es2)
            cosT = C.sb([32, L], F32, "cosT", es2)
            sinT = C.sb([32, L], F32, "sinT", es2)
            rt = [C.sb([32, 512], F32, f"rt{i}", es2) for i in range(2)]
            ksum = C.sb([128, 4, 8], F32, "ksum", es2)
            khi = C.sb([128, 4, 8], BF16, "khi", es2)
            klo = C.sb([128, 4, 8], BF16, "klo", es2)
            kd = C.sb([128, 4, 8], F32, "kd", es2)
            Gm = C.sb([128, 16, 8], F32, "Gm", es2)
            m8 = C.sb([128, 8], F32, "m8", es2)
            sel = C.sb([128, 16, 8], F32, "sel", es2)
            biasT = C.sb([8, 4, L], BF16, "biasT", es2)
            PT = [C.sb([128, 256], BF16, f"PT{i}", es2) for i in range(3)]
            rec = C.sb([128, 256], F32, "rec", es2)
            ld = lambda d, s_, rd, wr, pw=False: S.dma(
                "sp", (lambda o, i: (lambda e: e.dma_start(out=o, in_=i)))(d, s_), reads=rd,
                writes=() if pw else wr, pwrites=wr if pw else ())
            ld(cosT[:, :], C.consts["rot_cos"].t, (C.consts["rot_cos"],), (cosT,))
            ld(sinT[:, :], C.consts["rot_sin"].t, (C.consts["rot_sin"],), (sinT,))
            for (dst, part) in ((qT, 1), (kT, 2), (vst, 3)):
                for jj in range(2):
                    def ldq(e, dst=dst, part=part, jj=jj):
                        j512 = C.jval(e, 512)
                        b0 = jj * 1024
                        return e.dma_start(out=dst[:, :, jj * NT:(jj + 1) * NT],
                                           in_=zxp[part].t[b0:b0 + 1024, :][bass.ds(j512, 512), :].rearrange("(c p) t -> p c t", p=128))
                    S.dma("sp", ldq, reads=(zxp[part],), pwrites=(dst,))
            for hh in range(4):
                for k4 in range(4):
                    ps = C.psum()
                    psb = ps.t[:, 0:256].bitcast(BF16)
                    S.op("pe", [(lambda o, i_: (lambda e: e.transpose(o, i_, C.identb[:, :])))(
                        psb[:, i * 128:(i + 1) * 128], vst[:, hh, (k4 * 4 + i) * 128:(k4 * 4 + i + 1) * 128])
                                for i in range(4)], reads=(vst, C.identb), writes=(ps,))
                    eng = ("act", "dve")[k4 % 2]
                    S.op(eng, cp(eng, Vt[:, k4 * 4:k4 * 4 + 4, hh, :], psb.rearrange("p (k d) -> p k d", k=4)),
                         reads=(ps,), pwrites=(Vt,))
            it = 0
            for x in (qT, kT):
                for hh in range(4):
                    for tt_ in range(4):
                        ts = slice(tt_ * 512, tt_ * 512 + 512)
                        ps = C.psum()
                        S.op("pe", mm(ps[0:32, :], C.permT[0:32, 0:32], x[0:32, hh, ts], True, True),
                             reads=(x, C.permT), writes=(ps,))
                        r0, r1 = rt[0], rt[1]
                        S.op("dve", (lambda ps, ts: (lambda e: e.tensor_tensor(r0[:, :], ps[0:32, :], sinT[:, ts], ALU.mult)))(ps, ts),
                             reads=(ps, sinT), writes=(r0,))
                        S.op("dve", (lambda x, hh, ts: (lambda e: e.tensor_tensor(r1[:, :], x[0:32, hh, ts], cosT[:, ts], ALU.mult)))(x, hh, ts),
                             reads=(x, cosT), writes=(r1,))
                        S.op("dve", (lambda x, hh, ts: (lambda e: e.tensor_tensor(x[0:32, hh, ts], r0[:, :], r1[:, :], ALU.add)))(x, hh, ts),
                             reads=(r0, r1), writes=(x,))
                        it += 1
            for hh in range(4):
                S.op("dve", (lambda hh: (lambda e: e.tensor_reduce(ksum[:, hh, :], kT[:, hh, :].rearrange("p (n t) -> p n t", t=256), AX.X, ALU.add)))(hh),
                     reads=(kT,), pwrites=(ksum,))
            S.op("dve", lambda e: e.tensor_copy(khi[:, :, :], ksum[:, :, :]), reads=(ksum,), writes=(khi,))
            S.op("dve", lambda e: e.tensor_tensor(kd[:, :, :], ksum[:, :, :], khi[:, :, :], ALU.subtract), reads=(ksum, khi), writes=(kd,))
            S.op("dve", lambda e: e.tensor_copy(klo[:, :, :], kd[:, :, :]), reads=(kd,), writes=(klo,))
            for hh in range(4):
                ps = C.psum()
                ops = []
                for qt in range(16):
                    o = ps[:, qt * 8:(qt + 1) * 8]
                    ops.append(mm(o, qT[:, hh, qt * 128:(qt + 1) * 128], khi[:, hh, :], True, False))
                    ops.append(mm(o, qT[:, hh, qt * 128:(qt + 1) * 128], klo[:, hh, :], False, True))
                S.op("pe", ops, reads=(qT, khi, klo), writes=(ps,))
                S.op("dve", (lambda ps: (lambda e: e.tensor_tensor(Gm[:, :, :], ps[:, 0:128].rearrange("p (q n) -> p q n", n=8), C.negm[:, :, :], ALU.add)))(ps),
                     reads=(ps, C.negm), writes=(Gm,))
                for qt in range(16):
                    S.op("dve", (lambda qt: (lambda e: e.max(m8[:, :], Gm[:, qt, :])))(qt), reads=(Gm,), writes=(m8,))
                    S.op("dve", (lambda qt: (lambda e: e.tensor_scalar(sel[:, qt, :], Gm[:, qt, :], m8[:, 2:3], 1.0, ALU.is_ge, ALU.subtract)))(qt),
                         reads=(Gm, m8), pwrites=(sel,))
                for q4 in range(4):
                    ps2 = C.psum()
                    S.op("pe", [(lambda o, i_: (lambda e: e.transpose(o, i_, C.ident[:, :])))(
                        ps2[0:8, i * 128:(i + 1) * 128], sel[:, q4 * 4 + i, :])
                                for i in range(4)], reads=(sel, C.ident), writes=(ps2,))
                    S.op("act", cp("act", biasT[0:8, hh, q4 * 512:(q4 + 1) * 512], ps2[0:8, :]), reads=(ps2,), pwrites=(biasT,))
            O_acc, S_acc = C.pbank[6], C.pbank[7]
            for hh in range(4):
                for QB in range(8):
                    qs = QB * 256
                    nkt = 2 * QB + 2
                    pend = []

                    def score(kt, hh=hh, QB=QB, qs=qs):
                        sc = C.psum()
                        ops = [mm(sc[:, 0:256], kT[:, hh, kt * 128:(kt + 1) * 128], qT[:, hh, qs:qs + 256], True, False)]
                        rd = [kT, qT]
                        if kt < 2 * QB:
                            ops.append(mm(sc[:, 0:256], C.oh[0:8, kt // 2, :], biasT[0:8, hh, qs:qs + 256], False, True))
                            rd += [C.oh, biasT]
                        else:
                            ops.append(mm(sc[:, 0:256], C.identb[:, :], C.cm[:, kt - 2 * QB, :], False, True))
                            rd += [C.identb, C.cm]
                        S.op("pe", ops, reads=tuple(rd), writes=(sc,))
                        pt = PT[kt % 3]
                        S.op("act", (lambda sc, pt: (lambda e: e.activation(pt[:, :], sc[:, 0:256], AF.Exp, scale=SCALE)))(sc, pt),
                             reads=(sc,), writes=(pt,))
                        return pt

                    def accum(kt, pt, hh=hh, nkt=nkt):
                        S.op("pe", [mm(O_acc[:, 0:256], Vt[:, kt, hh, :], pt[:, :], kt == 0, kt == nkt - 1),
                                    mm(S_acc[:, 0:256], C.ones[:, :], pt[:, :], kt == 0, kt == nkt - 1)],
                             reads=(Vt, pt, C.ones), writes=(), pwrites=(O_acc, S_acc))
                    for kt in range(nkt):
                        pend.append((kt, score(kt)))
                        if len(pend) > 1:
                            accum(*pend.pop(0))
                    while pend:
                        accum(*pend.pop(0))
                    S.op("dve", lambda e: e.reciprocal(rec[:, :], S_acc[:, 0:256]), reads=(S_acc,), writes=(rec,))
                    S.op("dve", (lambda hh, qs: (lambda e: e.tensor_tensor(ao[:, hh, qs:qs + 256], O_acc[:, 0:256], rec[:, :], ALU.mult)))(hh, qs),
                         reads=(O_acc, rec), pwrites=(ao,))
            for hh in range(4):
                ld(yxa_in.t[hh * 128:(hh + 1) * 128, :], ao[:, hh, :], (ao,), (yxa_in,), pw=True)
            S.barrier()
    C.coll_pair(yxa_in, C.yx_out[l][1], pop=True)


CONST_SPECS = {
    "ident": ([128, 128], F32), "identb": ([128, 128], BF16), "permT": ([32, 32], BF16),
    "oh": ([8, 8, 128], BF16), "cm": ([128, 2, 256], BF16), "negm": ([128, 16, 8], F32),
    "rot_cos": ([32, L], F32), "rot_sin": ([32, L], F32),
}
SSM_SPECS = {
    "ssm_ar": [DEPTH, 128, 16], "ssm_ai": [DEPTH, 128, 16], "ssm_ldt": [DEPTH, 128, 16],
    "ssm_ct_re": [DEPTH, 128, 16, 16], "ssm_ct_im": [DEPTH, 128, 16, 16],
    "ssm_bt_re": [DEPTH, 128, 16, 16], "ssm_bt_im": [DEPTH, 128, 16, 16], "ssm_dd": [DEPTH, 128, 4],
}


def build_program(ncores=8, mode="full"):
    nc = bass.Bass("TRN2", target_bir_lowering=False)
    NW = ncores
    pairs = [[2 * i, 2 * i + 1] for i in range(ncores // 2)]
    with ExitStack() as es:
        C = Ctx(nc, es)
        S = C.S
        ext = lambda name, shape, dt: TB(nc.dram_tensor(name, shape, dt, kind="ExternalInput").ap(), S.buf(name))
        C.xT = ext("xT", [D, NT], F32)
        C.pT = ext("pT", [DEPTH, PLE, NT], F32)
        C.gains_d = ext("gains", [128, (DEPTH * 4 + 1) * KC], F32)
        C.ssm_in = {k: ext(k, shp, F32) for k, shp in SSM_SPECS.items()}
        C.consts = {k: ext(k, shp, dt) for k, (shp, dt) in CONST_SPECS.items()}
        C.wf = [{n: WMat(C, f"{l}_{n}", K, M, NW) for n, K, M in WSPEC} for l in range(DEPTH)]
        SEGA_W = ["ffn1_w_gate", "ffn1_w_up", "ffn1_w_down", "w_in"]
        wsh = {n: ext("w_" + n, [DEPTH if mode == "full" else 1, C.wf[0][n].per_rank], F32) for n, K, M in WSPEC
               if mode == "full" or (mode == "sega" and n in SEGA_W)}
        if mode == "segb":
            C.zin = ext("zin", [8192, NT], BF16)
        C.oT = TB(nc.dram_tensor("oT", [D, NT], F32, kind="ExternalOutput").ap(), S.buf("oT"))
        C.zx_in = [[C.dram(f"zx_in{l}_{i}", [1024, NT], BF16) for i in range(4)] for l in range(DEPTH)]
        C.zx_out = [[C.dram(f"zx_out{l}_{i}", [2048, NT], BF16) for i in range(4)] for l in range(DEPTH)]
        C.yx_in = [[C.dram(f"yx_in{l}_{i}", [512, L], BF16) for i in range(2)] for l in range(DEPTH)]
        C.yx_out = [[C.dram(f"yx_out{l}_{i}", [1024, L], BF16) for i in range(2)] for l in range(DEPTH)]
        C.sg_d = [C.dram(f"sg_d{l}", [4096, NT], BF16) for l in range(DEPTH)]
        C.hsp = C.dram("hsp", [D, NT], F32)
        C.zd = C.dram("zd", [8, 512, 256], BF16)
        C.yd = C.dram("yd", [8, 32, 16, 256], BF16)
        C.w1_d = [C.dram(f"w1_d{l}", [128, 32, 128], BF16) for l in range(DEPTH)]
        C.w2re_d = [C.dram(f"w2re_d{l}", [128, 32, 64], BF16) for l in range(DEPTH)]
        C.w2im_d = [C.dram(f"w2im_d{l}", [128, 32, 64], BF16) for l in range(DEPTH)]
        C.w4re_d = [C.dram(f"w4re_d{l}", [128, 16, 128], BF16) for l in range(DEPTH)]
        C.w4im_d = [C.dram(f"w4im_d{l}", [128, 16, 128], BF16) for l in range(DEPTH)]
        C.lam8_d = [C.dram(f"lam8_d{l}", [128, 16, 2], F32) for l in range(DEPTH)]
        C.ones = C.sb([128, 128], BF16, "ones")
        C.eps_ap = C.sb([128, 1], F32, "eps")
        C.gains = C.sb([128, (DEPTH * 4 + 1) * KC], F32, "gains")

        mid = ["ssm_w_glu", "w_branch_ssm", "w_branch_attn", "w_out"]
        tail = ["ple_w_gate", "ple_w_up"]

        def ffn_chunks(l, pre):
            out = []
            g, u, d = C.wf[l][pre + "_w_gate"], C.wf[l][pre + "_w_up"], C.wf[l][pre + "_w_down"]
            for i in range(len(g.chunks)):
                out += [(l, pre + "_w_gate", i), (l, pre + "_w_up", i), (l, pre + "_w_down", i)]
            return out

        quads = [[0, 1, 2, 3], [4, 5, 6, 7]]
        xpairs = [[0, 4], [1, 5], [2, 6], [3, 7]]

        def coll(groups, src, dst):
            S.dma("pool", lambda e: e.collective_compute("AllGather", ALU.bypass, replica_groups=groups,
                                                         ins=[src.t], outs=[dst.t]),
                  reads=(src,), writes=(dst,), inc=1)

        def chunk_list(l, names):
            return [(l, n, i) for n in names for i in range(len(C.wf[l][n].chunks))]

        def bounce(l, n, i):
            w = C.wf[l][n]
            r0, nr, c0, ncw = w.chunks[i]
            src = wsh[n].t[l, w.off[i]:w.off[i] + (nr // NW) * ncw].rearrange("(r c) -> r c", c=ncw)
            S.dma("pool", (lambda o, s_: (lambda e: e.dma_start(out=o, in_=s_)))(w.bounce[i].t, src),
                  reads=(wsh[n],), writes=(w.bounce[i],))

        def gather(chunks):
            LOOK = 12
            for k in range(min(LOOK, len(chunks))):
                bounce(*chunks[k])
            for k, (l, n, i) in enumerate(chunks):
                w = C.wf[l][n]
                if NW == 8:
                    coll(quads, w.bounce[i], w.half[i])
                    coll(xpairs, w.half[i], w.full[i])
                else:
                    coll([list(range(NW))], w.bounce[i], w.full[i])
                if k + LOOK < len(chunks):
                    bounce(*chunks[k + LOOK])

        def coll_pair(src, dst, pop=True):
            S.dma("pool", lambda e: e.collective_compute("AllGather", ALU.bypass, replica_groups=pairs,
                                                         ins=[src.t], outs=[dst.t]),
                  reads=(src,), writes=(dst,), inc=1)
            if pop and C.gbatches:
                gather(C.gbatches.pop(0))
        C.coll_pair = coll_pair

        def load_consts(names, scope):
            for k in names:
                shp, dt = CONST_SPECS[k]
                t = C.sb(shp, dt, k, scope)
                idx = tuple(slice(None) for _ in shp)
                S.dma("sp", (lambda o, i: (lambda e: e.dma_start(out=o, in_=i)))(t.t[idx], C.consts[k].t),
                      reads=(C.consts[k],), writes=(t,))
                setattr(C, k, t)

        def body():
            order0 = ["ffn1_w_gate", "ffn1_w_up", "ffn1_w_down"]
            C.gbatches = [ffn_chunks(1, "ffn1") + chunk_list(1, ["w_in"]), chunk_list(1, mid) + ffn_chunks(1, "ffn2") + chunk_list(1, tail)]
            S.op("dve", lambda e: e.memset(C.ones[:, :], 1.0), writes=(C.ones,))
            S.op("dve", lambda e: e.memset(C.eps_ap[:, :], EPS), writes=(C.eps_ap,))
            S.dma("sp", lambda e: e.dma_start(out=C.gains[:, :], in_=C.gains_d.t), reads=(C.gains_d,), writes=(C.gains,))
            gather(ffn_chunks(0, "ffn1") + chunk_list(0, ["w_in"]) + chunk_list(0, mid) + ffn_chunks(0, "ffn2") + chunk_list(0, tail))
            with ExitStack() as p0:
                load_consts(["ident"], p0)
                for l in range(DEPTH):
                    ssm_prep(C, l)
                S.barrier()
            for l in range(DEPTH):
                if l == 0:
                    ph = ExitStack()
                    T = TokTiles(C, ph)
                    C.wphase_begin()
                    S.dma("sp", (lambda h_: (lambda e: e.dma_start(out=h_[:, :, :], in_=C.xT.t.rearrange("(k p) t -> p k t", p=128))))(T.h),
                          reads=(C.xT,), writes=(T.h,))
                seg_a(C, T, l)
                S.barrier()
                ph.close()
                with ExitStack() as pb:
                    load_consts(["ident", "identb", "permT", "oh", "cm", "negm"], pb)
                    seg_b(C, l)
                ph = ExitStack()
                T = TokTiles(C, ph)
                C.wphase_begin()
                seg_c(C, T, l, last=(l == DEPTH - 1))
            S.barrier()
            ph.close()

        def plan_ffn(l, pre):
            out = []
            for ch in range(DFF // 512):
                out += [(pre + "_w_gate", 0, KC, ch * 512, 512), (pre + "_w_up", 0, KC, ch * 512, 512),
                        (pre + "_w_down", ch * 512, 4, 0, D)]
            return [(l,) + b for b in out]

        def plan_a(l):
            return plan_ffn(l, "ffn1") + [(l, "w_in", 0, KC, cb * 512, 512) for cb in range(16)]

        def plan_c(l):
            out = []
            for hf in range(2):
                out += [("ssm_w_glu", 0, 8, 0, 1024), ("ssm_w_glu", 0, 8, 1024, 1024)]
                for half in range(2):
                    out += [("w_branch_ssm", 0, 8, half * 1024, 1024), ("w_branch_attn", 0, 8, half * 1024, 1024)]
                out += [("w_out", 0, KC, cb * 512, 512) for cb in range(4)]
            out = [(l,) + b for b in out] + plan_ffn(l, "ffn2")
            for cb in range(4):
                out += [(l, "ple_w_gate", 0, KC, cb * 512, 512), (l, "ple_w_up", 0, 2, cb * 512, 512)]
            return out
        def body_prep():
            with ExitStack() as p0:
                load_consts(["ident"], p0)
                ssm_prep(C, 0)
                S.barrier()
            toks = []
            for nm, src in (("o_w1", C.w1_d[0]), ("o_w2re", C.w2re_d[0]), ("o_w2im", C.w2im_d[0]),
                            ("o_w4re", C.w4re_d[0]), ("o_w4im", C.w4im_d[0]), ("o_lam8", C.lam8_d[0])):
                shp = [int(x) for x in src.t.shape]
                dt = F32 if nm == "o_lam8" else BF16
                o = nc.dram_tensor(nm, shp, dt, kind="ExternalOutput").ap()
                toks.append(S.dma("sp", (lambda o, s_: (lambda e: e.dma_start(out=o, in_=s_)))(o, src.t), reads=(src,)))
            S.wait_all("sp", toks)

        def body_segb():
            S.op("dve", lambda e: e.memset(C.ones[:, :], 1.0), writes=(C.ones,))
            with ExitStack() as p0:
                load_consts(["ident"], p0)
                ssm_prep(C, 0)
                S.barrier()
            for jj in range(2):
                for part in range(4):
                    S.dma("sp", (lambda jj, part: (lambda e: e.dma_start(
                        out=C.zx_out[0][part].t[jj * 1024:(jj + 1) * 1024, :],
                        in_=C.zin.t[jj * 4096 + part * 1024:jj * 4096 + (part + 1) * 1024, :])))(jj, part),
                        reads=(C.zin,), pwrites=(C.zx_out[0][part],))
            C.coll_pair = lambda a, b, pop=True: None
            with ExitStack() as pb:
                load_consts(["ident", "identb", "permT", "oh", "cm", "negm"], pb)
                seg_b(C, 0)
            o = nc.dram_tensor("o_yx", [1024, L], BF16, kind="ExternalOutput").ap()
            tk = S.dma("sp", lambda e: e.dma_start(out=o[0:512, :], in_=C.yx_in[0][0].t), reads=(C.yx_in[0][0],))
            tk2 = S.dma("sp", lambda e: e.dma_start(out=o[512:1024, :], in_=C.yx_in[0][1].t), reads=(C.yx_in[0][1],))
            S.wait_all("sp", [tk, tk2])

        def body_sega():
            S.op("dve", lambda e: e.memset(C.ones[:, :], 1.0), writes=(C.ones,))
            S.op("dve", lambda e: e.memset(C.eps_ap[:, :], EPS), writes=(C.eps_ap,))
            S.dma("sp", lambda e: e.dma_start(out=C.gains[:, :], in_=C.gains_d.t), reads=(C.gains_d,), writes=(C.gains,))
            C.gbatches = []
            gather(ffn_chunks(0, "ffn1") + chunk_list(0, ["w_in"]))
            ph = ExitStack()
            T = TokTiles(C, ph)
            C.wphase_begin()
            S.dma("sp", (lambda h_: (lambda e: e.dma_start(out=h_[:, :, :], in_=C.xT.t.rearrange("(k p) t -> p k t", p=128))))(T.h),
                  reads=(C.xT,), writes=(T.h,))
            seg_a(C, T, 0)
            S.barrier()
            ph.close()
            o1 = nc.dram_tensor("o_zx", [8192, NT], BF16, kind="ExternalOutput").ap()
            o2 = nc.dram_tensor("o_h", [D, NT], F32, kind="ExternalOutput").ap()
            tks = [S.dma("sp", (lambda i: (lambda e: e.dma_start(out=o1[i * 2048:(i + 1) * 2048, :], in_=C.zx_out[0][i].t)))(i),
                         reads=(C.zx_out[0][i],)) for i in range(4)]
            tks.append(S.dma("sp", lambda e: e.dma_start(out=o2, in_=C.hsp.t), reads=(C.hsp,)))
            S.wait_all("sp", tks)

        if mode == "sega":
            for (l, n, k0, kc, c0, ncols) in plan_a(0):
                C.wplan.append((C.wf[l][n], k0, kc, c0, ncols))
            C.wphase.append(0)
            body_sega()
            S.emit()
            return nc
        if mode == "prep":
            body_prep()
            S.emit()
            return nc
        if mode == "segb":
            body_segb()
            S.emit()
            return nc
        phases = [plan_a(0), plan_c(0) + plan_a(1), plan_c(1)]
        for ph_ in phases:
            C.wphase.append(len(C.wplan))
            for (l, n, k0, kc, c0, ncols) in ph_:
                C.wplan.append((C.wf[l][n], k0, kc, c0, ncols))
        body()
        S.emit()
    return nc


def _consts():
    bf = ml_dtypes.bfloat16
    c = {}
    c["ident"] = np.eye(128, dtype=np.float32)
    c["identb"] = np.eye(128, dtype=np.float32).astype(bf)
    pm = np.zeros((32, 32), np.float32)
    for d in range(16):
        pm[d + 16, d] = -1.0
        pm[d, d + 16] = 1.0
    c["permT"] = pm.astype(bf)
    oh = np.zeros((8, 8, 128), np.float32)
    for n in range(8):
        oh[n, n, :] = BIG
    c["oh"] = oh.astype(bf)
    cm = np.zeros((128, 2, 256), np.float32)
    k = np.arange(128)[:, None]
    q = np.arange(256)[None, :]
    cm[:, 0, :] = np.where(k > q, -BIG, 0.0)
    cm[:, 1, :] = np.where(k + 128 > q, -BIG, 0.0)
    c["cm"] = cm.astype(bf)
    negm = np.zeros((128, 16, 8), np.float32)
    for qt in range(16):
        for n in range(8):
            if n >= qt // 2:
                negm[:, qt, n] = -1.0e30
    c["negm"] = negm
    pos = np.arange(L, dtype=np.float32)
    inv_freq = (1.0 / (np.float32(500000.0) ** (np.arange(0, 32, 2, dtype=np.float32) / np.float32(32)))).astype(np.float32)
    ang = pos[None, :] * inv_freq[:, None]
    c["rot_cos"] = np.concatenate([np.cos(ang), np.cos(ang)], 0).astype(np.float32)
    c["rot_sin"] = np.concatenate([np.sin(ang), np.sin(ang)], 0).astype(np.float32)
    return c


def make_in_maps(inputs, ncores=8, batch_of_core=None):
    NW = ncores
    consts = _consts()
    x = inputs["x"]
    p = inputs["p"]
    gl = [inputs[n][l] for l in range(DEPTH) for n in NORMS] + [inputs["final_norm"]]
    gains = np.ascontiguousarray(np.concatenate([g.reshape(KC, 128).T for g in gl], axis=1).astype(np.float32))
    in_maps = []
    for r in range(ncores):
        b = (r // 2) if batch_of_core is None else batch_of_core[r]
        j = r % 2
        m = {}
        m["xT"] = np.ascontiguousarray(x[b, j * NT:(j + 1) * NT, :].T)
        m["pT"] = np.ascontiguousarray(np.transpose(p[:, b, j * NT:(j + 1) * NT, :], (0, 2, 1)))
        m["gains"] = gains
        gs = slice(32 * j, 32 * j + 32)

        def nlay(a):
            a = a.reshape((DEPTH, 2, 16, 64) + a.shape[3:])
            a = np.moveaxis(a, 3, 2)
            return np.ascontiguousarray(a.reshape((DEPTH, 128, 16) + a.shape[4:]))
        m["ssm_ar"] = nlay(inputs["ssm_a_re"][:, gs])
        m["ssm_ai"] = nlay(inputs["ssm_a_im"][:, gs])
        m["ssm_ldt"] = nlay(np.broadcast_to(inputs["ssm_log_dt"][:, gs, None], (DEPTH, 32, 64)))
        m["ssm_bt_re"] = nlay(inputs["ssm_b_re"][:, gs])
        m["ssm_bt_im"] = nlay(inputs["ssm_b_im"][:, gs])
        m["ssm_ct_re"] = nlay(np.transpose(inputs["ssm_c_re"][:, gs], (0, 1, 3, 2)))
        m["ssm_ct_im"] = nlay(np.transpose(inputs["ssm_c_im"][:, gs], (0, 1, 3, 2)))
        m["ssm_dd"] = np.ascontiguousarray(np.transpose(
            inputs["ssm_d"][:, 512 * j:512 * j + 512].reshape(DEPTH, 4, 128), (0, 2, 1)))
        for name, K, M in WSPEC:
            w = inputs[name]
            parts = []
            for (r0, nr, c0, ncw) in wchunks(K, M):
                q = nr // NW
                parts.append(w[:, r0 + r * q:r0 + (r + 1) * q, c0:c0 + ncw].reshape(DEPTH, -1))
            m["w_" + name] = np.ascontiguousarray(np.concatenate(parts, axis=1))
        m.update(consts)
        in_maps.append(m)
    return in_maps


_NC_CACHE = {}


def kernel(**inputs):
    inputs = {k: np.asarray(v) for k, v in inputs.items()}
    if 8 not in _NC_CACHE:
        _NC_CACHE[8] = build_program(8)
    nc = _NC_CACHE[8]
    in_maps = make_in_maps(inputs, 8)
    res = run_bass_kernel_spmd(nc, in_maps, core_ids=list(range(8)))
    out = np.empty((4, L, D), np.float32)
    for r in range(8):
        out[r // 2, (r % 2) * NT:(r % 2 + 1) * NT, :] = res.results[r]["oT"].T
    return out
```

```python
import numpy as np
from contextlib import ExitStack
import ml_dtypes
import concourse.bass as bass
import concourse.mybir as mybir
from concourse.bass_utils import run_bass_kernel_spmd

F32 = mybir.dt.float32
BF16 = mybir.dt.bfloat16
I32 = mybir.dt.int32
AF = mybir.ActivationFunctionType
ALU = mybir.AluOpType
AX = mybir.AxisListType

D = 2048
DFF = 5632
NT = 1024
L = 2048
KC = D // 128
EPS = 1e-6
DEPTH = 2
PLE = 256
BIG = 1.0e4
SCALE = 1.0 / float(np.sqrt(128.0))
ENGS = ["pe", "act", "dve", "pool", "sp"]
TWO_PI = float(2.0 * np.pi)

WSPEC = [
    ("ffn1_w_gate", D, DFF), ("ffn1_w_up", D, DFF), ("ffn1_w_down", DFF, D),
    ("w_in", D, 8192), ("ssm_w_glu", 1024, 2048), ("w_branch_ssm", 1024, 2048),
    ("w_branch_attn", 1024, 2048), ("w_out", D, D),
    ("ffn2_w_gate", D, DFF), ("ffn2_w_up", D, DFF), ("ffn2_w_down", DFF, D),
    ("ple_w_up", PLE, D), ("ple_w_gate", D, D),
]
NORMS = ["ffn1_norm", "mix_norm", "ffn2_norm", "ple_norm"]


class Buf:
    __slots__ = ("name", "w", "r", "pw")

    def __init__(self, name):
        self.name = name
        self.w = None
        self.r = []
        self.pw = []


class TB:
    def __init__(self, t, b):
        self.t = t
        self.b = b

    def __getitem__(self, idx):
        return self.t[idx]


def _b(x):
    return x.b if isinstance(x, TB) else x


class Sched:
    def __init__(self, nc, es, ndma=16):
        self.nc = nc
        self.es = es
        self.prog = {e: [] for e in ENGS}
        self.cnt = {e: 0 for e in ENGS}
        self.known = {e: {} for e in ENGS}
        self.sems = {}
        self.ndma = ndma
        self.dma_val = {}
        self.dma_rr = {e: 0 for e in ENGS}
        self.dry = False
        self.nbuf = 0

    def buf(self, name=None):
        self.nbuf += 1
        return Buf(name or f"b{self.nbuf}")

    def sem(self, key):
        if key not in self.sems:
            self.sems[key] = self.es.enter_context(self.nc.semaphore(key))
        return self.sems[key]

    def _deps(self, eng, reads, writes, pwrites=()):
        toks = []
        for b in reads:
            b = _b(b)
            if b.w is not None:
                toks.append(b.w)
            toks.extend(b.pw)
        for b in writes:
            b = _b(b)
            if b.w is not None:
                toks.append(b.w)
            toks.extend(b.r)
            toks.extend(b.pw)
        for b in pwrites:
            b = _b(b)
            if b.w is not None:
                toks.append(b.w)
            toks.extend(b.r)
        best = {}
        for k, v in toks:
            if v > best.get(k, 0):
                best[k] = v
        waits = []
        kn = self.known[eng]
        for k, v in best.items():
            if eng == "pe" and k == "c_pe":
                continue
            if kn.get(k, 0) >= v:
                continue
            kn[k] = v
            waits.append((k, v))
        return waits

    @staticmethod
    def _compact(lst):
        best = {}
        for k, v in lst:
            if v > best.get(k, 0):
                best[k] = v
        return list(best.items())

    def _commit(self, tok, reads, writes, pwrites=()):
        for b in reads:
            b = _b(b)
            b.r.append(tok)
            if len(b.r) > 48:
                b.r = self._compact(b.r)
        for b in writes:
            b = _b(b)
            b.w = tok
            b.r = []
            b.pw = []
        for b in pwrites:
            b = _b(b)
            b.pw.append(tok)
            if len(b.pw) > 48:
                b.pw = self._compact(b.pw)

    def op(self, eng, fns, reads=(), writes=(), pwrites=()):
        if self.dry:
            return None
        if not isinstance(fns, (list, tuple)):
            fns = [fns]
        waits = self._deps(eng, reads, writes, pwrites)
        self.cnt[eng] += 1
        tok = ("c_" + eng, self.cnt[eng])
        self.prog[eng].append((waits, list(fns), (tok[0], 1)))
        self._commit(tok, reads, writes, pwrites)
        return tok

    def dma(self, eng, fn, reads=(), writes=(), pwrites=(), inc=16):
        if self.dry:
            return None
        i = self.dma_rr[eng]
        self.dma_rr[eng] = (i + 1) % self.ndma
        key = f"d_{eng}_{i}" if inc == 16 else f"x_{eng}_{i}"
        prev = self.dma_val.get(key, 0)
        waits = self._deps(eng, reads, writes, pwrites)
        if prev > 0 and self.known[eng].get(key, 0) < prev:
            self.known[eng][key] = prev
            waits.append((key, prev))
        val = prev + inc
        self.dma_val[key] = val
        tok = (key, val)
        self.prog[eng].append((waits, [fn], (key, inc)))
        self._commit(tok, reads, writes, pwrites)
        return tok

    def barrier(self):
        if self.dry:
            return
        toks = [("c_" + e, self.cnt[e]) for e in ENGS if self.cnt[e] > 0 and e != "pool"]
        toks += [(k, v) for k, v in self.dma_val.items() if "_pool_" not in k]
        for e in ENGS:
            if e != "pool":
                self.wait_all(e, toks)

    def wait_all(self, eng, toks):
        if self.dry:
            return
        waits = []
        for k, v in toks:
            if self.known[eng].get(k, 0) < v:
                self.known[eng][k] = v
                waits.append((k, v))
        if waits:
            self.prog[eng].append((waits, [], None))

    def emit(self):
        nc = self.nc
        for e in ENGS:
            for waits, fns, inc in self.prog[e]:
                for k, v in waits:
                    self.sem(k)
                if inc is not None:
                    self.sem(inc[0])
        with nc.Block() as block:
            def runner(ename):
                def run(eng):
                    for waits, fns, inc in self.prog[ename]:
                        for k, v in waits:
                            eng.wait_ge(self.sems[k], v)
                        ins = None
                        for f in fns:
                            ins = f(eng)
                        if inc is not None:
                            ins.then_inc(self.sems[inc[0]], inc[1])
                return run
            block.tensor(runner("pe"))
            block.scalar(runner("act"))
            block.vector(runner("dve"))
            block.gpsimd(runner("pool"))
            block.sync(runner("sp"))


class Ctx:
    def __init__(self, nc, es):
        self.nc = nc
        self.es = es
        self.S = Sched(nc, es)
        self.nname = 0
        self.pbank = []
        for i in range(8):
            t = es.enter_context(nc.psum_tensor(f"ps{i}", [128, 512], F32))
            self.pbank.append(TB(t, self.S.buf(f"ps{i}")))
        self.prr = 0
        self.NSLOT = 4
        self.wslots = None
        self.wplan = []
        self.wissued = 0
        self.wnext = 0
        self.wrel_count = 0
        self.wphase = []
        self.wlimit = 0

    def jval(self, e, mult):
        if not hasattr(self, "_jv"):
            self._jv = {}
        key = (id(e), mult)
        if key not in self._jv:
            self._jv[key] = e.snap((e.partition_id() % 2) * mult)
        return self._jv[key]

    ARENA_ELEMS = 106240

    def sb(self, shape, dtype, name=None, es=None):
        if not hasattr(self, "arena"):
            self.arena = self.es.enter_context(self.nc.sbuf_tensor("arena", [128, self.ARENA_ELEMS], BF16))
            self.aoff = 0
            self.scopes = set()
        if es is not None and id(es) not in self.scopes:
            self.scopes.add(id(es))
            mark = self.aoff

            def rel(mark=mark, key=id(es)):
                self.aoff = mark
                self.scopes.discard(key)
            es.callback(rel)
        self.nname += 1
        nm = f"{name or 't'}_{self.nname}"
        esz = 2 if dtype == BF16 else 4
        n = 1
        for d in shape[1:]:
            n *= d
        nbf = (n * esz + 1) // 2
        nbf = (nbf + 15) // 16 * 16
        assert self.aoff + nbf <= self.ARENA_ELEMS, (nm, self.aoff, nbf)
        ap = self.arena[0:shape[0], self.aoff:self.aoff + (n * esz) // 2]
        self.aoff += nbf
        self.apeak = max(getattr(self, "apeak", 0), self.aoff)
        if dtype != BF16:
            ap = ap.bitcast(dtype)
        if len(shape) > 2:
            names = " ".join(f"d{i}" for i in range(1, len(shape)))
            ap = ap.rearrange(f"p ({names}) -> p {names}", **{f"d{i}": shape[i] for i in range(1, len(shape) - 1)})
        return TB(ap, self.S.buf(nm))

    def dram(self, name, shape, dtype):
        return TB(self.nc.dram_tensor(name, shape, dtype).ap(), self.S.buf(name))

    def psum(self):
        b = self.pbank[self.prr]
        self.prr = (self.prr + 1) % 6
        return b

    def wphase_begin(self):
        starts = self.wphase
        cur = self.wnext
        nxt = [s for s in starts if s > cur]
        self.wlimit = nxt[0] if nxt else len(self.wplan)
        assert self.wissued == self.wnext, (self.wissued, self.wnext)
        self.wbase = self.wnext
        for _ in range(self.NSLOT):
            self._wissue()

    def _wissue(self):
        S = self.S
        j = self.wissued
        if j >= self.wlimit:
            return
        w2, k2, kc2, c2, n2 = self.wplan[j]
        ci, lr, lc = w2.find(k2, kc2, c2, n2)
        ch = w2.full[ci]
        sl = self.wslots[(j - self.wbase) % self.NSLOT]
        dst = sl.t[:, 0:kc2 * n2].rearrange("p (k c) -> p k c", k=kc2)
        src = ch.t[lr:lr + kc2 * 128, lc:lc + n2].rearrange("(k p) c -> p k c", p=128)
        S.dma("sp", (lambda d, s: (lambda e: e.dma_start(out=d, in_=s)))(dst, src),
              reads=(ch,), writes=(sl,))
        self.wissued += 1

    def wget(self, w, k0, kc, c0, ncols):
        S = self.S
        idx = self.wnext
        self.wnext += 1
        assert self.wplan[idx][1:] == (k0, kc, c0, ncols) and self.wplan[idx][0] is w, (idx, self.wplan[idx][1:], (k0, kc, c0, ncols))
        assert idx < self.wissued, (idx, self.wissued)
        sl = self.wslots[(idx - self.wbase) % self.NSLOT]
        return TB(sl.t[:, 0:kc * ncols].rearrange("p (k c) -> p k c", k=kc), sl.b)

    def wrel(self, n=1):
        if self.S.dry:
            return
        for _ in range(n):
            self._wissue()


def wchunks(K, M):
    if K > 2048:
        return [(r0, min(1024, K - r0), 0, M) for r0 in range(0, K, 1024)]
    if K == 2048:
        return [(0, K, c0, min(1024, M - c0)) for c0 in range(0, M, 1024)]
    return [(0, K, 0, M)]


class WMat:
    def __init__(self, C, tag, K, M, nw):
        self.K, self.M, self.nw = K, M, nw
        self.chunks = wchunks(K, M)
        self.bounce, self.half, self.full, self.off = [], [], [], []
        o = 0
        for i, (r0, nr, c0, ncw) in enumerate(self.chunks):
            self.off.append(o)
            o += (nr // nw) * ncw
            self.bounce.append(C.dram(f"wb_{tag}_{i}", [nr // nw, ncw], BF16))
            self.half.append(C.dram(f"wh_{tag}_{i}", [nr // 2, ncw], BF16) if nw == 8 else None)
            self.full.append(C.dram(f"wf_{tag}_{i}", [nr, ncw], BF16))
        self.per_rank = o

    def find(self, k0, kc, c0, ncols):
        for i, (r0, nr, cc0, ncw) in enumerate(self.chunks):
            if r0 <= k0 and k0 + kc * 128 <= r0 + nr and cc0 <= c0 and c0 + ncols <= cc0 + ncw:
                return i, k0 - r0, c0 - cc0
        raise AssertionError((k0, kc, c0, ncols))


def mm(out, lhsT, rhs, start, stop):
    return lambda e: e.matmul(out, lhsT, rhs, start=start, stop=stop)


def tsl(th):
    return slice(th * 512, th * 512 + 512)


def emit_rmsnorm(C, T, h, gain_col, out):
    S = C.S
    for th in range(NT // 512):
        ts = tsl(th)
        pt = C.psum()
        for kc in range(KC):
            sq = T.sq[kc % 2]
            S.op("act", (lambda o, i: (lambda e: e.activation(o, i, AF.Square)))(sq[:, :], h[:, kc, ts]),
                 reads=(h,), writes=(sq,))
            S.op("pe", mm(pt[:, :], C.ones[:, :], sq[:, :], kc == 0, kc == KC - 1),
                 reads=(sq, C.ones), writes=(pt,))
        rstd = T.rstd
        S.op("act", (lambda p_: (lambda e: e.activation(rstd[:, :], p_, AF.Sqrt, bias=C.eps_ap[:, 0:1], scale=1.0 / D)))(pt[:, :]),
             reads=(pt, C.eps_ap), writes=(rstd,))
        S.op("dve", lambda e: e.reciprocal(rstd[:, :], rstd[:, :]), reads=(rstd,), writes=(rstd,))
        for kc in range(KC):
            S.op("dve", (lambda o, i, g: (lambda e: e.scalar_tensor_tensor(o, i, g, rstd[:, :], ALU.mult, ALU.mult)))(
                out[:, kc, ts], h[:, kc, ts], C.gains[:, gain_col + kc:gain_col + kc + 1]),
                reads=(h, rstd, C.gains), pwrites=(out,))


def emit_ffn(C, T, h, xn, wg, wu, wd):
    S = C.S
    nchunk = DFF // 512
    for ch in range(nchunk):
        gw = C.wget(wg, 0, KC, ch * 512, 512)
        uw = C.wget(wu, 0, KC, ch * 512, 512)
        dw = C.wget(wd, ch * 512, 4, 0, D)
        if S.dry:
            continue
        hid = T.hid[ch % 2]
        for mi in range(4):
            for th in range(NT // 512):
                ts = tsl(th)
                pg = C.psum()
                S.op("pe", [mm(pg[:, :], gw[:, kc, mi * 128:mi * 128 + 128], xn[:, kc, ts], kc == 0, kc == KC - 1)
                            for kc in range(KC)], reads=(gw, xn), writes=(pg,))
                pu = C.psum()
                S.op("pe", [mm(pu[:, :], uw[:, kc, mi * 128:mi * 128 + 128], xn[:, kc, ts], kc == 0, kc == KC - 1)
                            for kc in range(KC)], reads=(uw, xn), writes=(pu,))
                sg = T.sg[(mi * 2 + th) % 2]
                S.op("act", (lambda o, i: (lambda e: e.activation(o, i, AF.Silu)))(sg[:, :], pg[:, :]),
                     reads=(pg,), writes=(sg,))
                S.op("dve", (lambda o, a, b: (lambda e: e.tensor_tensor(o, a, b, ALU.mult)))(
                    hid[:, mi, ts], pu[:, :], sg[:, :]), reads=(pu, sg), pwrites=(hid,))
        C.wrel(2)
        for mo in range(KC):
            for th in range(NT // 512):
                ts = tsl(th)
                pd = C.psum()
                S.op("pe", [mm(pd[:, :], dw[:, mi, mo * 128:mo * 128 + 128], hid[:, mi, ts], mi == 0, mi == 3)
                            for mi in range(4)], reads=(dw, hid), writes=(pd,))
                S.op("dve", (lambda o, a: (lambda e: e.scalar_tensor_tensor(o, a, 0.5, o, ALU.mult, ALU.add)))(
                    h[:, mo, ts], pd[:, :]), reads=(pd,), pwrites=(h,))
        C.wrel(1)


def emit_linear(C, w, K, c0, ncols, xs, epi, toks):
    S = C.S
    kc_n = K // 128
    bc = min(8192 // kc_n, ncols)
    for cb in range(ncols // bc):
        wb = C.wget(w, 0, kc_n, c0 + cb * bc, bc)
        if not S.dry:
            for mi in range(bc // 128):
                mo = (cb * bc) // 128 + mi
                for ti, ts in enumerate(toks):
                    ps = C.psum()
                    ops = []
                    rd = [wb]
                    for kc in range(kc_n):
                        ap, tb = xs(kc, ts)
                        ops.append(mm(ps[:, :], wb[:, kc, mi * 128:mi * 128 + 128], ap, kc == 0, kc == kc_n - 1))
                        if tb not in rd:
                            rd.append(tb)
                    S.op("pe", ops, reads=tuple(rd), writes=(ps,))
                    epi(mo, ti, ps)
        C.wrel(1)


class TokTiles:
    def __init__(self, C, es):
        S = C.S
        self.h = C.sb([128, KC, NT], F32, "h", es)
        self.xn = C.sb([128, KC, NT], BF16, "xn", es)
        C.wslots = [C.sb([128, 8192], BF16, f"wslot{i}", es) for i in range(C.NSLOT)]
        self.tarena = C.sb([128, 12288], BF16, "tarena", es)
        a = self.tarena.t
        def view(off, n, pat=None, **kw):
            ap = a[:, off:off + n]
            if pat:
                ap = ap.rearrange(pat, **kw)
            return TB(ap, S.buf())
        self.hid = [view(0, 4096, "p (m t) -> p m t", m=4), view(4096, 4096, "p (m t) -> p m t", m=4)]
        self.zst = [view(8192 + i * 512, 512) for i in range(4)]
        self.ys = view(0, 4096, "p (k t) -> p k t", k=8)
        self.y_a = view(4096, 4096, "p (k t) -> p k t", k=8)
        self.ya = view(8192, 4096, "p (k t) -> p k t", k=8)
        self.pb = view(0, 2048, "p (k t) -> p k t", k=2)
        self.sq = [C.sb([128, 512], BF16, f"sq{i}", es) for i in range(2)]
        self.rstd = C.sb([128, 512], F32, "rstd", es)
        self.sg = [C.sb([128, 512], BF16, f"sg{i}", es) for i in range(2)]
        self.sgs = [C.sb([128, 512], BF16, f"sgs{i}", es) for i in range(2)]
        self.sga = [C.sb([128, 512], BF16, f"sga{i}", es) for i in range(2)]
        self.t1 = [C.sb([128, 512], BF16, f"t1{i}", es) for i in range(2)]
        self.f32tmp = [C.sb([128, 512], F32, f"f32tmp{i}", es) for i in range(1)]
        self.pstage = C.sb([128, 2, 256], F32, "pstage", es)


def cp(eng_kind, o, i):
    if eng_kind == "act":
        return lambda e: e.activation(o, i, AF.Copy)
    return lambda e: e.tensor_copy(o, i)


def seg_a(C, T, l):
    S = C.S
    wf = C.wf[l]
    emit_rmsnorm(C, T, T.h, (l * 4 + 0) * KC, T.xn)
    emit_ffn(C, T, T.h, T.xn, wf["ffn1_w_gate"], wf["ffn1_w_up"], wf["ffn1_w_down"])
    emit_rmsnorm(C, T, T.h, (l * 4 + 1) * KC, T.xn)
    zx_in = C.zx_in[l]
    sg_d = C.sg_d[l]
    cnt = [0]

    def xs(kc, ts):
        return T.xn[:, kc, ts], T.xn

    def epi(mo, ti, ps):
        st = T.zst[cnt[0] % 4]
        eng = "act" if cnt[0] % 2 == 0 else "dve"
        cnt[0] += 1
        if mo < 32:
            S.op(eng, cp(eng, st[:, :], ps[:, :]), reads=(ps,), writes=(st,))
            zp = zx_in[mo // 8]
            S.dma("sp", (lambda o, i: (lambda e: e.dma_start(out=o, in_=i)))(
                zp.t[(mo % 8) * 128:(mo % 8 + 1) * 128, ti * 512:(ti + 1) * 512], st[:, :]),
                reads=(st,), pwrites=(zp,))
        else:
            S.op("act", (lambda o, i: (lambda e: e.activation(o, i, AF.Sigmoid)))(st[:, :], ps[:, :]),
                 reads=(ps,), writes=(st,))
            S.dma("sp", (lambda o, i: (lambda e: e.dma_start(out=o, in_=i)))(
                sg_d.t[(mo - 32) * 128:(mo - 31) * 128, ti * 512:(ti + 1) * 512], st[:, :]),
                reads=(st,), pwrites=(sg_d,))

    emit_linear(C, wf["w_in"], D, 0, 8192, xs, epi, [tsl(0), tsl(1)])
    for part in range(4):
        C.coll_pair(zx_in[part], C.zx_out[l][part], pop=(part == 3))
    S.dma("sp", lambda e: e.dma_start(out=C.hsp.t.rearrange("(k p) t -> p k t", p=128), in_=T.h[:, :, :]),
          reads=(T.h,), writes=(C.hsp,))


def seg_c(C, T, l, last):
    S = C.S
    wf = C.wf[l]
    yxs, yxa = C.yx_out[l]
    sg_d = C.sg_d[l]
    S.dma("sp", lambda e: e.dma_start(out=T.h[:, :, :], in_=C.hsp.t.rearrange("(k p) t -> p k t", p=128)),
          reads=(C.hsp,), writes=(T.h,))
    for hf in range(2):
        tcol = hf * 512
        for jj in range(2):
            for (dst, yx) in ((T.ys, yxs), (T.ya, yxa)):
                r0 = jj * 512

                def ld(e, dst=dst, r0=r0, jj=jj, tcol=tcol, yx=yx):
                    j1024 = C.jval(e, 1024)
                    return e.dma_start(out=dst[:, jj * 4:(jj + 1) * 4, :],
                                       in_=yx.t[r0:r0 + 512, tcol:tcol + 1536][:, bass.ds(j1024, 512)].rearrange("(c p) t -> p c t", p=128))
                S.dma("act", ld, reads=(yx,), pwrites=(dst,))
        for kc in range(8):
            x = T.ys[:, kc, :]
            tf = T.f32tmp[0]
            S.op("act", (lambda x: (lambda e: e.activation(tf[:, :], x, AF.Square)))(x), reads=(T.ys,), writes=(tf,))
            S.op("dve", lambda e: e.tensor_scalar(tf[:, :], tf[:, :], 0.044715, 1.0, ALU.mult, ALU.add),
                 reads=(tf,), writes=(tf,))
            S.op("dve", (lambda x: (lambda e: e.tensor_tensor(tf[:, :], tf[:, :], x, ALU.mult)))(x),
                 reads=(tf, T.ys), writes=(tf,))
            S.op("act", lambda e: e.activation(tf[:, :], tf[:, :], AF.Sigmoid, scale=1.5957691216057308),
                 reads=(tf,), writes=(tf,))
            S.op("dve", (lambda x: (lambda e: e.tensor_tensor(x, x, tf[:, :], ALU.mult)))(x),
                 reads=(tf,), pwrites=(T.ys,))
        wa = C.wget(wf["ssm_w_glu"], 0, 8, 0, 1024)
        wb = C.wget(wf["ssm_w_glu"], 0, 8, 1024, 1024)
        if not S.dry:
            for i in range(8):
                pa = C.psum()
                S.op("pe", [mm(pa[:, :], wa[:, kc, i * 128:i * 128 + 128], T.ys[:, kc, :], kc == 0, kc == 7)
                            for kc in range(8)], reads=(wa, T.ys), writes=(pa,))
                pb_ = C.psum()
                S.op("pe", [mm(pb_[:, :], wb[:, kc, i * 128:i * 128 + 128], T.ys[:, kc, :], kc == 0, kc == 7)
                            for kc in range(8)], reads=(wb, T.ys), writes=(pb_,))
                sg = T.sg[i % 2]
                S.op("act", (lambda o, i_: (lambda e: e.activation(o, i_, AF.Sigmoid)))(sg[:, :], pb_[:, :]),
                     reads=(pb_,), writes=(sg,))
                S.op("dve", (lambda o, a, b: (lambda e: e.tensor_tensor(o, a, b, ALU.mult)))(
                    T.y_a[:, i, :], pa[:, :], sg[:, :]), reads=(pa, sg), pwrites=(T.y_a,))
        C.wrel(2)
        for half in range(2):
            wA = C.wget(wf["w_branch_ssm"], 0, 8, half * 1024, 1024)
            wB = C.wget(wf["w_branch_attn"], 0, 8, half * 1024, 1024)
            if not S.dry:
                for mi in range(8):
                    mo = half * 8 + mi
                    sgs, sga = T.sgs[mo % 2], T.sga[mo % 2]
                    S.dma("sp", (lambda o, i: (lambda e: e.dma_start(out=o, in_=i)))(
                        sgs[:, :], sg_d.t[mo * 128:(mo + 1) * 128, tcol:tcol + 512]), reads=(sg_d,), writes=(sgs,))
                    S.dma("sp", (lambda o, i: (lambda e: e.dma_start(out=o, in_=i)))(
                        sga[:, :], sg_d.t[(16 + mo) * 128:(17 + mo) * 128, tcol:tcol + 512]), reads=(sg_d,), writes=(sga,))
                    p1 = C.psum()
                    S.op("pe", [mm(p1[:, :], wA[:, kc, mi * 128:mi * 128 + 128], T.y_a[:, kc, :], kc == 0, kc == 7)
                                for kc in range(8)], reads=(wA, T.y_a), writes=(p1,))
                    p2 = C.psum()
                    S.op("pe", [mm(p2[:, :], wB[:, kc, mi * 128:mi * 128 + 128], T.ya[:, kc, :], kc == 0, kc == 7)
                                for kc in range(8)], reads=(wB, T.ya), writes=(p2,))
                    t1, t2 = T.t1[0], T.t1[1]
                    S.op("dve", (lambda a, b: (lambda e: e.tensor_tensor(t1[:, :], a, b, ALU.mult)))(p1[:, :], sgs[:, :]),
                         reads=(p1, sgs), writes=(t1,))
                    S.op("dve", (lambda a, b: (lambda e: e.tensor_tensor(t2[:, :], a, b, ALU.mult)))(p2[:, :], sga[:, :]),
                         reads=(p2, sga), writes=(t2,))
                    S.op("dve", (lambda o: (lambda e: e.tensor_tensor(o, t1[:, :], t2[:, :], ALU.add)))(T.xn[:, mo, 0:512]),
                         reads=(t1, t2), pwrites=(T.xn,))
            C.wrel(2)

        def xs(kc, ts):
            return T.xn[:, kc, 0:512], T.xn

        def epi(mo, ti, ps, tcol=tcol):
            S.op("dve", (lambda o, a: (lambda e: e.tensor_tensor(o, a, o, ALU.add)))(
                T.h[:, mo, tcol:tcol + 512], ps[:, :]), reads=(ps,), pwrites=(T.h,))
        emit_linear(C, wf["w_out"], D, 0, D, xs, epi, [slice(0, 512)])
    S.barrier()
    emit_rmsnorm(C, T, T.h, (l * 4 + 2) * KC, T.xn)
    emit_ffn(C, T, T.h, T.xn, wf["ffn2_w_gate"], wf["ffn2_w_up"], wf["ffn2_w_down"])
    S.barrier()
    emit_rmsnorm(C, T, T.h, (l * 4 + 3) * KC, T.xn)
    for qd in range(4):
        S.dma("sp", (lambda qd: (lambda e: e.dma_start(
            out=T.pstage[:, :, :], in_=C.pT.t[l, :, qd * 256:(qd + 1) * 256].rearrange("(k p) t -> p k t", p=128))))(qd),
            reads=(C.pT,), writes=(T.pstage,))
        S.op("dve", (lambda qd: (lambda e: e.tensor_copy(T.pb[:, :, qd * 256:(qd + 1) * 256], T.pstage[:, :, :])))(qd),
             reads=(T.pstage,), pwrites=(T.pb,))
    for cb in range(4):
        wpg = C.wget(wf["ple_w_gate"], 0, KC, cb * 512, 512)
        wpu = C.wget(wf["ple_w_up"], 0, 2, cb * 512, 512)
        if not S.dry:
            for mi in range(4):
                mo = cb * 4 + mi
                for th in range(2):
                    ts = tsl(th)
                    pg = C.psum()
                    S.op("pe", [mm(pg[:, :], wpg[:, kc, mi * 128:mi * 128 + 128], T.xn[:, kc, ts], kc == 0, kc == KC - 1)
                                for kc in range(KC)], reads=(wpg, T.xn), writes=(pg,))
                    pu = C.psum()
                    S.op("pe", [mm(pu[:, :], wpu[:, kc, mi * 128:mi * 128 + 128], T.pb[:, kc, ts], kc == 0, kc == 1)
                                for kc in range(2)], reads=(wpu, T.pb), writes=(pu,))
                    sg = T.f32tmp[0]
                    S.op("act", (lambda i_: (lambda e: e.activation(sg[:, :], i_, AF.Sigmoid)))(pg[:, :]),
                         reads=(pg,), writes=(sg,))
                    S.op("dve", (lambda a: (lambda e: e.tensor_tensor(sg[:, :], a, sg[:, :], ALU.mult)))(pu[:, :]),
                         reads=(pu, sg), writes=(sg,))
                    S.op("dve", (lambda o: (lambda e: e.tensor_tensor(o, o, sg[:, :], ALU.add)))(T.h[:, mo, ts]),
                         reads=(sg,), pwrites=(T.h,))
        C.wrel(2)
    S.barrier()
    if last:
        emit_rmsnorm(C, T, T.h, DEPTH * 4 * KC, T.h)
        tok = S.dma("sp", lambda e: e.dma_start(out=C.oT.t.rearrange("(k p) t -> p k t", p=128), in_=T.h[:, :, :]),
                    reads=(T.h,), writes=(C.oT,))
        if not S.dry:
            S.wait_all("sp", [tok])


def tt(o, a, b, op):
    return lambda e: e.tensor_tensor(o, a, b, op)


def ssm_prep(C, l):
    S = C.S
    P = C.ssm_in
    with ExitStack() as es:
        def sm(name, shape=(128, 16), dt=F32):
            return C.sb(list(shape), dt, name, es)

        def ew(eng, fn, reads, writes):
            S.op(eng, fn, reads=reads, writes=writes)

        def load(name, shape):
            t = sm(name, shape)
            src = P[name].t[l]
            S.dma("sp", (lambda o, i: (lambda e: e.dma_start(out=o, in_=i)))(t.t[tuple(slice(None) for _ in shape)], src),
                  reads=(P[name],), writes=(t,))
            return t
        ar_in = load("ssm_ar", (128, 16))
        ai_in = load("ssm_ai", (128, 16))
        ldt = load("ssm_ldt", (128, 16))
        ctre = load("ssm_ct_re", (128, 16, 16))
        ctim = load("ssm_ct_im", (128, 16, 16))
        btre = load("ssm_bt_re", (128, 16, 16))
        btim = load("ssm_bt_im", (128, 16, 16))
        A = lambda t: t[:, :]
        dt_ = sm("dt")
        ew("act", lambda e: e.activation(A(dt_), A(ldt), AF.Exp), (ldt,), (dt_,))
        ar, ai, mag = sm("ar"), sm("ai"), sm("mag")
        ew("dve", tt(A(ar), A(ar_in), A(dt_), ALU.mult), (ar_in, dt_), (ar,))
        ew("dve", tt(A(ai), A(ai_in), A(dt_), ALU.mult), (ai_in, dt_), (ai,))
        ew("act", lambda e: e.activation(A(mag), A(ar), AF.Exp), (ar,), (mag,))

        def sin_of(x, offset, name):
            xo, y, kf, r, m = sm(name + "xo"), sm(name + "y"), sm(name + "kf"), sm(name + "r"), sm(name + "m")
            ki = sm(name + "ki", (128, 16), I32)
            ew("dve", lambda e: e.tensor_scalar(A(xo), A(x), float(offset), None, ALU.add), (x,), (xo,))
            ew("dve", lambda e: e.tensor_scalar(A(y), A(xo), 1.0 / TWO_PI, 0.5, ALU.mult, ALU.add), (xo,), (y,))
            ew("dve", lambda e: e.tensor_copy(A(ki), A(y)), (y,), (ki,))
            ew("dve", lambda e: e.tensor_copy(A(kf), A(ki)), (ki,), (kf,))
            ew("dve", lambda e: e.scalar_tensor_tensor(A(r), A(kf), -TWO_PI, A(xo), ALU.mult, ALU.add), (kf, xo), (r,))
            ew("dve", lambda e: e.tensor_scalar(A(m), A(r), float(np.pi), None, ALU.is_gt), (r,), (m,))
            ew("dve", lambda e: e.scalar_tensor_tensor(A(r), A(m), -TWO_PI, A(r), ALU.mult, ALU.add), (m, r), (r,))
            ew("dve", lambda e: e.tensor_scalar(A(m), A(r), float(-np.pi), None, ALU.is_lt), (r,), (m,))
            ew("dve", lambda e: e.scalar_tensor_tensor(A(r), A(m), TWO_PI, A(r), ALU.mult, ALU.add), (m, r), (r,))
            s = sm(name + "s")
            ew("act", lambda e: e.activation(A(s), A(r), AF.Sin), (r,), (s,))
            return s
        sn = sin_of(ai, 0.0, "sn")
        cs = sin_of(ai, np.pi / 2.0, "cs")
        lr, li = sm("lr"), sm("li")
        ew("dve", tt(A(lr), A(mag), A(cs), ALU.mult), (mag, cs), (lr,))
        ew("dve", tt(A(li), A(mag), A(sn), ALU.mult), (mag, sn), (li,))
        den, t0, t1_, nr, cr, ci = sm("den"), sm("t0"), sm("t1"), sm("nr"), sm("cr"), sm("ci")
        ew("dve", tt(A(den), A(ar_in), A(ar_in), ALU.mult), (ar_in,), (den,))
        ew("dve", tt(A(t0), A(ai_in), A(ai_in), ALU.mult), (ai_in,), (t0,))
        ew("dve", tt(A(den), A(den), A(t0), ALU.add), (den, t0), (den,))
        ew("dve", lambda e: e.reciprocal(A(den), A(den)), (den,), (den,))
        ew("dve", lambda e: e.tensor_scalar(A(nr), A(lr), -1.0, None, ALU.add), (lr,), (nr,))
        ew("dve", tt(A(t0), A(nr), A(ar_in), ALU.mult), (nr, ar_in), (t0,))
        ew("dve", tt(A(t1_), A(li), A(ai_in), ALU.mult), (li, ai_in), (t1_,))
        ew("dve", tt(A(t0), A(t0), A(t1_), ALU.add), (t0, t1_), (t0,))
        ew("dve", tt(A(cr), A(t0), A(den), ALU.mult), (t0, den), (cr,))
        ew("dve", tt(A(t0), A(li), A(ar_in), ALU.mult), (li, ar_in), (t0,))
        ew("dve", tt(A(t1_), A(nr), A(ai_in), ALU.mult), (nr, ai_in), (t1_,))
        ew("dve", tt(A(t0), A(t0), A(t1_), ALU.subtract), (t0, t1_), (t0,))
        ew("dve", tt(A(ci), A(t0), A(den), ALU.mult), (t0, den), (ci,))
        B3 = [128, 16, 16]
        bbre, bbim, u0, u1 = sm("bbre", B3), sm("bbim", B3), sm("u0", B3), sm("u1", B3)
        crb = cr.t[:, :, None].to_broadcast(B3)
        cib = ci.t[:, :, None].to_broadcast(B3)
        F3 = lambda t: t[:, :, :]
        ew("dve", tt(F3(u0), crb, F3(btre), ALU.mult), (cr, btre), (u0,))
        ew("dve", tt(F3(u1), cib, F3(btim), ALU.mult), (ci, btim), (u1,))
        ew("dve", tt(F3(bbre), F3(u0), F3(u1), ALU.subtract), (u0, u1), (bbre,))
        ew("dve", tt(F3(u0), crb, F3(btim), ALU.mult), (cr, btim), (u0,))
        ew("dve", tt(F3(u1), cib, F3(btre), ALU.mult), (ci, btre), (u1,))
        ew("dve", tt(F3(bbim), F3(u0), F3(u1), ALU.add), (u0, u1), (bbim,))
        powre, powim = sm("powre", (128, 16, 9)), sm("powim", (128, 16, 9))
        ew("dve", lambda e: e.memset(powre[:, :, 0:1], 1.0), (), (powre,))
        ew("dve", lambda e: e.memset(powim[:, :, 0:1], 0.0), (), (powim,))
        ew("dve", lambda e: e.tensor_copy(powre[:, :, 1], A(lr)), (lr, powre), (powre,))
        ew("dve", lambda e: e.tensor_copy(powim[:, :, 1], A(li)), (li, powim), (powim,))
        for j in range(2, 9):
            ew("dve", tt(A(t0), powre[:, :, j - 1], A(lr), ALU.mult), (powre, lr), (t0,))
            ew("dve", tt(A(t1_), powim[:, :, j - 1], A(li), ALU.mult), (powim, li), (t1_,))
            ew("dve", tt(powre[:, :, j], A(t0), A(t1_), ALU.subtract), (t0, t1_, powre), (powre,))
            ew("dve", tt(A(t0), powre[:, :, j - 1], A(li), ALU.mult), (powre, li), (t0,))
            ew("dve", tt(A(t1_), powim[:, :, j - 1], A(lr), ALU.mult), (powim, lr), (t1_,))
            ew("dve", tt(powim[:, :, j], A(t0), A(t1_), ALU.add), (t0, t1_, powim), (powim,))
        lam8 = sm("lam8", (128, 16, 2))
        ew("dve", lambda e: e.tensor_copy(lam8[:, :, 0], powre[:, :, 8]), (powre,), (lam8,))
        ew("dve", lambda e: e.tensor_copy(lam8[:, :, 1], powim[:, :, 8]), (powim, lam8), (lam8,))
        S.dma("sp", lambda e: e.dma_start(out=C.lam8_d[l].t, in_=lam8[:, :, :]), reads=(lam8,), writes=(C.lam8_d[l],))
        prre, prim = sm("prre", (128, 16, 8)), sm("prim", (128, 16, 8))
        for tau in range(8):
            ew("dve", (lambda tau: (lambda e: e.tensor_copy(prre[:, :, tau], powre[:, :, 7 - tau])))(tau), (powre, prre), (prre,))
            ew("dve", (lambda tau: (lambda e: e.tensor_copy(prim[:, :, tau], powim[:, :, 7 - tau])))(tau), (powim, prim), (prim,))
        C4 = [128, 16, 9, 16]
        clre, clim, v0, v1 = sm("clre", C4), sm("clim", C4), sm("v0", C4), sm("v1", C4)
        F4 = lambda t: t[:, :, :, :]
        ctre_b = ctre.t[:, :, None, :].to_broadcast(C4)
        ctim_b = ctim.t[:, :, None, :].to_broadcast(C4)
        pre_b = powre.t[:, :, :, None].to_broadcast(C4)
        pim_b = powim.t[:, :, :, None].to_broadcast(C4)
        ew("dve", tt(F4(v0), ctre_b, pre_b, ALU.mult), (ctre, powre), (v0,))
        ew("dve", tt(F4(v1), ctim_b, pim_b, ALU.mult), (ctim, powim), (v1,))
        ew("dve", tt(F4(clre), F4(v0), F4(v1), ALU.subtract), (v0, v1), (clre,))
        ew("dve", tt(F4(v0), ctre_b, pim_b, ALU.mult), (ctre, powim), (v0,))
        ew("dve", tt(F4(v1), ctim_b, pre_b, ALU.mult), (ctim, powre), (v1,))
        ew("dve", tt(F4(v0), F4(v0), F4(v1), ALU.add), (v0, v1), (v0,))
        ew("dve", lambda e: e.tensor_scalar(F4(clim), F4(v0), -1.0, None, ALU.mult), (v0,), (clim,))
        w4re, w4im = sm("w4re", (128, 16, 128), BF16), sm("w4im", (128, 16, 128), BF16)
        ew("act", lambda e: e.activation(w4re.t[:, :, :].rearrange("p g (t q) -> p g t q", t=8), clre[:, :, 1:9, :], AF.Copy), (clre,), (w4re,))
        ew("act", lambda e: e.activation(w4im.t[:, :, :].rearrange("p g (t q) -> p g t q", t=8), clim[:, :, 1:9, :], AF.Copy), (clim,), (w4im,))
        S.dma("sp", lambda e: e.dma_start(out=C.w4re_d[l].t, in_=w4re[:, :, :]), reads=(w4re,), writes=(C.w4re_d[l],))
        S.dma("sp", lambda e: e.dma_start(out=C.w4im_d[l].t, in_=w4im[:, :, :]), reads=(w4im,), writes=(C.w4im_d[l],))
        T4 = [128, 16, 8, 16]
        t2re, t2im = sm("t2re", T4), sm("t2im", T4)
        w0, w1 = v0.t[:, :, 0:8, :], v1.t[:, :, 0:8, :]
        prre_b = prre.t[:, :, :, None].to_broadcast(T4)
        prim_b = prim.t[:, :, :, None].to_broadcast(T4)
        bbre_b = bbre.t[:, :, None, :].to_broadcast(T4)
        bbim_b = bbim.t[:, :, None, :].to_broadcast(T4)
        ew("dve", tt(w0, prre_b, bbre_b, ALU.mult), (prre, bbre), (v0,))
        ew("dve", tt(w1, prim_b, bbim_b, ALU.mult), (prim, bbim), (v1,))
        ew("dve", tt(F4(t2re), w0, w1, ALU.subtract), (v0, v1), (t2re,))
        ew("dve", tt(w0, prre_b, bbim_b, ALU.mult), (prre, bbim), (v0,))
        ew("dve", tt(w1, prim_b, bbre_b, ALU.mult), (prim, bbre), (v1,))
        ew("dve", tt(F4(t2im), w0, w1, ALU.add), (v0, v1), (t2im,))
        w2re, w2im = sm("w2re", (128, 32, 64), BF16), sm("w2im", (128, 32, 64), BF16)
        for (src, dst) in ((t2re, w2re), (t2im, w2im)):
            for gh in range(2):
                for gb in range(2):
                    ps = C.psum()
                    ops = []
                    for gi in range(8):
                        gp = gb * 8 + gi
                        ops.append((lambda o, i, idn: (lambda e: e.transpose(o, i, idn)))(
                            ps[:, gi * 64:(gi + 1) * 64],
                            src.t[64 * gh:64 * gh + 64, gp, :, :].rearrange("p a b -> p (a b)"),
                            C.ident[64 * gh:64 * gh + 64, 64 * gh:64 * gh + 64]))
                    S.op("pe", ops, reads=(src, C.ident), writes=(ps,))
                    g0 = gh * 16 + gb * 8
                    ew("act", (lambda dst, g0, ps: (lambda e: e.activation(
                        dst.t[:, g0:g0 + 8, :], ps[:, :].rearrange("p (g n) -> p g n", g=8), AF.Copy)))(dst, g0, ps),
                        (ps,), (dst,))
        S.dma("sp", lambda e: e.dma_start(out=C.w2re_d[l].t, in_=w2re[:, :, :]), reads=(w2re,), writes=(C.w2re_d[l],))
        S.dma("sp", lambda e: e.dma_start(out=C.w2im_d[l].t, in_=w2im[:, :, :]), reads=(w2im,), writes=(C.w2im_d[l],))
        w1sb = sm("w1sb", (128, 32, 128), BF16)
        sets = []
        for i in range(2):
            st = dict(bre=sm(f"bwre{i}", (128, 240)), bim=sm(f"bwim{i}", (128, 240)),
                      cre=sm(f"cwre{i}", (128, 240)), cim=sm(f"cwim{i}", (128, 240)))
            for k in st:
                ew("dve", (lambda t: (lambda e: e.memset(t[:, :], 0.0)))(st[k]), (), (st[k],))
            sets.append(st)
        for gp in range(16):
            st = sets[gp % 2]
            ew("dve", (lambda st, gp: (lambda e: e.tensor_copy(st["bre"][:, 112:128], bbre[:, gp, :])))(st, gp), (bbre, st["bre"]), (st["bre"],))
            ew("dve", (lambda st, gp: (lambda e: e.tensor_copy(st["bim"][:, 112:128], bbim[:, gp, :])))(st, gp), (bbim, st["bim"]), (st["bim"],))
            ew("act", (lambda st, gp: (lambda e: e.activation(st["cre"].t[:, 112:240].rearrange("p (j q) -> p j q", j=8), clre[:, gp, 0:8, :], AF.Copy)))(st, gp), (clre, st["cre"]), (st["cre"],))
            ew("act", (lambda st, gp: (lambda e: e.activation(st["cim"].t[:, 112:240].rearrange("p (j q) -> p j q", j=8), clim[:, gp, 0:8, :], AF.Copy)))(st, gp), (clim, st["cim"]), (st["cim"],))
            for gh in range(2):
                ps = C.psum()
                ops = []
                rows = slice(64 * gh, 64 * gh + 64)
                for tau in range(8):
                    for ri, (bk, ck) in enumerate((("bre", "cre"), ("bim", "cim"))):
                        ops.append(mm(ps[:, 0:128], st[bk][rows, 112 - 16 * tau:240 - 16 * tau],
                                      st[ck][rows, (7 - tau) * 16:(7 - tau) * 16 + 128],
                                      tau == 0 and ri == 0, tau == 7 and ri == 1))
                S.op("pe", ops, reads=(st["bre"], st["bim"], st["cre"], st["cim"]), writes=(ps,))
                g = gh * 16 + gp
                S.op("dve", (lambda g, ps: (lambda e: e.tensor_copy(w1sb[:, g, :], ps[:, 0:128])))(g, ps),
                     reads=(ps,), pwrites=(w1sb,))
        S.dma("sp", lambda e: e.dma_start(out=C.w1_d[l].t, in_=w1sb[:, :, :]), reads=(w1sb,), writes=(C.w1_d[l],))
        S.barrier()


def seg_b(C, l):
    S = C.S
    zxp = C.zx_out[l]
    yxs_in, yxa_in = C.yx_in[l]
    with ExitStack() as es:
        with ExitStack() as es1:
            zs = C.sb([128, 4, L], BF16, "zs", es1)
            XY = C.sb([128, 32 * 256], BF16, "XY", es1)
            Sst = C.sb([128, 16, 256, 2], F32, "Sst", es1)
            spre = [C.sb([128, 16, 256], BF16, f"spre{i}", es1) for i in range(2)]
            YO = C.sb([128, 32 * 256], BF16, "YO", es1)
            w1 = C.sb([128, 32, 128], BF16, "w1", es1)
            w2 = [C.sb([128, 32, 64], BF16, f"w2{i}", es1) for i in range(2)]
            w4 = [C.sb([128, 16, 128], BF16, f"w4{i}", es1) for i in range(2)]
            lam8 = C.sb([128, 16, 2], F32, "lam8", es1)
            lrr = C.sb([128, 16, 2], F32, "lrr", es1)
            lii = C.sb([128, 16, 2], F32, "lii", es1)
            tA = C.sb([128, 16, 2], F32, "tA", es1)
            tB = C.sb([128, 16, 2], F32, "tB", es1)
            dd = C.sb([128, 4], F32, "dd", es1)

            def ld(dst_ap, src_ap, rd, wr, pw=False):
                S.dma("sp", (lambda o, i: (lambda e: e.dma_start(out=o, in_=i)))(dst_ap, src_ap), reads=rd,
                      writes=() if pw else wr, pwrites=wr if pw else ())
            ld(w1[:, :, :], C.w1_d[l].t, (C.w1_d[l],), (w1,))
            ld(w2[0][:, :, :], C.w2re_d[l].t, (C.w2re_d[l],), (w2[0],))
            ld(w2[1][:, :, :], C.w2im_d[l].t, (C.w2im_d[l],), (w2[1],))
            ld(w4[0][:, :, :], C.w4re_d[l].t, (C.w4re_d[l],), (w4[0],))
            ld(w4[1][:, :, :], C.w4im_d[l].t, (C.w4im_d[l],), (w4[1],))
            ld(lam8[:, :, :], C.lam8_d[l].t, (C.lam8_d[l],), (lam8,))
            ld(dd[:, :], C.ssm_in["ssm_dd"].t[l], (C.ssm_in["ssm_dd"],), (dd,))
            for jj in range(2):
                def ldz(e, jj=jj):
                    j512 = C.jval(e, 512)
                    b0 = jj * 1024
                    return e.dma_start(out=zs[:, :, jj * NT:(jj + 1) * NT],
                                       in_=zxp[0].t[b0:b0 + 1024, :][bass.ds(j512, 512), :].rearrange("(c p) t -> p c t", p=128))
                S.dma("sp", ldz, reads=(zxp[0],), pwrites=(zs,))
            S.op("dve", lambda e: e.tensor_copy(lrr[:, :, 0], lam8[:, :, 0]), reads=(lam8,), writes=(lrr,))
            S.op("dve", lambda e: e.tensor_copy(lrr[:, :, 1], lam8[:, :, 0]), reads=(lam8, lrr), writes=(lrr,))
            S.op("dve", lambda e: e.tensor_scalar(lii[:, :, 0], lam8[:, :, 1], -1.0, None, ALU.mult), reads=(lam8,), writes=(lii,))
            S.op("dve", lambda e: e.tensor_copy(lii[:, :, 1], lam8[:, :, 1]), reads=(lam8, lii), writes=(lii,))
            zperm = TB(YO.t[:, :].rearrange("p (k t c) -> p k t c", k=4, t=8), YO.b)
            for cc in range(4):
                eng = ("act", "dve")[cc % 2]
                S.op(eng, cp(eng, zperm[:, cc, :, :], zs[:, cc, :].rearrange("p (c t) -> p t c", t=8)),
                     reads=(zs,), pwrites=(YO,))
            for cc in range(4):
                ld(C.zd.t[:, cc * 128:(cc + 1) * 128, :].rearrange("t p c -> p t c"), zperm[:, cc, :, :], (YO,), (C.zd,), pw=True)
            X = TB(XY.t[:, :].rearrange("p (g c) -> p g c", g=32), XY.b)
            for tau in range(8):
                ld(X[16 * tau:16 * tau + 16, :, :], C.zd.t[tau].rearrange("(g q) c -> q g c", q=16), (C.zd,), (XY,), pw=True)
            for gp in range(16):
                ps = C.psum()
                ops = []
                for gh in range(2):
                    g = gh * 16 + gp
                    for ri in range(2):
                        ops.append(mm(ps[64 * gh:64 * gh + 64, ri * 256:(ri + 1) * 256], w2[ri][:, g, :], X[:, g, :], True, True))
                S.op("pe", ops, reads=(w2[0], w2[1], XY), writes=(ps,))
                eng = ("act", "dve")[gp % 2]
                S.op(eng, cp(eng, Sst[:, gp, :, :], ps[:, :].rearrange("p (r c) -> p c r", r=2)), reads=(ps,), pwrites=(Sst,))
            for c in range(1, 256):
                S.op("dve", (lambda c: (lambda e: e.tensor_tensor(tA[:, :, :], Sst[:, :, c - 1, :], lrr[:, :, :], ALU.mult)))(c),
                     reads=(Sst, lrr), writes=(tA,))
                S.op("dve", (lambda c: (lambda e: e.tensor_tensor(tB[:, :, :], Sst[:, :, c - 1, ::-1], lii[:, :, :], ALU.mult)))(c),
                     reads=(Sst, lii), writes=(tB,))
                S.op("dve", (lambda c: (lambda e: e.tensor_tensor(tA[:, :, :], tA[:, :, :], tB[:, :, :], ALU.add)))(c),
                     reads=(tA, tB), writes=(tA,))
                S.op("dve", (lambda c: (lambda e: e.tensor_tensor(Sst[:, :, c, :], Sst[:, :, c, :], tA[:, :, :], ALU.add)))(c),
                     reads=(tA,), pwrites=(Sst,))
            for ri in range(2):
                eng = ("act", "dve")[ri]
                S.op("dve", (lambda ri: (lambda e: e.memset(spre[ri][:, :, 0:1], 0.0)))(ri), writes=(spre[ri],))
                S.op(eng, cp(eng, spre[ri][:, :, 1:256], Sst[:, :, 0:255, ri]), reads=(Sst, spre[ri]), writes=(spre[ri],))
            Yb = TB(YO.t[:, :].rearrange("p (g c) -> p g c", g=32), YO.b)
            for g2 in range(16):
                ps = C.psum()
                ops = []
                for k in range(2):
                    g = g2 * 2 + k
                    gh, gp = g // 16, g % 16
                    rows = slice(64 * gh, 64 * gh + 64)
                    o = ps[:, k * 256:(k + 1) * 256]
                    ops.append(mm(o, w1[:, g, :], X[:, g, :], True, False))
                    ops.append(mm(o, w4[0][rows, gp, :], spre[0][rows, gp, :], False, False))
                    ops.append(mm(o, w4[1][rows, gp, :], spre[1][rows, gp, :], False, True))
                S.op("pe", ops, reads=(w1, XY, w4[0], w4[1], spre[0], spre[1]), writes=(ps,))
                eng = ("act", "dve")[g2 % 2]
                S.op(eng, cp(eng, Yb[:, 2 * g2:2 * g2 + 2, :], ps[:, :].rearrange("p (g c) -> p g c", g=2)),
                     reads=(ps,), pwrites=(YO,))
            for t in range(8):
                ld(C.yd.t[t].rearrange("g p c -> p g c"), Yb[16 * t:16 * t + 16, :, :], (YO,), (C.yd,), pw=True)
            Ysel = TB(XY.t[:, :].rearrange("p (k t c) -> p k t c", k=4, t=8), XY.b)
            for cc in range(4):
                ld(Ysel[:, cc, :, :], C.yd.t[:, cc * 8:(cc + 1) * 8, :, :].rearrange("t g p c -> (g p) t c"), (C.yd,), (XY,), pw=True)
            yout = TB(YO.t[:, :].rearrange("p (k n) -> p k n", k=4), YO.b)
            for cc in range(4):
                S.op("dve", (lambda cc: (lambda e: e.scalar_tensor_tensor(
                    yout[:, cc, :].rearrange("p (c t) -> p c t", t=8),
                    zs[:, cc, :].rearrange("p (c t) -> p c t", t=8), dd[:, cc:cc + 1],
                    Ysel[:, cc, :, :].rearrange("p t c -> p c t"), ALU.mult, ALU.add)))(cc),
                    reads=(zs, dd, XY), pwrites=(YO,))
            for cc in range(4):
                ld(yxs_in.t[cc * 128:(cc + 1) * 128, :], yout[:, cc, :], (YO,), (yxs_in,), pw=True)
            C.coll_pair(yxs_in, C.yx_out[l][0], pop=False)
            S.barrier()
        with ExitStack() as es2:
            qT = C.sb([128, 4, L], BF16, "qT", es2)
            kT = C.sb([128, 4, L], BF16, "kT", es2)
            vst = C.sb([128, 4, L], BF16, "vst", es2)
            Vt = C.sb([128, 16, 4, 128], BF16, "Vt", es2)
            ao = C.sb([128, 4, L], BF16, "ao", es2)
            cosT = C.sb([32, L], F32, "cosT", es2)
            sinT = C.sb([32, L], F32, "sinT", es2)
            rt = [C.sb([32, 512], F32, f"rt{i}", es2) for i in range(2)]
            ksum = C.sb([128, 4, 8], F32, "ksum", es2)
            khi = C.sb([128, 4, 8], BF16, "khi", es2)
            klo = C.sb([128, 4, 8], BF16, "klo", es2)
            kd = C.sb([128, 4, 8], F32, "kd", es2)
            Gm = C.sb([128, 16, 8], F32, "Gm", es2)
            m8 = C.sb([128, 8], F32, "m8", es2)
            sel = C.sb([128, 16, 8], F32, "sel", es2)
            biasT = C.sb([8, 4, L], BF16, "biasT", es2)
            PT = [C.sb([128, 256], BF16, f"PT{i}", es2) for i in range(3)]
            rec = C.sb([128, 256], F32, "rec", es2)
            ld = lambda d, s_, rd, wr, pw=False: S.dma(
                "sp", (lambda o, i: (lambda e: e.dma_start(out=o, in_=i)))(d, s_), reads=rd,
                writes=() if pw else wr, pwrites=wr if pw else ())
            ld(cosT[:, :], C.consts["rot_cos"].t, (C.consts["rot_cos"],), (cosT,))
            ld(sinT[:, :], C.consts["rot_sin"].t, (C.consts["rot_sin"],), (sinT,))
            for (dst, part) in ((qT, 1), (kT, 2), (vst, 3)):
                for jj in range(2):
                    def ldq(e, dst=dst, part=part, jj=jj):
                        j512 = C.jval(e, 512)
                        b0 = jj * 1024
                        return e.dma_start(out=dst[:, :, jj * NT:(jj + 1) * NT],
                                           in_=zxp[part].t[b0:b0 + 1024, :][bass.ds(j512, 512), :].rearrange("(c p) t -> p c t", p=128))
                    S.dma("sp", ldq, reads=(zxp[part],), pwrites=(dst,))
            for hh in range(4):
                for k4 in range(4):
                    ps = C.psum()
                    psb = ps.t[:, 0:256].bitcast(BF16)
                    S.op("pe", [(lambda o, i_: (lambda e: e.transpose(o, i_, C.identb[:, :])))(
                        psb[:, i * 128:(i + 1) * 128], vst[:, hh, (k4 * 4 + i) * 128:(k4 * 4 + i + 1) * 128])
                                for i in range(4)], reads=(vst, C.identb), writes=(ps,))
                    eng = ("act", "dve")[k4 % 2]
                    S.op(eng, cp(eng, Vt[:, k4 * 4:k4 * 4 + 4, hh, :], psb.rearrange("p (k d) -> p k d", k=4)),
                         reads=(ps,), pwrites=(Vt,))
            it = 0
            for x in (qT, kT):
                for hh in range(4):
                    for tt_ in range(4):
                        ts = slice(tt_ * 512, tt_ * 512 + 512)
                        ps = C.psum()
                        S.op("pe", mm(ps[0:32, :], C.permT[0:32, 0:32], x[0:32, hh, ts], True, True),
                             reads=(x, C.permT), writes=(ps,))
                        r0, r1 = rt[0], rt[1]
                        S.op("dve", (lambda ps, ts: (lambda e: e.tensor_tensor(r0[:, :], ps[0:32, :], sinT[:, ts], ALU.mult)))(ps, ts),
                             reads=(ps, sinT), writes=(r0,))
                        S.op("dve", (lambda x, hh, ts: (lambda e: e.tensor_tensor(r1[:, :], x[0:32, hh, ts], cosT[:, ts], ALU.mult)))(x, hh, ts),
                             reads=(x, cosT), writes=(r1,))
                        S.op("dve", (lambda x, hh, ts: (lambda e: e.tensor_tensor(x[0:32, hh, ts], r0[:, :], r1[:, :], ALU.add)))(x, hh, ts),
                             reads=(r0, r1), writes=(x,))
                        it += 1
            for hh in range(4):
                S.op("dve", (lambda hh: (lambda e: e.tensor_reduce(ksum[:, hh, :], kT[:, hh, :].rearrange("p (n t) -> p n t", t=256), AX.X, ALU.add)))(hh),
                     reads=(kT,), pwrites=(ksum,))
            S.op("dve", lambda e: e.tensor_copy(khi[:, :, :], ksum[:, :, :]), reads=(ksum,), writes=(khi,))
            S.op("dve", lambda e: e.tensor_tensor(kd[:, :, :], ksum[:, :, :], khi[:, :, :], ALU.subtract), reads=(ksum, khi), writes=(kd,))
            S.op("dve", lambda e: e.tensor_copy(klo[:, :, :], kd[:, :, :]), reads=(kd,), writes=(klo,))
            for hh in range(4):
                ps = C.psum()
                ops = []
                for qt in range(16):
                    o = ps[:, qt * 8:(qt + 1) * 8]
                    ops.append(mm(o, qT[:, hh, qt * 128:(qt + 1) * 128], khi[:, hh, :], True, False))
                    ops.append(mm(o, qT[:, hh, qt * 128:(qt + 1) * 128], klo[:, hh, :], False, True))
                S.op("pe", ops, reads=(qT, khi, klo), writes=(ps,))
                S.op("dve", (lambda ps: (lambda e: e.tensor_tensor(Gm[:, :, :], ps[:, 0:128].rearrange("p (q n) -> p q n", n=8), C.negm[:, :, :], ALU.add)))(ps),
                     reads=(ps, C.negm), writes=(Gm,))
                for qt in range(16):
                    S.op("dve", (lambda qt: (lambda e: e.max(m8[:, :], Gm[:, qt, :])))(qt), reads=(Gm,), writes=(m8,))
                    S.op("dve", (lambda qt: (lambda e: e.tensor_scalar(sel[:, qt, :], Gm[:, qt, :], m8[:, 2:3], 1.0, ALU.is_ge, ALU.subtract)))(qt),
                         reads=(Gm, m8), pwrites=(sel,))
                for q4 in range(4):
                    ps2 = C.psum()
                    S.op("pe", [(lambda o, i_: (lambda e: e.transpose(o, i_, C.ident[:, :])))(
                        ps2[0:8, i * 128:(i + 1) * 128], sel[:, q4 * 4 + i, :])
                                for i in range(4)], reads=(sel, C.ident), writes=(ps2,))
                    S.op("act", cp("act", biasT[0:8, hh, q4 * 512:(q4 + 1) * 512], ps2[0:8, :]), reads=(ps2,), pwrites=(biasT,))
            O_acc, S_acc = C.pbank[6], C.pbank[7]
            for hh in range(4):
                for QB in range(8):
                    qs = QB * 256
                    nkt = 2 * QB + 2
                    pend = []

                    def score(kt, hh=hh, QB=QB, qs=qs):
                        sc = C.psum()
                        ops = [mm(sc[:, 0:256], kT[:, hh, kt * 128:(kt + 1) * 128], qT[:, hh, qs:qs + 256], True, False)]
                        rd = [kT, qT]
                        if kt < 2 * QB:
                            ops.append(mm(sc[:, 0:256], C.oh[0:8, kt // 2, :], biasT[0:8, hh, qs:qs + 256], False, True))
                            rd += [C.oh, biasT]
                        else:
                            ops.append(mm(sc[:, 0:256], C.identb[:, :], C.cm[:, kt - 2 * QB, :], False, True))
                            rd += [C.identb, C.cm]
                        S.op("pe", ops, reads=tuple(rd), writes=(sc,))
                        pt = PT[kt % 3]
                        S.op("act", (lambda sc, pt: (lambda e: e.activation(pt[:, :], sc[:, 0:256], AF.Exp, scale=SCALE)))(sc, pt),
                             reads=(sc,), writes=(pt,))
                        return pt

                    def accum(kt, pt, hh=hh, nkt=nkt):
                        S.op("pe", [mm(O_acc[:, 0:256], Vt[:, kt, hh, :], pt[:, :], kt == 0, kt == nkt - 1),
                                    mm(S_acc[:, 0:256], C.ones[:, :], pt[:, :], kt == 0, kt == nkt - 1)],
                             reads=(Vt, pt, C.ones), writes=(), pwrites=(O_acc, S_acc))
                    for kt in range(nkt):
                        pend.append((kt, score(kt)))
                        if len(pend) > 1:
                            accum(*pend.pop(0))
                    while pend:
                        accum(*pend.pop(0))
                    S.op("dve", lambda e: e.reciprocal(rec[:, :], S_acc[:, 0:256]), reads=(S_acc,), writes=(rec,))
                    S.op("dve", (lambda hh, qs: (lambda e: e.tensor_tensor(ao[:, hh, qs:qs + 256], O_acc[:, 0:256], rec[:, :], ALU.mult)))(hh, qs),
                         reads=(O_acc, rec), pwrites=(ao,))
            for hh in range(4):
                ld(yxa_in.t[hh * 128:(hh + 1) * 128, :], ao[:, hh, :], (ao,), (yxa_in,), pw=True)
            S.barrier()
    C.coll_pair(yxa_in, C.yx_out[l][1], pop=True)


CONST_SPECS = {
    "ident": ([128, 128], F32), "identb": ([128, 128], BF16), "permT": ([32, 32], BF16),
    "oh": ([8, 8, 128], BF16), "cm": ([128, 2, 256], BF16), "negm": ([128, 16, 8], F32),
    "rot_cos": ([32, L], F32), "rot_sin": ([32, L], F32),
}
SSM_SPECS = {
    "ssm_ar": [DEPTH, 128, 16], "ssm_ai": [DEPTH, 128, 16], "ssm_ldt": [DEPTH, 128, 16],
    "ssm_ct_re": [DEPTH, 128, 16, 16], "ssm_ct_im": [DEPTH, 128, 16, 16],
    "ssm_bt_re": [DEPTH, 128, 16, 16], "ssm_bt_im": [DEPTH, 128, 16, 16], "ssm_dd": [DEPTH, 128, 4],
}


def build_program(ncores=8, mode="full"):
    nc = bass.Bass("TRN2", target_bir_lowering=False)
    NW = ncores
    pairs = [[2 * i, 2 * i + 1] for i in range(ncores // 2)]
    with ExitStack() as es:
        C = Ctx(nc, es)
        S = C.S
        ext = lambda name, shape, dt: TB(nc.dram_tensor(name, shape, dt, kind="ExternalInput").ap(), S.buf(name))
        C.xT = ext("xT", [D, NT], F32)
        C.pT = ext("pT", [DEPTH, PLE, NT], F32)
        C.gains_d = ext("gains", [128, (DEPTH * 4 + 1) * KC], F32)
        C.ssm_in = {k: ext(k, shp, F32) for k, shp in SSM_SPECS.items()}
        C.consts = {k: ext(k, shp, dt) for k, (shp, dt) in CONST_SPECS.items()}
        C.wf = [{n: WMat(C, f"{l}_{n}", K, M, NW) for n, K, M in WSPEC} for l in range(DEPTH)]
        SEGA_W = ["ffn1_w_gate", "ffn1_w_up", "ffn1_w_down", "w_in"]
        wsh = {n: ext("w_" + n, [DEPTH if mode == "full" else 1, C.wf[0][n].per_rank], F32) for n, K, M in WSPEC
               if mode == "full" or (mode == "sega" and n in SEGA_W)}
        if mode == "segb":
            C.zin = ext("zin", [8192, NT], BF16)
        C.oT = TB(nc.dram_tensor("oT", [D, NT], F32, kind="ExternalOutput").ap(), S.buf("oT"))
        C.zx_in = [[C.dram(f"zx_in{l}_{i}", [1024, NT], BF16) for i in range(4)] for l in range(DEPTH)]
        C.zx_out = [[C.dram(f"zx_out{l}_{i}", [2048, NT], BF16) for i in range(4)] for l in range(DEPTH)]
        C.yx_in = [[C.dram(f"yx_in{l}_{i}", [512, L], BF16) for i in range(2)] for l in range(DEPTH)]
        C.yx_out = [[C.dram(f"yx_out{l}_{i}", [1024, L], BF16) for i in range(2)] for l in range(DEPTH)]
        C.sg_d = [C.dram(f"sg_d{l}", [4096, NT], BF16) for l in range(DEPTH)]
        C.hsp = C.dram("hsp", [D, NT], F32)
        C.zd = C.dram("zd", [8, 512, 256], BF16)
        C.yd = C.dram("yd", [8, 32, 16, 256], BF16)
        C.w1_d = [C.dram(f"w1_d{l}", [128, 32, 128], BF16) for l in range(DEPTH)]
        C.w2re_d = [C.dram(f"w2re_d{l}", [128, 32, 64], BF16) for l in range(DEPTH)]
        C.w2im_d = [C.dram(f"w2im_d{l}", [128, 32, 64], BF16) for l in range(DEPTH)]
        C.w4re_d = [C.dram(f"w4re_d{l}", [128, 16, 128], BF16) for l in range(DEPTH)]
        C.w4im_d = [C.dram(f"w4im_d{l}", [128, 16, 128], BF16) for l in range(DEPTH)]
        C.lam8_d = [C.dram(f"lam8_d{l}", [128, 16, 2], F32) for l in range(DEPTH)]
        C.ones = C.sb([128, 128], BF16, "ones")
        C.eps_ap = C.sb([128, 1], F32, "eps")
        C.gains = C.sb([128, (DEPTH * 4 + 1) * KC], F32, "gains")

        mid = ["ssm_w_glu", "w_branch_ssm", "w_branch_attn", "w_out"]
        tail = ["ple_w_gate", "ple_w_up"]

        def ffn_chunks(l, pre):
            out = []
            g, u, d = C.wf[l][pre + "_w_gate"], C.wf[l][pre + "_w_up"], C.wf[l][pre + "_w_down"]
            for i in range(len(g.chunks)):
                out += [(l, pre + "_w_gate", i), (l, pre + "_w_up", i), (l, pre + "_w_down", i)]
            return out

        quads = [[0, 1, 2, 3], [4, 5, 6, 7]]
        xpairs = [[0, 4], [1, 5], [2, 6], [3, 7]]

        def coll(groups, src, dst):
            S.dma("pool", lambda e: e.collective_compute("AllGather", ALU.bypass, replica_groups=groups,
                                                         ins=[src.t], outs=[dst.t]),
                  reads=(src,), writes=(dst,), inc=1)

        def chunk_list(l, names):
            return [(l, n, i) for n in names for i in range(len(C.wf[l][n].chunks))]

        def bounce(l, n, i):
            w = C.wf[l][n]
            r0, nr, c0, ncw = w.chunks[i]
            src = wsh[n].t[l, w.off[i]:w.off[i] + (nr // NW) * ncw].rearrange("(r c) -> r c", c=ncw)
            S.dma("pool", (lambda o, s_: (lambda e: e.dma_start(out=o, in_=s_)))(w.bounce[i].t, src),
                  reads=(wsh[n],), writes=(w.bounce[i],))

        def gather(chunks):
            LOOK = 12
            for k in range(min(LOOK, len(chunks))):
                bounce(*chunks[k])
            for k, (l, n, i) in enumerate(chunks):
                w = C.wf[l][n]
                if NW == 8:
                    coll(quads, w.bounce[i], w.half[i])
                    coll(xpairs, w.half[i], w.full[i])
                else:
                    coll([list(range(NW))], w.bounce[i], w.full[i])
                if k + LOOK < len(chunks):
                    bounce(*chunks[k + LOOK])

        def coll_pair(src, dst, pop=True):
            S.dma("pool", lambda e: e.collective_compute("AllGather", ALU.bypass, replica_groups=pairs,
                                                         ins=[src.t], outs=[dst.t]),
                  reads=(src,), writes=(dst,), inc=1)
            if pop and C.gbatches:
                gather(C.gbatches.pop(0))
        C.coll_pair = coll_pair

        def load_consts(names, scope):
            for k in names:
                shp, dt = CONST_SPECS[k]
                t = C.sb(shp, dt, k, scope)
                idx = tuple(slice(None) for _ in shp)
                S.dma("sp", (lambda o, i: (lambda e: e.dma_start(out=o, in_=i)))(t.t[idx], C.consts[k].t),
                      reads=(C.consts[k],), writes=(t,))
                setattr(C, k, t)

        def body():
            def cpart(l):
                f2 = ffn_chunks(l, "ffn2")
                return chunk_list(l, mid) + f2[:9], f2[9:] + chunk_list(l, tail)
            a0 = ffn_chunks(0, "ffn1") + chunk_list(0, ["w_in"])
            a1 = ffn_chunks(1, "ffn1") + chunk_list(1, ["w_in"])
            c0a, c0b = cpart(0)
            c1a, c1b = cpart(1)
            C.gbatches = [c0a, c0b + a1, c1a, c1b]
            S.op("dve", lambda e: e.memset(C.ones[:, :], 1.0), writes=(C.ones,))
            S.op("dve", lambda e: e.memset(C.eps_ap[:, :], EPS), writes=(C.eps_ap,))
            S.dma("sp", lambda e: e.dma_start(out=C.gains[:, :], in_=C.gains_d.t), reads=(C.gains_d,), writes=(C.gains,))
            gather(a0)
            with ExitStack() as p0:
                load_consts(["ident"], p0)
                for l in range(DEPTH):
                    ssm_prep(C, l)
                S.barrier()
            for l in range(DEPTH):
                if l == 0:
                    ph = ExitStack()
                    T = TokTiles(C, ph)
                    C.wphase_begin()
                    S.dma("sp", (lambda h_: (lambda e: e.dma_start(out=h_[:, :, :], in_=C.xT.t.rearrange("(k p) t -> p k t", p=128))))(T.h),
                          reads=(C.xT,), writes=(T.h,))
                seg_a(C, T, l)
                S.barrier()
                ph.close()
                with ExitStack() as pb:
                    load_consts(["ident", "identb", "permT", "oh", "cm", "negm"], pb)
                    seg_b(C, l)
                ph = ExitStack()
                T = TokTiles(C, ph)
                C.wphase_begin()
                seg_c(C, T, l, last=(l == DEPTH - 1))
            S.barrier()
            ph.close()

        def plan_ffn(l, pre):
            out = []
            for ch in range(DFF // 512):
                out += [(pre + "_w_gate", 0, KC, ch * 512, 512), (pre + "_w_up", 0, KC, ch * 512, 512),
                        (pre + "_w_down", ch * 512, 4, 0, D)]
            return [(l,) + b for b in out]

        def plan_a(l):
            return plan_ffn(l, "ffn1") + [(l, "w_in", 0, KC, cb * 512, 512) for cb in range(16)]

        def plan_c(l):
            out = []
            for hf in range(2):
                out += [("ssm_w_glu", 0, 8, 0, 1024), ("ssm_w_glu", 0, 8, 1024, 1024)]
                for half in range(2):
                    out += [("w_branch_ssm", 0, 8, half * 1024, 1024), ("w_branch_attn", 0, 8, half * 1024, 1024)]
                out += [("w_out", 0, KC, cb * 512, 512) for cb in range(4)]
            out = [(l,) + b for b in out] + plan_ffn(l, "ffn2")
            for cb in range(4):
                out += [(l, "ple_w_gate", 0, KC, cb * 512, 512), (l, "ple_w_up", 0, 2, cb * 512, 512)]
            return out
        def body_prep():
            with ExitStack() as p0:
                load_consts(["ident"], p0)
                ssm_prep(C, 0)
                S.barrier()
            toks = []
            for nm, src in (("o_w1", C.w1_d[0]), ("o_w2re", C.w2re_d[0]), ("o_w2im", C.w2im_d[0]),
                            ("o_w4re", C.w4re_d[0]), ("o_w4im", C.w4im_d[0]), ("o_lam8", C.lam8_d[0])):
                shp = [int(x) for x in src.t.shape]
                dt = F32 if nm == "o_lam8" else BF16
                o = nc.dram_tensor(nm, shp, dt, kind="ExternalOutput").ap()
                toks.append(S.dma("sp", (lambda o, s_: (lambda e: e.dma_start(out=o, in_=s_)))(o, src.t), reads=(src,)))
            S.wait_all("sp", toks)

        def body_segb():
            S.op("dve", lambda e: e.memset(C.ones[:, :], 1.0), writes=(C.ones,))
            with ExitStack() as p0:
                load_consts(["ident"], p0)
                ssm_prep(C, 0)
                S.barrier()
            for jj in range(2):
                for part in range(4):
                    S.dma("sp", (lambda jj, part: (lambda e: e.dma_start(
                        out=C.zx_out[0][part].t[jj * 1024:(jj + 1) * 1024, :],
                        in_=C.zin.t[jj * 4096 + part * 1024:jj * 4096 + (part + 1) * 1024, :])))(jj, part),
                        reads=(C.zin,), pwrites=(C.zx_out[0][part],))
            C.coll_pair = lambda a, b, pop=True: None
            with ExitStack() as pb:
                load_consts(["ident", "identb", "permT", "oh", "cm", "negm"], pb)
                seg_b(C, 0)
            o = nc.dram_tensor("o_yx", [1024, L], BF16, kind="ExternalOutput").ap()
            tk = S.dma("sp", lambda e: e.dma_start(out=o[0:512, :], in_=C.yx_in[0][0].t), reads=(C.yx_in[0][0],))
            tk2 = S.dma("sp", lambda e: e.dma_start(out=o[512:1024, :], in_=C.yx_in[0][1].t), reads=(C.yx_in[0][1],))
            S.wait_all("sp", [tk, tk2])

        def body_sega():
            S.op("dve", lambda e: e.memset(C.ones[:, :], 1.0), writes=(C.ones,))
            S.op("dve", lambda e: e.memset(C.eps_ap[:, :], EPS), writes=(C.eps_ap,))
            S.dma("sp", lambda e: e.dma_start(out=C.gains[:, :], in_=C.gains_d.t), reads=(C.gains_d,), writes=(C.gains,))
            C.gbatches = []
            gather(ffn_chunks(0, "ffn1") + chunk_list(0, ["w_in"]))
            ph = ExitStack()
            T = TokTiles(C, ph)
            C.wphase_begin()
            S.dma("sp", (lambda h_: (lambda e: e.dma_start(out=h_[:, :, :], in_=C.xT.t.rearrange("(k p) t -> p k t", p=128))))(T.h),
                  reads=(C.xT,), writes=(T.h,))
            seg_a(C, T, 0)
            S.barrier()
            ph.close()
            o1 = nc.dram_tensor("o_zx", [8192, NT], BF16, kind="ExternalOutput").ap()
            o2 = nc.dram_tensor("o_h", [D, NT], F32, kind="ExternalOutput").ap()
            tks = [S.dma("sp", (lambda i: (lambda e: e.dma_start(out=o1[i * 2048:(i + 1) * 2048, :], in_=C.zx_out[0][i].t)))(i),
                         reads=(C.zx_out[0][i],)) for i in range(4)]
            tks.append(S.dma("sp", lambda e: e.dma_start(out=o2, in_=C.hsp.t), reads=(C.hsp,)))
            S.wait_all("sp", tks)

        if mode == "sega":
            for (l, n, k0, kc, c0, ncols) in plan_a(0):
                C.wplan.append((C.wf[l][n], k0, kc, c0, ncols))
            C.wphase.append(0)
            body_sega()
            S.emit()
            return nc
        if mode == "prep":
            body_prep()
            S.emit()
            return nc
        if mode == "segb":
            body_segb()
            S.emit()
            return nc
        phases = [plan_a(0), plan_c(0) + plan_a(1), plan_c(1)]
        for ph_ in phases:
            C.wphase.append(len(C.wplan))
            for (l, n, k0, kc, c0, ncols) in ph_:
                C.wplan.append((C.wf[l][n], k0, kc, c0, ncols))
        body()
        S.emit()
    return nc


def _consts():
    bf = ml_dtypes.bfloat16
    c = {}
    c["ident"] = np.eye(128, dtype=np.float32)
    c["identb"] = np.eye(128, dtype=np.float32).astype(bf)
    pm = np.zeros((32, 32), np.float32)
    for d in range(16):
        pm[d + 16, d] = -1.0
        pm[d, d + 16] = 1.0
    c["permT"] = pm.astype(bf)
    oh = np.zeros((8, 8, 128), np.float32)
    for n in range(8):
        oh[n, n, :] = BIG
    c["oh"] = oh.astype(bf)
    cm = np.zeros((128, 2, 256), np.float32)
    k = np.arange(128)[:, None]
    q = np.arange(256)[None, :]
    cm[:, 0, :] = np.where(k > q, -BIG, 0.0)
    cm[:, 1, :] = np.where(k + 128 > q, -BIG, 0.0)
    c["cm"] = cm.astype(bf)
    negm = np.zeros((128, 16, 8), np.float32)
    for qt in range(16):
        for n in range(8):
            if n >= qt // 2:
                negm[:, qt, n] = -1.0e30
    c["negm"] = negm
    pos = np.arange(L, dtype=np.float32)
    inv_freq = (1.0 / (np.float32(500000.0) ** (np.arange(0, 32, 2, dtype=np.float32) / np.float32(32)))).astype(np.float32)
    ang = pos[None, :] * inv_freq[:, None]
    c["rot_cos"] = np.concatenate([np.cos(ang), np.cos(ang)], 0).astype(np.float32)
    c["rot_sin"] = np.concatenate([np.sin(ang), np.sin(ang)], 0).astype(np.float32)
    return c


def make_in_maps(inputs, ncores=8, batch_of_core=None):
    NW = ncores
    consts = _consts()
    x = inputs["x"]
    p = inputs["p"]
    gl = [inputs[n][l] for l in range(DEPTH) for n in NORMS] + [inputs["final_norm"]]
    gains = np.ascontiguousarray(np.concatenate([g.reshape(KC, 128).T for g in gl], axis=1).astype(np.float32))
    in_maps = []
    for r in range(ncores):
        b = (r // 2) if batch_of_core is None else batch_of_core[r]
        j = r % 2
        m = {}
        m["xT"] = np.ascontiguousarray(x[b, j * NT:(j + 1) * NT, :].T)
        m["pT"] = np.ascontiguousarray(np.transpose(p[:, b, j * NT:(j + 1) * NT, :], (0, 2, 1)))
        m["gains"] = gains
        gs = slice(32 * j, 32 * j + 32)

        def nlay(a):
            a = a.reshape((DEPTH, 2, 16, 64) + a.shape[3:])
            a = np.moveaxis(a, 3, 2)
            return np.ascontiguousarray(a.reshape((DEPTH, 128, 16) + a.shape[4:]))
        m["ssm_ar"] = nlay(inputs["ssm_a_re"][:, gs])
        m["ssm_ai"] = nlay(inputs["ssm_a_im"][:, gs])
        m["ssm_ldt"] = nlay(np.broadcast_to(inputs["ssm_log_dt"][:, gs, None], (DEPTH, 32, 64)))
        m["ssm_bt_re"] = nlay(inputs["ssm_b_re"][:, gs])
        m["ssm_bt_im"] = nlay(inputs["ssm_b_im"][:, gs])
        m["ssm_ct_re"] = nlay(np.transpose(inputs["ssm_c_re"][:, gs], (0, 1, 3, 2)))
        m["ssm_ct_im"] = nlay(np.transpose(inputs["ssm_c_im"][:, gs], (0, 1, 3, 2)))
        m["ssm_dd"] = np.ascontiguousarray(np.transpose(
            inputs["ssm_d"][:, 512 * j:512 * j + 512].reshape(DEPTH, 4, 128), (0, 2, 1)))
        for name, K, M in WSPEC:
            w = inputs[name]
            parts = []
            for (r0, nr, c0, ncw) in wchunks(K, M):
                q = nr // NW
                parts.append(w[:, r0 + r * q:r0 + (r + 1) * q, c0:c0 + ncw].reshape(DEPTH, -1))
            m["w_" + name] = np.ascontiguousarray(np.concatenate(parts, axis=1))
        m.update(consts)
        in_maps.append(m)
    return in_maps


_NC_CACHE = {}


def kernel(**inputs):
    inputs = {k: np.asarray(v) for k, v in inputs.items()}
    if 8 not in _NC_CACHE:
        _NC_CACHE[8] = build_program(8)
    nc = _NC_CACHE[8]
    in_maps = make_in_maps(inputs, 8)
    res = run_bass_kernel_spmd(nc, in_maps, core_ids=list(range(8)))
    out = np.empty((4, L, D), np.float32)
    for r in range(8):
        out[r // 2, (r % 2) * NT:(r % 2 + 1) * NT, :] = res.results[r]["oT"].T
    return out
```

```python
import numpy as np
from contextlib import ExitStack
import ml_dtypes
import concourse.bass as bass
import concourse.mybir as mybir
from concourse.bass_utils import run_bass_kernel_spmd

F32 = mybir.dt.float32
BF16 = mybir.dt.bfloat16
I32 = mybir.dt.int32
AF = mybir.ActivationFunctionType
ALU = mybir.AluOpType
AX = mybir.AxisListType

D = 2048
DFF = 5632
NT = 1024
L = 2048
KC = D // 128
EPS = 1e-6
DEPTH = 2
PLE = 256
BIG = 1.0e4
SCALE = 1.0 / float(np.sqrt(128.0))
ENGS = ["pe", "act", "dve", "pool", "sp"]
TWO_PI = float(2.0 * np.pi)

WSPEC = [
    ("ffn1_w_gate", D, DFF), ("ffn1_w_up", D, DFF), ("ffn1_w_down", DFF, D),
    ("w_in", D, 8192), ("ssm_w_glu", 1024, 2048), ("w_branch_ssm", 1024, 2048),
    ("w_branch_attn", 1024, 2048), ("w_out", D, D),
    ("ffn2_w_gate", D, DFF), ("ffn2_w_up", D, DFF), ("ffn2_w_down", DFF, D),
    ("ple_w_up", PLE, D), ("ple_w_gate", D, D),
]
NORMS = ["ffn1_norm", "mix_norm", "ffn2_norm", "ple_norm"]


class Buf:
    __slots__ = ("name", "w", "r", "pw")

    def __init__(self, name):
        self.name = name
        self.w = None
        self.r = []
        self.pw = []


class TB:
    def __init__(self, t, b):
        self.t = t
        self.b = b

    def __getitem__(self, idx):
        return self.t[idx]


def _b(x):
    return x.b if isinstance(x, TB) else x


class Sched:
    def __init__(self, nc, es, ndma=16):
        self.nc = nc
        self.es = es
        self.prog = {e: [] for e in ENGS}
        self.cnt = {e: 0 for e in ENGS}
        self.known = {e: {} for e in ENGS}
        self.sems = {}
        self.ndma = ndma
        self.dma_val = {}
        self.dma_rr = {e: 0 for e in ENGS}
        self.dry = False
        self.nbuf = 0

    def buf(self, name=None):
        self.nbuf += 1
        return Buf(name or f"b{self.nbuf}")

    def sem(self, key):
        if key not in self.sems:
            self.sems[key] = self.es.enter_context(self.nc.semaphore(key))
        return self.sems[key]

    def _deps(self, eng, reads, writes, pwrites=()):
        toks = []
        for b in reads:
            b = _b(b)
            if b.w is not None:
                toks.append(b.w)
            toks.extend(b.pw)
        for b in writes:
            b = _b(b)
            if b.w is not None:
                toks.append(b.w)
            toks.extend(b.r)
            toks.extend(b.pw)
        for b in pwrites:
            b = _b(b)
            if b.w is not None:
                toks.append(b.w)
            toks.extend(b.r)
        best = {}
        for k, v in toks:
            if v > best.get(k, 0):
                best[k] = v
        waits = []
        kn = self.known[eng]
        for k, v in best.items():
            if eng == "pe" and k == "c_pe":
                continue
            if kn.get(k, 0) >= v:
                continue
            kn[k] = v
            waits.append((k, v))
        return waits

    @staticmethod
    def _compact(lst):
        best = {}
        for k, v in lst:
            if v > best.get(k, 0):
                best[k] = v
        return list(best.items())

    def _commit(self, tok, reads, writes, pwrites=()):
        for b in reads:
            b = _b(b)
            b.r.append(tok)
            if len(b.r) > 48:
                b.r = self._compact(b.r)
        for b in writes:
            b = _b(b)
            b.w = tok
            b.r = []
            b.pw = []
        for b in pwrites:
            b = _b(b)
            b.pw.append(tok)
            if len(b.pw) > 48:
                b.pw = self._compact(b.pw)

    def op(self, eng, fns, reads=(), writes=(), pwrites=()):
        if self.dry:
            return None
        if not isinstance(fns, (list, tuple)):
            fns = [fns]
        waits = self._deps(eng, reads, writes, pwrites)
        self.cnt[eng] += 1
        tok = ("c_" + eng, self.cnt[eng])
        self.prog[eng].append((waits, list(fns), (tok[0], 1)))
        self._commit(tok, reads, writes, pwrites)
        return tok

    def dma(self, eng, fn, reads=(), writes=(), pwrites=(), inc=16):
        if self.dry:
            return None
        i = self.dma_rr[eng]
        self.dma_rr[eng] = (i + 1) % self.ndma
        key = f"d_{eng}_{i}" if inc == 16 else f"x_{eng}_{i}"
        prev = self.dma_val.get(key, 0)
        waits = self._deps(eng, reads, writes, pwrites)
        if prev > 0 and self.known[eng].get(key, 0) < prev:
            self.known[eng][key] = prev
            waits.append((key, prev))
        val = prev + inc
        self.dma_val[key] = val
        tok = (key, val)
        self.prog[eng].append((waits, [fn], (key, inc)))
        self._commit(tok, reads, writes, pwrites)
        return tok

    def barrier(self):
        if self.dry:
            return
        toks = [("c_" + e, self.cnt[e]) for e in ENGS if self.cnt[e] > 0 and e != "pool"]
        toks += [(k, v) for k, v in self.dma_val.items() if "_pool_" not in k]
        for e in ENGS:
            if e != "pool":
                self.wait_all(e, toks)

    def wait_all(self, eng, toks):
        if self.dry:
            return
        waits = []
        for k, v in toks:
            if self.known[eng].get(k, 0) < v:
                self.known[eng][k] = v
                waits.append((k, v))
        if waits:
            self.prog[eng].append((waits, [], None))

    def emit(self):
        nc = self.nc
        for e in ENGS:
            for waits, fns, inc in self.prog[e]:
                for k, v in waits:
                    self.sem(k)
                if inc is not None:
                    self.sem(inc[0])
        with nc.Block() as block:
            def runner(ename):
                def run(eng):
                    for waits, fns, inc in self.prog[ename]:
                        for k, v in waits:
                            eng.wait_ge(self.sems[k], v)
                        ins = None
                        for f in fns:
                            ins = f(eng)
                        if inc is not None:
                            ins.then_inc(self.sems[inc[0]], inc[1])
                return run
            block.tensor(runner("pe"))
            block.scalar(runner("act"))
            block.vector(runner("dve"))
            block.gpsimd(runner("pool"))
            block.sync(runner("sp"))


class Ctx:
    def __init__(self, nc, es):
        self.nc = nc
        self.es = es
        self.S = Sched(nc, es)
        self.nname = 0
        self.pbank = []
        for i in range(8):
            t = es.enter_context(nc.psum_tensor(f"ps{i}", [128, 512], F32))
            self.pbank.append(TB(t, self.S.buf(f"ps{i}")))
        self.prr = 0
        self.NSLOT = 4
        self.wslots = None
        self.wplan = []
        self.wissued = 0
        self.wnext = 0
        self.wrel_count = 0
        self.wphase = []
        self.wlimit = 0

    def jval(self, e, mult):
        if not hasattr(self, "_jv"):
            self._jv = {}
        key = (id(e), mult)
        if key not in self._jv:
            self._jv[key] = e.snap((e.partition_id() % 2) * mult)
        return self._jv[key]

    ARENA_ELEMS = 106240

    def sb(self, shape, dtype, name=None, es=None):
        if not hasattr(self, "arena"):
            self.arena = self.es.enter_context(self.nc.sbuf_tensor("arena", [128, self.ARENA_ELEMS], BF16))
            self.aoff = 0
            self.scopes = set()
        if es is not None and id(es) not in self.scopes:
            self.scopes.add(id(es))
            mark = self.aoff

            def rel(mark=mark, key=id(es)):
                self.aoff = mark
                self.scopes.discard(key)
            es.callback(rel)
        self.nname += 1
        nm = f"{name or 't'}_{self.nname}"
        esz = 2 if dtype == BF16 else 4
        n = 1
        for d in shape[1:]:
            n *= d
        nbf = (n * esz + 1) // 2
        nbf = (nbf + 15) // 16 * 16
        assert self.aoff + nbf <= self.ARENA_ELEMS, (nm, self.aoff, nbf)
        ap = self.arena[0:shape[0], self.aoff:self.aoff + (n * esz) // 2]
        self.aoff += nbf
        self.apeak = max(getattr(self, "apeak", 0), self.aoff)
        if dtype != BF16:
            ap = ap.bitcast(dtype)
        if len(shape) > 2:
            names = " ".join(f"d{i}" for i in range(1, len(shape)))
            ap = ap.rearrange(f"p ({names}) -> p {names}", **{f"d{i}": shape[i] for i in range(1, len(shape) - 1)})
        return TB(ap, self.S.buf(nm))

    def dram(self, name, shape, dtype):
        return TB(self.nc.dram_tensor(name, shape, dtype).ap(), self.S.buf(name))

    def psum(self):
        b = self.pbank[self.prr]
        self.prr = (self.prr + 1) % 6
        return b

    def wphase_begin(self):
        starts = self.wphase
        cur = self.wnext
        nxt = [s for s in starts if s > cur]
        self.wlimit = nxt[0] if nxt else len(self.wplan)
        assert self.wissued == self.wnext, (self.wissued, self.wnext)
        self.wbase = self.wnext
        for _ in range(self.NSLOT):
            self._wissue()

    def _wissue(self):
        S = self.S
        j = self.wissued
        if j >= self.wlimit:
            return
        w2, k2, kc2, c2, n2 = self.wplan[j]
        ci, lr, lc = w2.find(k2, kc2, c2, n2)
        ch = w2.full[ci]
        sl = self.wslots[(j - self.wbase) % self.NSLOT]
        dst = sl.t[:, 0:kc2 * n2].rearrange("p (k c) -> p k c", k=kc2)
        src = ch.t[lr:lr + kc2 * 128, lc:lc + n2].rearrange("(k p) c -> p k c", p=128)
        S.dma("sp", (lambda d, s: (lambda e: e.dma_start(out=d, in_=s)))(dst, src),
              reads=(ch,), writes=(sl,))
        self.wissued += 1

    def wget(self, w, k0, kc, c0, ncols):
        S = self.S
        idx = self.wnext
        self.wnext += 1
        assert self.wplan[idx][1:] == (k0, kc, c0, ncols) and self.wplan[idx][0] is w, (idx, self.wplan[idx][1:], (k0, kc, c0, ncols))
        assert idx < self.wissued, (idx, self.wissued)
        sl = self.wslots[(idx - self.wbase) % self.NSLOT]
        return TB(sl.t[:, 0:kc * ncols].rearrange("p (k c) -> p k c", k=kc), sl.b)

    def wrel(self, n=1):
        if self.S.dry:
            return
        for _ in range(n):
            self._wissue()


def wchunks(K, M):
    if K > 2048:
        return [(r0, min(1024, K - r0), 0, M) for r0 in range(0, K, 1024)]
    if K == 2048:
        return [(0, K, c0, min(1024, M - c0)) for c0 in range(0, M, 1024)]
    return [(0, K, 0, M)]


class WMat:
    def __init__(self, C, tag, K, M, nw):
        self.K, self.M, self.nw = K, M, nw
        self.chunks = wchunks(K, M)
        self.bounce, self.half, self.full, self.off = [], [], [], []
        o = 0
        for i, (r0, nr, c0, ncw) in enumerate(self.chunks):
            self.off.append(o)
            o += (nr // nw) * ncw
            self.bounce.append(C.dram(f"wb_{tag}_{i}", [nr // nw, ncw], BF16))
            self.half.append(C.dram(f"wh_{tag}_{i}", [nr // 2, ncw], BF16) if nw == 8 else None)
            self.full.append(C.dram(f"wf_{tag}_{i}", [nr, ncw], BF16))
        self.per_rank = o

    def find(self, k0, kc, c0, ncols):
        for i, (r0, nr, cc0, ncw) in enumerate(self.chunks):
            if r0 <= k0 and k0 + kc * 128 <= r0 + nr and cc0 <= c0 and c0 + ncols <= cc0 + ncw:
                return i, k0 - r0, c0 - cc0
        raise AssertionError((k0, kc, c0, ncols))


def mm(out, lhsT, rhs, start, stop):
    return lambda e: e.matmul(out, lhsT, rhs, start=start, stop=stop)


def tsl(th):
    return slice(th * 512, th * 512 + 512)


def emit_rmsnorm(C, T, h, gain_col, out):
    S = C.S
    for th in range(NT // 512):
        ts = tsl(th)
        pt = C.psum()
        for kc in range(KC):
            sq = T.sq[kc % 2]
            S.op("act", (lambda o, i: (lambda e: e.activation(o, i, AF.Square)))(sq[:, :], h[:, kc, ts]),
                 reads=(h,), writes=(sq,))
            S.op("pe", mm(pt[:, :], C.ones[:, :], sq[:, :], kc == 0, kc == KC - 1),
                 reads=(sq, C.ones), writes=(pt,))
        rstd = T.rstd
        S.op("act", (lambda p_: (lambda e: e.activation(rstd[:, :], p_, AF.Sqrt, bias=C.eps_ap[:, 0:1], scale=1.0 / D)))(pt[:, :]),
             reads=(pt, C.eps_ap), writes=(rstd,))
        S.op("dve", lambda e: e.reciprocal(rstd[:, :], rstd[:, :]), reads=(rstd,), writes=(rstd,))
        for kc in range(KC):
            S.op("dve", (lambda o, i, g: (lambda e: e.scalar_tensor_tensor(o, i, g, rstd[:, :], ALU.mult, ALU.mult)))(
                out[:, kc, ts], h[:, kc, ts], C.gains[:, gain_col + kc:gain_col + kc + 1]),
                reads=(h, rstd, C.gains), pwrites=(out,))


def emit_ffn(C, T, h, xn, wg, wu, wd):
    S = C.S
    nchunk = DFF // 512
    for ch in range(nchunk):
        gw = C.wget(wg, 0, KC, ch * 512, 512)
        uw = C.wget(wu, 0, KC, ch * 512, 512)
        dw = C.wget(wd, ch * 512, 4, 0, D)
        if S.dry:
            continue
        hid = T.hid[ch % 2]
        for mi in range(4):
            for th in range(NT // 512):
                ts = tsl(th)
                pg = C.psum()
                S.op("pe", [mm(pg[:, :], gw[:, kc, mi * 128:mi * 128 + 128], xn[:, kc, ts], kc == 0, kc == KC - 1)
                            for kc in range(KC)], reads=(gw, xn), writes=(pg,))
                pu = C.psum()
                S.op("pe", [mm(pu[:, :], uw[:, kc, mi * 128:mi * 128 + 128], xn[:, kc, ts], kc == 0, kc == KC - 1)
                            for kc in range(KC)], reads=(uw, xn), writes=(pu,))
                sg = T.sg[(mi * 2 + th) % 2]
                S.op("act", (lambda o, i: (lambda e: e.activation(o, i, AF.Silu)))(sg[:, :], pg[:, :]),
                     reads=(pg,), writes=(sg,))
                S.op("dve", (lambda o, a, b: (lambda e: e.tensor_tensor(o, a, b, ALU.mult)))(
                    hid[:, mi, ts], pu[:, :], sg[:, :]), reads=(pu, sg), pwrites=(hid,))
        C.wrel(2)
        for mo in range(KC):
            for th in range(NT // 512):
                ts = tsl(th)
                pd = C.psum()
                S.op("pe", [mm(pd[:, :], dw[:, mi, mo * 128:mo * 128 + 128], hid[:, mi, ts], mi == 0, mi == 3)
                            for mi in range(4)], reads=(dw, hid), writes=(pd,))
                S.op("dve", (lambda o, a: (lambda e: e.scalar_tensor_tensor(o, a, 0.5, o, ALU.mult, ALU.add)))(
                    h[:, mo, ts], pd[:, :]), reads=(pd,), pwrites=(h,))
        C.wrel(1)


def emit_linear(C, w, K, c0, ncols, xs, epi, toks):
    S = C.S
    kc_n = K // 128
    bc = min(8192 // kc_n, ncols)
    for cb in range(ncols // bc):
        wb = C.wget(w, 0, kc_n, c0 + cb * bc, bc)
        if not S.dry:
            for mi in range(bc // 128):
                mo = (cb * bc) // 128 + mi
                for ti, ts in enumerate(toks):
                    ps = C.psum()
                    ops = []
                    rd = [wb]
                    for kc in range(kc_n):
                        ap, tb = xs(kc, ts)
                        ops.append(mm(ps[:, :], wb[:, kc, mi * 128:mi * 128 + 128], ap, kc == 0, kc == kc_n - 1))
                        if tb not in rd:
                            rd.append(tb)
                    S.op("pe", ops, reads=tuple(rd), writes=(ps,))
                    epi(mo, ti, ps)
        C.wrel(1)


class TokTiles:
    def __init__(self, C, es):
        S = C.S
        self.h = C.sb([128, KC, NT], F32, "h", es)
        self.xn = C.sb([128, KC, NT], BF16, "xn", es)
        C.wslots = [C.sb([128, 8192], BF16, f"wslot{i}", es) for i in range(C.NSLOT)]
        self.tarena = C.sb([128, 12288], BF16, "tarena", es)
        a = self.tarena.t
        def view(off, n, pat=None, **kw):
            ap = a[:, off:off + n]
            if pat:
                ap = ap.rearrange(pat, **kw)
            return TB(ap, S.buf())
        self.hid = [view(0, 4096, "p (m t) -> p m t", m=4), view(4096, 4096, "p (m t) -> p m t", m=4)]
        self.zst = [view(8192 + i * 512, 512) for i in range(4)]
        self.ys = view(0, 4096, "p (k t) -> p k t", k=8)
        self.y_a = view(4096, 4096, "p (k t) -> p k t", k=8)
        self.ya = view(8192, 4096, "p (k t) -> p k t", k=8)
        self.pb = view(0, 2048, "p (k t) -> p k t", k=2)
        self.sq = [C.sb([128, 512], BF16, f"sq{i}", es) for i in range(2)]
        self.rstd = C.sb([128, 512], F32, "rstd", es)
        self.sg = [C.sb([128, 512], BF16, f"sg{i}", es) for i in range(2)]
        self.sgs = [C.sb([128, 512], BF16, f"sgs{i}", es) for i in range(2)]
        self.sga = [C.sb([128, 512], BF16, f"sga{i}", es) for i in range(2)]
        self.t1 = [C.sb([128, 512], BF16, f"t1{i}", es) for i in range(2)]
        self.f32tmp = [C.sb([128, 512], F32, f"f32tmp{i}", es) for i in range(1)]
        self.pstage = C.sb([128, 2, 256], F32, "pstage", es)


def cp(eng_kind, o, i):
    if eng_kind == "act":
        return lambda e: e.activation(o, i, AF.Copy)
    return lambda e: e.tensor_copy(o, i)


def seg_a(C, T, l):
    S = C.S
    wf = C.wf[l]
    emit_rmsnorm(C, T, T.h, (l * 4 + 0) * KC, T.xn)
    emit_ffn(C, T, T.h, T.xn, wf["ffn1_w_gate"], wf["ffn1_w_up"], wf["ffn1_w_down"])
    emit_rmsnorm(C, T, T.h, (l * 4 + 1) * KC, T.xn)
    zx_in = C.zx_in[l]
    sg_d = C.sg_d[l]
    cnt = [0]

    def xs(kc, ts):
        return T.xn[:, kc, ts], T.xn

    def epi(mo, ti, ps):
        st = T.zst[cnt[0] % 4]
        eng = "act" if cnt[0] % 2 == 0 else "dve"
        cnt[0] += 1
        if mo < 32:
            S.op(eng, cp(eng, st[:, :], ps[:, :]), reads=(ps,), writes=(st,))
            zp = zx_in[mo // 8]
            S.dma("sp", (lambda o, i: (lambda e: e.dma_start(out=o, in_=i)))(
                zp.t[(mo % 8) * 128:(mo % 8 + 1) * 128, ti * 512:(ti + 1) * 512], st[:, :]),
                reads=(st,), pwrites=(zp,))
        else:
            S.op("act", (lambda o, i: (lambda e: e.activation(o, i, AF.Sigmoid)))(st[:, :], ps[:, :]),
                 reads=(ps,), writes=(st,))
            S.dma("sp", (lambda o, i: (lambda e: e.dma_start(out=o, in_=i)))(
                sg_d.t[(mo - 32) * 128:(mo - 31) * 128, ti * 512:(ti + 1) * 512], st[:, :]),
                reads=(st,), pwrites=(sg_d,))

    emit_linear(C, wf["w_in"], D, 0, 8192, xs, epi, [tsl(0), tsl(1)])
    for part in range(4):
        C.coll_pair(zx_in[part], C.zx_out[l][part], pop=(part == 3))
    S.dma("sp", lambda e: e.dma_start(out=C.hsp.t.rearrange("(k p) t -> p k t", p=128), in_=T.h[:, :, :]),
          reads=(T.h,), writes=(C.hsp,))


def seg_c(C, T, l, last):
    S = C.S
    wf = C.wf[l]
    yxs, yxa = C.yx_out[l]
    sg_d = C.sg_d[l]
    S.dma("sp", lambda e: e.dma_start(out=T.h[:, :, :], in_=C.hsp.t.rearrange("(k p) t -> p k t", p=128)),
          reads=(C.hsp,), writes=(T.h,))
    for hf in range(2):
        tcol = hf * 512
        for jj in range(2):
            for (dst, yx) in ((T.ys, yxs), (T.ya, yxa)):
                r0 = jj * 512

                def ld(e, dst=dst, r0=r0, jj=jj, tcol=tcol, yx=yx):
                    j1024 = C.jval(e, 1024)
                    return e.dma_start(out=dst[:, jj * 4:(jj + 1) * 4, :],
                                       in_=yx.t[r0:r0 + 512, tcol:tcol + 1536][:, bass.ds(j1024, 512)].rearrange("(c p) t -> p c t", p=128))
                S.dma("act", ld, reads=(yx,), pwrites=(dst,))
        for kc in range(8):
            x = T.ys[:, kc, :]
            tf = T.f32tmp[0]
            S.op("act", (lambda x: (lambda e: e.activation(tf[:, :], x, AF.Square)))(x), reads=(T.ys,), writes=(tf,))
            S.op("dve", lambda e: e.tensor_scalar(tf[:, :], tf[:, :], 0.044715, 1.0, ALU.mult, ALU.add),
                 reads=(tf,), writes=(tf,))
            S.op("dve", (lambda x: (lambda e: e.tensor_tensor(tf[:, :], tf[:, :], x, ALU.mult)))(x),
                 reads=(tf, T.ys), writes=(tf,))
            S.op("act", lambda e: e.activation(tf[:, :], tf[:, :], AF.Sigmoid, scale=1.5957691216057308),
                 reads=(tf,), writes=(tf,))
            S.op("dve", (lambda x: (lambda e: e.tensor_tensor(x, x, tf[:, :], ALU.mult)))(x),
                 reads=(tf,), pwrites=(T.ys,))
        wa = C.wget(wf["ssm_w_glu"], 0, 8, 0, 1024)
        wb = C.wget(wf["ssm_w_glu"], 0, 8, 1024, 1024)
        if not S.dry:
            for i in range(8):
                pa = C.psum()
                S.op("pe", [mm(pa[:, :], wa[:, kc, i * 128:i * 128 + 128], T.ys[:, kc, :], kc == 0, kc == 7)
                            for kc in range(8)], reads=(wa, T.ys), writes=(pa,))
                pb_ = C.psum()
                S.op("pe", [mm(pb_[:, :], wb[:, kc, i * 128:i * 128 + 128], T.ys[:, kc, :], kc == 0, kc == 7)
                            for kc in range(8)], reads=(wb, T.ys), writes=(pb_,))
                sg = T.sg[i % 2]
                S.op("act", (lambda o, i_: (lambda e: e.activation(o, i_, AF.Sigmoid)))(sg[:, :], pb_[:, :]),
                     reads=(pb_,), writes=(sg,))
                S.op("dve", (lambda o, a, b: (lambda e: e.tensor_tensor(o, a, b, ALU.mult)))(
                    T.y_a[:, i, :], pa[:, :], sg[:, :]), reads=(pa, sg), pwrites=(T.y_a,))
        C.wrel(2)
        for half in range(2):
            wA = C.wget(wf["w_branch_ssm"], 0, 8, half * 1024, 1024)
            wB = C.wget(wf["w_branch_attn"], 0, 8, half * 1024, 1024)
            if not S.dry:
                for mi in range(8):
                    mo = half * 8 + mi
                    sgs, sga = T.sgs[mo % 2], T.sga[mo % 2]
                    S.dma("sp", (lambda o, i: (lambda e: e.dma_start(out=o, in_=i)))(
                        sgs[:, :], sg_d.t[mo * 128:(mo + 1) * 128, tcol:tcol + 512]), reads=(sg_d,), writes=(sgs,))
                    S.dma("sp", (lambda o, i: (lambda e: e.dma_start(out=o, in_=i)))(
                        sga[:, :], sg_d.t[(16 + mo) * 128:(17 + mo) * 128, tcol:tcol + 512]), reads=(sg_d,), writes=(sga,))
                    p1 = C.psum()
                    S.op("pe", [mm(p1[:, :], wA[:, kc, mi * 128:mi * 128 + 128], T.y_a[:, kc, :], kc == 0, kc == 7)
                                for kc in range(8)], reads=(wA, T.y_a), writes=(p1,))
                    p2 = C.psum()
                    S.op("pe", [mm(p2[:, :], wB[:, kc, mi * 128:mi * 128 + 128], T.ya[:, kc, :], kc == 0, kc == 7)
                                for kc in range(8)], reads=(wB, T.ya), writes=(p2,))
                    t1, t2 = T.t1[0], T.t1[1]
                    S.op("dve", (lambda a, b: (lambda e: e.tensor_tensor(t1[:, :], a, b, ALU.mult)))(p1[:, :], sgs[:, :]),
                         reads=(p1, sgs), writes=(t1,))
                    S.op("dve", (lambda a, b: (lambda e: e.tensor_tensor(t2[:, :], a, b, ALU.mult)))(p2[:, :], sga[:, :]),
                         reads=(p2, sga), writes=(t2,))
                    S.op("dve", (lambda o: (lambda e: e.tensor_tensor(o, t1[:, :], t2[:, :], ALU.add)))(T.xn[:, mo, 0:512]),
                         reads=(t1, t2), pwrites=(T.xn,))
            C.wrel(2)

        def xs(kc, ts):
            return T.xn[:, kc, 0:512], T.xn

        def epi(mo, ti, ps, tcol=tcol):
            S.op("dve", (lambda o, a: (lambda e: e.tensor_tensor(o, a, o, ALU.add)))(
                T.h[:, mo, tcol:tcol + 512], ps[:, :]), reads=(ps,), pwrites=(T.h,))
        emit_linear(C, wf["w_out"], D, 0, D, xs, epi, [slice(0, 512)])
    S.barrier()
    emit_rmsnorm(C, T, T.h, (l * 4 + 2) * KC, T.xn)
    emit_ffn(C, T, T.h, T.xn, wf["ffn2_w_gate"], wf["ffn2_w_up"], wf["ffn2_w_down"])
    S.barrier()
    emit_rmsnorm(C, T, T.h, (l * 4 + 3) * KC, T.xn)
    for qd in range(4):
        S.dma("sp", (lambda qd: (lambda e: e.dma_start(
            out=T.pstage[:, :, :], in_=C.pT.t[l, :, qd * 256:(qd + 1) * 256].rearrange("(k p) t -> p k t", p=128))))(qd),
            reads=(C.pT,), writes=(T.pstage,))
        S.op("dve", (lambda qd: (lambda e: e.tensor_copy(T.pb[:, :, qd * 256:(qd + 1) * 256], T.pstage[:, :, :])))(qd),
             reads=(T.pstage,), pwrites=(T.pb,))
    for cb in range(4):
        wpg = C.wget(wf["ple_w_gate"], 0, KC, cb * 512, 512)
        wpu = C.wget(wf["ple_w_up"], 0, 2, cb * 512, 512)
        if not S.dry:
            for mi in range(4):
                mo = cb * 4 + mi
                for th in range(2):
                    ts = tsl(th)
                    pg = C.psum()
                    S.op("pe", [mm(pg[:, :], wpg[:, kc, mi * 128:mi * 128 + 128], T.xn[:, kc, ts], kc == 0, kc == KC - 1)
                                for kc in range(KC)], reads=(wpg, T.xn), writes=(pg,))
                    pu = C.psum()
                    S.op("pe", [mm(pu[:, :], wpu[:, kc, mi * 128:mi * 128 + 128], T.pb[:, kc, ts], kc == 0, kc == 1)
                                for kc in range(2)], reads=(wpu, T.pb), writes=(pu,))
                    sg = T.f32tmp[0]
                    S.op("act", (lambda i_: (lambda e: e.activation(sg[:, :], i_, AF.Sigmoid)))(pg[:, :]),
                         reads=(pg,), writes=(sg,))
                    S.op("dve", (lambda a: (lambda e: e.tensor_tensor(sg[:, :], a, sg[:, :], ALU.mult)))(pu[:, :]),
                         reads=(pu, sg), writes=(sg,))
                    S.op("dve", (lambda o: (lambda e: e.tensor_tensor(o, o, sg[:, :], ALU.add)))(T.h[:, mo, ts]),
                         reads=(sg,), pwrites=(T.h,))
        C.wrel(2)
    S.barrier()
    if last:
        emit_rmsnorm(C, T, T.h, DEPTH * 4 * KC, T.h)
        tok = S.dma("sp", lambda e: e.dma_start(out=C.oT.t.rearrange("(k p) t -> p k t", p=128), in_=T.h[:, :, :]),
                    reads=(T.h,), writes=(C.oT,))
        if not S.dry:
            S.wait_all("sp", [tok])


def tt(o, a, b, op):
    return lambda e: e.tensor_tensor(o, a, b, op)


def ssm_prep(C, l):
    S = C.S
    P = C.ssm_in
    with ExitStack() as es:
        def sm(name, shape=(128, 16), dt=F32):
            return C.sb(list(shape), dt, name, es)

        def ew(eng, fn, reads, writes):
            S.op(eng, fn, reads=reads, writes=writes)

        def load(name, shape):
            t = sm(name, shape)
            src = P[name].t[l]
            S.dma("sp", (lambda o, i: (lambda e: e.dma_start(out=o, in_=i)))(t.t[tuple(slice(None) for _ in shape)], src),
                  reads=(P[name],), writes=(t,))
            return t
        ar_in = load("ssm_ar", (128, 16))
        ai_in = load("ssm_ai", (128, 16))
        ldt = load("ssm_ldt", (128, 16))
        ctre = load("ssm_ct_re", (128, 16, 16))
        ctim = load("ssm_ct_im", (128, 16, 16))
        btre = load("ssm_bt_re", (128, 16, 16))
        btim = load("ssm_bt_im", (128, 16, 16))
        A = lambda t: t[:, :]
        dt_ = sm("dt")
        ew("act", lambda e: e.activation(A(dt_), A(ldt), AF.Exp), (ldt,), (dt_,))
        ar, ai, mag = sm("ar"), sm("ai"), sm("mag")
        ew("dve", tt(A(ar), A(ar_in), A(dt_), ALU.mult), (ar_in, dt_), (ar,))
        ew("dve", tt(A(ai), A(ai_in), A(dt_), ALU.mult), (ai_in, dt_), (ai,))
        ew("act", lambda e: e.activation(A(mag), A(ar), AF.Exp), (ar,), (mag,))

        def sin_of(x, offset, name):
            xo, y, kf, r, m = sm(name + "xo"), sm(name + "y"), sm(name + "kf"), sm(name + "r"), sm(name + "m")
            ki = sm(name + "ki", (128, 16), I32)
            ew("dve", lambda e: e.tensor_scalar(A(xo), A(x), float(offset), None, ALU.add), (x,), (xo,))
            ew("dve", lambda e: e.tensor_scalar(A(y), A(xo), 1.0 / TWO_PI, 0.5, ALU.mult, ALU.add), (xo,), (y,))
            ew("dve", lambda e: e.tensor_copy(A(ki), A(y)), (y,), (ki,))
            ew("dve", lambda e: e.tensor_copy(A(kf), A(ki)), (ki,), (kf,))
            ew("dve", lambda e: e.scalar_tensor_tensor(A(r), A(kf), -TWO_PI, A(xo), ALU.mult, ALU.add), (kf, xo), (r,))
            ew("dve", lambda e: e.tensor_scalar(A(m), A(r), float(np.pi), None, ALU.is_gt), (r,), (m,))
            ew("dve", lambda e: e.scalar_tensor_tensor(A(r), A(m), -TWO_PI, A(r), ALU.mult, ALU.add), (m, r), (r,))
            ew("dve", lambda e: e.tensor_scalar(A(m), A(r), float(-np.pi), None, ALU.is_lt), (r,), (m,))
            ew("dve", lambda e: e.scalar_tensor_tensor(A(r), A(m), TWO_PI, A(r), ALU.mult, ALU.add), (m, r), (r,))
            s = sm(name + "s")
            ew("act", lambda e: e.activation(A(s), A(r), AF.Sin), (r,), (s,))
            return s
        sn = sin_of(ai, 0.0, "sn")
        cs = sin_of(ai, np.pi / 2.0, "cs")
        lr, li = sm("lr"), sm("li")
        ew("dve", tt(A(lr), A(mag), A(cs), ALU.mult), (mag, cs), (lr,))
        ew("dve", tt(A(li), A(mag), A(sn), ALU.mult), (mag, sn), (li,))
        den, t0, t1_, nr, cr, ci = sm("den"), sm("t0"), sm("t1"), sm("nr"), sm("cr"), sm("ci")
        ew("dve", tt(A(den), A(ar_in), A(ar_in), ALU.mult), (ar_in,), (den,))
        ew("dve", tt(A(t0), A(ai_in), A(ai_in), ALU.mult), (ai_in,), (t0,))
        ew("dve", tt(A(den), A(den), A(t0), ALU.add), (den, t0), (den,))
        ew("dve", lambda e: e.reciprocal(A(den), A(den)), (den,), (den,))
        ew("dve", lambda e: e.tensor_scalar(A(nr), A(lr), -1.0, None, ALU.add), (lr,), (nr,))
        ew("dve", tt(A(t0), A(nr), A(ar_in), ALU.mult), (nr, ar_in), (t0,))
        ew("dve", tt(A(t1_), A(li), A(ai_in), ALU.mult), (li, ai_in), (t1_,))
        ew("dve", tt(A(t0), A(t0), A(t1_), ALU.add), (t0, t1_), (t0,))
        ew("dve", tt(A(cr), A(t0), A(den), ALU.mult), (t0, den), (cr,))
        ew("dve", tt(A(t0), A(li), A(ar_in), ALU.mult), (li, ar_in), (t0,))
        ew("dve", tt(A(t1_), A(nr), A(ai_in), ALU.mult), (nr, ai_in), (t1_,))
        ew("dve", tt(A(t0), A(t0), A(t1_), ALU.subtract), (t0, t1_), (t0,))
        ew("dve", tt(A(ci), A(t0), A(den), ALU.mult), (t0, den), (ci,))
        B3 = [128, 16, 16]
        bbre, bbim, u0, u1 = sm("bbre", B3), sm("bbim", B3), sm("u0", B3), sm("u1", B3)
        crb = cr.t[:, :, None].to_broadcast(B3)
        cib = ci.t[:, :, None].to_broadcast(B3)
        F3 = lambda t: t[:, :, :]
        ew("dve", tt(F3(u0), crb, F3(btre), ALU.mult), (cr, btre), (u0,))
        ew("dve", tt(F3(u1), cib, F3(btim), ALU.mult), (ci, btim), (u1,))
        ew("dve", tt(F3(bbre), F3(u0), F3(u1), ALU.subtract), (u0, u1), (bbre,))
        ew("dve", tt(F3(u0), crb, F3(btim), ALU.mult), (cr, btim), (u0,))
        ew("dve", tt(F3(u1), cib, F3(btre), ALU.mult), (ci, btre), (u1,))
        ew("dve", tt(F3(bbim), F3(u0), F3(u1), ALU.add), (u0, u1), (bbim,))
        powre, powim = sm("powre", (128, 16, 9)), sm("powim", (128, 16, 9))
        ew("dve", lambda e: e.memset(powre[:, :, 0:1], 1.0), (), (powre,))
        ew("dve", lambda e: e.memset(powim[:, :, 0:1], 0.0), (), (powim,))
        ew("dve", lambda e: e.tensor_copy(powre[:, :, 1], A(lr)), (lr, powre), (powre,))
        ew("dve", lambda e: e.tensor_copy(powim[:, :, 1], A(li)), (li, powim), (powim,))
        for j in range(2, 9):
            ew("dve", tt(A(t0), powre[:, :, j - 1], A(lr), ALU.mult), (powre, lr), (t0,))
            ew("dve", tt(A(t1_), powim[:, :, j - 1], A(li), ALU.mult), (powim, li), (t1_,))
            ew("dve", tt(powre[:, :, j], A(t0), A(t1_), ALU.subtract), (t0, t1_, powre), (powre,))
            ew("dve", tt(A(t0), powre[:, :, j - 1], A(li), ALU.mult), (powre, li), (t0,))
            ew("dve", tt(A(t1_), powim[:, :, j - 1], A(lr), ALU.mult), (powim, lr), (t1_,))
            ew("dve", tt(powim[:, :, j], A(t0), A(t1_), ALU.add), (t0, t1_, powim), (powim,))
        lam8 = sm("lam8", (128, 16, 2))
        ew("dve", lambda e: e.tensor_copy(lam8[:, :, 0], powre[:, :, 8]), (powre,), (lam8,))
        ew("dve", lambda e: e.tensor_copy(lam8[:, :, 1], powim[:, :, 8]), (powim, lam8), (lam8,))
        S.dma("sp", lambda e: e.dma_start(out=C.lam8_d[l].t, in_=lam8[:, :, :]), reads=(lam8,), writes=(C.lam8_d[l],))
        prre, prim = sm("prre", (128, 16, 8)), sm("prim", (128, 16, 8))
        for tau in range(8):
            ew("dve", (lambda tau: (lambda e: e.tensor_copy(prre[:, :, tau], powre[:, :, 7 - tau])))(tau), (powre, prre), (prre,))
            ew("dve", (lambda tau: (lambda e: e.tensor_copy(prim[:, :, tau], powim[:, :, 7 - tau])))(tau), (powim, prim), (prim,))
        C4 = [128, 16, 9, 16]
        clre, clim, v0, v1 = sm("clre", C4), sm("clim", C4), sm("v0", C4), sm("v1", C4)
        F4 = lambda t: t[:, :, :, :]
        ctre_b = ctre.t[:, :, None, :].to_broadcast(C4)
        ctim_b = ctim.t[:, :, None, :].to_broadcast(C4)
        pre_b = powre.t[:, :, :, None].to_broadcast(C4)
        pim_b = powim.t[:, :, :, None].to_broadcast(C4)
        ew("dve", tt(F4(v0), ctre_b, pre_b, ALU.mult), (ctre, powre), (v0,))
        ew("dve", tt(F4(v1), ctim_b, pim_b, ALU.mult), (ctim, powim), (v1,))
        ew("dve", tt(F4(clre), F4(v0), F4(v1), ALU.subtract), (v0, v1), (clre,))
        ew("dve", tt(F4(v0), ctre_b, pim_b, ALU.mult), (ctre, powim), (v0,))
        ew("dve", tt(F4(v1), ctim_b, pre_b, ALU.mult), (ctim, powre), (v1,))
        ew("dve", tt(F4(v0), F4(v0), F4(v1), ALU.add), (v0, v1), (v0,))
        ew("dve", lambda e: e.tensor_scalar(F4(clim), F4(v0), -1.0, None, ALU.mult), (v0,), (clim,))
        w4re, w4im = sm("w4re", (128, 16, 128), BF16), sm("w4im", (128, 16, 128), BF16)
        ew("act", lambda e: e.activation(w4re.t[:, :, :].rearrange("p g (t q) -> p g t q", t=8), clre[:, :, 1:9, :], AF.Copy), (clre,), (w4re,))
        ew("act", lambda e: e.activation(w4im.t[:, :, :].rearrange("p g (t q) -> p g t q", t=8), clim[:, :, 1:9, :], AF.Copy), (clim,), (w4im,))
        S.dma("sp", lambda e: e.dma_start(out=C.w4re_d[l].t, in_=w4re[:, :, :]), reads=(w4re,), writes=(C.w4re_d[l],))
        S.dma("sp", lambda e: e.dma_start(out=C.w4im_d[l].t, in_=w4im[:, :, :]), reads=(w4im,), writes=(C.w4im_d[l],))
        T4 = [128, 16, 8, 16]
        t2re, t2im = sm("t2re", T4), sm("t2im", T4)
        w0, w1 = v0.t[:, :, 0:8, :], v1.t[:, :, 0:8, :]
        prre_b = prre.t[:, :, :, None].to_broadcast(T4)
        prim_b = prim.t[:, :, :, None].to_broadcast(T4)
        bbre_b = bbre.t[:, :, None, :].to_broadcast(T4)
        bbim_b = bbim.t[:, :, None, :].to_broadcast(T4)
        ew("dve", tt(w0, prre_b, bbre_b, ALU.mult), (prre, bbre), (v0,))
        ew("dve", tt(w1, prim_b, bbim_b, ALU.mult), (prim, bbim), (v1,))
        ew("dve", tt(F4(t2re), w0, w1, ALU.subtract), (v0, v1), (t2re,))
        ew("dve", tt(w0, prre_b, bbim_b, ALU.mult), (prre, bbim), (v0,))
        ew("dve", tt(w1, prim_b, bbre_b, ALU.mult), (prim, bbre), (v1,))
        ew("dve", tt(F4(t2im), w0, w1, ALU.add), (v0, v1), (t2im,))
        w2re, w2im = sm("w2re", (128, 32, 64), BF16), sm("w2im", (128, 32, 64), BF16)
        for (src, dst) in ((t2re, w2re), (t2im, w2im)):
            for gh in range(2):
                for gb in range(2):
                    ps = C.psum()
                    ops = []
                    for gi in range(8):
                        gp = gb * 8 + gi
                        ops.append((lambda o, i, idn: (lambda e: e.transpose(o, i, idn)))(
                            ps[:, gi * 64:(gi + 1) * 64],
                            src.t[64 * gh:64 * gh + 64, gp, :, :].rearrange("p a b -> p (a b)"),
                            C.ident[64 * gh:64 * gh + 64, 64 * gh:64 * gh + 64]))
                    S.op("pe", ops, reads=(src, C.ident), writes=(ps,))
                    g0 = gh * 16 + gb * 8
                    ew("act", (lambda dst, g0, ps: (lambda e: e.activation(
                        dst.t[:, g0:g0 + 8, :], ps[:, :].rearrange("p (g n) -> p g n", g=8), AF.Copy)))(dst, g0, ps),
                        (ps,), (dst,))
        S.dma("sp", lambda e: e.dma_start(out=C.w2re_d[l].t, in_=w2re[:, :, :]), reads=(w2re,), writes=(C.w2re_d[l],))
        S.dma("sp", lambda e: e.dma_start(out=C.w2im_d[l].t, in_=w2im[:, :, :]), reads=(w2im,), writes=(C.w2im_d[l],))
        w1sb = sm("w1sb", (128, 32, 128), BF16)
        sets = []
        for i in range(2):
            st = dict(bre=sm(f"bwre{i}", (128, 240)), bim=sm(f"bwim{i}", (128, 240)),
                      cre=sm(f"cwre{i}", (128, 240)), cim=sm(f"cwim{i}", (128, 240)))
            for k in st:
                ew("dve", (lambda t: (lambda e: e.memset(t[:, :], 0.0)))(st[k]), (), (st[k],))
            sets.append(st)
        for gp in range(16):
            st = sets[gp % 2]
            ew("dve", (lambda st, gp: (lambda e: e.tensor_copy(st["bre"][:, 112:128], bbre[:, gp, :])))(st, gp), (bbre, st["bre"]), (st["bre"],))
            ew("dve", (lambda st, gp: (lambda e: e.tensor_copy(st["bim"][:, 112:128], bbim[:, gp, :])))(st, gp), (bbim, st["bim"]), (st["bim"],))
            ew("act", (lambda st, gp: (lambda e: e.activation(st["cre"].t[:, 112:240].rearrange("p (j q) -> p j q", j=8), clre[:, gp, 0:8, :], AF.Copy)))(st, gp), (clre, st["cre"]), (st["cre"],))
            ew("act", (lambda st, gp: (lambda e: e.activation(st["cim"].t[:, 112:240].rearrange("p (j q) -> p j q", j=8), clim[:, gp, 0:8, :], AF.Copy)))(st, gp), (clim, st["cim"]), (st["cim"],))
            for gh in range(2):
                ps = C.psum()
                ops = []
                rows = slice(64 * gh, 64 * gh + 64)
                for tau in range(8):
                    for ri, (bk, ck) in enumerate((("bre", "cre"), ("bim", "cim"))):
                        ops.append(mm(ps[:, 0:128], st[bk][rows, 112 - 16 * tau:240 - 16 * tau],
                                      st[ck][rows, (7 - tau) * 16:(7 - tau) * 16 + 128],
                                      tau == 0 and ri == 0, tau == 7 and ri == 1))
                S.op("pe", ops, reads=(st["bre"], st["bim"], st["cre"], st["cim"]), writes=(ps,))
                g = gh * 16 + gp
                S.op("dve", (lambda g, ps: (lambda e: e.tensor_copy(w1sb[:, g, :], ps[:, 0:128])))(g, ps),
                     reads=(ps,), pwrites=(w1sb,))
        S.dma("sp", lambda e: e.dma_start(out=C.w1_d[l].t, in_=w1sb[:, :, :]), reads=(w1sb,), writes=(C.w1_d[l],))
        S.barrier()


def seg_b(C, l):
    S = C.S
    zxp = C.zx_out[l]
    yxs_in, yxa_in = C.yx_in[l]
    with ExitStack() as es:
        with ExitStack() as es1:
            zs = C.sb([128, 4, L], BF16, "zs", es1)
            XY = C.sb([128, 32 * 256], BF16, "XY", es1)
            Sst = C.sb([128, 16, 256, 2], F32, "Sst", es1)
            spre = [C.sb([128, 16, 256], BF16, f"spre{i}", es1) for i in range(2)]
            YO = C.sb([128, 32 * 256], BF16, "YO", es1)
            w1 = C.sb([128, 32, 128], BF16, "w1", es1)
            w2 = [C.sb([128, 32, 64], BF16, f"w2{i}", es1) for i in range(2)]
            w4 = [C.sb([128, 16, 128], BF16, f"w4{i}", es1) for i in range(2)]
            lam8 = C.sb([128, 16, 2], F32, "lam8", es1)
            lrr = C.sb([128, 16, 2], F32, "lrr", es1)
            lii = C.sb([128, 16, 2], F32, "lii", es1)
            tA = C.sb([128, 16, 2], F32, "tA", es1)
            tB = C.sb([128, 16, 2], F32, "tB", es1)
            dd = C.sb([128, 4], F32, "dd", es1)

            def ld(dst_ap, src_ap, rd, wr, pw=False):
                S.dma("sp", (lambda o, i: (lambda e: e.dma_start(out=o, in_=i)))(dst_ap, src_ap), reads=rd,
                      writes=() if pw else wr, pwrites=wr if pw else ())
            ld(w1[:, :, :], C.w1_d[l].t, (C.w1_d[l],), (w1,))
            ld(w2[0][:, :, :], C.w2re_d[l].t, (C.w2re_d[l],), (w2[0],))
            ld(w2[1][:, :, :], C.w2im_d[l].t, (C.w2im_d[l],), (w2[1],))
            ld(w4[0][:, :, :], C.w4re_d[l].t, (C.w4re_d[l],), (w4[0],))
            ld(w4[1][:, :, :], C.w4im_d[l].t, (C.w4im_d[l],), (w4[1],))
            ld(lam8[:, :, :], C.lam8_d[l].t, (C.lam8_d[l],), (lam8,))
            ld(dd[:, :], C.ssm_in["ssm_dd"].t[l], (C.ssm_in["ssm_dd"],), (dd,))
            for jj in range(2):
                def ldz(e, jj=jj):
                    j512 = C.jval(e, 512)
                    b0 = jj * 1024
                    return e.dma_start(out=zs[:, :, jj * NT:(jj + 1) * NT],
                                       in_=zxp[0].t[b0:b0 + 1024, :][bass.ds(j512, 512), :].rearrange("(c p) t -> p c t", p=128))
                S.dma("sp", ldz, reads=(zxp[0],), pwrites=(zs,))
            S.op("dve", lambda e: e.tensor_copy(lrr[:, :, 0], lam8[:, :, 0]), reads=(lam8,), writes=(lrr,))
            S.op("dve", lambda e: e.tensor_copy(lrr[:, :, 1], lam8[:, :, 0]), reads=(lam8, lrr), writes=(lrr,))
            S.op("dve", lambda e: e.tensor_scalar(lii[:, :, 0], lam8[:, :, 1], -1.0, None, ALU.mult), reads=(lam8,), writes=(lii,))
            S.op("dve", lambda e: e.tensor_copy(lii[:, :, 1], lam8[:, :, 1]), reads=(lam8, lii), writes=(lii,))
            zperm = TB(YO.t[:, :].rearrange("p (k t c) -> p k t c", k=4, t=8), YO.b)
            for cc in range(4):
                eng = ("act", "dve")[cc % 2]
                S.op(eng, cp(eng, zperm[:, cc, :, :], zs[:, cc, :].rearrange("p (c t) -> p t c", t=8)),
                     reads=(zs,), pwrites=(YO,))
            for cc in range(4):
                ld(C.zd.t[:, cc * 128:(cc + 1) * 128, :].rearrange("t p c -> p t c"), zperm[:, cc, :, :], (YO,), (C.zd,), pw=True)
            X = TB(XY.t[:, :].rearrange("p (g c) -> p g c", g=32), XY.b)
            for tau in range(8):
                ld(X[16 * tau:16 * tau + 16, :, :], C.zd.t[tau].rearrange("(g q) c -> q g c", q=16), (C.zd,), (XY,), pw=True)
            for gp in range(16):
                ps = C.psum()
                ops = []
                for gh in range(2):
                    g = gh * 16 + gp
                    for ri in range(2):
                        ops.append(mm(ps[64 * gh:64 * gh + 64, ri * 256:(ri + 1) * 256], w2[ri][:, g, :], X[:, g, :], True, True))
                S.op("pe", ops, reads=(w2[0], w2[1], XY), writes=(ps,))
                eng = ("act", "dve")[gp % 2]
                S.op(eng, cp(eng, Sst[:, gp, :, :], ps[:, :].rearrange("p (r c) -> p c r", r=2)), reads=(ps,), pwrites=(Sst,))
            for c in range(1, 256):
                S.op("dve", (lambda c: (lambda e: e.tensor_tensor(tA[:, :, :], Sst[:, :, c - 1, :], lrr[:, :, :], ALU.mult)))(c),
                     reads=(Sst, lrr), writes=(tA,))
                S.op("dve", (lambda c: (lambda e: e.tensor_tensor(tB[:, :, :], Sst[:, :, c - 1, ::-1], lii[:, :, :], ALU.mult)))(c),
                     reads=(Sst, lii), writes=(tB,))
                S.op("dve", (lambda c: (lambda e: e.tensor_tensor(tA[:, :, :], tA[:, :, :], tB[:, :, :], ALU.add)))(c),
                     reads=(tA, tB), writes=(tA,))
                S.op("dve", (lambda c: (lambda e: e.tensor_tensor(Sst[:, :, c, :], Sst[:, :, c, :], tA[:, :, :], ALU.add)))(c),
                     reads=(tA,), pwrites=(Sst,))
            for ri in range(2):
                eng = ("act", "dve")[ri]
                S.op("dve", (lambda ri: (lambda e: e.memset(spre[ri][:, :, 0:1], 0.0)))(ri), writes=(spre[ri],))
                S.op(eng, cp(eng, spre[ri][:, :, 1:256], Sst[:, :, 0:255, ri]), reads=(Sst, spre[ri]), writes=(spre[ri],))
            Yb = TB(YO.t[:, :].rearrange("p (g c) -> p g c", g=32), YO.b)
            for g2 in range(16):
                ps = C.psum()
                ops = []
                for k in range(2):
                    g = g2 * 2 + k
                    gh, gp = g // 16, g % 16
                    rows = slice(64 * gh, 64 * gh + 64)
                    o = ps[:, k * 256:(k + 1) * 256]
                    ops.append(mm(o, w1[:, g, :], X[:, g, :], True, False))
                    ops.append(mm(o, w4[0][rows, gp, :], spre[0][rows, gp, :], False, False))
                    ops.append(mm(o, w4[1][rows, gp, :], spre[1][rows, gp, :], False, True))
                S.op("pe", ops, reads=(w1, XY, w4[0], w4[1], spre[0], spre[1]), writes=(ps,))
                eng = ("act", "dve")[g2 % 2]
                S.op(eng, cp(eng, Yb[:, 2 * g2:2 * g2 + 2, :], ps[:, :].rearrange("p (g c) -> p g c", g=2)),
                     reads=(ps,), pwrites=(YO,))
            for t in range(8):
                ld(C.yd.t[t].rearrange("g p c -> p g c"), Yb[16 * t:16 * t + 16, :, :], (YO,), (C.yd,), pw=True)
            Ysel = TB(XY.t[:, :].rearrange("p (k t c) -> p k t c", k=4, t=8), XY.b)
            for cc in range(4):
                ld(Ysel[:, cc, :, :], C.yd.t[:, cc * 8:(cc + 1) * 8, :, :].rearrange("t g p c -> (g p) t c"), (C.yd,), (XY,), pw=True)
            yout = TB(YO.t[:, :].rearrange("p (k n) -> p k n", k=4), YO.b)
            for cc in range(4):
                S.op("dve", (lambda cc: (lambda e: e.scalar_tensor_tensor(
                    yout[:, cc, :].rearrange("p (c t) -> p c t", t=8),
                    zs[:, cc, :].rearrange("p (c t) -> p c t", t=8), dd[:, cc:cc + 1],
                    Ysel[:, cc, :, :].rearrange("p t c -> p c t"), ALU.mult, ALU.add)))(cc),
                    reads=(zs, dd, XY), pwrites=(YO,))
            for cc in range(4):
                ld(yxs_in.t[cc * 128:(cc + 1) * 128, :], yout[:, cc, :], (YO,), (yxs_in,), pw=True)
            C.coll_pair(yxs_in, C.yx_out[l][0], pop=False)
            S.barrier()
        with ExitStack() as es2:
            qT = C.sb([128, 4, L], BF16, "qT", es2)
            kT = C.sb([128, 4, L], BF16, "kT", es2)
            vst = C.sb([128, 4, L], BF16, "vst", es2)
            Vt = C.sb([128, 16, 4, 128], BF16, "Vt", es2)
            ao = C.sb([128, 4, L], BF16, "ao", es2)
            cosT = C.sb([32, L], F32, "cosT", es2)
            sinT = C.sb([32, L], F32, "sinT", es2)
            rt = [C.sb([32, 512], F32, f"rt{i}", es2) for i in range(2)]
            ksum = C.sb([128, 4, 8], F32, "ksum", es2)
            khi = C.sb([128, 4, 8], BF16, "khi", es2)
            klo = C.sb([128, 4, 8], BF16, "klo", es2)
            kd = C.sb([128, 4, 8], F32, "kd", es2)
            Gm = C.sb([128, 16, 8], F32, "Gm", es2)
            m8 = C.sb([128, 8], F32, "m8", es2)
            sel = C.sb([128, 16, 8], F32, "sel", es2)
            biasT = C.sb([8, 4, L], BF16, "biasT", es2)
            PT = [C.sb([128, 256], BF16, f"PT{i}", es2) for i in range(3)]
            rec = C.sb([128, 256], F32, "rec", es2)
            ld = lambda d, s_, rd, wr, pw=False: S.dma(
                "sp", (lambda o, i: (lambda e: e.dma_start(out=o, in_=i)))(d, s_), reads=rd,
                writes=() if pw else wr, pwrites=wr if pw else ())
            ld(cosT[:, :], C.consts["rot_cos"].t, (C.consts["rot_cos"],), (cosT,))
            ld(sinT[:, :], C.consts["rot_sin"].t, (C.consts["rot_sin"],), (sinT,))
            for (dst, part) in ((qT, 1), (kT, 2), (vst, 3)):
                for jj in range(2):
                    def ldq(e, dst=dst, part=part, jj=jj):
                        j512 = C.jval(e, 512)
                        b0 = jj * 1024
                        return e.dma_start(out=dst[:, :, jj * NT:(jj + 1) * NT],
                                           in_=zxp[part].t[b0:b0 + 1024, :][bass.ds(j512, 512), :].rearrange("(c p) t -> p c t", p=128))
                    S.dma("sp", ldq, reads=(zxp[part],), pwrites=(dst,))
            for hh in range(4):
                for k4 in range(4):
                    ps = C.psum()
                    psb = ps.t[:, 0:256].bitcast(BF16)
                    S.op("pe", [(lambda o, i_: (lambda e: e.transpose(o, i_, C.identb[:, :])))(
                        psb[:, i * 128:(i + 1) * 128], vst[:, hh, (k4 * 4 + i) * 128:(k4 * 4 + i + 1) * 128])
                                for i in range(4)], reads=(vst, C.identb), writes=(ps,))
                    eng = ("act", "dve")[k4 % 2]
                    S.op(eng, cp(eng, Vt[:, k4 * 4:k4 * 4 + 4, hh, :], psb.rearrange("p (k d) -> p k d", k=4)),
                         reads=(ps,), pwrites=(Vt,))
            it = 0
            for x in (qT, kT):
                for hh in range(4):
                    for tt_ in range(4):
                        ts = slice(tt_ * 512, tt_ * 512 + 512)
                        ps = C.psum()
                        S.op("pe", mm(ps[0:32, :], C.permT[0:32, 0:32], x[0:32, hh, ts], True, True),
                             reads=(x, C.permT), writes=(ps,))
                        r0, r1 = rt[0], rt[1]
                        S.op("dve", (lambda ps, ts: (lambda e: e.tensor_tensor(r0[:, :], ps[0:32, :], sinT[:, ts], ALU.mult)))(ps, ts),
                             reads=(ps, sinT), writes=(r0,))
                        S.op("dve", (lambda x, hh, ts: (lambda e: e.tensor_tensor(r1[:, :], x[0:32, hh, ts], cosT[:, ts], ALU.mult)))(x, hh, ts),
                             reads=(x, cosT), writes=(r1,))
                        S.op("dve", (lambda x, hh, ts: (lambda e: e.tensor_tensor(x[0:32, hh, ts], r0[:, :], r1[:, :], ALU.add)))(x, hh, ts),
                             reads=(r0, r1), writes=(x,))
                        it += 1
            for hh in range(4):
                S.op("dve", (lambda hh: (lambda e: e.tensor_reduce(ksum[:, hh, :], kT[:, hh, :].rearrange("p (n t) -> p n t", t=256), AX.X, ALU.add)))(hh),
                     reads=(kT,), pwrites=(ksum,))
            S.op("dve", lambda e: e.tensor_copy(khi[:, :, :], ksum[:, :, :]), reads=(ksum,), writes=(khi,))
            S.op("dve", lambda e: e.tensor_tensor(kd[:, :, :], ksum[:, :, :], khi[:, :, :], ALU.subtract), reads=(ksum, khi), writes=(kd,))
            S.op("dve", lambda e: e.tensor_copy(klo[:, :, :], kd[:, :, :]), reads=(kd,), writes=(klo,))
            for hh in range(4):
                ps = C.psum()
                ops = []
                for qt in range(16):
                    o = ps[:, qt * 8:(qt + 1) * 8]
                    ops.append(mm(o, qT[:, hh, qt * 128:(qt + 1) * 128], khi[:, hh, :], True, False))
                    ops.append(mm(o, qT[:, hh, qt * 128:(qt + 1) * 128], klo[:, hh, :], False, True))
                S.op("pe", ops, reads=(qT, khi, klo), writes=(ps,))
                S.op("dve", (lambda ps: (lambda e: e.tensor_tensor(Gm[:, :, :], ps[:, 0:128].rearrange("p (q n) -> p q n", n=8), C.negm[:, :, :], ALU.add)))(ps),
                     reads=(ps, C.negm), writes=(Gm,))
                for qt in range(16):
                    S.op("dve", (lambda qt: (lambda e: e.max(m8[:, :], Gm[:, qt, :])))(qt), reads=(Gm,), writes=(m8,))
                    S.op("dve", (lambda qt: (lambda e: e.tensor_scalar(sel[:, qt, :], Gm[:, qt, :], m8[:, 2:3], 1.0, ALU.is_ge, ALU.subtract)))(qt),
                         reads=(Gm, m8), pwrites=(sel,))
                for q4 in range(4):
                    ps2 = C.psum()
                    S.op("pe", [(lambda o, i_: (lambda e: e.transpose(o, i_, C.ident[:, :])))(
                        ps2[0:8, i * 128:(i + 1) * 128], sel[:, q4 * 4 + i, :])
                                for i in range(4)], reads=(sel, C.ident), writes=(ps2,))
                    S.op("act", cp("act", biasT[0:8, hh, q4 * 512:(q4 + 1) * 512], ps2[0:8, :]), reads=(ps2,), pwrites=(biasT,))
            O_acc, S_acc = C.pbank[6], C.pbank[7]
            for hh in range(4):
                for QB in range(8):
                    qs = QB * 256
                    nkt = 2 * QB + 2
                    pend = []

                    def score(kt, hh=hh, QB=QB, qs=qs):
                        sc = C.psum()
                        ops = [mm(sc[:, 0:256], kT[:, hh, kt * 128:(kt + 1) * 128], qT[:, hh, qs:qs + 256], True, False)]
                        rd = [kT, qT]
                        if kt < 2 * QB:
                            ops.append(mm(sc[:, 0:256], C.oh[0:8, kt // 2, :], biasT[0:8, hh, qs:qs + 256], False, True))
                            rd += [C.oh, biasT]
                        else:
                            ops.append(mm(sc[:, 0:256], C.identb[:, :], C.cm[:, kt - 2 * QB, :], False, True))
                            rd += [C.identb, C.cm]
                        S.op("pe", ops, reads=tuple(rd), writes=(sc,))
                        pt = PT[kt % 3]
                        S.op("act", (lambda sc, pt: (lambda e: e.activation(pt[:, :], sc[:, 0:256], AF.Exp, scale=SCALE)))(sc, pt),
                             reads=(sc,), writes=(pt,))
                        return pt

                    def accum(kt, pt, hh=hh, nkt=nkt):
                        S.op("pe", [mm(O_acc[:, 0:256], Vt[:, kt, hh, :], pt[:, :], kt == 0, kt == nkt - 1),
                                    mm(S_acc[:, 0:256], C.ones[:, :], pt[:, :], kt == 0, kt == nkt - 1)],
                             reads=(Vt, pt, C.ones), writes=(), pwrites=(O_acc, S_acc))
                    for kt in range(nkt):
                        pend.append((kt, score(kt)))
                        if len(pend) > 1:
                            accum(*pend.pop(0))
                    while pend:
                        accum(*pend.pop(0))
                    S.op("dve", lambda e: e.reciprocal(rec[:, :], S_acc[:, 0:256]), reads=(S_acc,), writes=(rec,))
                    S.op("dve", (lambda hh, qs: (lambda e: e.tensor_tensor(ao[:, hh, qs:qs + 256], O_acc[:, 0:256], rec[:, :], ALU.mult)))(hh, qs),
                         reads=(O_acc, rec), pwrites=(ao,))
            for hh in range(4):
                ld(yxa_in.t[hh * 128:(hh + 1) * 128, :], ao[:, hh, :], (ao,), (yxa_in,), pw=True)
            S.barrier()
    C.coll_pair(yxa_in, C.yx_out[l][1], pop=True)


CONST_SPECS = {
    "ident": ([128, 128], F32), "identb": ([128, 128], BF16), "permT": ([32, 32], BF16),
    "oh": ([8, 8, 128], BF16), "cm": ([128, 2, 256], BF16), "negm": ([128, 16, 8], F32),
    "rot_cos": ([32, L], F32), "rot_sin": ([32, L], F32),
}
SSM_SPECS = {
    "ssm_ar": [DEPTH, 128, 16], "ssm_ai": [DEPTH, 128, 16], "ssm_ldt": [DEPTH, 128, 16],
    "ssm_ct_re": [DEPTH, 128, 16, 16], "ssm_ct_im": [DEPTH, 128, 16, 16],
    "ssm_bt_re": [DEPTH, 128, 16, 16], "ssm_bt_im": [DEPTH, 128, 16, 16], "ssm_dd": [DEPTH, 128, 4],
}


def build_program(ncores=8, mode="full"):
    nc = bass.Bass("TRN2", target_bir_lowering=False)
    NW = ncores
    pairs = [[2 * i, 2 * i + 1] for i in range(ncores // 2)]
    with ExitStack() as es:
        C = Ctx(nc, es)
        S = C.S
        ext = lambda name, shape, dt: TB(nc.dram_tensor(name, shape, dt, kind="ExternalInput").ap(), S.buf(name))
        C.xT = ext("xT", [D, NT], F32)
        C.pT = ext("pT", [DEPTH, PLE, NT], F32)
        C.gains_d = ext("gains", [128, (DEPTH * 4 + 1) * KC], F32)
        C.ssm_in = {k: ext(k, shp, F32) for k, shp in SSM_SPECS.items()}
        C.consts = {k: ext(k, shp, dt) for k, (shp, dt) in CONST_SPECS.items()}
        C.wf = [{n: WMat(C, f"{l}_{n}", K, M, NW) for n, K, M in WSPEC} for l in range(DEPTH)]
        SEGA_W = ["ffn1_w_gate", "ffn1_w_up", "ffn1_w_down", "w_in"]
        wsh = {n: ext("w_" + n, [DEPTH if mode == "full" else 1, C.wf[0][n].per_rank], F32) for n, K, M in WSPEC
               if mode == "full" or (mode == "sega" and n in SEGA_W)}
        if mode == "segb":
            C.zin = ext("zin", [8192, NT], BF16)
        C.oT = TB(nc.dram_tensor("oT", [D, NT], F32, kind="ExternalOutput").ap(), S.buf("oT"))
        C.zx_in = [[C.dram(f"zx_in{l}_{i}", [1024, NT], BF16) for i in range(4)] for l in range(DEPTH)]
        C.zx_out = [[C.dram(f"zx_out{l}_{i}", [2048, NT], BF16) for i in range(4)] for l in range(DEPTH)]
        C.yx_in = [[C.dram(f"yx_in{l}_{i}", [512, L], BF16) for i in range(2)] for l in range(DEPTH)]
        C.yx_out = [[C.dram(f"yx_out{l}_{i}", [1024, L], BF16) for i in range(2)] for l in range(DEPTH)]
        C.sg_d = [C.dram(f"sg_d{l}", [4096, NT], BF16) for l in range(DEPTH)]
        C.hsp = C.dram("hsp", [D, NT], F32)
        C.zd = C.dram("zd", [8, 512, 256], BF16)
        C.yd = C.dram("yd", [8, 32, 16, 256], BF16)
        C.w1_d = [C.dram(f"w1_d{l}", [128, 32, 128], BF16) for l in range(DEPTH)]
        C.w2re_d = [C.dram(f"w2re_d{l}", [128, 32, 64], BF16) for l in range(DEPTH)]
        C.w2im_d = [C.dram(f"w2im_d{l}", [128, 32, 64], BF16) for l in range(DEPTH)]
        C.w4re_d = [C.dram(f"w4re_d{l}", [128, 16, 128], BF16) for l in range(DEPTH)]
        C.w4im_d = [C.dram(f"w4im_d{l}", [128, 16, 128], BF16) for l in range(DEPTH)]
        C.lam8_d = [C.dram(f"lam8_d{l}", [128, 16, 2], F32) for l in range(DEPTH)]
        C.ones = C.sb([128, 128], BF16, "ones")
        C.eps_ap = C.sb([128, 1], F32, "eps")
        C.gains = C.sb([128, (DEPTH * 4 + 1) * KC], F32, "gains")

        mid = ["ssm_w_glu", "w_branch_ssm", "w_branch_attn", "w_out"]
        tail = ["ple_w_gate", "ple_w_up"]

        def ffn_chunks(l, pre):
            out = []
            g, u, d = C.wf[l][pre + "_w_gate"], C.wf[l][pre + "_w_up"], C.wf[l][pre + "_w_down"]
            for i in range(len(g.chunks)):
                out += [(l, pre + "_w_gate", i), (l, pre + "_w_up", i), (l, pre + "_w_down", i)]
            return out

        quads = [[0, 1, 2, 3], [4, 5, 6, 7]]
        xpairs = [[0, 4], [1, 5], [2, 6], [3, 7]]

        def coll(groups, src, dst):
            S.dma("pool", lambda e: e.collective_compute("AllGather", ALU.bypass, replica_groups=groups,
                                                         ins=[src.t], outs=[dst.t]),
                  reads=(src,), writes=(dst,), inc=1)

        def chunk_list(l, names):
            return [(l, n, i) for n in names for i in range(len(C.wf[l][n].chunks))]

        def bounce(l, n, i):
            w = C.wf[l][n]
            r0, nr, c0, ncw = w.chunks[i]
            src = wsh[n].t[l, w.off[i]:w.off[i] + (nr // NW) * ncw].rearrange("(r c) -> r c", c=ncw)
            S.dma("pool", (lambda o, s_: (lambda e: e.dma_start(out=o, in_=s_)))(w.bounce[i].t, src),
                  reads=(wsh[n],), writes=(w.bounce[i],))

        def gather(chunks):
            LOOK = 12
            for k in range(min(LOOK, len(chunks))):
                bounce(*chunks[k])
            if NW == 8 and chunks:
                l0, n0, i0 = chunks[0]
                coll(quads, C.wf[l0][n0].bounce[i0], C.wf[l0][n0].half[i0])
            for k, (l, n, i) in enumerate(chunks):
                w = C.wf[l][n]
                if NW == 8:
                    if k + 1 < len(chunks):
                        l1, n1, i1 = chunks[k + 1]
                        coll(quads, C.wf[l1][n1].bounce[i1], C.wf[l1][n1].half[i1])
                    coll(xpairs, w.half[i], w.full[i])
                else:
                    coll([list(range(NW))], w.bounce[i], w.full[i])
                if k + LOOK < len(chunks):
                    bounce(*chunks[k + LOOK])

        def coll_pair(src, dst, pop=True):
            S.dma("pool", lambda e: e.collective_compute("AllGather", ALU.bypass, replica_groups=pairs,
                                                         ins=[src.t], outs=[dst.t]),
                  reads=(src,), writes=(dst,), inc=1)
            if pop and C.gbatches:
                gather(C.gbatches.pop(0))
        C.coll_pair = coll_pair

        def load_consts(names, scope):
            for k in names:
                shp, dt = CONST_SPECS[k]
                t = C.sb(shp, dt, k, scope)
                idx = tuple(slice(None) for _ in shp)
                S.dma("sp", (lambda o, i: (lambda e: e.dma_start(out=o, in_=i)))(t.t[idx], C.consts[k].t),
                      reads=(C.consts[k],), writes=(t,))
                setattr(C, k, t)

        def body():
            def cpart(l):
                f2 = ffn_chunks(l, "ffn2")
                return chunk_list(l, mid) + f2[:9], f2[9:] + chunk_list(l, tail)
            a0 = ffn_chunks(0, "ffn1") + chunk_list(0, ["w_in"])
            a1 = ffn_chunks(1, "ffn1") + chunk_list(1, ["w_in"])
            c0a, c0b = cpart(0)
            c1a, c1b = cpart(1)
            C.gbatches = [c0a, c0b + a1, c1a, c1b]
            S.op("dve", lambda e: e.memset(C.ones[:, :], 1.0), writes=(C.ones,))
            S.op("dve", lambda e: e.memset(C.eps_ap[:, :], EPS), writes=(C.eps_ap,))
            S.dma("sp", lambda e: e.dma_start(out=C.gains[:, :], in_=C.gains_d.t), reads=(C.gains_d,), writes=(C.gains,))
            gather(a0)
            with ExitStack() as p0:
                load_consts(["ident"], p0)
                for l in range(DEPTH):
                    ssm_prep(C, l)
                S.barrier()
            for l in range(DEPTH):
                if l == 0:
                    ph = ExitStack()
                    T = TokTiles(C, ph)
                    C.wphase_begin()
                    S.dma("sp", (lambda h_: (lambda e: e.dma_start(out=h_[:, :, :], in_=C.xT.t.rearrange("(k p) t -> p k t", p=128))))(T.h),
                          reads=(C.xT,), writes=(T.h,))
                seg_a(C, T, l)
                S.barrier()
                ph.close()
                with ExitStack() as pb:
                    load_consts(["ident", "identb", "permT", "oh", "cm", "negm"], pb)
                    seg_b(C, l)
                ph = ExitStack()
                T = TokTiles(C, ph)
                C.wphase_begin()
                seg_c(C, T, l, last=(l == DEPTH - 1))
            S.barrier()
            ph.close()

        def plan_ffn(l, pre):
            out = []
            for ch in range(DFF // 512):
                out += [(pre + "_w_gate", 0, KC, ch * 512, 512), (pre + "_w_up", 0, KC, ch * 512, 512),
                        (pre + "_w_down", ch * 512, 4, 0, D)]
            return [(l,) + b for b in out]

        def plan_a(l):
            return plan_ffn(l, "ffn1") + [(l, "w_in", 0, KC, cb * 512, 512) for cb in range(16)]

        def plan_c(l):
            out = []
            for hf in range(2):
                out += [("ssm_w_glu", 0, 8, 0, 1024), ("ssm_w_glu", 0, 8, 1024, 1024)]
                for half in range(2):
                    out += [("w_branch_ssm", 0, 8, half * 1024, 1024), ("w_branch_attn", 0, 8, half * 1024, 1024)]
                out += [("w_out", 0, KC, cb * 512, 512) for cb in range(4)]
            out = [(l,) + b for b in out] + plan_ffn(l, "ffn2")
            for cb in range(4):
                out += [(l, "ple_w_gate", 0, KC, cb * 512, 512), (l, "ple_w_up", 0, 2, cb * 512, 512)]
            return out
        def body_prep():
            with ExitStack() as p0:
                load_consts(["ident"], p0)
                ssm_prep(C, 0)
                S.barrier()
            toks = []
            for nm, src in (("o_w1", C.w1_d[0]), ("o_w2re", C.w2re_d[0]), ("o_w2im", C.w2im_d[0]),
                            ("o_w4re", C.w4re_d[0]), ("o_w4im", C.w4im_d[0]), ("o_lam8", C.lam8_d[0])):
                shp = [int(x) for x in src.t.shape]
                dt = F32 if nm == "o_lam8" else BF16
                o = nc.dram_tensor(nm, shp, dt, kind="ExternalOutput").ap()
                toks.append(S.dma("sp", (lambda o, s_: (lambda e: e.dma_start(out=o, in_=s_)))(o, src.t), reads=(src,)))
            S.wait_all("sp", toks)

        def body_segb():
            S.op("dve", lambda e: e.memset(C.ones[:, :], 1.0), writes=(C.ones,))
            with ExitStack() as p0:
                load_consts(["ident"], p0)
                ssm_prep(C, 0)
                S.barrier()
            for jj in range(2):
                for part in range(4):
                    S.dma("sp", (lambda jj, part: (lambda e: e.dma_start(
                        out=C.zx_out[0][part].t[jj * 1024:(jj + 1) * 1024, :],
                        in_=C.zin.t[jj * 4096 + part * 1024:jj * 4096 + (part + 1) * 1024, :])))(jj, part),
                        reads=(C.zin,), pwrites=(C.zx_out[0][part],))
            C.coll_pair = lambda a, b, pop=True: None
            with ExitStack() as pb:
                load_consts(["ident", "identb", "permT", "oh", "cm", "negm"], pb)
                seg_b(C, 0)
            o = nc.dram_tensor("o_yx", [1024, L], BF16, kind="ExternalOutput").ap()
            tk = S.dma("sp", lambda e: e.dma_start(out=o[0:512, :], in_=C.yx_in[0][0].t), reads=(C.yx_in[0][0],))
            tk2 = S.dma("sp", lambda e: e.dma_start(out=o[512:1024, :], in_=C.yx_in[0][1].t), reads=(C.yx_in[0][1],))
            S.wait_all("sp", [tk, tk2])

        def body_sega():
            S.op("dve", lambda e: e.memset(C.ones[:, :], 1.0), writes=(C.ones,))
            S.op("dve", lambda e: e.memset(C.eps_ap[:, :], EPS), writes=(C.eps_ap,))
            S.dma("sp", lambda e: e.dma_start(out=C.gains[:, :], in_=C.gains_d.t), reads=(C.gains_d,), writes=(C.gains,))
            C.gbatches = []
            gather(ffn_chunks(0, "ffn1") + chunk_list(0, ["w_in"]))
            ph = ExitStack()
            T = TokTiles(C, ph)
            C.wphase_begin()
            S.dma("sp", (lambda h_: (lambda e: e.dma_start(out=h_[:, :, :], in_=C.xT.t.rearrange("(k p) t -> p k t", p=128))))(T.h),
                  reads=(C.xT,), writes=(T.h,))
            seg_a(C, T, 0)
            S.barrier()
            ph.close()
            o1 = nc.dram_tensor("o_zx", [8192, NT], BF16, kind="ExternalOutput").ap()
            o2 = nc.dram_tensor("o_h", [D, NT], F32, kind="ExternalOutput").ap()
            tks = [S.dma("sp", (lambda i: (lambda e: e.dma_start(out=o1[i * 2048:(i + 1) * 2048, :], in_=C.zx_out[0][i].t)))(i),
                         reads=(C.zx_out[0][i],)) for i in range(4)]
            tks.append(S.dma("sp", lambda e: e.dma_start(out=o2, in_=C.hsp.t), reads=(C.hsp,)))
            S.wait_all("sp", tks)

        if mode == "sega":
            for (l, n, k0, kc, c0, ncols) in plan_a(0):
                C.wplan.append((C.wf[l][n], k0, kc, c0, ncols))
            C.wphase.append(0)
            body_sega()
            S.emit()
            return nc
        if mode == "prep":
            body_prep()
            S.emit()
            return nc
        if mode == "segb":
            body_segb()
            S.emit()
            return nc
        phases = [plan_a(0), plan_c(0) + plan_a(1), plan_c(1)]
        for ph_ in phases:
            C.wphase.append(len(C.wplan))
            for (l, n, k0, kc, c0, ncols) in ph_:
                C.wplan.append((C.wf[l][n], k0, kc, c0, ncols))
        body()
        S.emit()
    return nc


def _consts():
    bf = ml_dtypes.bfloat16
    c = {}
    c["ident"] = np.eye(128, dtype=np.float32)
    c["identb"] = np.eye(128, dtype=np.float32).astype(bf)
    pm = np.zeros((32, 32), np.float32)
    for d in range(16):
        pm[d + 16, d] = -1.0
        pm[d, d + 16] = 1.0
    c["permT"] = pm.astype(bf)
    oh = np.zeros((8, 8, 128), np.float32)
    for n in range(8):
        oh[n, n, :] = BIG
    c["oh"] = oh.astype(bf)
    cm = np.zeros((128, 2, 256), np.float32)
    k = np.arange(128)[:, None]
    q = np.arange(256)[None, :]
    cm[:, 0, :] = np.where(k > q, -BIG, 0.0)
    cm[:, 1, :] = np.where(k + 128 > q, -BIG, 0.0)
    c["cm"] = cm.astype(bf)
    negm = np.zeros((128, 16, 8), np.float32)
    for qt in range(16):
        for n in range(8):
            if n >= qt // 2:
                negm[:, qt, n] = -1.0e30
    c["negm"] = negm
    pos = np.arange(L, dtype=np.float32)
    inv_freq = (1.0 / (np.float32(500000.0) ** (np.arange(0, 32, 2, dtype=np.float32) / np.float32(32)))).astype(np.float32)
    ang = pos[None, :] * inv_freq[:, None]
    c["rot_cos"] = np.concatenate([np.cos(ang), np.cos(ang)], 0).astype(np.float32)
    c["rot_sin"] = np.concatenate([np.sin(ang), np.sin(ang)], 0).astype(np.float32)
    return c


def make_in_maps(inputs, ncores=8, batch_of_core=None):
    NW = ncores
    consts = _consts()
    x = inputs["x"]
    p = inputs["p"]
    gl = [inputs[n][l] for l in range(DEPTH) for n in NORMS] + [inputs["final_norm"]]
    gains = np.ascontiguousarray(np.concatenate([g.reshape(KC, 128).T for g in gl], axis=1).astype(np.float32))
    in_maps = []
    for r in range(ncores):
        b = (r // 2) if batch_of_core is None else batch_of_core[r]
        j = r % 2
        m = {}
        m["xT"] = np.ascontiguousarray(x[b, j * NT:(j + 1) * NT, :].T)
        m["pT"] = np.ascontiguousarray(np.transpose(p[:, b, j * NT:(j + 1) * NT, :], (0, 2, 1)))
        m["gains"] = gains
        gs = slice(32 * j, 32 * j + 32)

        def nlay(a):
            a = a.reshape((DEPTH, 2, 16, 64) + a.shape[3:])
            a = np.moveaxis(a, 3, 2)
            return np.ascontiguousarray(a.reshape((DEPTH, 128, 16) + a.shape[4:]))
        m["ssm_ar"] = nlay(inputs["ssm_a_re"][:, gs])
        m["ssm_ai"] = nlay(inputs["ssm_a_im"][:, gs])
        m["ssm_ldt"] = nlay(np.broadcast_to(inputs["ssm_log_dt"][:, gs, None], (DEPTH, 32, 64)))
        m["ssm_bt_re"] = nlay(inputs["ssm_b_re"][:, gs])
        m["ssm_bt_im"] = nlay(inputs["ssm_b_im"][:, gs])
        m["ssm_ct_re"] = nlay(np.transpose(inputs["ssm_c_re"][:, gs], (0, 1, 3, 2)))
        m["ssm_ct_im"] = nlay(np.transpose(inputs["ssm_c_im"][:, gs], (0, 1, 3, 2)))
        m["ssm_dd"] = np.ascontiguousarray(np.transpose(
            inputs["ssm_d"][:, 512 * j:512 * j + 512].reshape(DEPTH, 4, 128), (0, 2, 1)))
        for name, K, M in WSPEC:
            w = inputs[name]
            parts = []
            for (r0, nr, c0, ncw) in wchunks(K, M):
                q = nr // NW
                parts.append(w[:, r0 + r * q:r0 + (r + 1) * q, c0:c0 + ncw].reshape(DEPTH, -1))
            m["w_" + name] = np.ascontiguousarray(np.concatenate(parts, axis=1))
        m.update(consts)
        in_maps.append(m)
    return in_maps


_NC_CACHE = {}


def kernel(**inputs):
    inputs = {k: np.asarray(v) for k, v in inputs.items()}
    if 8 not in _NC_CACHE:
        _NC_CACHE[8] = build_program(8)
    nc = _NC_CACHE[8]
    in_maps = make_in_maps(inputs, 8)
    res = run_bass_kernel_spmd(nc, in_maps, core_ids=list(range(8)))
    out = np.empty((4, L, D), np.float32)
    for r in range(8):
        out[r // 2, (r % 2) * NT:(r % 2 + 1) * NT, :] = res.results[r]["oT"].T
    return out
```

```python
import numpy as np
from contextlib import ExitStack
import ml_dtypes
import concourse.bass as bass
import concourse.mybir as mybir
from concourse.bass_utils import run_bass_kernel_spmd

F32 = mybir.dt.float32
BF16 = mybir.dt.bfloat16
I32 = mybir.dt.int32
AF = mybir.ActivationFunctionType
ALU = mybir.AluOpType
AX = mybir.AxisListType

D = 2048
DFF = 5632
NT = 1024
L = 2048
KC = D // 128
EPS = 1e-6
DEPTH = 2
PLE = 256
BIG = 1.0e4
SCALE = 1.0 / float(np.sqrt(128.0))
ENGS = ["pe", "act", "dve", "pool", "sp"]
TWO_PI = float(2.0 * np.pi)

WSPEC = [
    ("ffn1_w_gate", D, DFF), ("ffn1_w_up", D, DFF), ("ffn1_w_down", DFF, D),
    ("w_in", D, 8192), ("ssm_w_glu", 1024, 2048), ("w_branch_ssm", 1024, 2048),
    ("w_branch_attn", 1024, 2048), ("w_out", D, D),
    ("ffn2_w_gate", D, DFF), ("ffn2_w_up", D, DFF), ("ffn2_w_down", DFF, D),
    ("ple_w_up", PLE, D), ("ple_w_gate", D, D),
]
NORMS = ["ffn1_norm", "mix_norm", "ffn2_norm", "ple_norm"]


class Buf:
    __slots__ = ("name", "w", "r", "pw")

    def __init__(self, name):
        self.name = name
        self.w = None
        self.r = []
        self.pw = []


class TB:
    def __init__(self, t, b):
        self.t = t
        self.b = b

    def __getitem__(self, idx):
        return self.t[idx]


def _b(x):
    return x.b if isinstance(x, TB) else x


class Sched:
    def __init__(self, nc, es, ndma=16):
        self.nc = nc
        self.es = es
        self.prog = {e: [] for e in ENGS}
        self.cnt = {e: 0 for e in ENGS}
        self.known = {e: {} for e in ENGS}
        self.sems = {}
        self.ndma = ndma
        self.dma_val = {}
        self.dma_rr = {e: 0 for e in ENGS}
        self.dry = False
        self.nbuf = 0

    def buf(self, name=None):
        self.nbuf += 1
        return Buf(name or f"b{self.nbuf}")

    def sem(self, key):
        if key not in self.sems:
            self.sems[key] = self.es.enter_context(self.nc.semaphore(key))
        return self.sems[key]

    def _deps(self, eng, reads, writes, pwrites=()):
        toks = []
        for b in reads:
            b = _b(b)
            if b.w is not None:
                toks.append(b.w)
            toks.extend(b.pw)
        for b in writes:
            b = _b(b)
            if b.w is not None:
                toks.append(b.w)
            toks.extend(b.r)
            toks.extend(b.pw)
        for b in pwrites:
            b = _b(b)
            if b.w is not None:
                toks.append(b.w)
            toks.extend(b.r)
        best = {}
        for k, v in toks:
            if v > best.get(k, 0):
                best[k] = v
        waits = []
        kn = self.known[eng]
        for k, v in best.items():
            if eng == "pe" and k == "c_pe":
                continue
            if kn.get(k, 0) >= v:
                continue
            kn[k] = v
            waits.append((k, v))
        return waits

    @staticmethod
    def _compact(lst):
        best = {}
        for k, v in lst:
            if v > best.get(k, 0):
                best[k] = v
        return list(best.items())

    def _commit(self, tok, reads, writes, pwrites=()):
        for b in reads:
            b = _b(b)
            b.r.append(tok)
            if len(b.r) > 48:
                b.r = self._compact(b.r)
        for b in writes:
            b = _b(b)
            b.w = tok
            b.r = []
            b.pw = []
        for b in pwrites:
            b = _b(b)
            b.pw.append(tok)
            if len(b.pw) > 48:
                b.pw = self._compact(b.pw)

    def op(self, eng, fns, reads=(), writes=(), pwrites=()):
        if self.dry:
            return None
        if not isinstance(fns, (list, tuple)):
            fns = [fns]
        waits = self._deps(eng, reads, writes, pwrites)
        self.cnt[eng] += 1
        tok = ("c_" + eng, self.cnt[eng])
        self.prog[eng].append((waits, list(fns), (tok[0], 1)))
        self._commit(tok, reads, writes, pwrites)
        return tok

    def dma(self, eng, fn, reads=(), writes=(), pwrites=(), inc=16):
        if self.dry:
            return None
        i = self.dma_rr[eng]
        self.dma_rr[eng] = (i + 1) % self.ndma
        key = f"d_{eng}_{i}" if inc == 16 else f"x_{eng}_{i}"
        prev = self.dma_val.get(key, 0)
        waits = self._deps(eng, reads, writes, pwrites)
        if prev > 0 and self.known[eng].get(key, 0) < prev:
            self.known[eng][key] = prev
            waits.append((key, prev))
        val = prev + inc
        self.dma_val[key] = val
        tok = (key, val)
        self.prog[eng].append((waits, [fn], (key, inc)))
        self._commit(tok, reads, writes, pwrites)
        return tok

    def barrier(self):
        if self.dry:
            return
        toks = [("c_" + e, self.cnt[e]) for e in ENGS if self.cnt[e] > 0 and e != "pool"]
        toks += [(k, v) for k, v in self.dma_val.items() if "_pool_" not in k]
        for e in ENGS:
            if e != "pool":
                self.wait_all(e, toks)

    def wait_all(self, eng, toks):
        if self.dry:
            return
        waits = []
        for k, v in toks:
            if self.known[eng].get(k, 0) < v:
                self.known[eng][k] = v
                waits.append((k, v))
        if waits:
            self.prog[eng].append((waits, [], None))

    def emit(self):
        nc = self.nc
        for e in ENGS:
            for waits, fns, inc in self.prog[e]:
                for k, v in waits:
                    self.sem(k)
                if inc is not None:
                    self.sem(inc[0])
        with nc.Block() as block:
            def runner(ename):
                def run(eng):
                    for waits, fns, inc in self.prog[ename]:
                        for k, v in waits:
                            eng.wait_ge(self.sems[k], v)
                        ins = None
                        for f in fns:
                            ins = f(eng)
                        if inc is not None:
                            ins.then_inc(self.sems[inc[0]], inc[1])
                return run
            block.tensor(runner("pe"))
            block.scalar(runner("act"))
            block.vector(runner("dve"))
            block.gpsimd(runner("pool"))
            block.sync(runner("sp"))


class Ctx:
    def __init__(self, nc, es):
        self.nc = nc
        self.es = es
        self.S = Sched(nc, es)
        self.nname = 0
        self.pbank = []
        for i in range(8):
            t = es.enter_context(nc.psum_tensor(f"ps{i}", [128, 512], F32))
            self.pbank.append(TB(t, self.S.buf(f"ps{i}")))
        self.prr = 0
        self.NSLOT = 4
        self.wslots = None
        self.wplan = []
        self.wissued = 0
        self.wnext = 0
        self.wrel_count = 0
        self.wphase = []
        self.wlimit = 0

    def jval(self, e, mult):
        if not hasattr(self, "_jv"):
            self._jv = {}
        key = (id(e), mult)
        if key not in self._jv:
            self._jv[key] = e.snap((e.partition_id() % 2) * mult)
        return self._jv[key]

    ARENA_ELEMS = 106240

    def sb(self, shape, dtype, name=None, es=None):
        if not hasattr(self, "arena"):
            self.arena = self.es.enter_context(self.nc.sbuf_tensor("arena", [128, self.ARENA_ELEMS], BF16))
            self.aoff = 0
            self.scopes = set()
        if es is not None and id(es) not in self.scopes:
            self.scopes.add(id(es))
            mark = self.aoff

            def rel(mark=mark, key=id(es)):
                self.aoff = mark
                self.scopes.discard(key)
            es.callback(rel)
        self.nname += 1
        nm = f"{name or 't'}_{self.nname}"
        esz = 2 if dtype == BF16 else 4
        n = 1
        for d in shape[1:]:
            n *= d
        nbf = (n * esz + 1) // 2
        nbf = (nbf + 15) // 16 * 16
        assert self.aoff + nbf <= self.ARENA_ELEMS, (nm, self.aoff, nbf)
        ap = self.arena[0:shape[0], self.aoff:self.aoff + (n * esz) // 2]
        self.aoff += nbf
        self.apeak = max(getattr(self, "apeak", 0), self.aoff)
        if dtype != BF16:
            ap = ap.bitcast(dtype)
        if len(shape) > 2:
            names = " ".join(f"d{i}" for i in range(1, len(shape)))
            ap = ap.rearrange(f"p ({names}) -> p {names}", **{f"d{i}": shape[i] for i in range(1, len(shape) - 1)})
        return TB(ap, self.S.buf(nm))

    def dram(self, name, shape, dtype):
        return TB(self.nc.dram_tensor(name, shape, dtype).ap(), self.S.buf(name))

    def psum(self):
        b = self.pbank[self.prr]
        self.prr = (self.prr + 1) % 6
        return b

    def wphase_begin(self):
        starts = self.wphase
        cur = self.wnext
        nxt = [s for s in starts if s > cur]
        self.wlimit = nxt[0] if nxt else len(self.wplan)
        assert self.wissued == self.wnext, (self.wissued, self.wnext)
        self.wbase = self.wnext
        for _ in range(self.NSLOT):
            self._wissue()

    def _wissue(self):
        S = self.S
        j = self.wissued
        if j >= self.wlimit:
            return
        w2, k2, kc2, c2, n2 = self.wplan[j]
        ci, lr, lc = w2.find(k2, kc2, c2, n2)
        ch = w2.full[ci]
        sl = self.wslots[(j - self.wbase) % self.NSLOT]
        dst = sl.t[:, 0:kc2 * n2].rearrange("p (k c) -> p k c", k=kc2)
        src = ch.t[lr:lr + kc2 * 128, lc:lc + n2].rearrange("(k p) c -> p k c", p=128)
        S.dma("sp", (lambda d, s: (lambda e: e.dma_start(out=d, in_=s)))(dst, src),
              reads=(ch,), writes=(sl,))
        self.wissued += 1

    def wget(self, w, k0, kc, c0, ncols):
        S = self.S
        idx = self.wnext
        self.wnext += 1
        assert self.wplan[idx][1:] == (k0, kc, c0, ncols) and self.wplan[idx][0] is w, (idx, self.wplan[idx][1:], (k0, kc, c0, ncols))
        assert idx < self.wissued, (idx, self.wissued)
        sl = self.wslots[(idx - self.wbase) % self.NSLOT]
        return TB(sl.t[:, 0:kc * ncols].rearrange("p (k c) -> p k c", k=kc), sl.b)

    def wrel(self, n=1):
        if self.S.dry:
            return
        for _ in range(n):
            self._wissue()


def wchunks(K, M):
    if K > 2048:
        return [(r0, min(1024, K - r0), 0, M) for r0 in range(0, K, 1024)]
    if K == 2048:
        return [(0, K, c0, min(1024, M - c0)) for c0 in range(0, M, 1024)]
    return [(0, K, 0, M)]


class WMat:
    def __init__(self, C, tag, K, M, nw):
        self.K, self.M, self.nw = K, M, nw
        self.chunks = wchunks(K, M)
        self.bounce, self.half, self.full, self.off = [], [], [], []
        o = 0
        for i, (r0, nr, c0, ncw) in enumerate(self.chunks):
            self.off.append(o)
            o += (nr // nw) * ncw
            self.bounce.append(C.dram(f"wb_{tag}_{i}", [nr // nw, ncw], BF16))
            self.half.append(C.dram(f"wh_{tag}_{i}", [nr // 2, ncw], BF16) if nw == 8 else None)
            self.full.append(C.dram(f"wf_{tag}_{i}", [nr, ncw], BF16))
        self.per_rank = o

    def find(self, k0, kc, c0, ncols):
        for i, (r0, nr, cc0, ncw) in enumerate(self.chunks):
            if r0 <= k0 and k0 + kc * 128 <= r0 + nr and cc0 <= c0 and c0 + ncols <= cc0 + ncw:
                return i, k0 - r0, c0 - cc0
        raise AssertionError((k0, kc, c0, ncols))


def mm(out, lhsT, rhs, start, stop):
    return lambda e: e.matmul(out, lhsT, rhs, start=start, stop=stop)


def tsl(th):
    return slice(th * 512, th * 512 + 512)


def emit_rmsnorm(C, T, h, gain_col, out):
    S = C.S
    for th in range(NT // 512):
        ts = tsl(th)
        pt = C.psum()
        for kc in range(KC):
            sq = T.sq[kc % 2]
            S.op("act", (lambda o, i: (lambda e: e.activation(o, i, AF.Square)))(sq[:, :], h[:, kc, ts]),
                 reads=(h,), writes=(sq,))
            S.op("pe", mm(pt[:, :], C.ones[:, :], sq[:, :], kc == 0, kc == KC - 1),
                 reads=(sq, C.ones), writes=(pt,))
        rstd = T.rstd
        S.op("act", (lambda p_: (lambda e: e.activation(rstd[:, :], p_, AF.Sqrt, bias=C.eps_ap[:, 0:1], scale=1.0 / D)))(pt[:, :]),
             reads=(pt, C.eps_ap), writes=(rstd,))
        S.op("dve", lambda e: e.reciprocal(rstd[:, :], rstd[:, :]), reads=(rstd,), writes=(rstd,))
        for kc in range(KC):
            S.op("dve", (lambda o, i, g: (lambda e: e.scalar_tensor_tensor(o, i, g, rstd[:, :], ALU.mult, ALU.mult)))(
                out[:, kc, ts], h[:, kc, ts], C.gains[:, gain_col + kc:gain_col + kc + 1]),
                reads=(h, rstd, C.gains), pwrites=(out,))


def emit_ffn(C, T, h, xn, wg, wu, wd):
    S = C.S
    nchunk = DFF // 512
    for ch in range(nchunk):
        gw = C.wget(wg, 0, KC, ch * 512, 512)
        uw = C.wget(wu, 0, KC, ch * 512, 512)
        dw = C.wget(wd, ch * 512, 4, 0, D)
        if S.dry:
            continue
        hid = T.hid[ch % 2]
        for mi in range(4):
            for th in range(NT // 512):
                ts = tsl(th)
                pg = C.psum()
                S.op("pe", [mm(pg[:, :], gw[:, kc, mi * 128:mi * 128 + 128], xn[:, kc, ts], kc == 0, kc == KC - 1)
                            for kc in range(KC)], reads=(gw, xn), writes=(pg,))
                pu = C.psum()
                S.op("pe", [mm(pu[:, :], uw[:, kc, mi * 128:mi * 128 + 128], xn[:, kc, ts], kc == 0, kc == KC - 1)
                            for kc in range(KC)], reads=(uw, xn), writes=(pu,))
                sg = T.sg[(mi * 2 + th) % 2]
                S.op("act", (lambda o, i: (lambda e: e.activation(o, i, AF.Silu)))(sg[:, :], pg[:, :]),
                     reads=(pg,), writes=(sg,))
                S.op("dve", (lambda o, a, b: (lambda e: e.tensor_tensor(o, a, b, ALU.mult)))(
                    hid[:, mi, ts], pu[:, :], sg[:, :]), reads=(pu, sg), pwrites=(hid,))
        C.wrel(2)
        for mo in range(KC):
            for th in range(NT // 512):
                ts = tsl(th)
                pd = C.psum()
                S.op("pe", [mm(pd[:, :], dw[:, mi, mo * 128:mo * 128 + 128], hid[:, mi, ts], mi == 0, mi == 3)
                            for mi in range(4)], reads=(dw, hid), writes=(pd,))
                S.op("dve", (lambda o, a: (lambda e: e.scalar_tensor_tensor(o, a, 0.5, o, ALU.mult, ALU.add)))(
                    h[:, mo, ts], pd[:, :]), reads=(pd,), pwrites=(h,))
        C.wrel(1)


def emit_linear(C, w, K, c0, ncols, xs, epi, toks):
    S = C.S
    kc_n = K // 128
    bc = min(8192 // kc_n, ncols)
    for cb in range(ncols // bc):
        wb = C.wget(w, 0, kc_n, c0 + cb * bc, bc)
        if not S.dry:
            for mi in range(bc // 128):
                mo = (cb * bc) // 128 + mi
                for ti, ts in enumerate(toks):
                    ps = C.psum()
                    ops = []
                    rd = [wb]
                    for kc in range(kc_n):
                        ap, tb = xs(kc, ts)
                        ops.append(mm(ps[:, :], wb[:, kc, mi * 128:mi * 128 + 128], ap, kc == 0, kc == kc_n - 1))
                        if tb not in rd:
                            rd.append(tb)
                    S.op("pe", ops, reads=tuple(rd), writes=(ps,))
                    epi(mo, ti, ps)
        C.wrel(1)


class TokTiles:
    def __init__(self, C, es):
        S = C.S
        self.h = C.sb([128, KC, NT], F32, "h", es)
        self.xn = C.sb([128, KC, NT], BF16, "xn", es)
        C.wslots = [C.sb([128, 8192], BF16, f"wslot{i}", es) for i in range(C.NSLOT)]
        self.tarena = C.sb([128, 12288], BF16, "tarena", es)
        a = self.tarena.t
        def view(off, n, pat=None, **kw):
            ap = a[:, off:off + n]
            if pat:
                ap = ap.rearrange(pat, **kw)
            return TB(ap, S.buf())
        self.hid = [view(0, 4096, "p (m t) -> p m t", m=4), view(4096, 4096, "p (m t) -> p m t", m=4)]
        self.zst = [view(8192 + i * 512, 512) for i in range(4)]
        self.ys = view(0, 4096, "p (k t) -> p k t", k=8)
        self.y_a = view(4096, 4096, "p (k t) -> p k t", k=8)
        self.ya = view(8192, 4096, "p (k t) -> p k t", k=8)
        self.pb = view(0, 2048, "p (k t) -> p k t", k=2)
        self.sq = [C.sb([128, 512], BF16, f"sq{i}", es) for i in range(2)]
        self.rstd = C.sb([128, 512], F32, "rstd", es)
        self.sg = [C.sb([128, 512], BF16, f"sg{i}", es) for i in range(2)]
        self.sgs = [C.sb([128, 512], BF16, f"sgs{i}", es) for i in range(2)]
        self.sga = [C.sb([128, 512], BF16, f"sga{i}", es) for i in range(2)]
        self.t1 = [C.sb([128, 512], BF16, f"t1{i}", es) for i in range(2)]
        self.f32tmp = [C.sb([128, 512], F32, f"f32tmp{i}", es) for i in range(1)]
        self.pstage = C.sb([128, 2, 256], F32, "pstage", es)


def cp(eng_kind, o, i):
    if eng_kind == "act":
        return lambda e: e.activation(o, i, AF.Copy)
    return lambda e: e.tensor_copy(o, i)


def seg_a(C, T, l):
    S = C.S
    wf = C.wf[l]
    emit_rmsnorm(C, T, T.h, (l * 4 + 0) * KC, T.xn)
    emit_ffn(C, T, T.h, T.xn, wf["ffn1_w_gate"], wf["ffn1_w_up"], wf["ffn1_w_down"])
    emit_rmsnorm(C, T, T.h, (l * 4 + 1) * KC, T.xn)
    zx_in = C.zx_in[l]
    sg_d = C.sg_d[l]
    cnt = [0]

    def xs(kc, ts):
        return T.xn[:, kc, ts], T.xn

    def epi(mo, ti, ps):
        st = T.zst[cnt[0] % 4]
        eng = "act" if cnt[0] % 2 == 0 else "dve"
        cnt[0] += 1
        if mo < 32:
            S.op(eng, cp(eng, st[:, :], ps[:, :]), reads=(ps,), writes=(st,))
            zp = zx_in[mo // 8]
            S.dma("sp", (lambda o, i: (lambda e: e.dma_start(out=o, in_=i)))(
                zp.t[(mo % 8) * 128:(mo % 8 + 1) * 128, ti * 512:(ti + 1) * 512], st[:, :]),
                reads=(st,), pwrites=(zp,))
        else:
            S.op("act", (lambda o, i: (lambda e: e.activation(o, i, AF.Sigmoid)))(st[:, :], ps[:, :]),
                 reads=(ps,), writes=(st,))
            S.dma("sp", (lambda o, i: (lambda e: e.dma_start(out=o, in_=i)))(
                sg_d.t[(mo - 32) * 128:(mo - 31) * 128, ti * 512:(ti + 1) * 512], st[:, :]),
                reads=(st,), pwrites=(sg_d,))

    emit_linear(C, wf["w_in"], D, 0, 8192, xs, epi, [tsl(0), tsl(1)])
    for part in range(4):
        C.coll_pair(zx_in[part], C.zx_out[l][part], pop=(part == 3))
    S.dma("sp", lambda e: e.dma_start(out=C.hsp.t.rearrange("(k p) t -> p k t", p=128), in_=T.h[:, :, :]),
          reads=(T.h,), writes=(C.hsp,))


def seg_c(C, T, l, last):
    S = C.S
    wf = C.wf[l]
    yxs, yxa = C.yx_out[l]
    sg_d = C.sg_d[l]
    S.dma("sp", lambda e: e.dma_start(out=T.h[:, :, :], in_=C.hsp.t.rearrange("(k p) t -> p k t", p=128)),
          reads=(C.hsp,), writes=(T.h,))
    for hf in range(2):
        tcol = hf * 512
        for jj in range(2):
            for (dst, yx) in ((T.ys, yxs), (T.ya, yxa)):
                r0 = jj * 512

                def ld(e, dst=dst, r0=r0, jj=jj, tcol=tcol, yx=yx):
                    j1024 = C.jval(e, 1024)
                    return e.dma_start(out=dst[:, jj * 4:(jj + 1) * 4, :],
                                       in_=yx.t[r0:r0 + 512, tcol:tcol + 1536][:, bass.ds(j1024, 512)].rearrange("(c p) t -> p c t", p=128))
                S.dma("act", ld, reads=(yx,), pwrites=(dst,))
        for kc in range(8):
            x = T.ys[:, kc, :]
            tf = T.f32tmp[0]
            S.op("act", (lambda x: (lambda e: e.activation(tf[:, :], x, AF.Square)))(x), reads=(T.ys,), writes=(tf,))
            S.op("dve", lambda e: e.tensor_scalar(tf[:, :], tf[:, :], 0.044715, 1.0, ALU.mult, ALU.add),
                 reads=(tf,), writes=(tf,))
            S.op("dve", (lambda x: (lambda e: e.tensor_tensor(tf[:, :], tf[:, :], x, ALU.mult)))(x),
                 reads=(tf, T.ys), writes=(tf,))
            S.op("act", lambda e: e.activation(tf[:, :], tf[:, :], AF.Sigmoid, scale=1.5957691216057308),
                 reads=(tf,), writes=(tf,))
            S.op("dve", (lambda x: (lambda e: e.tensor_tensor(x, x, tf[:, :], ALU.mult)))(x),
                 reads=(tf,), pwrites=(T.ys,))
        wa = C.wget(wf["ssm_w_glu"], 0, 8, 0, 1024)
        wb = C.wget(wf["ssm_w_glu"], 0, 8, 1024, 1024)
        if not S.dry:
            for i in range(8):
                pa = C.psum()
                S.op("pe", [mm(pa[:, :], wa[:, kc, i * 128:i * 128 + 128], T.ys[:, kc, :], kc == 0, kc == 7)
                            for kc in range(8)], reads=(wa, T.ys), writes=(pa,))
                pb_ = C.psum()
                S.op("pe", [mm(pb_[:, :], wb[:, kc, i * 128:i * 128 + 128], T.ys[:, kc, :], kc == 0, kc == 7)
                            for kc in range(8)], reads=(wb, T.ys), writes=(pb_,))
                sg = T.sg[i % 2]
                S.op("act", (lambda o, i_: (lambda e: e.activation(o, i_, AF.Sigmoid)))(sg[:, :], pb_[:, :]),
                     reads=(pb_,), writes=(sg,))
                S.op("dve", (lambda o, a, b: (lambda e: e.tensor_tensor(o, a, b, ALU.mult)))(
                    T.y_a[:, i, :], pa[:, :], sg[:, :]), reads=(pa, sg), pwrites=(T.y_a,))
        C.wrel(2)
        for half in range(2):
            wA = C.wget(wf["w_branch_ssm"], 0, 8, half * 1024, 1024)
            wB = C.wget(wf["w_branch_attn"], 0, 8, half * 1024, 1024)
            if not S.dry:
                for mi in range(8):
                    mo = half * 8 + mi
                    sgs, sga = T.sgs[mo % 2], T.sga[mo % 2]
                    S.dma("sp", (lambda o, i: (lambda e: e.dma_start(out=o, in_=i)))(
                        sgs[:, :], sg_d.t[mo * 128:(mo + 1) * 128, tcol:tcol + 512]), reads=(sg_d,), writes=(sgs,))
                    S.dma("sp", (lambda o, i: (lambda e: e.dma_start(out=o, in_=i)))(
                        sga[:, :], sg_d.t[(16 + mo) * 128:(17 + mo) * 128, tcol:tcol + 512]), reads=(sg_d,), writes=(sga,))
                    p1 = C.psum()
                    S.op("pe", [mm(p1[:, :], wA[:, kc, mi * 128:mi * 128 + 128], T.y_a[:, kc, :], kc == 0, kc == 7)
                                for kc in range(8)], reads=(wA, T.y_a), writes=(p1,))
                    p2 = C.psum()
                    S.op("pe", [mm(p2[:, :], wB[:, kc, mi * 128:mi * 128 + 128], T.ya[:, kc, :], kc == 0, kc == 7)
                                for kc in range(8)], reads=(wB, T.ya), writes=(p2,))
                    t1, t2 = T.t1[0], T.t1[1]
                    S.op("dve", (lambda a, b: (lambda e: e.tensor_tensor(t1[:, :], a, b, ALU.mult)))(p1[:, :], sgs[:, :]),
                         reads=(p1, sgs), writes=(t1,))
                    S.op("dve", (lambda a, b: (lambda e: e.tensor_tensor(t2[:, :], a, b, ALU.mult)))(p2[:, :], sga[:, :]),
                         reads=(p2, sga), writes=(t2,))
                    S.op("dve", (lambda o: (lambda e: e.tensor_tensor(o, t1[:, :], t2[:, :], ALU.add)))(T.xn[:, mo, 0:512]),
                         reads=(t1, t2), pwrites=(T.xn,))
            C.wrel(2)

        def xs(kc, ts):
            return T.xn[:, kc, 0:512], T.xn

        def epi(mo, ti, ps, tcol=tcol):
            S.op("dve", (lambda o, a: (lambda e: e.tensor_tensor(o, a, o, ALU.add)))(
                T.h[:, mo, tcol:tcol + 512], ps[:, :]), reads=(ps,), pwrites=(T.h,))
        emit_linear(C, wf["w_out"], D, 0, D, xs, epi, [slice(0, 512)])
    S.barrier()
    emit_rmsnorm(C, T, T.h, (l * 4 + 2) * KC, T.xn)
    emit_ffn(C, T, T.h, T.xn, wf["ffn2_w_gate"], wf["ffn2_w_up"], wf["ffn2_w_down"])
    S.barrier()
    emit_rmsnorm(C, T, T.h, (l * 4 + 3) * KC, T.xn)
    for qd in range(4):
        S.dma("sp", (lambda qd: (lambda e: e.dma_start(
            out=T.pstage[:, :, :], in_=C.pT.t[l, :, qd * 256:(qd + 1) * 256].rearrange("(k p) t -> p k t", p=128))))(qd),
            reads=(C.pT,), writes=(T.pstage,))
        S.op("dve", (lambda qd: (lambda e: e.tensor_copy(T.pb[:, :, qd * 256:(qd + 1) * 256], T.pstage[:, :, :])))(qd),
             reads=(T.pstage,), pwrites=(T.pb,))
    for cb in range(4):
        wpg = C.wget(wf["ple_w_gate"], 0, KC, cb * 512, 512)
        wpu = C.wget(wf["ple_w_up"], 0, 2, cb * 512, 512)
        if not S.dry:
            for mi in range(4):
                mo = cb * 4 + mi
                for th in range(2):
                    ts = tsl(th)
                    pg = C.psum()
                    S.op("pe", [mm(pg[:, :], wpg[:, kc, mi * 128:mi * 128 + 128], T.xn[:, kc, ts], kc == 0, kc == KC - 1)
                                for kc in range(KC)], reads=(wpg, T.xn), writes=(pg,))
                    pu = C.psum()
                    S.op("pe", [mm(pu[:, :], wpu[:, kc, mi * 128:mi * 128 + 128], T.pb[:, kc, ts], kc == 0, kc == 1)
                                for kc in range(2)], reads=(wpu, T.pb), writes=(pu,))
                    sg = T.f32tmp[0]
                    S.op("act", (lambda i_: (lambda e: e.activation(sg[:, :], i_, AF.Sigmoid)))(pg[:, :]),
                         reads=(pg,), writes=(sg,))
                    S.op("dve", (lambda a: (lambda e: e.tensor_tensor(sg[:, :], a, sg[:, :], ALU.mult)))(pu[:, :]),
                         reads=(pu, sg), writes=(sg,))
                    S.op("dve", (lambda o: (lambda e: e.tensor_tensor(o, o, sg[:, :], ALU.add)))(T.h[:, mo, ts]),
                         reads=(sg,), pwrites=(T.h,))
        C.wrel(2)
    S.barrier()
    if last:
        emit_rmsnorm(C, T, T.h, DEPTH * 4 * KC, T.h)
        tok = S.dma("sp", lambda e: e.dma_start(out=C.oT.t.rearrange("(k p) t -> p k t", p=128), in_=T.h[:, :, :]),
                    reads=(T.h,), writes=(C.oT,))
        if not S.dry:
            S.wait_all("sp", [tok])


def tt(o, a, b, op):
    return lambda e: e.tensor_tensor(o, a, b, op)


def ssm_prep(C, l):
    S = C.S
    P = C.ssm_in
    with ExitStack() as es:
        def sm(name, shape=(128, 16), dt=F32):
            return C.sb(list(shape), dt, name, es)

        def ew(eng, fn, reads, writes):
            S.op(eng, fn, reads=reads, writes=writes)

        def load(name, shape):
            t = sm(name, shape)
            src = P[name].t[l]
            S.dma("sp", (lambda o, i: (lambda e: e.dma_start(out=o, in_=i)))(t.t[tuple(slice(None) for _ in shape)], src),
                  reads=(P[name],), writes=(t,))
            return t
        ar_in = load("ssm_ar", (128, 16))
        ai_in = load("ssm_ai", (128, 16))
        ldt = load("ssm_ldt", (128, 16))
        ctre = load("ssm_ct_re", (128, 16, 16))
        ctim = load("ssm_ct_im", (128, 16, 16))
        btre = load("ssm_bt_re", (128, 16, 16))
        btim = load("ssm_bt_im", (128, 16, 16))
        A = lambda t: t[:, :]
        dt_ = sm("dt")
        ew("act", lambda e: e.activation(A(dt_), A(ldt), AF.Exp), (ldt,), (dt_,))
        ar, ai, mag = sm("ar"), sm("ai"), sm("mag")
        ew("dve", tt(A(ar), A(ar_in), A(dt_), ALU.mult), (ar_in, dt_), (ar,))
        ew("dve", tt(A(ai), A(ai_in), A(dt_), ALU.mult), (ai_in, dt_), (ai,))
        ew("act", lambda e: e.activation(A(mag), A(ar), AF.Exp), (ar,), (mag,))

        def sin_of(x, offset, name):
            xo, y, kf, r, m = sm(name + "xo"), sm(name + "y"), sm(name + "kf"), sm(name + "r"), sm(name + "m")
            ki = sm(name + "ki", (128, 16), I32)
            ew("dve", lambda e: e.tensor_scalar(A(xo), A(x), float(offset), None, ALU.add), (x,), (xo,))
            ew("dve", lambda e: e.tensor_scalar(A(y), A(xo), 1.0 / TWO_PI, 0.5, ALU.mult, ALU.add), (xo,), (y,))
            ew("dve", lambda e: e.tensor_copy(A(ki), A(y)), (y,), (ki,))
            ew("dve", lambda e: e.tensor_copy(A(kf), A(ki)), (ki,), (kf,))
            ew("dve", lambda e: e.scalar_tensor_tensor(A(r), A(kf), -TWO_PI, A(xo), ALU.mult, ALU.add), (kf, xo), (r,))
            ew("dve", lambda e: e.tensor_scalar(A(m), A(r), float(np.pi), None, ALU.is_gt), (r,), (m,))
            ew("dve", lambda e: e.scalar_tensor_tensor(A(r), A(m), -TWO_PI, A(r), ALU.mult, ALU.add), (m, r), (r,))
            ew("dve", lambda e: e.tensor_scalar(A(m), A(r), float(-np.pi), None, ALU.is_lt), (r,), (m,))
            ew("dve", lambda e: e.scalar_tensor_tensor(A(r), A(m), TWO_PI, A(r), ALU.mult, ALU.add), (m, r), (r,))
            s = sm(name + "s")
            ew("act", lambda e: e.activation(A(s), A(r), AF.Sin), (r,), (s,))
            return s
        sn = sin_of(ai, 0.0, "sn")
        cs = sin_of(ai, np.pi / 2.0, "cs")
        lr, li = sm("lr"), sm("li")
        ew("dve", tt(A(lr), A(mag), A(cs), ALU.mult), (mag, cs), (lr,))
        ew("dve", tt(A(li), A(mag), A(sn), ALU.mult), (mag, sn), (li,))
        den, t0, t1_, nr, cr, ci = sm("den"), sm("t0"), sm("t1"), sm("nr"), sm("cr"), sm("ci")
        ew("dve", tt(A(den), A(ar_in), A(ar_in), ALU.mult), (ar_in,), (den,))
        ew("dve", tt(A(t0), A(ai_in), A(ai_in), ALU.mult), (ai_in,), (t0,))
        ew("dve", tt(A(den), A(den), A(t0), ALU.add), (den, t0), (den,))
        ew("dve", lambda e: e.reciprocal(A(den), A(den)), (den,), (den,))
        ew("dve", lambda e: e.tensor_scalar(A(nr), A(lr), -1.0, None, ALU.add), (lr,), (nr,))
        ew("dve", tt(A(t0), A(nr), A(ar_in), ALU.mult), (nr, ar_in), (t0,))
        ew("dve", tt(A(t1_), A(li), A(ai_in), ALU.mult), (li, ai_in), (t1_,))
        ew("dve", tt(A(t0), A(t0), A(t1_), ALU.add), (t0, t1_), (t0,))
        ew("dve", tt(A(cr), A(t0), A(den), ALU.mult), (t0, den), (cr,))
        ew("dve", tt(A(t0), A(li), A(ar_in), ALU.mult), (li, ar_in), (t0,))
        ew("dve", tt(A(t1_), A(nr), A(ai_in), ALU.mult), (nr, ai_in), (t1_,))
        ew("dve", tt(A(t0), A(t0), A(t1_), ALU.subtract), (t0, t1_), (t0,))
        ew("dve", tt(A(ci), A(t0), A(den), ALU.mult), (t0, den), (ci,))
        B3 = [128, 16, 16]
        bbre, bbim, u0, u1 = sm("bbre", B3), sm("bbim", B3), sm("u0", B3), sm("u1", B3)
        crb = cr.t[:, :, None].to_broadcast(B3)
        cib = ci.t[:, :, None].to_broadcast(B3)
        F3 = lambda t: t[:, :, :]
        ew("dve", tt(F3(u0), crb, F3(btre), ALU.mult), (cr, btre), (u0,))
        ew("dve", tt(F3(u1), cib, F3(btim), ALU.mult), (ci, btim), (u1,))
        ew("dve", tt(F3(bbre), F3(u0), F3(u1), ALU.subtract), (u0, u1), (bbre,))
        ew("dve", tt(F3(u0), crb, F3(btim), ALU.mult), (cr, btim), (u0,))
        ew("dve", tt(F3(u1), cib, F3(btre), ALU.mult), (ci, btre), (u1,))
        ew("dve", tt(F3(bbim), F3(u0), F3(u1), ALU.add), (u0, u1), (bbim,))
        powre, powim = sm("powre", (128, 16, 9)), sm("powim", (128, 16, 9))
        ew("dve", lambda e: e.memset(powre[:, :, 0:1], 1.0), (), (powre,))
        ew("dve", lambda e: e.memset(powim[:, :, 0:1], 0.0), (), (powim,))
        ew("dve", lambda e: e.tensor_copy(powre[:, :, 1], A(lr)), (lr, powre), (powre,))
        ew("dve", lambda e: e.tensor_copy(powim[:, :, 1], A(li)), (li, powim), (powim,))
        for j in range(2, 9):
            ew("dve", tt(A(t0), powre[:, :, j - 1], A(lr), ALU.mult), (powre, lr), (t0,))
            ew("dve", tt(A(t1_), powim[:, :, j - 1], A(li), ALU.mult), (powim, li), (t1_,))
            ew("dve", tt(powre[:, :, j], A(t0), A(t1_), ALU.subtract), (t0, t1_, powre), (powre,))
            ew("dve", tt(A(t0), powre[:, :, j - 1], A(li), ALU.mult), (powre, li), (t0,))
            ew("dve", tt(A(t1_), powim[:, :, j - 1], A(lr), ALU.mult), (powim, lr), (t1_,))
            ew("dve", tt(powim[:, :, j], A(t0), A(t1_), ALU.add), (t0, t1_, powim), (powim,))
        lam8 = sm("lam8", (128, 16, 2))
        ew("dve", lambda e: e.tensor_copy(lam8[:, :, 0], powre[:, :, 8]), (powre,), (lam8,))
        ew("dve", lambda e: e.tensor_copy(lam8[:, :, 1], powim[:, :, 8]), (powim, lam8), (lam8,))
        S.dma("sp", lambda e: e.dma_start(out=C.lam8_d[l].t, in_=lam8[:, :, :]), reads=(lam8,), writes=(C.lam8_d[l],))
        prre, prim = sm("prre", (128, 16, 8)), sm("prim", (128, 16, 8))
        for tau in range(8):
            ew("dve", (lambda tau: (lambda e: e.tensor_copy(prre[:, :, tau], powre[:, :, 7 - tau])))(tau), (powre, prre), (prre,))
            ew("dve", (lambda tau: (lambda e: e.tensor_copy(prim[:, :, tau], powim[:, :, 7 - tau])))(tau), (powim, prim), (prim,))
        C4 = [128, 16, 9, 16]
        clre, clim, v0, v1 = sm("clre", C4), sm("clim", C4), sm("v0", C4), sm("v1", C4)
        F4 = lambda t: t[:, :, :, :]
        ctre_b = ctre.t[:, :, None, :].to_broadcast(C4)
        ctim_b = ctim.t[:, :, None, :].to_broadcast(C4)
        pre_b = powre.t[:, :, :, None].to_broadcast(C4)
        pim_b = powim.t[:, :, :, None].to_broadcast(C4)
        ew("dve", tt(F4(v0), ctre_b, pre_b, ALU.mult), (ctre, powre), (v0,))
        ew("dve", tt(F4(v1), ctim_b, pim_b, ALU.mult), (ctim, powim), (v1,))
        ew("dve", tt(F4(clre), F4(v0), F4(v1), ALU.subtract), (v0, v1), (clre,))
        ew("dve", tt(F4(v0), ctre_b, pim_b, ALU.mult), (ctre, powim), (v0,))
        ew("dve", tt(F4(v1), ctim_b, pre_b, ALU.mult), (ctim, powre), (v1,))
        ew("dve", tt(F4(v0), F4(v0), F4(v1), ALU.add), (v0, v1), (v0,))
        ew("dve", lambda e: e.tensor_scalar(F4(clim), F4(v0), -1.0, None, ALU.mult), (v0,), (clim,))
        w4re, w4im = sm("w4re", (128, 16, 128), BF16), sm("w4im", (128, 16, 128), BF16)
        ew("act", lambda e: e.activation(w4re.t[:, :, :].rearrange("p g (t q) -> p g t q", t=8), clre[:, :, 1:9, :], AF.Copy), (clre,), (w4re,))
        ew("act", lambda e: e.activation(w4im.t[:, :, :].rearrange("p g (t q) -> p g t q", t=8), clim[:, :, 1:9, :], AF.Copy), (clim,), (w4im,))
        S.dma("sp", lambda e: e.dma_start(out=C.w4re_d[l].t, in_=w4re[:, :, :]), reads=(w4re,), writes=(C.w4re_d[l],))
        S.dma("sp", lambda e: e.dma_start(out=C.w4im_d[l].t, in_=w4im[:, :, :]), reads=(w4im,), writes=(C.w4im_d[l],))
        T4 = [128, 16, 8, 16]
        t2re, t2im = sm("t2re", T4), sm("t2im", T4)
        w0, w1 = v0.t[:, :, 0:8, :], v1.t[:, :, 0:8, :]
        prre_b = prre.t[:, :, :, None].to_broadcast(T4)
        prim_b = prim.t[:, :, :, None].to_broadcast(T4)
        bbre_b = bbre.t[:, :, None, :].to_broadcast(T4)
        bbim_b = bbim.t[:, :, None, :].to_broadcast(T4)
        ew("dve", tt(w0, prre_b, bbre_b, ALU.mult), (prre, bbre), (v0,))
        ew("dve", tt(w1, prim_b, bbim_b, ALU.mult), (prim, bbim), (v1,))
        ew("dve", tt(F4(t2re), w0, w1, ALU.subtract), (v0, v1), (t2re,))
        ew("dve", tt(w0, prre_b, bbim_b, ALU.mult), (prre, bbim), (v0,))
        ew("dve", tt(w1, prim_b, bbre_b, ALU.mult), (prim, bbre), (v1,))
        ew("dve", tt(F4(t2im), w0, w1, ALU.add), (v0, v1), (t2im,))
        w2re, w2im = sm("w2re", (128, 32, 64), BF16), sm("w2im", (128, 32, 64), BF16)
        for (src, dst) in ((t2re, w2re), (t2im, w2im)):
            for gh in range(2):
                for gb in range(2):
                    ps = C.psum()
                    ops = []
                    for gi in range(8):
                        gp = gb * 8 + gi
                        ops.append((lambda o, i, idn: (lambda e: e.transpose(o, i, idn)))(
                            ps[:, gi * 64:(gi + 1) * 64],
                            src.t[64 * gh:64 * gh + 64, gp, :, :].rearrange("p a b -> p (a b)"),
                            C.ident[64 * gh:64 * gh + 64, 64 * gh:64 * gh + 64]))
                    S.op("pe", ops, reads=(src, C.ident), writes=(ps,))
                    g0 = gh * 16 + gb * 8
                    ew("act", (lambda dst, g0, ps: (lambda e: e.activation(
                        dst.t[:, g0:g0 + 8, :], ps[:, :].rearrange("p (g n) -> p g n", g=8), AF.Copy)))(dst, g0, ps),
                        (ps,), (dst,))
        S.dma("sp", lambda e: e.dma_start(out=C.w2re_d[l].t, in_=w2re[:, :, :]), reads=(w2re,), writes=(C.w2re_d[l],))
        S.dma("sp", lambda e: e.dma_start(out=C.w2im_d[l].t, in_=w2im[:, :, :]), reads=(w2im,), writes=(C.w2im_d[l],))
        w1sb = sm("w1sb", (128, 32, 128), BF16)
        sets = []
        for i in range(2):
            st = dict(bre=sm(f"bwre{i}", (128, 240)), bim=sm(f"bwim{i}", (128, 240)),
                      cre=sm(f"cwre{i}", (128, 240)), cim=sm(f"cwim{i}", (128, 240)))
            for k in st:
                ew("dve", (lambda t: (lambda e: e.memset(t[:, :], 0.0)))(st[k]), (), (st[k],))
            sets.append(st)
        for gp in range(16):
            st = sets[gp % 2]
            ew("dve", (lambda st, gp: (lambda e: e.tensor_copy(st["bre"][:, 112:128], bbre[:, gp, :])))(st, gp), (bbre, st["bre"]), (st["bre"],))
            ew("dve", (lambda st, gp: (lambda e: e.tensor_copy(st["bim"][:, 112:128], bbim[:, gp, :])))(st, gp), (bbim, st["bim"]), (st["bim"],))
            ew("act", (lambda st, gp: (lambda e: e.activation(st["cre"].t[:, 112:240].rearrange("p (j q) -> p j q", j=8), clre[:, gp, 0:8, :], AF.Copy)))(st, gp), (clre, st["cre"]), (st["cre"],))
            ew("act", (lambda st, gp: (lambda e: e.activation(st["cim"].t[:, 112:240].rearrange("p (j q) -> p j q", j=8), clim[:, gp, 0:8, :], AF.Copy)))(st, gp), (clim, st["cim"]), (st["cim"],))
            for gh in range(2):
                ps = C.psum()
                ops = []
                rows = slice(64 * gh, 64 * gh + 64)
                for tau in range(8):
                    for ri, (bk, ck) in enumerate((("bre", "cre"), ("bim", "cim"))):
                        ops.append(mm(ps[:, 0:128], st[bk][rows, 112 - 16 * tau:240 - 16 * tau],
                                      st[ck][rows, (7 - tau) * 16:(7 - tau) * 16 + 128],
                                      tau == 0 and ri == 0, tau == 7 and ri == 1))
                S.op("pe", ops, reads=(st["bre"], st["bim"], st["cre"], st["cim"]), writes=(ps,))
                g = gh * 16 + gp
                S.op("dve", (lambda g, ps: (lambda e: e.tensor_copy(w1sb[:, g, :], ps[:, 0:128])))(g, ps),
                     reads=(ps,), pwrites=(w1sb,))
        S.dma("sp", lambda e: e.dma_start(out=C.w1_d[l].t, in_=w1sb[:, :, :]), reads=(w1sb,), writes=(C.w1_d[l],))
        S.barrier()


def seg_b(C, l):
    S = C.S
    zxp = C.zx_out[l]
    yxs_in, yxa_in = C.yx_in[l]
    with ExitStack() as es:
        with ExitStack() as es1:
            zs = C.sb([128, 4, L], BF16, "zs", es1)
            XY = C.sb([128, 32 * 256], BF16, "XY", es1)
            Sst = C.sb([128, 16, 256, 2], F32, "Sst", es1)
            spre = [C.sb([128, 16, 256], BF16, f"spre{i}", es1) for i in range(2)]
            YO = C.sb([128, 32 * 256], BF16, "YO", es1)
            w1 = C.sb([128, 32, 128], BF16, "w1", es1)
            w2 = [C.sb([128, 32, 64], BF16, f"w2{i}", es1) for i in range(2)]
            w4 = [C.sb([128, 16, 128], BF16, f"w4{i}", es1) for i in range(2)]
            lam8 = C.sb([128, 16, 2], F32, "lam8", es1)
            lrr = C.sb([128, 16, 2], F32, "lrr", es1)
            lii = C.sb([128, 16, 2], F32, "lii", es1)
            tA = C.sb([128, 16, 2], F32, "tA", es1)
            tB = C.sb([128, 16, 2], F32, "tB", es1)
            dd = C.sb([128, 4], F32, "dd", es1)

            def ld(dst_ap, src_ap, rd, wr, pw=False):
                S.dma("sp", (lambda o, i: (lambda e: e.dma_start(out=o, in_=i)))(dst_ap, src_ap), reads=rd,
                      writes=() if pw else wr, pwrites=wr if pw else ())
            ld(w1[:, :, :], C.w1_d[l].t, (C.w1_d[l],), (w1,))
            ld(w2[0][:, :, :], C.w2re_d[l].t, (C.w2re_d[l],), (w2[0],))
            ld(w2[1][:, :, :], C.w2im_d[l].t, (C.w2im_d[l],), (w2[1],))
            ld(w4[0][:, :, :], C.w4re_d[l].t, (C.w4re_d[l],), (w4[0],))
            ld(w4[1][:, :, :], C.w4im_d[l].t, (C.w4im_d[l],), (w4[1],))
            ld(lam8[:, :, :], C.lam8_d[l].t, (C.lam8_d[l],), (lam8,))
            ld(dd[:, :], C.ssm_in["ssm_dd"].t[l], (C.ssm_in["ssm_dd"],), (dd,))
            for jj in range(2):
                def ldz(e, jj=jj):
                    j512 = C.jval(e, 512)
                    b0 = jj * 1024
                    return e.dma_start(out=zs[:, :, jj * NT:(jj + 1) * NT],
                                       in_=zxp[0].t[b0:b0 + 1024, :][bass.ds(j512, 512), :].rearrange("(c p) t -> p c t", p=128))
                S.dma("sp", ldz, reads=(zxp[0],), pwrites=(zs,))
            S.op("dve", lambda e: e.tensor_copy(lrr[:, :, 0], lam8[:, :, 0]), reads=(lam8,), writes=(lrr,))
            S.op("dve", lambda e: e.tensor_copy(lrr[:, :, 1], lam8[:, :, 0]), reads=(lam8, lrr), writes=(lrr,))
            S.op("dve", lambda e: e.tensor_scalar(lii[:, :, 0], lam8[:, :, 1], -1.0, None, ALU.mult), reads=(lam8,), writes=(lii,))
            S.op("dve", lambda e: e.tensor_copy(lii[:, :, 1], lam8[:, :, 1]), reads=(lam8, lii), writes=(lii,))
            zperm = TB(YO.t[:, :].rearrange("p (k t c) -> p k t c", k=4, t=8), YO.b)
            for cc in range(4):
                eng = ("act", "dve")[cc % 2]
                S.op(eng, cp(eng, zperm[:, cc, :, :], zs[:, cc, :].rearrange("p (c t) -> p t c", t=8)),
                     reads=(zs,), pwrites=(YO,))
            for cc in range(4):
                ld(C.zd.t[:, cc * 128:(cc + 1) * 128, :].rearrange("t p c -> p t c"), zperm[:, cc, :, :], (YO,), (C.zd,), pw=True)
            X = TB(XY.t[:, :].rearrange("p (g c) -> p g c", g=32), XY.b)
            for tau in range(8):
                ld(X[16 * tau:16 * tau + 16, :, :], C.zd.t[tau].rearrange("(g q) c -> q g c", q=16), (C.zd,), (XY,), pw=True)
            for gp in range(16):
                ps = C.psum()
                ops = []
                for gh in range(2):
                    g = gh * 16 + gp
                    for ri in range(2):
                        ops.append(mm(ps[64 * gh:64 * gh + 64, ri * 256:(ri + 1) * 256], w2[ri][:, g, :], X[:, g, :], True, True))
                S.op("pe", ops, reads=(w2[0], w2[1], XY), writes=(ps,))
                eng = ("act", "dve")[gp % 2]
                S.op(eng, cp(eng, Sst[:, gp, :, :], ps[:, :].rearrange("p (r c) -> p c r", r=2)), reads=(ps,), pwrites=(Sst,))
            for c in range(1, 256):
                S.op("dve", (lambda c: (lambda e: e.tensor_tensor(tA[:, :, :], Sst[:, :, c - 1, :], lrr[:, :, :], ALU.mult)))(c),
                     reads=(Sst, lrr), writes=(tA,))
                S.op("dve", (lambda c: (lambda e: e.tensor_tensor(tB[:, :, :], Sst[:, :, c - 1, ::-1], lii[:, :, :], ALU.mult)))(c),
                     reads=(Sst, lii), writes=(tB,))
                S.op("dve", (lambda c: (lambda e: e.tensor_tensor(tA[:, :, :], tA[:, :, :], tB[:, :, :], ALU.add)))(c),
                     reads=(tA, tB), writes=(tA,))
                S.op("dve", (lambda c: (lambda e: e.tensor_tensor(Sst[:, :, c, :], Sst[:, :, c, :], tA[:, :, :], ALU.add)))(c),
                     reads=(tA,), pwrites=(Sst,))
            for ri in range(2):
                eng = ("act", "dve")[ri]
                S.op("dve", (lambda ri: (lambda e: e.memset(spre[ri][:, :, 0:1], 0.0)))(ri), writes=(spre[ri],))
                S.op(eng, cp(eng, spre[ri][:, :, 1:256], Sst[:, :, 0:255, ri]), reads=(Sst, spre[ri]), writes=(spre[ri],))
            Yb = TB(YO.t[:, :].rearrange("p (g c) -> p g c", g=32), YO.b)
            for g2 in range(16):
                ps = C.psum()
                ops = []
                for k in range(2):
                    g = g2 * 2 + k
                    gh, gp = g // 16, g % 16
                    rows = slice(64 * gh, 64 * gh + 64)
                    o = ps[:, k * 256:(k + 1) * 256]
                    ops.append(mm(o, w1[:, g, :], X[:, g, :], True, False))
                    ops.append(mm(o, w4[0][rows, gp, :], spre[0][rows, gp, :], False, False))
                    ops.append(mm(o, w4[1][rows, gp, :], spre[1][rows, gp, :], False, True))
                S.op("pe", ops, reads=(w1, XY, w4[0], w4[1], spre[0], spre[1]), writes=(ps,))
                eng = ("act", "dve")[g2 % 2]
                S.op(eng, cp(eng, Yb[:, 2 * g2:2 * g2 + 2, :], ps[:, :].rearrange("p (g c) -> p g c", g=2)),
                     reads=(ps,), pwrites=(YO,))
            for t in range(8):
                ld(C.yd.t[t].rearrange("g p c -> p g c"), Yb[16 * t:16 * t + 16, :, :], (YO,), (C.yd,), pw=True)
            Ysel = TB(XY.t[:, :].rearrange("p (k t c) -> p k t c", k=4, t=8), XY.b)
            for cc in range(4):
                ld(Ysel[:, cc, :, :], C.yd.t[:, cc * 8:(cc + 1) * 8, :, :].rearrange("t g p c -> (g p) t c"), (C.yd,), (XY,), pw=True)
            yout = TB(YO.t[:, :].rearrange("p (k n) -> p k n", k=4), YO.b)
            for cc in range(4):
                S.op("dve", (lambda cc: (lambda e: e.scalar_tensor_tensor(
                    yout[:, cc, :].rearrange("p (c t) -> p c t", t=8),
                    zs[:, cc, :].rearrange("p (c t) -> p c t", t=8), dd[:, cc:cc + 1],
                    Ysel[:, cc, :, :].rearrange("p t c -> p c t"), ALU.mult, ALU.add)))(cc),
                    reads=(zs, dd, XY), pwrites=(YO,))
            for cc in range(4):
                ld(yxs_in.t[cc * 128:(cc + 1) * 128, :], yout[:, cc, :], (YO,), (yxs_in,), pw=True)
            C.coll_pair(yxs_in, C.yx_out[l][0], pop=False)
            S.barrier()
        with ExitStack() as es2:
            qT = C.sb([128, 4, L], BF16, "qT", es2)
            kT = C.sb([128, 4, L], BF16, "kT", es2)
            vst = C.sb([128, 4, L], BF16, "vst", es2)
            Vt = C.sb([128, 16, 4, 128], BF16, "Vt", es2)
            ao = C.sb([128, 4, L], BF16, "ao", es2)
            cosT = C.sb([32, L], F32, "cosT", es2)
            sinT = C.sb([32, L], F32, "sinT", es2)
            rt = [C.sb([32, 512], F32, f"rt{i}", es2) for i in range(2)]
            ksum = C.sb([128, 4, 8], F32, "ksum", es2)
            khi = C.sb([128, 4, 8], BF16, "khi", es2)
            klo = C.sb([128, 4, 8], BF16, "klo", es2)
            kd = C.sb([128, 4, 8], F32, "kd", es2)
            Gm = C.sb([128, 16, 8], F32, "Gm", es2)
            m8 = C.sb([128, 8], F32, "m8", es2)
            sel = C.sb([128, 16, 8], F32, "sel", es2)
            biasT = C.sb([8, 4, L], BF16, "biasT", es2)
            PT = [C.sb([128, 256], BF16, f"PT{i}", es2) for i in range(3)]
            rec = C.sb([128, 256], F32, "rec", es2)
            ld = lambda d, s_, rd, wr, pw=False: S.dma(
                "sp", (lambda o, i: (lambda e: e.dma_start(out=o, in_=i)))(d, s_), reads=rd,
                writes=() if pw else wr, pwrites=wr if pw else ())
            ld(cosT[:, :], C.consts["rot_cos"].t, (C.consts["rot_cos"],), (cosT,))
            ld(sinT[:, :], C.consts["rot_sin"].t, (C.consts["rot_sin"],), (sinT,))
            ldb = {id(qT): S.buf("qload"), id(kT): S.buf("kload"), id(vst): S.buf("vload")}
            for (dst, part) in ((qT, 1), (kT, 2), (vst, 3)):
                for jj in range(2):
                    def ldq(e, dst=dst, part=part, jj=jj):
                        j512 = C.jval(e, 512)
                        b0 = jj * 1024
                        return e.dma_start(out=dst[:, :, jj * NT:(jj + 1) * NT],
                                           in_=zxp[part].t[b0:b0 + 1024, :][bass.ds(j512, 512), :].rearrange("(c p) t -> p c t", p=128))
                    S.dma("sp", ldq, reads=(zxp[part],), pwrites=(dst, ldb[id(dst)]))
            for hh in range(4):
                for k4 in range(4):
                    ps = C.psum()
                    psb = ps.t[:, 0:256].bitcast(BF16)
                    S.op("pe", [(lambda o, i_: (lambda e: e.transpose(o, i_, C.identb[:, :])))(
                        psb[:, i * 128:(i + 1) * 128], vst[:, hh, (k4 * 4 + i) * 128:(k4 * 4 + i + 1) * 128])
                                for i in range(4)], reads=(vst, C.identb), writes=(ps,))
                    eng = ("act", "dve")[k4 % 2]
                    S.op(eng, cp(eng, Vt[:, k4 * 4:k4 * 4 + 4, hh, :], psb.rearrange("p (k d) -> p k d", k=4)),
                         reads=(ps,), pwrites=(Vt,))
            it = 0
            for x in (qT, kT):
                for hh in range(4):
                    for tt_ in range(4):
                        ts = slice(tt_ * 512, tt_ * 512 + 512)
                        ps = C.psum()
                        S.op("pe", mm(ps[0:32, :], C.permT[0:32, 0:32], x[0:32, hh, ts], True, True),
                             reads=(ldb[id(x)], C.permT), writes=(ps,))
                        r0, r1 = rt[0], rt[1]
                        S.op("dve", (lambda ps, ts: (lambda e: e.tensor_tensor(r0[:, :], ps[0:32, :], sinT[:, ts], ALU.mult)))(ps, ts),
                             reads=(ps, sinT), writes=(r0,))
                        S.op("dve", (lambda x, hh, ts: (lambda e: e.tensor_tensor(r1[:, :], x[0:32, hh, ts], cosT[:, ts], ALU.mult)))(x, hh, ts),
                             reads=(ldb[id(x)], cosT), writes=(r1,))
                        S.op("dve", (lambda x, hh, ts: (lambda e: e.tensor_tensor(x[0:32, hh, ts], r0[:, :], r1[:, :], ALU.add)))(x, hh, ts),
                             reads=(r0, r1), pwrites=(x,))
                        it += 1
            for hh in range(4):
                S.op("dve", (lambda hh: (lambda e: e.tensor_reduce(ksum[:, hh, :], kT[:, hh, :].rearrange("p (n t) -> p n t", t=256), AX.X, ALU.add)))(hh),
                     reads=(kT,), pwrites=(ksum,))
            S.op("dve", lambda e: e.tensor_copy(khi[:, :, :], ksum[:, :, :]), reads=(ksum,), writes=(khi,))
            S.op("dve", lambda e: e.tensor_tensor(kd[:, :, :], ksum[:, :, :], khi[:, :, :], ALU.subtract), reads=(ksum, khi), writes=(kd,))
            S.op("dve", lambda e: e.tensor_copy(klo[:, :, :], kd[:, :, :]), reads=(kd,), writes=(klo,))
            for hh in range(4):
                ps = C.psum()
                ops = []
                for qt in range(16):
                    o = ps[:, qt * 8:(qt + 1) * 8]
                    ops.append(mm(o, qT[:, hh, qt * 128:(qt + 1) * 128], khi[:, hh, :], True, False))
                    ops.append(mm(o, qT[:, hh, qt * 128:(qt + 1) * 128], klo[:, hh, :], False, True))
                S.op("pe", ops, reads=(qT, khi, klo), writes=(ps,))
                S.op("dve", (lambda ps: (lambda e: e.tensor_tensor(Gm[:, :, :], ps[:, 0:128].rearrange("p (q n) -> p q n", n=8), C.negm[:, :, :], ALU.add)))(ps),
                     reads=(ps, C.negm), writes=(Gm,))
                for qt in range(16):
                    S.op("dve", (lambda qt: (lambda e: e.max(m8[:, :], Gm[:, qt, :])))(qt), reads=(Gm,), writes=(m8,))
                    S.op("dve", (lambda qt: (lambda e: e.tensor_scalar(sel[:, qt, :], Gm[:, qt, :], m8[:, 2:3], 1.0, ALU.is_ge, ALU.subtract)))(qt),
                         reads=(Gm, m8), pwrites=(sel,))
                for q4 in range(4):
                    ps2 = C.psum()
                    S.op("pe", [(lambda o, i_: (lambda e: e.transpose(o, i_, C.ident[:, :])))(
                        ps2[0:8, i * 128:(i + 1) * 128], sel[:, q4 * 4 + i, :])
                                for i in range(4)], reads=(sel, C.ident), writes=(ps2,))
                    S.op("act", cp("act", biasT[0:8, hh, q4 * 512:(q4 + 1) * 512], ps2[0:8, :]), reads=(ps2,), pwrites=(biasT,))
            O_acc, S_acc = C.pbank[6], C.pbank[7]
            for hh in range(4):
                for QB in range(8):
                    qs = QB * 256
                    nkt = 2 * QB + 2
                    pend = []

                    def score(kt, hh=hh, QB=QB, qs=qs):
                        sc = C.psum()
                        ops = [mm(sc[:, 0:256], kT[:, hh, kt * 128:(kt + 1) * 128], qT[:, hh, qs:qs + 256], True, False)]
                        rd = [kT, qT]
                        if kt < 2 * QB:
                            ops.append(mm(sc[:, 0:256], C.oh[0:8, kt // 2, :], biasT[0:8, hh, qs:qs + 256], False, True))
                            rd += [C.oh, biasT]
                        else:
                            ops.append(mm(sc[:, 0:256], C.identb[:, :], C.cm[:, kt - 2 * QB, :], False, True))
                            rd += [C.identb, C.cm]
                        S.op("pe", ops, reads=tuple(rd), writes=(sc,))
                        pt = PT[kt % 3]
                        S.op("act", (lambda sc, pt: (lambda e: e.activation(pt[:, :], sc[:, 0:256], AF.Exp, scale=SCALE)))(sc, pt),
                             reads=(sc,), writes=(pt,))
                        return pt

                    def accum(kt, pt, hh=hh, nkt=nkt):
                        S.op("pe", [mm(O_acc[:, 0:256], Vt[:, kt, hh, :], pt[:, :], kt == 0, kt == nkt - 1),
                                    mm(S_acc[:, 0:256], C.ones[:, :], pt[:, :], kt == 0, kt == nkt - 1)],
                             reads=(Vt, pt, C.ones), writes=(), pwrites=(O_acc, S_acc))
                    for kt in range(nkt):
                        pend.append((kt, score(kt)))
                        if len(pend) > 1:
                            accum(*pend.pop(0))
                    while pend:
                        accum(*pend.pop(0))
                    S.op("dve", lambda e: e.reciprocal(rec[:, :], S_acc[:, 0:256]), reads=(S_acc,), writes=(rec,))
                    S.op("dve", (lambda hh, qs: (lambda e: e.tensor_tensor(ao[:, hh, qs:qs + 256], O_acc[:, 0:256], rec[:, :], ALU.mult)))(hh, qs),
                         reads=(O_acc, rec), pwrites=(ao,))
            for hh in range(4):
                ld(yxa_in.t[hh * 128:(hh + 1) * 128, :], ao[:, hh, :], (ao,), (yxa_in,), pw=True)
            S.barrier()
    C.coll_pair(yxa_in, C.yx_out[l][1], pop=True)


CONST_SPECS = {
    "ident": ([128, 128], F32), "identb": ([128, 128], BF16), "permT": ([32, 32], BF16),
    "oh": ([8, 8, 128], BF16), "cm": ([128, 2, 256], BF16), "negm": ([128, 16, 8], F32),
    "rot_cos": ([32, L], F32), "rot_sin": ([32, L], F32),
}
SSM_SPECS = {
    "ssm_ar": [DEPTH, 128, 16], "ssm_ai": [DEPTH, 128, 16], "ssm_ldt": [DEPTH, 128, 16],
    "ssm_ct_re": [DEPTH, 128, 16, 16], "ssm_ct_im": [DEPTH, 128, 16, 16],
    "ssm_bt_re": [DEPTH, 128, 16, 16], "ssm_bt_im": [DEPTH, 128, 16, 16], "ssm_dd": [DEPTH, 128, 4],
}


def build_program(ncores=8, mode="full"):
    nc = bass.Bass("TRN2", target_bir_lowering=False)
    NW = ncores
    pairs = [[2 * i, 2 * i + 1] for i in range(ncores // 2)]
    with ExitStack() as es:
        C = Ctx(nc, es)
        S = C.S
        ext = lambda name, shape, dt: TB(nc.dram_tensor(name, shape, dt, kind="ExternalInput").ap(), S.buf(name))
        C.xT = ext("xT", [D, NT], F32)
        C.pT = ext("pT", [DEPTH, PLE, NT], F32)
        C.gains_d = ext("gains", [128, (DEPTH * 4 + 1) * KC], F32)
        C.ssm_in = {k: ext(k, shp, F32) for k, shp in SSM_SPECS.items()}
        C.consts = {k: ext(k, shp, dt) for k, (shp, dt) in CONST_SPECS.items()}
        C.wf = [{n: WMat(C, f"{l}_{n}", K, M, NW) for n, K, M in WSPEC} for l in range(DEPTH)]
        SEGA_W = ["ffn1_w_gate", "ffn1_w_up", "ffn1_w_down", "w_in"]
        wsh = {n: ext("w_" + n, [DEPTH if mode == "full" else 1, C.wf[0][n].per_rank], F32) for n, K, M in WSPEC
               if mode == "full" or (mode == "sega" and n in SEGA_W)}
        if mode == "segb":
            C.zin = ext("zin", [8192, NT], BF16)
        C.oT = TB(nc.dram_tensor("oT", [D, NT], F32, kind="ExternalOutput").ap(), S.buf("oT"))
        C.zx_in = [[C.dram(f"zx_in{l}_{i}", [1024, NT], BF16) for i in range(4)] for l in range(DEPTH)]
        C.zx_out = [[C.dram(f"zx_out{l}_{i}", [2048, NT], BF16) for i in range(4)] for l in range(DEPTH)]
        C.yx_in = [[C.dram(f"yx_in{l}_{i}", [512, L], BF16) for i in range(2)] for l in range(DEPTH)]
        C.yx_out = [[C.dram(f"yx_out{l}_{i}", [1024, L], BF16) for i in range(2)] for l in range(DEPTH)]
        C.sg_d = [C.dram(f"sg_d{l}", [4096, NT], BF16) for l in range(DEPTH)]
        C.hsp = C.dram("hsp", [D, NT], F32)
        C.zd = C.dram("zd", [8, 512, 256], BF16)
        C.yd = C.dram("yd", [8, 32, 16, 256], BF16)
        C.w1_d = [C.dram(f"w1_d{l}", [128, 32, 128], BF16) for l in range(DEPTH)]
        C.w2re_d = [C.dram(f"w2re_d{l}", [128, 32, 64], BF16) for l in range(DEPTH)]
        C.w2im_d = [C.dram(f"w2im_d{l}", [128, 32, 64], BF16) for l in range(DEPTH)]
        C.w4re_d = [C.dram(f"w4re_d{l}", [128, 16, 128], BF16) for l in range(DEPTH)]
        C.w4im_d = [C.dram(f"w4im_d{l}", [128, 16, 128], BF16) for l in range(DEPTH)]
        C.lam8_d = [C.dram(f"lam8_d{l}", [128, 16, 2], F32) for l in range(DEPTH)]
        C.ones = C.sb([128, 128], BF16, "ones")
        C.eps_ap = C.sb([128, 1], F32, "eps")
        C.gains = C.sb([128, (DEPTH * 4 + 1) * KC], F32, "gains")

        mid = ["ssm_w_glu", "w_branch_ssm", "w_branch_attn", "w_out"]
        tail = ["ple_w_gate", "ple_w_up"]

        def ffn_chunks(l, pre):
            out = []
            g, u, d = C.wf[l][pre + "_w_gate"], C.wf[l][pre + "_w_up"], C.wf[l][pre + "_w_down"]
            for i in range(len(g.chunks)):
                out += [(l, pre + "_w_gate", i), (l, pre + "_w_up", i), (l, pre + "_w_down", i)]
            return out

        quads = [[0, 1, 2, 3], [4, 5, 6, 7]]
        xpairs = [[0, 4], [1, 5], [2, 6], [3, 7]]

        def coll(groups, src, dst):
            S.dma("pool", lambda e: e.collective_compute("AllGather", ALU.bypass, replica_groups=groups,
                                                         ins=[src.t], outs=[dst.t]),
                  reads=(src,), writes=(dst,), inc=1)

        def chunk_list(l, names):
            return [(l, n, i) for n in names for i in range(len(C.wf[l][n].chunks))]

        def bounce(l, n, i):
            w = C.wf[l][n]
            r0, nr, c0, ncw = w.chunks[i]
            src = wsh[n].t[l, w.off[i]:w.off[i] + (nr // NW) * ncw].rearrange("(r c) -> r c", c=ncw)
            S.dma("pool", (lambda o, s_: (lambda e: e.dma_start(out=o, in_=s_)))(w.bounce[i].t, src),
                  reads=(wsh[n],), writes=(w.bounce[i],))

        def gather(chunks):
            LOOK = 12
            for k in range(min(LOOK, len(chunks))):
                bounce(*chunks[k])
            if NW == 8 and chunks:
                l0, n0, i0 = chunks[0]
                coll(quads, C.wf[l0][n0].bounce[i0], C.wf[l0][n0].half[i0])
            for k, (l, n, i) in enumerate(chunks):
                w = C.wf[l][n]
                if NW == 8:
                    if k + 1 < len(chunks):
                        l1, n1, i1 = chunks[k + 1]
                        coll(quads, C.wf[l1][n1].bounce[i1], C.wf[l1][n1].half[i1])
                    coll(xpairs, w.half[i], w.full[i])
                else:
                    coll([list(range(NW))], w.bounce[i], w.full[i])
                if k + LOOK < len(chunks):
                    bounce(*chunks[k + LOOK])

        def coll_pair(src, dst, pop=True):
            S.dma("pool", lambda e: e.collective_compute("AllGather", ALU.bypass, replica_groups=pairs,
                                                         ins=[src.t], outs=[dst.t]),
                  reads=(src,), writes=(dst,), inc=1)
            if pop and C.gbatches:
                gather(C.gbatches.pop(0))
        C.coll_pair = coll_pair

        def load_consts(names, scope):
            for k in names:
                shp, dt = CONST_SPECS[k]
                t = C.sb(shp, dt, k, scope)
                idx = tuple(slice(None) for _ in shp)
                S.dma("sp", (lambda o, i: (lambda e: e.dma_start(out=o, in_=i)))(t.t[idx], C.consts[k].t),
                      reads=(C.consts[k],), writes=(t,))
                setattr(C, k, t)

        def body():
            def cpart(l):
                f2 = ffn_chunks(l, "ffn2")
                return chunk_list(l, mid) + f2[:9], f2[9:] + chunk_list(l, tail)
            a0 = ffn_chunks(0, "ffn1") + chunk_list(0, ["w_in"])
            a1 = ffn_chunks(1, "ffn1") + chunk_list(1, ["w_in"])
            c0a, c0b = cpart(0)
            c1a, c1b = cpart(1)
            C.gbatches = [c0a, c0b + a1, c1a, c1b]
            S.op("dve", lambda e: e.memset(C.ones[:, :], 1.0), writes=(C.ones,))
            S.op("dve", lambda e: e.memset(C.eps_ap[:, :], EPS), writes=(C.eps_ap,))
            S.dma("sp", lambda e: e.dma_start(out=C.gains[:, :], in_=C.gains_d.t), reads=(C.gains_d,), writes=(C.gains,))
            gather(a0)
            with ExitStack() as p0:
                load_consts(["ident"], p0)
                for l in range(DEPTH):
                    ssm_prep(C, l)
                S.barrier()
            for l in range(DEPTH):
                if l == 0:
                    ph = ExitStack()
                    T = TokTiles(C, ph)
                    C.wphase_begin()
                    S.dma("sp", (lambda h_: (lambda e: e.dma_start(out=h_[:, :, :], in_=C.xT.t.rearrange("(k p) t -> p k t", p=128))))(T.h),
                          reads=(C.xT,), writes=(T.h,))
                seg_a(C, T, l)
                S.barrier()
                ph.close()
                with ExitStack() as pb:
                    load_consts(["ident", "identb", "permT", "oh", "cm", "negm"], pb)
                    seg_b(C, l)
                ph = ExitStack()
                T = TokTiles(C, ph)
                C.wphase_begin()
                seg_c(C, T, l, last=(l == DEPTH - 1))
            S.barrier()
            ph.close()

        def plan_ffn(l, pre):
            out = []
            for ch in range(DFF // 512):
                out += [(pre + "_w_gate", 0, KC, ch * 512, 512), (pre + "_w_up", 0, KC, ch * 512, 512),
                        (pre + "_w_down", ch * 512, 4, 0, D)]
            return [(l,) + b for b in out]

        def plan_a(l):
            return plan_ffn(l, "ffn1") + [(l, "w_in", 0, KC, cb * 512, 512) for cb in range(16)]

        def plan_c(l):
            out = []
            for hf in range(2):
                out += [("ssm_w_glu", 0, 8, 0, 1024), ("ssm_w_glu", 0, 8, 1024, 1024)]
                for half in range(2):
                    out += [("w_branch_ssm", 0, 8, half * 1024, 1024), ("w_branch_attn", 0, 8, half * 1024, 1024)]
                out += [("w_out", 0, KC, cb * 512, 512) for cb in range(4)]
            out = [(l,) + b for b in out] + plan_ffn(l, "ffn2")
            for cb in range(4):
                out += [(l, "ple_w_gate", 0, KC, cb * 512, 512), (l, "ple_w_up", 0, 2, cb * 512, 512)]
            return out
        def body_prep():
            with ExitStack() as p0:
                load_consts(["ident"], p0)
                ssm_prep(C, 0)
                S.barrier()
            toks = []
            for nm, src in (("o_w1", C.w1_d[0]), ("o_w2re", C.w2re_d[0]), ("o_w2im", C.w2im_d[0]),
                            ("o_w4re", C.w4re_d[0]), ("o_w4im", C.w4im_d[0]), ("o_lam8", C.lam8_d[0])):
                shp = [int(x) for x in src.t.shape]
                dt = F32 if nm == "o_lam8" else BF16
                o = nc.dram_tensor(nm, shp, dt, kind="ExternalOutput").ap()
                toks.append(S.dma("sp", (lambda o, s_: (lambda e: e.dma_start(out=o, in_=s_)))(o, src.t), reads=(src,)))
            S.wait_all("sp", toks)

        def body_segb():
            S.op("dve", lambda e: e.memset(C.ones[:, :], 1.0), writes=(C.ones,))
            with ExitStack() as p0:
                load_consts(["ident"], p0)
                ssm_prep(C, 0)
                S.barrier()
            for jj in range(2):
                for part in range(4):
                    S.dma("sp", (lambda jj, part: (lambda e: e.dma_start(
                        out=C.zx_out[0][part].t[jj * 1024:(jj + 1) * 1024, :],
                        in_=C.zin.t[jj * 4096 + part * 1024:jj * 4096 + (part + 1) * 1024, :])))(jj, part),
                        reads=(C.zin,), pwrites=(C.zx_out[0][part],))
            C.coll_pair = lambda a, b, pop=True: None
            with ExitStack() as pb:
                load_consts(["ident", "identb", "permT", "oh", "cm", "negm"], pb)
                seg_b(C, 0)
            o = nc.dram_tensor("o_yx", [1024, L], BF16, kind="ExternalOutput").ap()
            tk = S.dma("sp", lambda e: e.dma_start(out=o[0:512, :], in_=C.yx_in[0][0].t), reads=(C.yx_in[0][0],))
            tk2 = S.dma("sp", lambda e: e.dma_start(out=o[512:1024, :], in_=C.yx_in[0][1].t), reads=(C.yx_in[0][1],))
            S.wait_all("sp", [tk, tk2])

        def body_sega():
            S.op("dve", lambda e: e.memset(C.ones[:, :], 1.0), writes=(C.ones,))
            S.op("dve", lambda e: e.memset(C.eps_ap[:, :], EPS), writes=(C.eps_ap,))
            S.dma("sp", lambda e: e.dma_start(out=C.gains[:, :], in_=C.gains_d.t), reads=(C.gains_d,), writes=(C.gains,))
            C.gbatches = []
            gather(ffn_chunks(0, "ffn1") + chunk_list(0, ["w_in"]))
            ph = ExitStack()
            T = TokTiles(C, ph)
            C.wphase_begin()
            S.dma("sp", (lambda h_: (lambda e: e.dma_start(out=h_[:, :, :], in_=C.xT.t.rearrange("(k p) t -> p k t", p=128))))(T.h),
                  reads=(C.xT,), writes=(T.h,))
            seg_a(C, T, 0)
            S.barrier()
            ph.close()
            o1 = nc.dram_tensor("o_zx", [8192, NT], BF16, kind="ExternalOutput").ap()
            o2 = nc.dram_tensor("o_h", [D, NT], F32, kind="ExternalOutput").ap()
            tks = [S.dma("sp", (lambda i: (lambda e: e.dma_start(out=o1[i * 2048:(i + 1) * 2048, :], in_=C.zx_out[0][i].t)))(i),
                         reads=(C.zx_out[0][i],)) for i in range(4)]
            tks.append(S.dma("sp", lambda e: e.dma_start(out=o2, in_=C.hsp.t), reads=(C.hsp,)))
            S.wait_all("sp", tks)

        if mode == "sega":
            for (l, n, k0, kc, c0, ncols) in plan_a(0):
                C.wplan.append((C.wf[l][n], k0, kc, c0, ncols))
            C.wphase.append(0)
            body_sega()
            S.emit()
            return nc
        if mode == "prep":
            body_prep()
            S.emit()
            return nc
        if mode == "segb":
            body_segb()
            S.emit()
            return nc
        phases = [plan_a(0), plan_c(0) + plan_a(1), plan_c(1)]
        for ph_ in phases:
            C.wphase.append(len(C.wplan))
            for (l, n, k0, kc, c0, ncols) in ph_:
                C.wplan.append((C.wf[l][n], k0, kc, c0, ncols))
        body()
        S.emit()
    return nc


def _consts():
    bf = ml_dtypes.bfloat16
    c = {}
    c["ident"] = np.eye(128, dtype=np.float32)
    c["identb"] = np.eye(128, dtype=np.float32).astype(bf)
    pm = np.zeros((32, 32), np.float32)
    for d in range(16):
        pm[d + 16, d] = -1.0
        pm[d, d + 16] = 1.0
    c["permT"] = pm.astype(bf)
    oh = np.zeros((8, 8, 128), np.float32)
    for n in range(8):
        oh[n, n, :] = BIG
    c["oh"] = oh.astype(bf)
    cm = np.zeros((128, 2, 256), np.float32)
    k = np.arange(128)[:, None]
    q = np.arange(256)[None, :]
    cm[:, 0, :] = np.where(k > q, -BIG, 0.0)
    cm[:, 1, :] = np.where(k + 128 > q, -BIG, 0.0)
    c["cm"] = cm.astype(bf)
    negm = np.zeros((128, 16, 8), np.float32)
    for qt in range(16):
        for n in range(8):
            if n >= qt // 2:
                negm[:, qt, n] = -1.0e30
    c["negm"] = negm
    pos = np.arange(L, dtype=np.float32)
    inv_freq = (1.0 / (np.float32(500000.0) ** (np.arange(0, 32, 2, dtype=np.float32) / np.float32(32)))).astype(np.float32)
    ang = pos[None, :] * inv_freq[:, None]
    c["rot_cos"] = np.concatenate([np.cos(ang), np.cos(ang)], 0).astype(np.float32)
    c["rot_sin"] = np.concatenate([np.sin(ang), np.sin(ang)], 0).astype(np.float32)
    return c


def make_in_maps(inputs, ncores=8, batch_of_core=None):
    NW = ncores
    consts = _consts()
    x = inputs["x"]
    p = inputs["p"]
    gl = [inputs[n][l] for l in range(DEPTH) for n in NORMS] + [inputs["final_norm"]]
    gains = np.ascontiguousarray(np.concatenate([g.reshape(KC, 128).T for g in gl], axis=1).astype(np.float32))
    in_maps = []
    for r in range(ncores):
        b = (r // 2) if batch_of_core is None else batch_of_core[r]
        j = r % 2
        m = {}
        m["xT"] = np.ascontiguousarray(x[b, j * NT:(j + 1) * NT, :].T)
        m["pT"] = np.ascontiguousarray(np.transpose(p[:, b, j * NT:(j + 1) * NT, :], (0, 2, 1)))
        m["gains"] = gains
        gs = slice(32 * j, 32 * j + 32)

        def nlay(a):
            a = a.reshape((DEPTH, 2, 16, 64) + a.shape[3:])
            a = np.moveaxis(a, 3, 2)
            return np.ascontiguousarray(a.reshape((DEPTH, 128, 16) + a.shape[4:]))
        m["ssm_ar"] = nlay(inputs["ssm_a_re"][:, gs])
        m["ssm_ai"] = nlay(inputs["ssm_a_im"][:, gs])
        m["ssm_ldt"] = nlay(np.broadcast_to(inputs["ssm_log_dt"][:, gs, None], (DEPTH, 32, 64)))
        m["ssm_bt_re"] = nlay(inputs["ssm_b_re"][:, gs])
        m["ssm_bt_im"] = nlay(inputs["ssm_b_im"][:, gs])
        m["ssm_ct_re"] = nlay(np.transpose(inputs["ssm_c_re"][:, gs], (0, 1, 3, 2)))
        m["ssm_ct_im"] = nlay(np.transpose(inputs["ssm_c_im"][:, gs], (0, 1, 3, 2)))
        m["ssm_dd"] = np.ascontiguousarray(np.transpose(
            inputs["ssm_d"][:, 512 * j:512 * j + 512].reshape(DEPTH, 4, 128), (0, 2, 1)))
        for name, K, M in WSPEC:
            w = inputs[name]
            parts = []
            for (r0, nr, c0, ncw) in wchunks(K, M):
                q = nr // NW
                parts.append(w[:, r0 + r * q:r0 + (r + 1) * q, c0:c0 + ncw].reshape(DEPTH, -1))
            m["w_" + name] = np.ascontiguousarray(np.concatenate(parts, axis=1))
        m.update(consts)
        in_maps.append(m)
    return in_maps


_NC_CACHE = {}


def kernel(**inputs):
    inputs = {k: np.asarray(v) for k, v in inputs.items()}
    if 8 not in _NC_CACHE:
        _NC_CACHE[8] = build_program(8)
    nc = _NC_CACHE[8]
    in_maps = make_in_maps(inputs, 8)
    res = run_bass_kernel_spmd(nc, in_maps, core_ids=list(range(8)))
    out = np.empty((4, L, D), np.float32)
    for r in range(8):
        out[r // 2, (r % 2) * NT:(r % 2 + 1) * NT, :] = res.results[r]["oT"].T
    return out
```
